# Optimizing a Trainium2 kernel written in Bass

```python
import math
import jax, jax.numpy as jnp
from jax import lax
import numpy as np

D_MODEL = 1024
BATCH = 4
SEQ = 4096
DEPTH = 2
DEC_BATCH = 32
DEC_SEQ = 64
PAST_LEN = 1024

CHUNK = 64
HEAD_DIM = 64
A_HEADS = 8
A_WIDTH = A_HEADS * HEAD_DIM
A_LEFT_CHUNKS = 8
A_BAND = (A_LEFT_CHUNKS + 1) * CHUNK
A_WIN = A_LEFT_CHUNKS * CHUNK
A_REL_CLIP = 128
B_HEADS = 8
B_HEAD_DIM = 64
B_WIDTH = B_HEADS * B_HEAD_DIM
B_GROUPS = 2
B_STATE = 64
B_CONV = 4
B_XBC = B_WIDTH + 2 * B_GROUPS * B_STATE
C_HEADS = 8
C_WIDTH = C_HEADS * HEAD_DIM
C_BLOCK = 128
FORGET_BIAS_INIT = 3.0
M_HEADS = 4
M_WIDTH = M_HEADS * HEAD_DIM
N_MEM = 256
N_BRANCH = 4
EPS = 1e-6
NEG = -1e30
F32 = jnp.float32

SPLITS = (('a_q', A_WIDTH), ('a_k', A_WIDTH), ('a_v', A_WIDTH), ('a_z', A_WIDTH),
          ('b_z', B_WIDTH), ('b_xbc', B_XBC), ('b_dt', B_HEADS),
          ('c_q', C_WIDTH), ('c_k', C_WIDTH), ('c_v', C_WIDTH), ('c_f', C_HEADS), ('c_z', C_WIDTH),
          ('m_q', M_WIDTH), ('m_z', M_WIDTH),
          ('gate', N_BRANCH * D_MODEL))
D_IN = sum(w for _, w in SPLITS)

kernel_name = 'hybrid_streaming_encoder_step'


def rmsnorm(x, g):
    xf = x.astype(F32)
    y = xf * lax.rsqrt(jnp.mean(xf * xf, axis=-1, keepdims=True) + EPS)
    return (y * g.astype(F32)).astype(x.dtype)


def heads(t, n):
    return t.reshape(t.shape[:-1] + (n, t.shape[-1] // n))


def split_proj(h, w_in):
    u = h @ w_in
    parts = {}
    off = 0
    for name, width in SPLITS:
        parts[name] = u[..., off:off + width]
        off += width
    return parts


def qkv_heads(u, pre, n, gq, gk):
    q = rmsnorm(heads(u[pre + '_q'], n), gq)
    k = rmsnorm(heads(u[pre + '_k'], n), gk)
    v = heads(u[pre + '_v'], n)
    return q, k, v


def rel_bias(table, dist):
    idx = jnp.clip(dist, -A_REL_CLIP, A_REL_CLIP) + A_REL_CLIP
    return jnp.moveaxis(table[idx].astype(F32), -1, 0)


def chunk_band_prompt(q, k, v, table):
    b, S, H, dh = q.shape
    nc = S // CHUNK
    nb = A_LEFT_CHUNKS + 1

    def band(t):
        pad = jnp.zeros((b, A_WIN, H, dh), t.dtype)
        tc = jnp.concatenate([pad, t], axis=1).reshape(b, nc + A_LEFT_CHUNKS, CHUNK, H, dh)
        return jnp.stack([tc[:, j:j + nc] for j in range(nb)], axis=2).reshape(b, nc, A_BAND, H, dh)

    kb, vb = band(k), band(v)
    qc = q.reshape(b, nc, CHUNK, H, dh)
    s = jnp.einsum('bcqhd,bckhd->bchqk', qc, kb).astype(F32) * (dh ** -0.5)
    dist = A_WIN + jnp.arange(CHUNK)[:, None] - jnp.arange(A_BAND)[None, :]
    s = s + rel_bias(table, dist)
    key_chunk = jnp.arange(nc)[:, None] - A_LEFT_CHUNKS + jnp.arange(A_BAND)[None, :] // CHUNK
    s = jnp.where((key_chunk >= 0)[None, :, None, None, :], s, NEG)
    p = jax.nn.softmax(s, axis=-1).astype(v.dtype)
    return jnp.einsum('bchqk,bckhd->bcqhd', p, vb).reshape(b, S, H, dh)


def chunk_band_sample(q, k_new, v_new, k_cache, v_cache, table):
    Lc, T, dh = k_cache.shape[1], q.shape[1], q.shape[-1]
    k = jnp.concatenate([k_cache.astype(k_new.dtype), k_new], axis=1)
    v = jnp.concatenate([v_cache.astype(v_new.dtype), v_new], axis=1)
    s = jnp.einsum('bqhd,bkhd->bhqk', q, k).astype(F32) * (dh ** -0.5)
    dist = (Lc + jnp.arange(T))[:, None] - jnp.arange(Lc + T)[None, :]
    s = s + rel_bias(table, dist)
    p = jax.nn.softmax(s, axis=-1).astype(v.dtype)
    return jnp.einsum('bhqk,bkhd->bqhd', p, v)


def forget_log(u, fbias):
    return jax.nn.log_sigmoid(u['c_f'].astype(F32) + fbias.astype(F32))


def fox_prompt(q, k, v, logf):
    b, S, H, dh = q.shape
    cum = jnp.cumsum(logf, axis=1).transpose(0, 2, 1)
    kpos = jnp.arange(S)

    def block(i):
        start = i * C_BLOCK
        qb = lax.dynamic_slice_in_dim(q, start, C_BLOCK, axis=1)
        cb = lax.dynamic_slice_in_dim(cum, start, C_BLOCK, axis=2)
        s = jnp.einsum('bqhd,bkhd->bhqk', qb, k).astype(F32) * (dh ** -0.5)
        s = s + cb[..., :, None] - cum[..., None, :]
        qpos = start + jnp.arange(C_BLOCK)
        s = jnp.where(kpos[None, :] <= qpos[:, None], s, NEG)
        p = jax.nn.softmax(s, axis=-1).astype(v.dtype)
        return jnp.einsum('bhqk,bkhd->bqhd', p, v)

    o = lax.map(block, jnp.arange(S // C_BLOCK))
    return jnp.moveaxis(o, 0, 1).reshape(b, S, H, dh)


def fox_sample(q, k_new, v_new, logf_new, k_cache, v_cache, logf_cache):
    T, dh = q.shape[1], q.shape[-1]
    P = k_cache.shape[1]
    k = jnp.concatenate([k_cache.astype(k_new.dtype), k_new], axis=1)
    v = jnp.concatenate([v_cache.astype(v_new.dtype), v_new], axis=1)
    cum_c = jnp.cumsum(logf_cache.astype(F32), axis=1)
    cum_n = cum_c[:, -1:] + jnp.cumsum(logf_new, axis=1)
    cum_k = jnp.concatenate([cum_c, cum_n], axis=1).transpose(0, 2, 1)
    cum_q = cum_n.transpose(0, 2, 1)
    s = jnp.einsum('bqhd,bkhd->bhqk', q, k).astype(F32) * (dh ** -0.5)
    s = s + cum_q[..., :, None] - cum_k[..., None, :]
    visible = jnp.arange(P + T)[None, :] <= (P + jnp.arange(T))[:, None]
    s = jnp.where(visible, s, NEG)
    p = jax.nn.softmax(s, axis=-1).astype(v.dtype)
    return jnp.einsum('bhqk,bkhd->bqhd', p, v)


def causal_conv(xbc, conv_state, w, bias):
    L = xbc.shape[1]
    xp = jnp.concatenate([conv_state.astype(xbc.dtype), xbc], axis=1)
    y = bias
    for tap in range(B_CONV):
        y = y + xp[:, tap:tap + L] * w[tap]
    return jax.nn.silu(y), xp[:, -(B_CONV - 1):]


def ssd(x, dt, A, Bm, Cm, h0):
    b, L, H, P = x.shape
    N = Bm.shape[-1]
    Q = CHUNK if L % CHUNK == 0 else L
    nc = L // Q
    rep = H // B_GROUPS
    Bc = jnp.repeat(Bm, rep, axis=2).reshape(b, nc, Q, H, N)
    Cc = jnp.repeat(Cm, rep, axis=2).reshape(b, nc, Q, H, N)
    xc = x.reshape(b, nc, Q, H, P)
    dtc = dt.reshape(b, nc, Q, H)
    acum = jnp.cumsum(dtc * A, axis=2)
    seg = acum[:, :, :, None, :] - acum[:, :, None, :, :]
    causal = jnp.tril(jnp.ones((Q, Q), bool))
    Lmat = jnp.exp(jnp.where(causal[None, None, :, :, None], seg, -jnp.inf))
    xdt = xc * dtc[..., None]
    CB = jnp.einsum('bclhn,bcshn->bclsh', Cc, Bc)
    y_diag = jnp.einsum('bclsh,bcshp->bclhp', CB * Lmat, xdt)
    decay = jnp.exp(acum[:, :, -1:, :] - acum)
    states = jnp.einsum('bcshn,bcsh,bcshp->bchpn', Bc, decay, xdt)
    chunk_decay = jnp.exp(acum[:, :, -1, :])

    def step(h, inp):
        st, dec = inp
        return dec[:, :, None, None] * h + st, h

    h_last, h_prev = lax.scan(step, h0, (jnp.moveaxis(states, 1, 0), jnp.moveaxis(chunk_decay, 1, 0)))
    h_prev = jnp.moveaxis(h_prev, 0, 1)
    y_off = jnp.einsum('bclhn,bchpn,bclh->bclhp', Cc, h_prev, jnp.exp(acum))
    return (y_diag + y_off).reshape(b, L, H, P), h_last


def mamba_branch(u, conv_state, h0, conv_w, conv_b, dt_bias, a_log, d_skip, norm_g):
    xbc, new_conv = causal_conv(u['b_xbc'], conv_state, conv_w, conv_b)
    b, L, _ = xbc.shape
    xs = heads(xbc[..., :B_WIDTH], B_HEADS)
    Bm = heads(xbc[..., B_WIDTH:B_WIDTH + B_GROUPS * B_STATE], B_GROUPS)
    Cm = heads(xbc[..., B_WIDTH + B_GROUPS * B_STATE:], B_GROUPS)
    dt = jax.nn.softplus(u['b_dt'].astype(F32) + dt_bias.astype(F32))
    A = -jnp.exp(a_log.astype(F32))
    xf = xs.astype(F32)
    y, h_last = ssd(xf, dt, A, Bm.astype(F32), Cm.astype(F32), h0.astype(F32))
    y = y + d_skip.astype(F32)[:, None] * xf
    y = y.reshape(b, L, B_WIDTH).astype(xbc.dtype)
    y = rmsnorm(y * jax.nn.silu(u['b_z']), norm_g)
    return y, new_conv, h_last.astype(xbc.dtype)


def memory_kv(mem, g, w_mkv, gk):
    kv = rmsnorm(mem, g) @ w_mkv
    k = rmsnorm(heads(kv[..., :M_WIDTH], M_HEADS), gk)
    v = heads(kv[..., M_WIDTH:], M_HEADS)
    return k, v


def cross_attend(q, k, v):
    dh = q.shape[-1]
    s = jnp.einsum('bqhd,bkhd->bhqk', q, k.astype(q.dtype)).astype(F32) * (dh ** -0.5)
    p = jax.nn.softmax(s, axis=-1).astype(q.dtype)
    return jnp.einsum('bhqk,bkhd->bqhd', p, v.astype(q.dtype))


def merge(u, o_a, o_b, o_c, o_m, w_pa, w_pb, w_pc, w_pm, w_out):
    def flat(o):
        return o.reshape(o.shape[:2] + (-1,))
    gates = jax.nn.sigmoid(u['gate'].astype(F32)).astype(o_b.dtype)
    g_a, g_b, g_c, g_m = jnp.split(gates, N_BRANCH, axis=-1)
    p_a = (flat(o_a) * jax.nn.silu(u['a_z'])) @ w_pa
    p_b = o_b @ w_pb
    p_c = (flat(o_c) * jax.nn.silu(u['c_z'])) @ w_pc
    p_m = (flat(o_m) * jax.nn.silu(u['m_z'])) @ w_pm
    return (g_a * p_a + g_b * p_b + g_c * p_c + g_m * p_m) @ w_out


def setup_inputs(seed: int = 0) -> dict:
    key = jax.random.key(seed)
    ks = iter(jax.random.split(key, 48))

    def nrm(shape, scale=1.0):
        return scale * jax.random.normal(next(ks), shape, F32)

    la = min(A_WIN, PAST_LEN)
    dt0 = jnp.exp(jax.random.uniform(next(ks), (DEPTH, B_HEADS), F32, math.log(1e-3), math.log(1e-1)))
    return {
        'x_prompt': nrm((BATCH, SEQ, D_MODEL)),
        'x_sample': nrm((DEC_BATCH, DEC_SEQ, D_MODEL)),
        'mem_prompt': nrm((BATCH, N_MEM, D_MODEL)),
        'cache_a_k': nrm((DEPTH, DEC_BATCH, la, A_HEADS, HEAD_DIM)),
        'cache_a_v': nrm((DEPTH, DEC_BATCH, la, A_HEADS, HEAD_DIM)),
        'cache_c_k': nrm((DEPTH, DEC_BATCH, PAST_LEN, C_HEADS, HEAD_DIM)),
        'cache_c_v': nrm((DEPTH, DEC_BATCH, PAST_LEN, C_HEADS, HEAD_DIM)),
        'cache_c_logf': jax.nn.log_sigmoid(FORGET_BIAS_INIT + nrm((DEPTH, DEC_BATCH, PAST_LEN, C_HEADS), 0.5)),
        'state_b_ssm': nrm((DEPTH, DEC_BATCH, B_HEADS, B_HEAD_DIM, B_STATE), 0.1),
        'state_b_conv': nrm((DEPTH, DEC_BATCH, B_CONV - 1, B_XBC)),
        'cache_mem_k': nrm((DEPTH, DEC_BATCH, N_MEM, M_HEADS, HEAD_DIM)),
        'cache_mem_v': nrm((DEPTH, DEC_BATCH, N_MEM, M_HEADS, HEAD_DIM)),
        'g_norm': 1.0 + nrm((DEPTH, D_MODEL), 0.02),
        'w_in': nrm((DEPTH, D_MODEL, D_IN), D_MODEL ** -0.5),
        'a_qnorm': 1.0 + nrm((DEPTH, HEAD_DIM), 0.02),
        'a_knorm': 1.0 + nrm((DEPTH, HEAD_DIM), 0.02),
        'a_rel': nrm((DEPTH, 2 * A_REL_CLIP + 1, A_HEADS), 0.1),
        'b_conv_w': nrm((DEPTH, B_CONV, B_XBC), B_CONV ** -0.5),
        'b_conv_b': nrm((DEPTH, B_XBC), 0.02),
        'b_dt_bias': dt0 + jnp.log(-jnp.expm1(-dt0)),
        'b_a_log': jnp.log(jax.random.uniform(next(ks), (DEPTH, B_HEADS), F32, 1.0, 16.0)),
        'b_d': 1.0 + nrm((DEPTH, B_HEADS), 0.02),
        'b_norm': 1.0 + nrm((DEPTH, B_WIDTH), 0.02),
        'c_qnorm': 1.0 + nrm((DEPTH, HEAD_DIM), 0.02),
        'c_knorm': 1.0 + nrm((DEPTH, HEAD_DIM), 0.02),
        'c_fbias': FORGET_BIAS_INIT + nrm((DEPTH, C_HEADS), 0.1),
        'm_norm': 1.0 + nrm((DEPTH, D_MODEL), 0.02),
        'w_mkv': nrm((DEPTH, D_MODEL, 2 * M_WIDTH), D_MODEL ** -0.5),
        'm_qnorm': 1.0 + nrm((DEPTH, HEAD_DIM), 0.02),
        'm_knorm': 1.0 + nrm((DEPTH, HEAD_DIM), 0.02),
        'w_pa': nrm((DEPTH, A_WIDTH, D_MODEL), A_WIDTH ** -0.5),
        'w_pb': nrm((DEPTH, B_WIDTH, D_MODEL), B_WIDTH ** -0.5),
        'w_pc': nrm((DEPTH, C_WIDTH, D_MODEL), C_WIDTH ** -0.5),
        'w_pm': nrm((DEPTH, M_WIDTH, D_MODEL), M_WIDTH ** -0.5),
        'w_out': nrm((DEPTH, D_MODEL, D_MODEL), D_MODEL ** -0.5),
    }


def reference(x_prompt, x_sample, mem_prompt, cache_a_k, cache_a_v, cache_c_k, cache_c_v, cache_c_logf,
              state_b_ssm, state_b_conv, cache_mem_k, cache_mem_v,
              g_norm, w_in, a_qnorm, a_knorm, a_rel, b_conv_w, b_conv_b, b_dt_bias, b_a_log, b_d, b_norm,
              c_qnorm, c_knorm, c_fbias, m_norm, w_mkv, m_qnorm, m_knorm, w_pa, w_pb, w_pc, w_pm, w_out):
    xp, xd = x_prompt, x_sample
    bp = xp.shape[0]
    pa_k, pa_v, pc_k, pc_v, pc_f, pb_s, pb_c, pm_k, pm_v = [], [], [], [], [], [], [], [], []
    sa_k, sa_v, sc_k, sc_v, sc_f, sb_s, sb_c = [], [], [], [], [], [], []
    for l in range(DEPTH):
        u = split_proj(rmsnorm(xp, g_norm[l]), w_in[l])
        qa, ka, va = qkv_heads(u, 'a', A_HEADS, a_qnorm[l], a_knorm[l])
        o_a = chunk_band_prompt(qa, ka, va, a_rel[l])
        o_b, conv_p, ssm_p = mamba_branch(
            u, jnp.zeros((bp, B_CONV - 1, B_XBC), xp.dtype), jnp.zeros((bp, B_HEADS, B_HEAD_DIM, B_STATE), F32),
            b_conv_w[l], b_conv_b[l], b_dt_bias[l], b_a_log[l], b_d[l], b_norm[l])
        qc, kc, vc = qkv_heads(u, 'c', C_HEADS, c_qnorm[l], c_knorm[l])
        logf = forget_log(u, c_fbias[l])
        o_c = fox_prompt(qc, kc, vc, logf)
        mk, mv = memory_kv(mem_prompt, m_norm[l], w_mkv[l], m_knorm[l])
        qm = rmsnorm(heads(u['m_q'], M_HEADS), m_qnorm[l])
        o_m = cross_attend(qm, mk, mv)
        xp = xp + merge(u, o_a, o_b, o_c, o_m, w_pa[l], w_pb[l], w_pc[l], w_pm[l], w_out[l])
        la = min(A_WIN, ka.shape[1])
        pa_k.append(ka[:, -la:]); pa_v.append(va[:, -la:])
        pc_k.append(kc); pc_v.append(vc); pc_f.append(logf)
        pb_s.append(ssm_p); pb_c.append(conv_p)
        pm_k.append(mk); pm_v.append(mv)

        u = split_proj(rmsnorm(xd, g_norm[l]), w_in[l])
        qa, ka, va = qkv_heads(u, 'a', A_HEADS, a_qnorm[l], a_knorm[l])
        o_a = chunk_band_sample(qa, ka, va, cache_a_k[l], cache_a_v[l], a_rel[l])
        o_b, conv_s, ssm_s = mamba_branch(
            u, state_b_conv[l], state_b_ssm[l],
            b_conv_w[l], b_conv_b[l], b_dt_bias[l], b_a_log[l], b_d[l], b_norm[l])
        qc, kc, vc = qkv_heads(u, 'c', C_HEADS, c_qnorm[l], c_knorm[l])
        logf = forget_log(u, c_fbias[l])
        o_c = fox_sample(qc, kc, vc, logf, cache_c_k[l], cache_c_v[l], cache_c_logf[l])
        qm = rmsnorm(heads(u['m_q'], M_HEADS), m_qnorm[l])
        o_m = cross_attend(qm, cache_mem_k[l], cache_mem_v[l])
        xd = xd + merge(u, o_a, o_b, o_c, o_m, w_pa[l], w_pb[l], w_pc[l], w_pm[l], w_out[l])
        sa_k.append(ka); sa_v.append(va)
        sc_k.append(kc); sc_v.append(vc); sc_f.append(logf)
        sb_s.append(ssm_s); sb_c.append(conv_s)

    st = jnp.stack
    return (xp, xd,
            st(pa_k), st(pa_v), st(pc_k), st(pc_v), st(pc_f), st(pb_s), st(pb_c), st(pm_k), st(pm_v),
            st(sa_k), st(sa_v), st(sc_k), st(sc_v), st(sc_f), st(sb_s), st(sb_c))
```

```python
import contextlib
import numpy as np
import ml_dtypes
import concourse.bass as bass
import concourse.mybir as mybir
from concourse.bass_utils import run_bass_kernel_spmd

F32 = mybir.dt.float32
BF16 = mybir.dt.bfloat16
AF = mybir.ActivationFunctionType
ALU = mybir.AluOpType
AX = mybir.AxisListType

DM = 1024
DIN = 10000
SEQ = 4096
NS = 4
TS = 64
EPS = 1e-6
COL = dict(a_q=0, a_k=512, a_v=1024, a_z=1536, b_z=2048, b_xbc=2560, b_dt=3328,
           c_q=3336, c_k=3848, c_v=4360, c_f=4872, c_z=4880, m_q=5392, m_z=5648, gate=5904)
SM = {}
_o = 0
for _n, _w in (('g_norm', 1024), ('b_norm', 512), ('a_qnorm', 64), ('a_knorm', 64),
               ('c_qnorm', 64), ('c_knorm', 64), ('m_qnorm', 64), ('m_knorm', 64),
               ('b_dt_bias', 8), ('b_a_log', 8), ('b_d', 8), ('c_fbias', 8)):
    SM[_n] = (_o, _w)
    _o += _w
NSM = _o
C_M1, C_M2, C_S127, C_S63 = 0, 128, 256, 384
NCST = 512


def cp(e, out, in_):
    if hasattr(e, 'tensor_copy'):
        return e.tensor_copy(out=out, in_=in_)
    return e.activation(out=out, in_=in_, func=AF.Copy)


SAME_ENGINE_SYNC = True


class StopBuild(Exception):
    pass


class Prog:
    EPOCH = 24000
    ND = 48

    def __init__(self, nc, es):
        self.nc, self.es = nc, es
        self.eng = {'pe': nc.tensor, 'act': nc.scalar, 'dve': nc.vector, 'pool': nc.gpsimd, 'sp': nc.sync}
        self.cnt = {e: 0 for e in self.eng}
        self.sems = {}
        self.waited = {e: {} for e in self.eng}
        self.last_w = {}
        self.readers = {}
        self.dma_n = 0
        self.dma_sems = [es.enter_context(nc.semaphore("dq%d" % i)) for i in range(self.ND)]
        self.dma_tokens = []
        self.nops = 0
        self.maxops = None

    def _sem(self, e, epoch):
        k = (e, epoch)
        if k not in self.sems:
            self.sems[k] = self.es.enter_context(self.nc.semaphore("s_%s_%d" % (e, epoch)))
        return self.sems[k]

    def _wait(self, e, tok):
        if tok[0] == 'c':
            _, pe, c = tok
            if pe == e and (e == 'pe' or not SAME_ENGINE_SYNC):
                return
            epoch, v = divmod(c, self.EPOCH)
            key, val, sem = ('c', pe, epoch), v + 1, self._sem(pe, epoch)
        else:
            _, slot, rnd = tok
            key, val, sem = ('d', slot), 16 * (rnd + 1), self.dma_sems[slot]
        if self.waited[e].get(key, 0) >= val:
            return
        self.waited[e][key] = val
        self.eng[e].wait_ge(sem, val)

    def _deps(self, r, w):
        deps = set()
        for k in r:
            if k in self.last_w:
                deps.add(self.last_w[k])
            if isinstance(k, str) and k.startswith('ps'):
                deps.update(self.readers.get(k, ()))
        for k in w:
            if k in self.last_w:
                deps.add(self.last_w[k])
            deps.update(self.readers.get(k, ()))
        return deps

    def _reg(self, tok, r, w):
        for k in r:
            self.readers.setdefault(k, []).append(tok)
        for k in w:
            self.last_w[k] = tok
            self.readers[k] = []

    def op(self, e, fn, r=(), w=(), inc=True):
        if self.maxops is not None and self.nops >= self.maxops:
            raise StopBuild()
        for tok in self._deps(r, w):
            self._wait(e, tok)
        inst = fn(self.eng[e])
        c = self.cnt[e]
        if inc:
            epoch, _ = divmod(c, self.EPOCH)
            inst.then_inc(self._sem(e, epoch), 1)
            self.cnt[e] += 1
        self._reg(('c', e, c), r, w)
        self.nops += 1

    def dma(self, q, out, in_, r=(), w=(), nonc=False):
        if self.maxops is not None and self.nops >= self.maxops:
            raise StopBuild()
        n = self.dma_n
        self.dma_n += 1
        slot, rnd = n % self.ND, n // self.ND
        if rnd > 0:
            self._wait(q, ('d', slot, rnd - 1))
        for tok in self._deps(r, w):
            self._wait(q, tok)
        kw = {}
        if nonc:
            kw['allow_slow_non_contiguous'] = True
        inst = self.eng[q].dma_start(out=out, in_=in_, **kw)
        inst.then_inc(self.dma_sems[slot], 16)
        tok = ('d', slot, rnd)
        self._reg(tok, r, w)
        self.dma_tokens.append(tok)
        self.nops += 1
        return tok

    def finish(self, out_tokens):
        for tok in out_tokens:
            self._wait('sp', tok)
        last = {}
        for tok in self.dma_tokens:
            last[tok[1]] = tok
        for tok in last.values():
            self._wait('sp', tok)


def build(cfg):
    NL = cfg.get('layers', 2)
    NQ = cfg.get('nq', 8)
    DO_S = cfg.get('sample', True)
    NOSTATE = cfg.get('nostate', False)
    nc = bass.Bass("TRN2", target_bir_lowering=False)
    es = contextlib.ExitStack()
    D = {}

    def din(name, shape):
        D[name] = nc.dram_tensor(name, list(shape), F32, kind="ExternalInput").ap()

    def dout(name, shape):
        D[name] = nc.dram_tensor(name, list(shape), F32, kind="ExternalOutput").ap()

    din('xp', [SEQ, DM]); din('xs', [NS * TS, DM]); din('memp', [256, DM])
    din('ca_k', [2, NS, 512, 512]); din('ca_v', [2, NS, 512, 512])
    din('cc_k', [2, NS, 1024, 512]); din('cc_v', [2, NS, 1024, 512]); din('cc_f', [2, NS, 1024, 8])
    din('sb_ssm', [2, NS, 8, 64, 64]); din('sb_conv', [2, NS, 3, 768])
    din('cm_k', [2, NS, 256, 256]); din('cm_v', [2, NS, 256, 256])
    din('w_in', [2, DM, DIN]); din('w_mkv', [2, DM, 512])
    din('w_pa', [2, 512, DM]); din('w_pb', [2, 512, DM]); din('w_pc', [2, 512, DM]); din('w_pm', [2, 256, DM])
    din('w_out', [2, DM, DM])
    din('small', [2, NSM]); din('mnorm', [2, 1024]); din('relb', [2, 128, 8, 640]); din('convw', [2, 128, 6, 5]); din('cst', [128, NCST]); din('band', [128, 640]); din('ident', [128, 128])
    dout('yp', [SEQ, DM]); dout('ys', [NS * TS, DM])
    dout('pa_k', [2, 512, 512]); dout('pa_v', [2, 512, 512])
    dout('pc_k', [2, SEQ, 512]); dout('pc_v', [2, SEQ, 512]); dout('pc_f', [2, SEQ, 8])
    dout('pb_s', [2, 8, 64, 64]); dout('pb_c', [2, 3, 768])
    dout('pm_k', [2, 256, 256]); dout('pm_v', [2, 256, 256])
    dout('sa_k', [2, NS * TS, 512]); dout('sa_v', [2, NS * TS, 512])
    dout('sc_k', [2, NS * TS, 512]); dout('sc_v', [2, NS * TS, 512]); dout('sc_f', [2, NS * TS, 8])
    dout('sb_s', [2, NS, 8, 64, 64]); dout('sb_c', [2, NS, 3, 768])
    x1p = nc.dram_tensor("x1p", [SEQ, DM], F32, kind="Internal").ap()
    x1s = nc.dram_tensor("x1s", [NS * TS, DM], F32, kind="Internal").ap()

    with es:
        pg = Prog(nc, es)
        pg.maxops = cfg.get('maxops')
        out_tokens = []

        def sb(name, shape, dt):
            return es.enter_context(nc.sbuf_tensor("sb_" + name, list(shape), dt))

        def ps(name, shape, dt):
            return es.enter_context(nc.psum_tensor("ps_" + name, list(shape), dt))

        cst = sb("cst", [128, NCST], F32)
        cstb = sb("cstb", [128, 256], BF16)
        small = sb("small", [128, NSM], F32)
        expB = sb("expB", [128, 8, 640], BF16)
        cw = sb("cw", [128, 6, 5], F32)
        aneg = sb("aneg", [128, 8], F32)
        xt = sb("xt", [128, 1, DM], F32)
        W3 = sb("W3", [128, 4096], BF16)
        xnb = W3[:, 0:1024]
        sq = W3[:, 1024:2048].bitcast(F32)
        f2 = W3[:, 2048:3072].bitcast(F32)
        f3 = W3[:, 3072:4096].bitcast(F32)
        xnT = sb("xnT", [128, 8, 512], BF16)
        NWB = 3
        wbuf = [sb("wbuf%d" % i, [128, 8, 512], BF16) for i in range(NWB)]
        wbuf.append(W3[:, :].rearrange("p (k n) -> p k n", n=512))
        WKEYS = [['wbuf0'], ['wbuf1'], ['wbuf2'], ['wbuf3', 'xnb', 'sq', 'f2', 'f3']]
        kT_c = sb("kT_c", [128, 4, SEQ], BF16)
        va_c = sb("va_c", [128, 32, 8, 66], BF16)
        kT_a = sb("kT_a", [128, 4, 1024], BF16)
        va_a = sb("va_a", [128, 8, 8, 66], BF16)
        kT_m = sb("kT_m", [128, 2, 256], BF16)
        va_m = sb("va_m", [128, 2, 4, 66], BF16)
        cum = sb("cum", [128, 32, 8], F32)
        biasQ = sb("biasQ", [128, 32, 8], F32)
        qT = sb("qT", [128, 4, 512], BF16)
        sz = sb("sz", [128, 4, 512], BF16)
        oz = sb("oz", [128, 4, 512], BF16)
        ozT = {k: sb("ozT_" + k, [128, n, 512], BF16) for k, n in (('a', 4), ('b', 4), ('c', 4), ('m', 2))}
        f1 = sb("f1", [128, 512], F32)
        b1 = sb("b1", [128, 512], BF16)
        PT = [sb("PT%d" % i, [128, 512], BF16) for i in range(2)]
        st8 = sb("st8", [128, 8, 8], F32)
        raw = sb("raw", [128, 3, 4, 131], F32)
        halo = sb("halo", [128, 6, 3], F32)
        AR2 = sb("AR2", [128, 4096], BF16)
        xbc = AR2[:, 0:3072].rearrange("p (j n) -> p j n", n=512)
        Mh = AR2[:, 3072:4096].rearrange("p (h n) -> p h n", n=128)
        mT = AR2[:, :].rearrange("p (k n) -> p k n", n=512)
        cva = sb("cva", [128, 512], F32)
        sig = cva
        dts = sb("dts", [128, 4, 8], F32)
        asb = sb("asb", [128, 4, 8], F32)
        btk = sb("btk", [128, 128], BF16)
        AE = sb("AE", [128, 2048], F32)
        Ah = AE[:, 0:512].rearrange("p (h n) -> p h n", n=128)
        Ee = AE[:, 512:1024].rearrange("p (h n) -> p h n", n=128)
        Gm = AE[:, 1024:1280].rearrange("p (g n) -> p g n", n=128)
        bdec = AE[:, 1280:1536].bitcast(BF16).rearrange("p (a g n) -> p a g n", a=4, g=2)
        xdt = AE[:, 1536:1792].bitcast(BF16).rearrange("p (h d) -> p h d", d=64)
        xtk = AE[:, 1792:2048].bitcast(BF16)
        macc = AE[:, :].rearrange("p (t n) -> p t n", n=512)
        hT = sb("hT", [128, 4, 64], F32)
        hTb = sb("hTb", [128, 4, 64], BF16)
        ssd8 = sb("ssd8", [128, 4, 8], F32)
        psA = [ps("psA%d" % i, [128, 512], F32) for i in range(2)]
        psT = [ps("psT%d" % i, [128, 1024], BF16) for i in range(2)]
        psS = [ps("psS%d" % i, [128, 512], F32) for i in range(2)]
        psO = [ps("psO%d" % i, [128, 512], F32) for i in range(2)]
        rr = {'A': 0, 'T': 0, 'S': 0, 'O': 0, 'W': 0, 'P': 0, 'X': 0, 'WP': 0}
        rrn = {'X': 1, 'WP': 1}

        def rot(k, n=2):
            v = rr[k]
            rr[k] = (v + 1) % rrn.get(k, n)
            return v

        ident_b = cstb[:, 0:128]
        M1f = cst[:, C_M1:C_M1 + 128]
        M2f = cst[:, C_M2:C_M2 + 128]
        S127f = cst[:, C_S127:C_S127 + 128]
        S63f = cst[:, C_S63:C_S63 + 128]
        M1b = cstb[:, 128:256]

        def smv(name, rows=128):
            o, w = SM[name]
            return small[0:rows, o:o + w]

        bar = sb("bar", [128, 2], F32)
        epst = sb("epst", [128, 1], F32)
        nhalf = sb("nhalf", [128, 8], F32)
        identf = sb("identf", [128, 64], F32)

        def barrier(keys):
            pg.op('pool', lambda e: e.memset(bar[:, 0:1], 0.0), w=['bar'] + list(keys))

        AEK = ['Ah%d' % h for h in range(8)] + ['Ee', 'Gm', 'bdec', 'xdt', 'xtk', 'AErel'] + ['macc%d' % t for t in range(4)]
        pg.op('pool', lambda e: e.memset(epst[:, :], EPS), w=['epst'])
        pg.op('pool', lambda e: e.memset(nhalf[:, :], -0.5), w=['epst'])
        pg.dma('sp', identf[0:64, :], D['ident'][0:64, 0:64], w=['identf'])
        pg.dma('sp', identf[64:128, :], D['ident'][0:64, 0:64], w=['identf'])
        pg.dma('sp', cst[:, :], D['cst'][:, :], w=['cst'])
        pg.dma('sp', AE[:, 0:128], D['ident'][:, :], w=AEK)
        pg.op('dve', lambda e: cp(e, out=cstb[:, 0:128], in_=AE[:, 0:128]), r=AEK, w=['cstb'])
        pg.op('dve', lambda e: cp(e, out=cstb[:, 128:256], in_=cst[:, C_M1:C_M1 + 128]), r=['cst'], w=['cstb'])
        for t_, k_ in ((va_c, 'va_c'), (va_a, 'va_a'), (va_m, 'va_m')):
            pg.op('pool', lambda e, t_=t_: e.memset(t_[:], 1.0), w=[k_])

        def group_wlist(l):
            wl = []
            for nm in ('c_q', 'c_k', 'c_v'):
                wl.append(('w_in', l, COL[nm], 512, 8))
            wl.append(('w_in', l, COL['c_f'], 8, 8))
            wl.append(('w_in', l, COL['c_z'], 512, 8))
            for nm in ('a_q', 'a_k', 'a_v', 'a_z', 'm_q', 'b_z'):
                wl.append(('w_in', l, COL[nm], 512, 8))
            wl.append(('w_in', l, COL['b_dt'], 8, 8))
            for half in range(2):
                wl.append(('w_in', l, COL['b_xbc'] + half * 384, 384, 8))
            for c in range(2):
                for bi, (wn, nkc) in enumerate((('w_pa', 4), ('w_pb', 4), ('w_pc', 4), ('w_pm', 2))):
                    wl.append(('w_in', l, COL['gate'] + bi * 1024 + c * 512, 512, 8, True))
                    wl.append((wn, l, c * 512, 512, nkc, True))
            for c in range(2):
                wl.append(('w_out', l, c * 512, 512, 8, True))
            return wl

        WL = []
        for l_ in range(NL):
            WL.append(('w_mkv', l_, 0, 512, 8))
            for _ in range(NQ + (1 if DO_S else 0)):
                WL += group_wlist(l_)
        wbi, prev_occ, lastocc, r3, r4 = [], [], {}, 0, 0
        for k_, ent in enumerate(WL):
            if len(ent) > 5 and ent[5]:
                b_ = r4 % 4; r4 += 1
            else:
                b_ = r3 % 3; r3 += 1; r4 = r3
            wbi.append(b_)
            prev_occ.append(lastocc.get(b_, -1))
            lastocc[b_] = k_
        wstate = {'ptr': 0, 'issued': 0}

        def load_w(dram, l, c0, n, nk=8, buf=None):
            i = wstate['ptr']
            exp = WL[i]
            assert exp[1] == l and exp[2] == c0 and exp[3] == n and exp[4] == nk and D[exp[0]] is dram, (exp, l, c0, n, nk)
            in_merge = len(exp) > 5 and exp[5]
            while wstate['issued'] < len(WL) and wstate['issued'] <= i + 3 and \
                    (wstate['issued'] <= i or prev_occ[wstate['issued']] <= i - 2) and \
                    (wbi[wstate['issued']] != 3 or in_merge):
                k = wstate['issued']
                nm, l2, c2, n2, nk2 = WL[k][0:5]
                src = D[nm][l2, :, c2:c2 + n2].rearrange("(k p) n -> p k n", p=128)
                pg.dma('pool', wbuf[wbi[k]][:, 0:nk2, 0:n2], src, w=WKEYS[wbi[k]])
                wstate['issued'] += 1
            wstate['ptr'] += 1
            return wbuf[wbi[i]], WKEYS[wbi[i]]

        def transposes(src_ap_fn, nblk, np_, dst_fn, rkeys, wkeys, evac='dve'):
            i = rot('T'); pt = psT[i]; pk = 'psT%d' % i
            for j in range(nblk):
                pg.op('pe', lambda e, j=j: e.transpose(out=pt[:, j * 128:j * 128 + np_], in_=src_ap_fn(j),
                                                       identity=ident_b[0:np_, 0:np_]),
                      r=list(rkeys) + ['cstb'], w=[pk], inc=(j == nblk - 1))
            view = pt[:, 0:nblk * 128].rearrange("p (b n) -> p b n", n=128)[:, :, 0:np_]
            pg.op(evac, lambda e: dst_fn(e, view), r=[pk], w=list(wkeys))

        def rstd_from_ss(ss_ap, rs_ap, n, rows, key):
            w_ = rs_ap.shape[-1]
            pg.op('dve', lambda e: e.tensor_scalar(out=rs_ap, in0=ss_ap, scalar1=1.0 / n, scalar2=EPS,
                                                    op0=ALU.mult, op1=ALU.add), r=[key], w=[key])
            pg.op('pool', lambda e: e.tensor_tensor(out=rs_ap, in0=rs_ap, in1=nhalf[0:rows, 0:w_], op=ALU.pow),
                  r=[key, 'epst'], w=[key])

        def head_norm(psap, pk, nh, gain_ap, out_ap, outkeys, np_, scale=None):
            n = nh * 64
            pg.op('act', lambda e: e.activation(out=sq[0:np_, 0:n], in_=psap, func=AF.Square), r=[pk], w=['sq'])
            ssv = st8[0:np_, 0, 0:nh]
            pg.op('dve', lambda e: e.tensor_reduce(out=ssv, in_=sq[0:np_, 0:n].rearrange("p (h d) -> p h d", d=64),
                                                   axis=AX.X, op=ALU.add), r=['sq'], w=['st8'])
            rstd_from_ss(ssv, ssv, 64, np_, 'st8')
            pg.op('dve', lambda e: e.tensor_tensor(
                out=f1[0:np_, 0:n].rearrange("p (h d) -> p h d", d=64),
                in0=psap.rearrange("p (h d) -> p h d", d=64),
                in1=ssv.unsqueeze(2).to_broadcast([np_, nh, 64]), op=ALU.mult), r=[pk, 'st8'], w=['f1'])
            g = gain_ap.unsqueeze(1).to_broadcast([np_, nh, 64])
            pg.op('dve', lambda e: e.tensor_tensor(
                out=out_ap.rearrange("p (h d) -> p h d", d=64),
                in0=f1[0:np_, 0:n].rearrange("p (h d) -> p h d", d=64), in1=g, op=ALU.mult),
                r=['f1', 'small'], w=list(outkeys))

        def pipe_tiles(NT, proj_fn):
            nxt = proj_fn(0)
            for t in range(NT):
                cur = nxt
                if t + 1 < NT:
                    nxt = proj_fn(t + 1)
                yield (t,) + tuple(cur)

        def proj_tm(xT, xkey, tcol, np_, wt, wkey, n, nk=8, wc0=0, xkeys=()):
            i = rot('A'); p = psA[i]; pk = 'psA%d' % i
            for kc in range(nk):
                pg.op('pe', lambda e, kc=kc: e.matmul(p[0:np_, 0:n], lhsT=xT[:, kc, tcol:tcol + np_],
                                                      rhs=wt[:, kc, wc0:wc0 + n], start=(kc == 0), stop=(kc == nk - 1)),
                      r=[xkey] + list(wkey) + list(xkeys), w=[pk], inc=(kc == nk - 1))
            return p[0:np_, 0:n], pk

        def layer_consts(l):
            pg.dma('sp', small[:, :], D['small'][l:l + 1, :].broadcast_to([128, NSM]), w=['small'])
            pg.dma('sp', cw[:, :, :], D['convw'][l], w=['cw'])
            for name in ('a_qnorm', 'c_qnorm', 'm_qnorm'):
                a = smv(name)
                pg.op('dve', lambda e, a=a: e.tensor_scalar(out=a, in0=a, scalar1=0.125, scalar2=None, op0=ALU.mult),
                      r=['small'], w=['small'])
            pg.op('act', lambda e: e.activation(out=aneg[:, :], in_=smv('b_a_log'), func=AF.Exp), r=['small'], w=['aneg'])
            pg.op('dve', lambda e: e.tensor_scalar(out=aneg[:, :], in0=aneg[:, :], scalar1=-1.0, scalar2=None, op0=ALU.mult),
                  r=['aneg'], w=['aneg'])
            barrier(AEK)
            pg.dma('sp', AE[:, 0:640], D['band'][:, :], w=AEK)
            pg.op('dve', lambda e: e.tensor_scalar(out=AE[:, 0:640], in0=AE[:, 0:640], scalar1=30000.0, scalar2=-30000.0,
                                                    op0=ALU.mult, op1=ALU.add), r=AEK, w=AEK)
            for h in range(8):
                pg.dma('sp', AE[:, 1024:1664], D['relb'][l, :, h, :], w=['AErel'])
                pg.op('dve', lambda e, h=h: e.tensor_tensor(out=expB[:, h, :], in0=AE[:, 1024:1664],
                                                            in1=AE[:, 0:640], op=ALU.add),
                      r=['AErel'] + AEK, w=['expB'])
            barrier(AEK)

        def norm_tiles(src_fn, NT, np_, gain_fn, rk_fn=None):
            for t in range(NT):
                i = rot('X')
                xk = 'xt%d' % i
                pg.dma('sp', xt[0:np_, i, :], src_fn(t), r=(rk_fn(t) if rk_fn else []), w=[xk])
                ssv = st8[0:np_, 1, 0:1]
                pg.op('act', lambda e, i=i: e.activation(out=xnb[0:np_, :], in_=xt[0:np_, i, :], func=AF.Square, accum_out=ssv),
                      r=[xk], w=['xnb', 'st8'])
                rstd_from_ss(ssv, ssv, DM, np_, 'st8')
                pg.op('dve', lambda e, i=i: e.scalar_tensor_tensor(out=xnb[0:np_, :], in0=xt[0:np_, i, :], scalar=ssv,
                                                                  in1=gain_fn(np_), op0=ALU.mult, op1=ALU.mult),
                      r=[xk, 'st8', 'small', 'f2'], w=['xnb'])
                transposes(lambda j: xnb[0:np_, j * 128:(j + 1) * 128], 8, np_,
                           lambda e, v, t=t: cp(e, out=xnT[:, :, t * np_:(t + 1) * np_], in_=v),
                           ['xnb'], ['xnT'], evac='act' if t % 2 else 'dve')

        def attend(qT_ap, NQc, qtiles, kblocks, qkeys, out_fn, acc=None):
            if acc is None:
                io = rot('O'); po = psO[io]; pok = 'psO%d' % io
                pov = po[:, 0:260].rearrange("p (j c) -> p j c", c=65)
                jbase, bank_first = 0, True
            else:
                pov, pok, jbase, bank_first = acc
            lastkb, firstkb = {}, {}
            for bi, kb in enumerate(kblocks):
                for j, (c0, nq) in enumerate(qtiles):
                    if c0 >= kb['q0']:
                        lastkb[j] = bi
                        firstkb.setdefault(j, bi)
            st = {}

            def stage_s(bi):
                kb = kblocks[bi]
                isx = rot('S'); psx = psS[isx]; psk = 'psS%d' % isx
                badd = kb.get('badd')
                pg.op('pe', lambda e: e.matmul(psx[0:kb['nk'], kb['q0']:NQc], lhsT=kb['kT'],
                                               rhs=qT_ap[:, kb['q0']:NQc], start=True, stop=(badd is None)),
                      r=list(qkeys) + list(kb['keys']), w=[psk], inc=(badd is None))
                if badd is not None:
                    pg.op('pe', lambda e: e.matmul(psx[0:kb['nk'], kb['q0']:NQc], lhsT=ident_b[:, 0:kb['nk']],
                                                   rhs=badd, start=False, stop=True),
                          r=['cstb'] + list(kb.get('mkeys', [])), w=[psk])
                st[bi] = (psx, psk)

            def stage_e(bi):
                kb = kblocks[bi]
                psx, psk = st[bi]
                ip = rot('P'); pt = PT[ip]; ptk = 'PT%d' % ip
                if kb.get('bias') is not None:
                    pg.op('act', lambda e: e.activation(out=pt[0:kb['nk'], kb['q0']:NQc], in_=psx[0:kb['nk'], kb['q0']:NQc],
                                                        func=AF.Exp, bias=kb['bias'], scale=1.0),
                          r=[psk] + list(kb.get('bkeys', [])), w=[ptk])
                else:
                    pg.op('act', lambda e: e.activation(out=pt[0:kb['nk'], kb['q0']:NQc], in_=psx[0:kb['nk'], kb['q0']:NQc],
                                                        func=AF.Exp), r=[psk], w=[ptk])
                if kb.get('diagmask'):
                    pg.op('dve', lambda e: e.tensor_tensor(
                        out=pt[0:kb['nk'], kb['q0']:kb['q0'] + kb['nk']], in0=pt[0:kb['nk'], kb['q0']:kb['q0'] + kb['nk']],
                        in1=M1b[0:kb['nk'], 0:kb['nk']], op=ALU.mult), r=[ptk, 'cstb'], w=[ptk])
                st[bi] = (pt, ptk)

            def stage_v(bi):
                kb = kblocks[bi]
                pt, ptk = st[bi]
                for j, (c0, nq) in enumerate(qtiles):
                    if c0 < kb['q0']:
                        continue
                    pg.op('pe', lambda e, j=j, c0=c0, nq=nq: e.matmul(
                        pov[0:nq, jbase + j, :], lhsT=pt[0:kb['nk'], c0:c0 + nq], rhs=kb['v'],
                        start=(bank_first and bi == 0 and j == min(firstkb)), stop=(bi == lastkb[j]), skip_group_check=True),
                        r=[ptk] + list(kb['keys']), w=[pok], inc=(c0 + nq >= NQc))

            n = len(kblocks)
            stage_s(0)
            for bi in range(n):
                if bi + 1 < n:
                    stage_s(bi + 1)
                stage_e(bi)
                stage_v(bi)
            if out_fn is not None:
                out_fn(pov, pok)
            return pov, pok

        def attn_out(h0col, NT, np_, szkey='sz'):
            def fn(pov, pok):
                rl = st8[0:np_, 2, 0:NT]
                pg.op('dve', lambda e: e.reciprocal(out=rl, in_=pov[0:np_, 0:NT, 64]), r=[pok], w=['st8'])
                tmp = f2[0:np_, 0:NT * 64].rearrange("p (j d) -> p j d", d=64)
                pg.op('dve', lambda e: e.tensor_tensor(out=tmp, in0=pov[0:np_, 0:NT, 0:64],
                                                       in1=rl.unsqueeze(2).to_broadcast([np_, NT, 64]), op=ALU.mult),
                      r=[pok, 'st8'], w=['f2'])
                pg.op('pool', lambda e: e.tensor_tensor(out=oz[0:np_, 0:NT, h0col:h0col + 64], in0=tmp,
                                                        in1=sz[0:np_, 0:NT, h0col:h0col + 64], op=ALU.mult),
                      r=['f2', szkey], w=['oz'])
            return fn

        def oz_to_T(br, NT, np_, nblk):
            for t in range(NT):
                transposes(lambda j, t=t: oz[0:np_, t, j * 128:(j + 1) * 128], nblk, np_,
                           lambda e, v, t=t: cp(e, out=ozT[br][:, 0:nblk, t * np_:(t + 1) * np_], in_=v),
                           ['oz'], ['ozT_' + br], evac='act' if t % 2 else 'dve')

        def evac_q(psap, pk, t, np_, nh, gname):
            head_norm(psap, pk, nh, smv(gname, np_), b1[0:np_, 0:nh * 64], ['b1'], np_)
            transposes(lambda j: b1[0:np_, j * 128:(j + 1) * 128], nh // 2, np_,
                       lambda e, v: cp(e, out=qT[:, 0:nh // 2, t * np_:(t + 1) * np_], in_=v),
                       ['b1'], ['qT'], evac='act')

        def evac_k(psap, pk, np_, nh, gname, out_dram, kT_dst_fn, kT_key):
            head_norm(psap, pk, nh, smv(gname, np_), f3[0:np_, 0:nh * 64], ['f3'], np_)
            if out_dram is not None:
                out_tokens.append(pg.dma('sp', out_dram, f3[0:np_, 0:nh * 64], r=['f3']))
            pg.op('dve', lambda e: cp(e, out=b1[0:np_, 0:nh * 64], in_=f3[0:np_, 0:nh * 64]), r=['f3'], w=['b1'])
            transposes(lambda j: b1[0:np_, j * 128:(j + 1) * 128], nh // 2, np_,
                       lambda e, v: cp(e, out=kT_dst_fn(), in_=v), ['b1'], [kT_key], evac='act')

        def evac_v(psap, pk, np_, nh, out_dram, va_dst, va_key):
            if out_dram is not None:
                pg.op('act', lambda e: e.activation(out=f3[0:np_, 0:nh * 64], in_=psap, func=AF.Copy), r=[pk], w=['f3'])
                out_tokens.append(pg.dma('sp', out_dram, f3[0:np_, 0:nh * 64], r=['f3']))
            pg.op('dve', lambda e: cp(e, out=va_dst, in_=psap.rearrange("p (h d) -> p h d", d=64)),
                  r=[pk], w=[va_key])

        def softplus_from(psap, pk, bias_ap, out_ap, okey, np_, neg=False):
            tmp = st8[0:np_, 3, :]
            pg.op('dve', lambda e: e.tensor_tensor(out=tmp, in0=psap, in1=bias_ap, op=ALU.add), r=[pk, 'small'], w=['st8'])
            pg.op('act', lambda e: e.activation(out=tmp, in_=tmp, func=AF.Exp, scale=(-1.0 if neg else 1.0)),
                  r=['st8'], w=['st8'])
            pg.op('dve', lambda e: e.tensor_scalar(out=tmp, in0=tmp, scalar1=1.0, scalar2=None, op0=ALU.add),
                  r=['st8'], w=['st8'])
            pg.op('act', lambda e: e.activation(out=tmp, in_=tmp, func=AF.Ln), r=['st8'], w=['st8'])
            pg.op('dve', lambda e: e.tensor_scalar(out=out_ap, in0=tmp, scalar1=(-1.0 if neg else 1.0), scalar2=None,
                                                    op0=ALU.mult), r=['st8'], w=[okey])

        def cumsum_tile(lf_ap, lfkey, np_, dst_ap, dkey, carry_ap, ckeys):
            i = rot('S'); p = psS[i]; pk = 'psS%d' % i
            pg.op('pe', lambda e: e.matmul(p[0:np_, 0:8], lhsT=M1f[0:np_, 0:np_], rhs=lf_ap, start=True,
                                           stop=(carry_ap is None)), r=[lfkey, 'cst'], w=[pk])
            if carry_ap is not None:
                pg.op('pe', lambda e: e.matmul(p[0:np_, 0:8], lhsT=carry_ap[0], rhs=carry_ap[1], start=False, stop=True),
                      r=list(ckeys) + ['cst'], w=[pk])
            pg.op('dve', lambda e: cp(e, out=dst_ap, in_=p[0:np_, 0:8]), r=[pk], w=[dkey])

        def ssd_tile(t, np_, NT):
            c0 = t * np_
            if t == 0:
                barrier(AEK)
            def ev(e, v):
                return cp(e, out=xtk[0:np_, :].rearrange("p (b n) -> p b n", n=128), in_=v[0:np_, 0:4, :])
            i = rot('T'); pt = psT[i]; pk = 'psT%d' % i
            for j in range(5):
                pg.op('pe', lambda e, j=j: e.transpose(out=pt[0:np_, j * 128:(j + 1) * 128], in_=xbc[:, j, c0:c0 + np_],
                                                       identity=ident_b[:, :]), r=['xbc', 'cstb'], w=[pk], inc=(j == 4))
            pg.op('act', lambda e: e.activation(out=xtk[0:np_, :], in_=pt[0:np_, 0:512], func=AF.Copy), r=[pk], w=['xtk'])
            pg.op('act', lambda e: e.activation(out=btk[0:np_, :], in_=pt[0:np_, 512:640], func=AF.Copy), r=[pk], w=['btk'])
            pg.op('dve', lambda e: e.tensor_tensor(out=xdt[0:np_, :, :], in0=pt[0:np_, 0:512].rearrange("p (h d) -> p h d", d=64),
                                                   in1=dts[0:np_, t, :].unsqueeze(2).to_broadcast([np_, 8, 64]), op=ALU.mult),
                  r=[pk, 'dts'], w=['xdt'])
            a_ap = asb[0:np_, t, :]
            i = rot('A'); p = psA[i]; pk2 = 'psA%d' % i
            pg.op('pe', lambda e: e.matmul(p[0:np_, 0:8], lhsT=M1f[0:np_, 0:np_], rhs=a_ap, start=True, stop=True),
                  r=['asb', 'cst'], w=[pk2], inc=False)
            pg.op('pe', lambda e: e.matmul(p[0:np_, 8:16], lhsT=M2f[0:np_, 0:np_], rhs=a_ap, start=True, stop=True),
                  r=['asb', 'cst'], w=[pk2], inc=False)
            pg.op('pe', lambda e: e.matmul(p[:, 16:24], lhsT=M1f[0:np_, :], rhs=a_ap, start=True, stop=False),
                  r=['asb', 'cst'], w=[pk2], inc=False)
            pg.op('pe', lambda e: e.matmul(p[:, 16:24], lhsT=M2f[0:np_, :], rhs=a_ap, start=False, stop=True),
                  r=['asb', 'cst'], w=[pk2])
            pg.op('act', lambda e: e.activation(out=ssd8[0:np_, 0, :], in_=p[0:np_, 0:8], func=AF.Exp), r=[pk2], w=['ssd8'])
            pg.op('act', lambda e: e.activation(out=ssd8[0:np_, 1, :], in_=p[0:np_, 8:16], func=AF.Exp), r=[pk2], w=['ssd8'])
            pg.op('act', lambda e: e.activation(out=ssd8[:, 2, :], in_=p[:, 16:24], func=AF.Exp), r=[pk2], w=['ssd8'])
            for g in range(2):
                i = rot('A'); pgm = psA[i]; pk3 = 'psA%d' % i
                pg.op('pe', lambda e, g=g, pgm=pgm: e.matmul(pgm[0:np_, 0:np_], lhsT=xbc[g * 64:(g + 1) * 64, 4, c0:c0 + np_],
                                                    rhs=xbc[g * 64:(g + 1) * 64, 5, c0:c0 + np_], start=True, stop=True),
                      r=['xbc'], w=[pk3])
                pg.op('dve', lambda e, g=g, pgm=pgm: e.tensor_tensor(
                    out=Gm[0:np_, g, 0:np_], in0=pgm[0:np_, 0:np_], in1=M1f[0:np_, 0:np_], op=ALU.mult),
                    r=[pk3, 'cst'], w=['Gm'])
            for hh in range(2):
                for h4 in range(4):
                    h = hh * 4 + h4
                    pg.op('dve', lambda e, h=h, h4=h4: e.tensor_scalar(
                        out=Ah[0:np_, h4, 0:np_], in0=M2f[0:np_, 0:np_], scalar1=asb[0:np_, t, h:h + 1], scalar2=None,
                        op0=ALU.mult), r=['cst', 'asb'], w=['Ah%d' % h4])
                psx = psS[hh]; psk = 'psS%d' % hh
                for h4 in range(4):
                    pg.op('pe', lambda e, h4=h4, psx=psx: e.matmul(
                        psx[0:np_, h4 * 128:h4 * 128 + np_], lhsT=Ah[0:np_, h4, 0:np_], rhs=M1f[0:np_, 0:np_],
                        start=True, stop=True), r=['Ah%d' % h4, 'cst'], w=[psk], inc=(h4 == 3))
                pg.op('act', lambda e, psx=psx: e.activation(
                    out=Ee[0:np_, :, 0:np_],
                    in_=psx[0:np_, :].rearrange("p (h n) -> p h n", n=128)[:, :, 0:np_], func=AF.Exp),
                    r=[psk], w=['Ee'])
                pg.op('dve', lambda e, hh=hh: e.tensor_tensor(
                    out=Mh[0:np_, hh * 4:(hh + 1) * 4, 0:np_], in0=Ee[0:np_, :, 0:np_],
                    in1=Gm[0:np_, hh, 0:np_].unsqueeze(1).to_broadcast([np_, 4, np_]), op=ALU.mult),
                    r=['Ee', 'Gm'], w=['Mh'])
            io = rot('O'); py = psO[io]; pyk = 'psO%d' % io
            io2 = rot('O'); py2 = psO[io2]; py2k = 'psO%d' % io2
            for h in range(8):
                pg.op('pe', lambda e, h=h: e.matmul(py[0:np_, h * 64:(h + 1) * 64], lhsT=Mh[0:np_, h, 0:np_],
                                                    rhs=xdt[0:np_, h, :], start=True, stop=True),
                      r=['Mh', 'xdt'], w=[pyk], inc=(h == 7))
            py3 = psS[1]; py3k = 'psS1'
            for h in range(8):
                g = h // 4
                dstp = (py2 if g == 0 else py3)
                pg.op('pe', lambda e, h=h, g=g, dstp=dstp: e.matmul(dstp[0:np_, (h % 4) * 64:(h % 4 + 1) * 64],
                                                         lhsT=xbc[g * 64:(g + 1) * 64, 5, c0:c0 + np_],
                                                         rhs=hTb[g * 64:(g + 1) * 64, h % 4, :], start=True, stop=True),
                      r=['xbc', 'hTb'], w=[py2k if g == 0 else py3k], inc=(h % 4 == 3))
            yv = f1[0:np_, :].rearrange("p (h d) -> p h d", d=64)
            for g, (dstp, dk) in enumerate(((py2, py2k), (py3, py3k))):
                pg.op('dve', lambda e, g=g, dstp=dstp: e.tensor_tensor(
                    out=yv[:, g * 4:(g + 1) * 4, :], in0=dstp[0:np_, 0:256].rearrange("p (h d) -> p h d", d=64),
                    in1=ssd8[0:np_, 0, g * 4:(g + 1) * 4].unsqueeze(2).to_broadcast([np_, 4, 64]), op=ALU.mult),
                    r=[dk, 'ssd8'], w=['f1'])
            pg.op('dve', lambda e: e.tensor_tensor(out=f1[0:np_, :], in0=f1[0:np_, :], in1=py[0:np_, :], op=ALU.add),
                  r=['f1', pyk], w=['f1'])
            pg.op('pool', lambda e: e.tensor_tensor(out=f2[0:np_, :].rearrange("p (h d) -> p h d", d=64),
                                                    in0=xtk[0:np_, :].rearrange("p (h d) -> p h d", d=64),
                                                    in1=smv('b_d', np_).unsqueeze(2).to_broadcast([np_, 8, 64]), op=ALU.mult),
                  r=['xtk', 'small'], w=['f2'])
            pg.op('dve', lambda e: e.tensor_tensor(out=f1[0:np_, :], in0=f1[0:np_, :], in1=f2[0:np_, :], op=ALU.add),
                  r=['f1', 'f2'], w=['f1'])
            pg.op('dve', lambda e: e.tensor_tensor(out=f1[0:np_, :], in0=f1[0:np_, :], in1=sz[0:np_, t, :], op=ALU.mult),
                  r=['f1', 'szb'], w=['f1'])
            ssv = st8[0:np_, 4, 0:1]
            pg.op('act', lambda e: e.activation(out=sq[0:np_, 0:512], in_=f1[0:np_, :], func=AF.Square, accum_out=ssv),
                  r=['f1'], w=['sq', 'st8'])
            rstd_from_ss(ssv, ssv, 512, np_, 'st8')
            pg.op('dve', lambda e: e.scalar_tensor_tensor(out=oz[0:np_, t, :], in0=f1[0:np_, :], scalar=ssv,
                                                          in1=smv('b_norm', np_), op0=ALU.mult, op1=ALU.mult),
                  r=['f1', 'st8', 'small'], w=['oz'])
            pg.op('dve', lambda e: e.tensor_tensor(
                out=bdec[0:np_, :, :, :].rearrange("p a g n -> p g a n"),
                in0=btk[0:np_, :].rearrange("p (g n) -> p g n", n=64).unsqueeze(2).to_broadcast([np_, 2, 4, 64]),
                in1=ssd8[0:np_, 1, :].rearrange("p (g a) -> p g a", a=4).unsqueeze(3).to_broadcast([np_, 2, 4, 64]),
                op=ALU.mult), r=['btk', 'ssd8'], w=['bdec'])
            i = rot('A'); ph = psA[i]; phk = 'psA%d' % i
            for a4 in range(4):
                pg.op('pe', lambda e, a4=a4: e.matmul(
                    ph[:, a4 * 128:(a4 + 1) * 128], lhsT=bdec[0:np_, a4, :, :].rearrange("p g n -> p (g n)"),
                    rhs=xdt[0:np_, :, :].rearrange("p (g a) d -> p a g d", a=4)[:, a4, :, :],
                    start=True, stop=True), r=['bdec', 'xdt'], w=[phk], inc=(a4 == 3))
            phv = ph[:, :].rearrange("p (a c) -> p a c", c=128)
            for g in range(2):
                sl = slice(g * 64, (g + 1) * 64)
                pg.op('dve', lambda e, g=g, sl=sl: e.tensor_tensor(
                    out=hT[sl, :, :], in0=hT[sl, :, :],
                    in1=ssd8[sl, 2, g * 4:(g + 1) * 4].unsqueeze(2).to_broadcast([64, 4, 64]), op=ALU.mult),
                    r=['hT', 'ssd8', py2k, py3k], w=['hT'])
                pg.op('dve', lambda e, g=g, sl=sl: e.tensor_tensor(
                    out=hT[sl, :, :], in0=hT[sl, :, :], in1=phv[sl, :, g * 64:(g + 1) * 64], op=ALU.add),
                    r=['hT', phk], w=['hT'])
            pg.op('act', lambda e: e.activation(out=hTb[:, :, :], in_=hT[:, :, :], func=AF.Copy), r=['hT'], w=['hTb'])

        def process_group(l, kind, Q):
            samp = (kind == 's')
            np_ = 64 if samp else 128
            NT = 4
            NTOK = NT * np_
            xsrc = (D['xs'] if samp else D['xp']) if l == 0 else (x1s if samp else x1p)
            xdst = (D['ys'] if samp else D['yp']) if l == NL - 1 else (x1s if samp else x1p)
            row0 = 0 if samp else Q * 512

            def rows(t):
                return slice(row0 + t * np_, row0 + (t + 1) * np_)

            norm_tiles(lambda t: xsrc[rows(t), :], NT, np_, lambda n: smv('g_norm', n),
                       (lambda t: [('xd', kind, row0 + t * np_)]) if l > 0 else None)

            if cfg.get('marks'): print('MARK', kind, Q, 'C_proj', pg.nops)
            wt, wk = load_w(D['w_in'], l, COL['c_q'], 512)
            for t, p, pk in pipe_tiles(NT, lambda t: proj_tm(xnT, 'xnT', t * np_, np_, wt, wk, 512)):
                evac_q(p, pk, t, np_, 8, 'c_qnorm')
            wt, wk = load_w(D['w_in'], l, COL['c_k'], 512)
            if samp:
                kTs = sb_kTs
            for t, p, pk in pipe_tiles(NT, lambda t: proj_tm(xnT, 'xnT', t * np_, np_, wt, wk, 512)):
                if samp:
                    evac_k(p, pk, np_, 8, 'c_knorm', D['sc_k'][l, rows(t), :],
                           lambda t=t: kTs[:, 0:4, t * 64:(t + 1) * 64], 'kTs')
                else:
                    gt = Q * 4 + t
                    evac_k(p, pk, np_, 8, 'c_knorm', D['pc_k'][l, rows(t), :],
                           lambda gt=gt: kT_c[:, :, gt * 128:(gt + 1) * 128], 'kT_c')
            wt, wk = load_w(D['w_in'], l, COL['c_v'], 512)
            for t, p, pk in pipe_tiles(NT, lambda t: proj_tm(xnT, 'xnT', t * np_, np_, wt, wk, 512)):
                if samp:
                    evac_v(p, pk, np_, 8, D['sc_v'][l, rows(t), :], vas[0:np_, t, :, 0:64], 'vas')
                else:
                    evac_v(p, pk, np_, 8, D['pc_v'][l, rows(t), :], va_c[:, Q * 4 + t, :, 0:64], 'va_c')
            wt, wk = load_w(D['w_in'], l, COL['c_f'], 8)
            for t, p, pk in pipe_tiles(NT, lambda t: proj_tm(xnT, 'xnT', t * np_, np_, wt, wk, 8)):
                lf = st8[0:np_, 5, :]
                softplus_from(p, pk, smv('c_fbias', np_), lf, 'st8lf', np_, neg=True)
                if samp:
                    out_tokens.append(pg.dma('sp', D['sc_f'][l, rows(t), :], lf, r=['st8lf'], nonc=True))
                    carry = None
                    for kb in range(8):
                        pg.dma('sp', st8[:, 7, :], D['cc_f'][l, t, kb * 128:(kb + 1) * 128, :], w=['st8x'], nonc=True)
                        cumsum_tile(st8[:, 7, :], 'st8x', 128, cums[:, t, kb, :], 'cums',
                                    None if kb == 0 else (S127f, cums[:, t, kb - 1, :]), ['cums'])
                    cumsum_tile(lf, 'st8lf', 64, cums[0:64, t, 8, :], 'cums', (S127f[:, 0:64], cums[:, t, 7, :]), ['cums'])
                else:
                    gt = Q * 4 + t
                    out_tokens.append(pg.dma('sp', D['pc_f'][l, rows(t), :], lf, r=['st8lf'], nonc=True))
                    cumsum_tile(lf, 'st8lf', 128, cum[:, gt, :], 'cum',
                                None if gt == 0 else (S127f, cum[:, gt - 1, :]), ['cum'])
            wt, wk = load_w(D['w_in'], l, COL['c_z'], 512)
            for t, p, pk in pipe_tiles(NT, lambda t: proj_tm(xnT, 'xnT', t * np_, np_, wt, wk, 512)):
                pg.op('act', lambda e, p=p, t=t: e.activation(out=sz[0:np_, t, :], in_=p, func=AF.Silu), r=[pk], w=['sz'])
            if cfg.get('marks'): print('MARK', kind, Q, 'C_attn', pg.nops)
            if not samp:
                nkb = 4 * Q + 4
                i = rot('A'); pb = psA[i]; pbk = 'psA%d' % i
                pg.op('pe', lambda e: e.matmul(pb[:, 0:8], lhsT=S127f, rhs=cum[:, nkb - 1, :], start=True, stop=True),
                      r=['cum', 'cst'], w=[pbk])
                pg.op('act', lambda e: e.activation(out=st8[:, 6, :], in_=pb[:, 0:8], func=AF.Copy), r=[pbk], w=['st8c'])
                pg.op('dve', lambda e: e.tensor_tensor(out=biasQ[:, 0:nkb, :],
                                                       in0=st8[:, 6, :].unsqueeze(1).to_broadcast([128, nkb, 8]),
                                                       in1=cum[:, 0:nkb, :], op=ALU.subtract), r=['st8c', 'cum'], w=['biasQ'])
                for h in range(8):
                    hp, hb = h // 2, (h % 2) * 64
                    kbs = []
                    for kb in range(nkb):
                        dI = kb - 4 * Q
                        d = dict(kT=kT_c[hb:hb + 64, hp, kb * 128:(kb + 1) * 128], v=va_c[:, kb, h, 0:65], nk=128,
                                 bias=biasQ[:, kb, h:h + 1], bkeys=['biasQ'], q0=max(dI, 0) * 128, keys=['kT_c', 'va_c'])
                        kbs.append(d)
                    kb2 = []
                    for kb, d in enumerate(kbs):
                        dI = kb - 4 * Q
                        if dI < 0:
                            kb2.append(d)
                        else:
                            d['diag'] = True
                            kb2.append(d)
                    attend_fox(qT[hb:hb + 64, hp, :], 512, [(j * 128, 128) for j in range(4)], kb2, ['qT'],
                               attn_out(h * 64, 4, 128))
            else:
                for t in range(NT):
                    i = rot('A'); pb = psA[i]; pbk = 'psA%d' % i
                    pg.op('pe', lambda e, t=t: e.matmul(pb[:, 0:8], lhsT=S63f[0:64, :], rhs=cums[0:64, t, 8, :],
                                                        start=True, stop=True), r=['cums', 'cst'], w=[pbk])
                    pg.op('act', lambda e: e.activation(out=st8[:, 6, :], in_=pb[:, 0:8], func=AF.Copy), r=[pbk], w=['st8c'])
                    pg.op('dve', lambda e, t=t: e.tensor_tensor(out=biasQ[:, 0:9, :],
                                                                in0=st8[:, 6, :].unsqueeze(1).to_broadcast([128, 9, 8]),
                                                                in1=cums[:, t, :, :], op=ALU.subtract),
                          r=['st8c', 'cums'], w=['biasQ'])
                    load_cache_kv(D['cc_k'][l, t], D['cc_v'][l, t], 8, 8)
                    for h in range(8):
                        hp, hb = h // 2, (h % 2) * 64
                        kbs = []
                        for kb in range(8):
                            kbs.append(dict(kT=ckT[hb:hb + 64, hp, kb * 128:(kb + 1) * 128], v=cva_[:, kb, h, 0:65], nk=128,
                                            bias=biasQ[:, kb, h:h + 1], bkeys=['biasQ'], q0=0, keys=['ckT', 'cvaS']))
                        kbs.append(dict(kT=kTs[hb:hb + 64, hp, t * 64:(t + 1) * 64], v=vas[0:64, t, h, 0:65], nk=64,
                                        bias=biasQ[0:64, 8, h:h + 1], bkeys=['biasQ'], q0=0, keys=['kTs', 'vas'], diag=True))
                        attend_fox(qT[hb:hb + 64, hp, t * 64:(t + 1) * 64], 64, [(0, 64)], kbs, ['qT'],
                                   attn_out_s(h * 64, t))
            oz_to_T('c', NT, np_, 4)

            if cfg.get('marks'): print('MARK', kind, Q, 'A_proj', pg.nops)
            wt, wk = load_w(D['w_in'], l, COL['a_q'], 512)
            for t, p, pk in pipe_tiles(NT, lambda t: proj_tm(xnT, 'xnT', t * np_, np_, wt, wk, 512)):
                evac_q(p, pk, t, np_, 8, 'a_qnorm')
            wt, wk = load_w(D['w_in'], l, COL['a_k'], 512)
            for t, p, pk in pipe_tiles(NT, lambda t: proj_tm(xnT, 'xnT', t * np_, np_, wt, wk, 512)):
                if samp:
                    evac_k(p, pk, np_, 8, 'a_knorm', D['sa_k'][l, rows(t), :],
                           lambda t=t: kTs[:, 0:4, t * 64:(t + 1) * 64], 'kTs')
                else:
                    gt = Q * 4 + t
                    od = D['pa_k'][l, (gt - 28) * 128:(gt - 27) * 128, :] if gt >= 28 else None
                    evac_k(p, pk, np_, 8, 'a_knorm', od,
                           lambda gt=gt: kT_a[:, :, (gt % 8) * 128:(gt % 8 + 1) * 128], 'kT_a')
            wt, wk = load_w(D['w_in'], l, COL['a_v'], 512)
            for t, p, pk in pipe_tiles(NT, lambda t: proj_tm(xnT, 'xnT', t * np_, np_, wt, wk, 512)):
                if samp:
                    evac_v(p, pk, np_, 8, D['sa_v'][l, rows(t), :], vas[0:np_, t, :, 0:64], 'vas')
                else:
                    gt = Q * 4 + t
                    od = D['pa_v'][l, (gt - 28) * 128:(gt - 27) * 128, :] if gt >= 28 else None
                    evac_v(p, pk, np_, 8, od, va_a[:, gt % 8, :, 0:64], 'va_a')
            wt, wk = load_w(D['w_in'], l, COL['a_z'], 512)
            for t, p, pk in pipe_tiles(NT, lambda t: proj_tm(xnT, 'xnT', t * np_, np_, wt, wk, 512)):
                pg.op('act', lambda e, p=p, t=t: e.activation(out=sz[0:np_, t, :], in_=p, func=AF.Silu), r=[pk], w=['sz'])
            for t in range(NT):
                gt = Q * 4 + t
                if samp:
                    load_cache_kv(D['ca_k'][l, t], D['ca_v'][l, t], 4, 8)
                for hq in range(2):
                    acc = None
                    for h4 in range(4):
                        h = hq * 4 + h4
                        hp, hb = h // 2, (h % 2) * 64
                        kbs = []
                        if not samp:
                            for i5 in range(5):
                                gk = gt - 4 + i5
                                if gk < 0:
                                    continue
                                s8 = gk % 8
                                d_ = dict(kT=kT_a[hb:hb + 64, hp, s8 * 128:(s8 + 1) * 128], v=va_a[:, s8, h, 0:65], nk=128,
                                          bias=None, q0=0, keys=['kT_a', 'va_a'])
                                if i5 in (1, 2):
                                    d_['bias'] = expB[:, h, i5 * 128:i5 * 128 + 1]; d_['bkeys'] = ['expB']
                                else:
                                    d_['badd'] = expB[:, h, i5 * 128:(i5 + 1) * 128]; d_['mkeys'] = ['expB']
                                kbs.append(d_)
                        else:
                            for kb in range(4):
                                d_ = dict(kT=ckT[hb:hb + 64, hp, kb * 128:(kb + 1) * 128], v=cva_[:, kb, h, 0:65], nk=128,
                                          bias=None, q0=0, keys=['ckT', 'cvaS'])
                                if kb in (0, 1, 2):
                                    d_['bias'] = expB[:, h, kb * 128:kb * 128 + 1]; d_['bkeys'] = ['expB']
                                else:
                                    d_['badd'] = expB[:, h, kb * 128:kb * 128 + 64]; d_['mkeys'] = ['expB']
                                kbs.append(d_)
                            kbs.append(dict(kT=kTs[hb:hb + 64, hp, t * 64:(t + 1) * 64], v=vas[0:64, t, h, 0:65], nk=64,
                                            bias=None, badd=expB[:, h, 512:576], mkeys=['expB'], q0=0, keys=['kTs', 'vas']))
                        a2 = None if acc is None else (acc[0], acc[1], h4, False)
                        acc = attend(qT[hb:hb + 64, hp, t * np_:(t + 1) * np_], np_, [(0, np_)], kbs, ['qT'], None, acc=a2)
                    pov, pok = acc
                    rl = st8[0:np_, 2, 0:4]
                    pg.op('dve', lambda e, pov=pov: e.reciprocal(out=rl, in_=pov[0:np_, 0:4, 64]), r=[pok], w=['st8'])
                    tmp = f2[0:np_, 0:256].rearrange("p (j d) -> p j d", d=64)
                    pg.op('dve', lambda e, pov=pov, tmp=tmp: e.tensor_tensor(out=tmp, in0=pov[0:np_, 0:4, 0:64],
                                                                           in1=rl.unsqueeze(2).to_broadcast([np_, 4, 64]), op=ALU.mult),
                          r=[pok, 'st8'], w=['f2'])
                    pg.op('pool', lambda e, tmp=tmp, hq=hq, t=t: e.tensor_tensor(
                        out=oz[0:np_, t, hq * 256:(hq + 1) * 256].rearrange("p (j d) -> p j d", d=64), in0=tmp,
                        in1=sz[0:np_, t, hq * 256:(hq + 1) * 256].rearrange("p (j d) -> p j d", d=64), op=ALU.mult),
                        r=['f2', 'sz'], w=['oz'])
            oz_to_T('a', NT, np_, 4)

            if cfg.get('marks'): print('MARK', kind, Q, 'M', pg.nops)
            wt, wk = load_w(D['w_in'], l, COL['m_q'], 512)
            for t, p, pk in pipe_tiles(NT, lambda t: proj_tm(xnT, 'xnT', t * np_, np_, wt, wk, 512)):
                pg.op('act', lambda e, p=p, t=t: e.activation(out=sz[0:np_, t, 0:256], in_=p[:, 256:512], func=AF.Silu),
                      r=[pk], w=['sz'])
                evac_q(p[:, 0:256], pk, t, np_, 4, 'm_qnorm')
            if not samp:
                for h in range(4):
                    hp, hb = h // 2, (h % 2) * 64
                    kbs = [dict(kT=kT_m[hb:hb + 64, hp, kb * 128:(kb + 1) * 128], v=va_m[:, kb, h, 0:65], nk=128, bias=None,
                                q0=0, keys=['kT_m', 'va_m']) for kb in range(2)]
                    attend(qT[hb:hb + 64, hp, :], 512, [(j * 128, 128) for j in range(4)], kbs, ['qT'],
                           attn_out(h * 64, 4, 128))
            else:
                for t in range(NT):
                    load_cache_kv(D['cm_k'][l, t], D['cm_v'][l, t], 2, 4)
                    for h in range(4):
                        hp, hb = h // 2, (h % 2) * 64
                        kbs = [dict(kT=ckT[hb:hb + 64, hp, kb * 128:(kb + 1) * 128], v=cva_[:, kb, h, 0:65], nk=128, bias=None,
                                    q0=0, keys=['ckT', 'cvaS']) for kb in range(2)]
                        attend(qT[hb:hb + 64, hp, t * 64:(t + 1) * 64], 64, [(0, 64)], kbs, ['qT'], attn_out_s(h * 64, t))
            oz_to_T('m', NT, np_, 2)

            if cfg.get('marks'): print('MARK', kind, Q, 'B_proj', pg.nops)
            wt, wk = load_w(D['w_in'], l, COL['b_z'], 512)
            for t, p, pk in pipe_tiles(NT, lambda t: proj_tm(xnT, 'xnT', t * np_, np_, wt, wk, 512)):
                pg.op('act', lambda e, p=p, t=t: e.activation(out=sz[0:np_, t, :], in_=p, func=AF.Silu), r=[pk], w=['sz', 'szb'])
            wt, wk = load_w(D['w_in'], l, COL['b_dt'], 8)
            for t, p, pk in pipe_tiles(NT, lambda t: proj_tm(xnT, 'xnT', t * np_, np_, wt, wk, 8)):
                softplus_from(p, pk, smv('b_dt_bias', np_), dts[0:np_, t, :], 'dts', np_)
                pg.op('dve', lambda e, t=t: e.tensor_tensor(out=asb[0:np_, t, :], in0=dts[0:np_, t, :], in1=aneg[0:np_, :],
                                                            op=ALU.mult), r=['dts', 'aneg'], w=['asb'])
            for half in range(2):
                js = slice(half * 3, half * 3 + 3)
                if samp:
                    for t in range(NT):
                        for jj in range(3):
                            j = half * 3 + jj
                            pg.dma('sp', raw[:, jj, t, 0:3],
                                   D['sb_conv'][l, t][:, j * 128:(j + 1) * 128].rearrange("r p -> p r"), w=['raw'], nonc=True)
                else:
                    pg.op('pool', lambda e, js=js: cp(e, out=raw[:, :, 0, 0:3], in_=halo[:, js, :]), r=['halo'], w=['raw'])
                wt, wk = load_w(D['w_in'], l, COL['b_xbc'] + half * 384, 384)
                for jj in range(3):
                    i = rot('A'); p = psA[i]; pk = 'psA%d' % i
                    for kc in range(8):
                        pg.op('pe', lambda e, kc=kc, jj=jj, p=p, wt=wt: e.matmul(p[:, 0:NTOK], lhsT=wt[:, kc, jj * 128:(jj + 1) * 128],
                                                                     rhs=xnT[:, kc, 0:NTOK], start=(kc == 0), stop=(kc == 7)),
                              r=['xnT'] + list(wk), w=[pk], inc=(kc == 7))
                    pg.op('act', lambda e, jj=jj, p=p: e.activation(out=raw[:, jj, :, 3:3 + np_],
                                                             in_=p[:, 0:NTOK].rearrange("p (t n) -> p t n", n=np_), func=AF.Copy),
                          r=[pk], w=['raw'])
                if not samp:
                    for t in range(1, NT):
                        pg.op('pool', lambda e, t=t: cp(e, out=raw[:, :, t, 0:3], in_=raw[:, :, t - 1, np_:np_ + 3]),
                              r=['raw'], w=['raw'])
                    pg.op('pool', lambda e, js=js: cp(e, out=halo[:, js, :], in_=raw[:, :, NT - 1, np_:np_ + 3]),
                          r=['raw'], w=['halo'])
                    if Q == NQ - 1:
                        for jj in range(3):
                            j = half * 3 + jj
                            out_tokens.append(pg.dma('sp', D['pb_c'][l][:, j * 128:(j + 1) * 128].rearrange("r p -> p r"),
                                                     halo[:, j, :], r=['halo'], nonc=True))
                else:
                    for t in range(NT):
                        for jj in range(3):
                            j = half * 3 + jj
                            out_tokens.append(pg.dma('sp', D['sb_c'][l, t][:, j * 128:(j + 1) * 128].rearrange("r p -> p r"),
                                                     raw[:, jj, t, np_:np_ + 3], r=['raw'], nonc=True))
                for jj in range(3):
                    j = half * 3 + jj
                    cv = cva[:, 0:NTOK].rearrange("p (t n) -> p t n", n=np_)
                    pg.op('dve', lambda e, j=j, jj=jj, cv=cv: e.tensor_scalar(out=cv, in0=raw[:, jj, :, 0:np_], scalar1=cw[:, j, 0:1],
                                                                scalar2=cw[:, j, 4:5], op0=ALU.mult, op1=ALU.add),
                          r=['raw', 'cw'], w=['cva'])
                    for tap in range(1, 4):
                        pg.op('dve', lambda e, j=j, jj=jj, tap=tap, cv=cv: e.scalar_tensor_tensor(
                            out=cv, in0=raw[:, jj, :, tap:tap + np_], scalar=cw[:, j, tap:tap + 1], in1=cv,
                            op0=ALU.mult, op1=ALU.add), r=['raw', 'cw', 'cva'], w=['cva'])
                    pg.op('act', lambda e, j=j: e.activation(out=xbc[:, j, 0:NTOK], in_=cva[:, 0:NTOK], func=AF.Silu),
                          r=['cva'], w=['xbc'])
            for t in range(NT):
                if samp and not NOSTATE:
                    state_load(D['sb_ssm'][l, t])
                ssd_tile(t, np_, NT)
                if samp and not NOSTATE:
                    state_store(D['sb_s'][l, t])
            if (not samp) and Q == NQ - 1 and not NOSTATE:
                state_store(D['pb_s'][l])
            oz_to_T('b', NT, np_, 4)

            if cfg.get('marks'): print('MARK', kind, Q, 'merge', pg.nops)
            barrier(AEK)
            brs = (('a', 'w_pa', 4), ('b', 'w_pb', 4), ('c', 'w_pc', 4), ('m', 'w_pm', 2))
            for c in range(2):
                for bi, (br, wn, nkc) in enumerate(brs):
                    wg, wgk = load_w(D['w_in'], l, COL['gate'] + bi * 1024 + c * 512, 512)
                    wp, wpk = load_w(D[wn], l, c * 512, 512, nk=nkc)
                    def mproj(t, wg=wg, wgk=wgk, wp=wp, wpk=wpk, br=br, nkc=nkc):
                        p1, pk1 = proj_tm(xnT, 'xnT', t * np_, np_, wg, wgk, 512)
                        isx = rot('S'); p2 = psS[isx]; pk2 = 'psS%d' % isx
                        for kc in range(nkc):
                            pg.op('pe', lambda e, kc=kc: e.matmul(
                                p2[0:np_, :], lhsT=ozT[br][:, kc, t * np_:(t + 1) * np_], rhs=wp[:, kc, :],
                                start=(kc == 0), stop=(kc == nkc - 1)), r=['ozT_' + br] + list(wpk), w=[pk2], inc=(kc == nkc - 1))
                        return p1, pk1, p2, pk2
                    for t, p1, pk1, p2, pk2 in pipe_tiles(NT, mproj):
                        pg.op('act', lambda e, p1=p1: e.activation(out=sig[0:np_, :], in_=p1, func=AF.Sigmoid), r=[pk1], w=['cva'])
                        if bi == 0:
                            pg.op('dve', lambda e, t=t, p2=p2: e.tensor_tensor(out=macc[0:np_, t, :], in0=sig[0:np_, :],
                                                                               in1=p2[0:np_, :], op=ALU.mult),
                                  r=['cva', pk2], w=['macc%d' % t])
                        else:
                            pg.op('dve', lambda e, t=t, p2=p2: e.tensor_tensor(out=f1[0:np_, :], in0=sig[0:np_, :],
                                                                               in1=p2[0:np_, :], op=ALU.mult),
                                  r=['cva', pk2], w=['f1'])
                            pg.op('pool', lambda e, t=t: e.tensor_tensor(out=macc[0:np_, t, :], in0=macc[0:np_, t, :],
                                                                         in1=f1[0:np_, :], op=ALU.add),
                                  r=['f1', 'macc%d' % t], w=['macc%d' % t])
                for t in range(NT):
                    pg.op('act', lambda e, t=t: e.activation(out=b1[0:np_, :], in_=macc[0:np_, t, :], func=AF.Copy),
                          r=['macc%d' % t], w=['b1'])
                    transposes(lambda j: b1[0:np_, j * 128:(j + 1) * 128], 4, np_,
                               lambda e, v, t=t, c=c: cp(e, out=mT[:, c * 4:(c + 1) * 4, t * np_:(t + 1) * np_], in_=v),
                               ['b1'], ['xbc', 'Mh'], evac='dve')
            wos = [load_w(D['w_out'], l, c * 512, 512) for c in range(2)]
            for t in range(NT):
                i = rot('X'); xk = 'xt%d' % i
                pg.dma('sp', xt[0:np_, i, :], xsrc[rows(t), :], r=[('xd', kind, row0 + t * np_)] if l > 0 else [], w=[xk])
                for c in range(2):
                    wo, wok = wos[c]
                    p, pk = proj_tm(mT, 'xbc', t * np_, np_, wo, wok, 512, xkeys=['Mh'])
                    pg.op('dve', lambda e, i=i, p=p, c=c: e.tensor_tensor(out=xt[0:np_, i, c * 512:(c + 1) * 512],
                                                                          in0=xt[0:np_, i, c * 512:(c + 1) * 512], in1=p,
                                                                          op=ALU.add), r=[xk, pk], w=[xk])
                tok = pg.dma('sp', xdst[rows(t), :], xt[0:np_, i, :], r=[xk], w=[('xd', kind, row0 + t * np_)])
                out_tokens.append(tok)

        vflat = va_c[:, :, :, :].rearrange("p a h c -> p (a h c)")
        sb_kTs = vflat[:, 0:1024].rearrange("p (k n) -> p k n", n=256)
        vas = vflat[:, 1024:1024 + 2112].rearrange("p (t h c) -> p t h c", t=4, h=8)
        ckT = vflat[:, 3136:3136 + 4096].rearrange("p (k n) -> p k n", n=1024)
        cva_ = vflat[:, 7232:7232 + 4224].rearrange("p (a h c) -> p a h c", a=8, h=8)
        cums = kT_a[:, 0, 0:576].bitcast(F32).rearrange("p (t k h) -> p t k h", t=4, k=9)
        SKEYS = ['kTs', 'vas', 'ckT', 'cvaS']

        def load_cache_kv(kd, vd, nblk, nh):
            n = nh * 64
            for kb in range(nblk):
                stg, sk = ((b1, 'b1'), (xnb, 'xnb'))[kb % 2]
                pg.dma('pool', stg[:, 0:n], kd[kb * 128:(kb + 1) * 128, :], w=[sk])
                transposes(lambda j, stg=stg: stg[:, j * 128:(j + 1) * 128], nh // 2, 128,
                           lambda e, v, kb=kb: cp(e, out=ckT[:, 0:nh // 2, kb * 128:(kb + 1) * 128], in_=v),
                           [sk], ['ckT'], evac='act' if kb % 2 else 'dve')
                pg.dma('pool', cva_[:, kb, 0:nh, 0:64], vd[kb * 128:(kb + 1) * 128, :].rearrange("p (h d) -> p h d", d=64),
                       w=['cvaS'])

        def state_load(src):
            pg.dma('sp', f3[0:64, :].rearrange("p (h n) -> p h n", n=64), src.rearrange("h p n -> p h n"), w=['f3'])
            i = rot('A'); p = psA[i]; pk = 'psA%d' % i
            for h in range(8):
                pg.op('pe', lambda e, h=h: e.matmul(p[0:64, h * 64:(h + 1) * 64], lhsT=f3[0:64, h * 64:(h + 1) * 64],
                                                    rhs=identf[0:64, :], start=True, stop=True),
                      r=['f3', 'identf'], w=[pk], inc=(h == 7))
            for g in range(2):
                pg.op('act' if g else 'dve', lambda e, g=g: cp(
                    e, out=hT[g * 64:(g + 1) * 64, :, :], in_=p[0:64, g * 256:(g + 1) * 256].rearrange("p (a d) -> p a d", d=64)),
                    r=[pk], w=['hT'])
            pg.op('act', lambda e: e.activation(out=hTb[:, :, :], in_=hT[:, :, :], func=AF.Copy), r=['hT'], w=['hTb'])

        def state_store(dst):
            for g in range(2):
                i = rot('A'); p = psA[i]; pk = 'psA%d' % i
                for a in range(4):
                    pg.op('pe', lambda e, g=g, a=a, p=p: e.matmul(p[0:64, a * 64:(a + 1) * 64], lhsT=hT[g * 64:(g + 1) * 64, a, :],
                                                             rhs=identf[g * 64:(g + 1) * 64, :], start=True, stop=True),
                          r=['hT', 'identf'], w=[pk], inc=(a == 3))
                pg.op('act' if g else 'dve', lambda e, g=g, p=p: cp(e, out=f3[0:64, g * 256:(g + 1) * 256], in_=p[0:64, 0:256]),
                      r=[pk], w=['f3'])
            out_tokens.append(pg.dma('sp', dst.rearrange("h p n -> p h n"), f3[0:64, :].rearrange("p (h n) -> p h n", n=64),
                                     r=['f3']))

        def attn_out_t(h0col, t, np_):
            def fn(pov, pok):
                rl = st8[0:np_, 2, 0:1]
                pg.op('dve', lambda e: e.reciprocal(out=rl, in_=pov[0:np_, 0, 64:65]), r=[pok], w=['st8'])
                pg.op('dve', lambda e: e.scalar_tensor_tensor(out=oz[0:np_, t, h0col:h0col + 64], in0=pov[0:np_, 0, 0:64],
                                                              scalar=rl, in1=sz[0:np_, t, h0col:h0col + 64],
                                                              op0=ALU.mult, op1=ALU.mult), r=[pok, 'st8', 'sz'], w=['oz'])
            return fn

        def attn_out_s(h0col, t):
            return attn_out_t(h0col, t, 64)

        def attend_fox(qT_ap, NQc, qtiles, kblocks, qkeys, out_fn):
            for kb in kblocks:
                if kb.get('diag'):
                    kb['diagmask'] = True
            attend(qT_ap, NQc, qtiles, kblocks, qkeys, out_fn)

        def memory_kv(l):
            def src(t):
                return D['memp'][t * 128:(t + 1) * 128, :]
            barrier(AEK)
            pg.dma('sp', AE[:, 0:1024], D['mnorm'][l:l + 1, :].broadcast_to([128, 1024]), w=AEK)
            norm_tiles(src, 2, 128, lambda n: AE[0:n, 0:1024], lambda t: AEK)
            barrier(AEK)
            wt, wk = load_w(D['w_mkv'], l, 0, 512)
            for t, p, pk in pipe_tiles(2, lambda t: proj_tm(xnT, 'xnT', t * 128, 128, wt, wk, 512)):
                evac_v(p[:, 256:512], pk, 128, 4, D['pm_v'][l, t * 128:(t + 1) * 128, :], va_m[:, t, :, 0:64], 'va_m')
                evac_k(p[:, 0:256], pk, 128, 4, 'm_knorm', D['pm_k'][l, t * 128:(t + 1) * 128, :],
                       lambda t=t: kT_m[:, :, t * 128:(t + 1) * 128], 'kT_m')

        try:
          for l in range(NL):
            layer_consts(l)
            memory_kv(l)
            pg.op('pool', lambda e: e.memset(hT[:], 0.0), w=['hT'])
            pg.op('pool', lambda e: e.memset(hTb[:], 0.0), w=['hTb'])
            pg.op('pool', lambda e: e.memset(halo[:], 0.0), w=['halo'])
            for Q in range(NQ):
                process_group(l, 'p', Q)
            if DO_S:
                barrier(['va_c', 'kT_a', 'cums'] + SKEYS)
                pg.op('pool', lambda e: e.memset(vas, 1.0), w=['vas'])
                pg.op('pool', lambda e: e.memset(cva_, 1.0), w=['cvaS'])
                process_group(l, 's', 0)
                barrier(['va_c', 'kT_a', 'cums'] + SKEYS)
                pg.op('pool', lambda e: e.memset(va_c[:], 1.0), w=['va_c'])
        except StopBuild:
            pass
        pg.maxops = None
        pg.op('pe', lambda e: e.matmul(psA[0][0:1, 0:1], lhsT=cstb[:, 0:1], rhs=cstb[:, 0:1], start=True, stop=True), r=['cstb'], w=['psA0'])
        pg.op('act', lambda e: e.activation(out=bar[:, 1:2], in_=bar[:, 1:2], func=AF.Copy), r=['psA0'], w=['bar2'])
        pg.op('dve', lambda e: e.tensor_copy(out=bar[:, 1:2], in_=bar[:, 1:2]), w=['bar2'])
        pg.op('pool', lambda e: e.tensor_copy(out=bar[:, 1:2], in_=bar[:, 1:2]), w=['bar2'])
        out_tokens.append(pg.dma('sp', D['pb_c'][0, 0:1, 0:2], bar[0:1, 0:2], r=['bar2', 'bar'], w=['zz']) if False else None)
        out_tokens[:] = [t for t in out_tokens if t is not None]
        pg.op('pool', lambda e: e.memset(bar[:, 0:1], 0.0), r=['bar2'], w=['bar'])
        pg.finish(out_tokens)
        pg._wait('sp', ('c', 'pool', pg.cnt['pool'] - 1))
        print("ops:", pg.nops, "dmas:", pg.dma_n, "cnt:", pg.cnt)
    return nc


def _consts():
    c = np.zeros((128, NCST), np.float32)
    k = np.arange(128)[:, None]
    m = np.arange(128)[None, :]
    c[:, C_M1:C_M1 + 128] = (k <= m)
    c[:, C_M2:C_M2 + 128] = (k > m)
    c[:, C_S127:C_S127 + 128] = (k == 127)
    c[:, C_S63:C_S63 + 128] = (k == 63)
    band = np.ones((128, 5, 128), np.float32)
    s = np.arange(128)[:, None]
    t = np.arange(128)[None, :]
    band[:, 0, :] = 1.0 - ((s < 64) & (t >= 64))
    band[:, 4, :] = 1.0 - ((s >= 64) & (t < 64))
    ident = (k == m).astype(np.float32)
    return c, np.ascontiguousarray(band.reshape(128, 640)), ident


def _prep(inputs, cfg=None):
    f = lambda a: np.ascontiguousarray(np.asarray(a, dtype=np.float32))
    I = {k: f(v) for k, v in inputs.items()}
    small = np.concatenate([I[n].reshape(2, -1) for n in SM], axis=1)
    s = np.arange(128)[:, None]
    j = np.arange(640)[None, :]
    dist = 512 + (j % 128) - 128 * (j // 128) - s
    idx = np.clip(dist, -128, 128) + 128
    relb = np.ascontiguousarray(np.transpose(I['a_rel'][:, idx, :], (0, 1, 3, 2)))
    cwt = np.concatenate([I['b_conv_w'], I['b_conv_b'][:, None, :]], axis=1)
    convw = np.ascontiguousarray(np.transpose(cwt.reshape(2, 5, 6, 128), (0, 3, 2, 1)))
    cst, band, ident = _consts()
    maps = []
    for c in range(8):
        b = c % 4
        ss = slice(c * NS, (c + 1) * NS)
        m = dict(
            xp=I['x_prompt'][b], xs=I['x_sample'][ss].reshape(NS * TS, DM), memp=I['mem_prompt'][b],
            ca_k=I['cache_a_k'][:, ss].reshape(2, NS, 512, 512), ca_v=I['cache_a_v'][:, ss].reshape(2, NS, 512, 512),
            cc_k=I['cache_c_k'][:, ss].reshape(2, NS, 1024, 512), cc_v=I['cache_c_v'][:, ss].reshape(2, NS, 1024, 512),
            cc_f=I['cache_c_logf'][:, ss], sb_ssm=I['state_b_ssm'][:, ss], sb_conv=I['state_b_conv'][:, ss],
            cm_k=I['cache_mem_k'][:, ss].reshape(2, NS, 256, 256), cm_v=I['cache_mem_v'][:, ss].reshape(2, NS, 256, 256),
            w_in=I['w_in'], w_mkv=I['w_mkv'], w_pa=I['w_pa'], w_pb=I['w_pb'], w_pc=I['w_pc'], w_pm=I['w_pm'],
            w_out=I['w_out'], small=small, mnorm=I['m_norm'], band=band, ident=ident, relb=relb, convw=convw, cst=cst)
        maps.append({k: np.ascontiguousarray(v) for k, v in m.items()})
    return maps


_NC_CACHE = {}


def kernel(**inputs):
    cfg = {}
    key = 'full'
    if key not in _NC_CACHE:
        _NC_CACHE[key] = build(cfg)
    nc = _NC_CACHE[key]
    maps = _prep(inputs)
    res = run_bass_kernel_spmd(nc, maps, core_ids=list(range(8)))
    R = res.results
    P4 = range(4)
    st = lambda name, shape: np.stack([R[b][name] for b in P4], axis=1).reshape(shape)
    cat = lambda name: np.concatenate([R[c][name] for c in range(8)], axis=1)
    y_prompt = np.stack([R[b]['yp'] for b in P4], axis=0)
    y_sample = np.concatenate([R[c]['ys'].reshape(NS, TS, DM) for c in range(8)], axis=0)
    outs = [y_prompt, y_sample,
            st('pa_k', (2, 4, 512, 8, 64)), st('pa_v', (2, 4, 512, 8, 64)),
            st('pc_k', (2, 4, SEQ, 8, 64)), st('pc_v', (2, 4, SEQ, 8, 64)), st('pc_f', (2, 4, SEQ, 8)),
            st('pb_s', (2, 4, 8, 64, 64)), st('pb_c', (2, 4, 3, 768)),
            st('pm_k', (2, 4, 256, 4, 64)), st('pm_v', (2, 4, 256, 4, 64))]
    for name, tail in (('sa_k', (8, 64)), ('sa_v', (8, 64)), ('sc_k', (8, 64)), ('sc_v', (8, 64)), ('sc_f', (8,))):
        a = np.concatenate([R[c][name].reshape((2, NS, TS) + tail) for c in range(8)], axis=1)
        outs.append(a)
    outs.append(cat('sb_s'))
    outs.append(cat('sb_c'))
    return tuple(np.ascontiguousarray(o.astype(np.float32)) for o in outs)
```

```python
import contextlib
import numpy as np
import ml_dtypes
import concourse.bass as bass
import concourse.mybir as mybir
from concourse.bass_utils import run_bass_kernel_spmd

F32 = mybir.dt.float32
BF16 = mybir.dt.bfloat16
AF = mybir.ActivationFunctionType
ALU = mybir.AluOpType
AX = mybir.AxisListType

DM = 1024
DIN = 10000
SEQ = 4096
NS = 4
TS = 64
EPS = 1e-6
COL = dict(a_q=0, a_k=512, a_v=1024, a_z=1536, b_z=2048, b_xbc=2560, b_dt=3328,
           c_q=3336, c_k=3848, c_v=4360, c_f=4872, c_z=4880, m_q=5392, m_z=5648, gate=5904)
SM = {}
_o = 0
for _n, _w in (('g_norm', 1024), ('b_norm', 512), ('a_qnorm', 64), ('a_knorm', 64),
               ('c_qnorm', 64), ('c_knorm', 64), ('m_qnorm', 64), ('m_knorm', 64),
               ('b_dt_bias', 8), ('b_a_log', 8), ('b_d', 8), ('c_fbias', 8)):
    SM[_n] = (_o, _w)
    _o += _w
NSM = _o
C_M1, C_M2, C_S127, C_S63 = 0, 128, 256, 384
NCST = 512


def cp(e, out, in_):
    if hasattr(e, 'tensor_copy'):
        return e.tensor_copy(out=out, in_=in_)
    return e.activation(out=out, in_=in_, func=AF.Copy)


SAME_ENGINE_SYNC = True


class StopBuild(Exception):
    pass


class Prog:
    EPOCH = 24000
    ND = 48

    def __init__(self, nc, es):
        self.nc, self.es = nc, es
        self.eng = {'pe': nc.tensor, 'act': nc.scalar, 'dve': nc.vector, 'pool': nc.gpsimd, 'sp': nc.sync}
        self.cnt = {e: 0 for e in self.eng}
        self.sems = {}
        self.waited = {e: {} for e in self.eng}
        self.last_w = {}
        self.readers = {}
        self.dma_n = 0
        self.dma_sems = [es.enter_context(nc.semaphore("dq%d" % i)) for i in range(self.ND)]
        self.dma_tokens = []
        self.nops = 0
        self.maxops = None

    def _sem(self, e, epoch):
        k = (e, epoch)
        if k not in self.sems:
            self.sems[k] = self.es.enter_context(self.nc.semaphore("s_%s_%d" % (e, epoch)))
        return self.sems[k]

    def _wait(self, e, tok):
        if tok[0] == 'c':
            _, pe, c = tok
            if pe == e and (e == 'pe' or not SAME_ENGINE_SYNC):
                return
            epoch, v = divmod(c, self.EPOCH)
            key, val, sem = ('c', pe, epoch), v + 1, self._sem(pe, epoch)
        else:
            _, slot, rnd = tok
            key, val, sem = ('d', slot), 16 * (rnd + 1), self.dma_sems[slot]
        if self.waited[e].get(key, 0) >= val:
            return
        self.waited[e][key] = val
        self.eng[e].wait_ge(sem, val)

    def _deps(self, r, w):
        deps = set()
        for k in r:
            if k in self.last_w:
                deps.add(self.last_w[k])
            if isinstance(k, str) and k.startswith('ps'):
                deps.update(self.readers.get(k, ()))
        for k in w:
            if k in self.last_w:
                deps.add(self.last_w[k])
            deps.update(self.readers.get(k, ()))
        return deps

    def _reg(self, tok, r, w):
        for k in r:
            self.readers.setdefault(k, []).append(tok)
        for k in w:
            self.last_w[k] = tok
            self.readers[k] = []

    def op(self, e, fn, r=(), w=(), inc=True):
        if self.maxops is not None and self.nops >= self.maxops:
            raise StopBuild()
        for tok in self._deps(r, w):
            self._wait(e, tok)
        inst = fn(self.eng[e])
        c = self.cnt[e]
        if inc:
            epoch, _ = divmod(c, self.EPOCH)
            inst.then_inc(self._sem(e, epoch), 1)
            self.cnt[e] += 1
        self._reg(('c', e, c), r, w)
        self.nops += 1

    def dma(self, q, out, in_, r=(), w=(), nonc=False):
        if self.maxops is not None and self.nops >= self.maxops:
            raise StopBuild()
        n = self.dma_n
        self.dma_n += 1
        slot, rnd = n % self.ND, n // self.ND
        if rnd > 0:
            self._wait(q, ('d', slot, rnd - 1))
        for tok in self._deps(r, w):
            self._wait(q, tok)
        kw = {}
        if nonc:
            kw['allow_slow_non_contiguous'] = True
        inst = self.eng[q].dma_start(out=out, in_=in_, **kw)
        inst.then_inc(self.dma_sems[slot], 16)
        tok = ('d', slot, rnd)
        self._reg(tok, r, w)
        self.dma_tokens.append(tok)
        self.nops += 1
        return tok

    def finish(self, out_tokens):
        for tok in out_tokens:
            self._wait('sp', tok)
        last = {}
        for tok in self.dma_tokens:
            last[tok[1]] = tok
        for tok in last.values():
            self._wait('sp', tok)


def build(cfg):
    NL = cfg.get('layers', 2)
    NQ = cfg.get('nq', 8)
    DO_S = cfg.get('sample', True)
    NOSTATE = cfg.get('nostate', False)
    nc = bass.Bass("TRN2", target_bir_lowering=False)
    es = contextlib.ExitStack()
    D = {}

    def din(name, shape):
        D[name] = nc.dram_tensor(name, list(shape), F32, kind="ExternalInput").ap()

    def dout(name, shape):
        D[name] = nc.dram_tensor(name, list(shape), F32, kind="ExternalOutput").ap()

    din('xp', [SEQ, DM]); din('xs', [NS * TS, DM]); din('memp', [256, DM])
    din('ca_k', [2, NS, 512, 512]); din('ca_v', [2, NS, 512, 512])
    din('cc_k', [2, NS, 1024, 512]); din('cc_v', [2, NS, 1024, 512]); din('cc_f', [2, NS, 1024, 8])
    din('sb_ssm', [2, NS, 8, 64, 64]); din('sb_conv', [2, NS, 3, 768])
    din('cm_k', [2, NS, 256, 256]); din('cm_v', [2, NS, 256, 256])
    din('w_in', [2, DM, DIN]); din('w_mkv', [2, DM, 512])
    din('w_pa', [2, 512, DM]); din('w_pb', [2, 512, DM]); din('w_pc', [2, 512, DM]); din('w_pm', [2, 256, DM])
    din('w_out', [2, DM, DM])
    din('small', [2, NSM]); din('mnorm', [2, 1024]); din('relb', [2, 128, 8, 640]); din('convw', [2, 128, 6, 5]); din('cst', [128, NCST]); din('band', [128, 640]); din('ident', [128, 128])
    dout('yp', [SEQ, DM]); dout('ys', [NS * TS, DM])
    dout('pa_k', [2, 512, 512]); dout('pa_v', [2, 512, 512])
    dout('pc_k', [2, SEQ, 512]); dout('pc_v', [2, SEQ, 512]); dout('pc_f', [2, SEQ, 8])
    dout('pb_s', [2, 8, 64, 64]); dout('pb_c', [2, 3, 768])
    dout('pm_k', [2, 256, 256]); dout('pm_v', [2, 256, 256])
    dout('sa_k', [2, NS * TS, 512]); dout('sa_v', [2, NS * TS, 512])
    dout('sc_k', [2, NS * TS, 512]); dout('sc_v', [2, NS * TS, 512]); dout('sc_f', [2, NS * TS, 8])
    dout('sb_s', [2, NS, 8, 64, 64]); dout('sb_c', [2, NS, 3, 768])
    x1p = nc.dram_tensor("x1p", [SEQ, DM], F32, kind="Internal").ap()
    x1s = nc.dram_tensor("x1s", [NS * TS, DM], F32, kind="Internal").ap()

    with es:
        pg = Prog(nc, es)
        pg.maxops = cfg.get('maxops')
        out_tokens = []

        def sb(name, shape, dt):
            return es.enter_context(nc.sbuf_tensor("sb_" + name, list(shape), dt))

        def ps(name, shape, dt):
            return es.enter_context(nc.psum_tensor("ps_" + name, list(shape), dt))

        cst = sb("cst", [128, NCST], F32)
        cstb = sb("cstb", [128, 256], BF16)
        small = sb("small", [128, NSM], F32)
        expB = sb("expB", [128, 8, 640], BF16)
        cw = sb("cw", [128, 6, 5], F32)
        aneg = sb("aneg", [128, 8], F32)
        xt = sb("xt", [128, 1, DM], F32)
        W3 = sb("W3", [128, 4096], BF16)
        xnb = W3[:, 0:1024]
        sq = W3[:, 1024:2048].bitcast(F32)
        f2 = W3[:, 2048:3072].bitcast(F32)
        f3 = W3[:, 3072:4096].bitcast(F32)
        xnT = sb("xnT", [128, 8, 512], BF16)
        NWB = 3
        wbuf = [sb("wbuf%d" % i, [128, 8, 512], BF16) for i in range(NWB)]
        wbuf.append(W3[:, :].rearrange("p (k n) -> p k n", n=512))
        WKEYS = [['wbuf0'], ['wbuf1'], ['wbuf2'], ['wbuf3', 'xnb', 'sq', 'f2', 'f3']]
        kT_c = sb("kT_c", [128, 4, SEQ], BF16)
        va_c = sb("va_c", [128, 32, 8, 66], BF16)
        kT_a = sb("kT_a", [128, 4, 1024], BF16)
        va_a = sb("va_a", [128, 8, 8, 66], BF16)
        kT_m = sb("kT_m", [128, 2, 256], BF16)
        va_m = sb("va_m", [128, 2, 4, 66], BF16)
        cum = sb("cum", [128, 32, 8], F32)
        biasQ = sb("biasQ", [128, 32, 8], F32)
        qT = sb("qT", [128, 4, 512], BF16)
        sz = sb("sz", [128, 4, 512], BF16)
        oz = sb("oz", [128, 4, 512], BF16)
        ozT = {k: sb("ozT_" + k, [128, n, 512], BF16) for k, n in (('a', 4), ('b', 4), ('c', 4), ('m', 2))}
        f1 = sb("f1", [128, 512], F32)
        b1 = sb("b1", [128, 512], BF16)
        PT = [sb("PT%d" % i, [128, 512], BF16) for i in range(2)]
        st8 = sb("st8", [128, 8, 8], F32)
        raw = sb("raw", [128, 3, 4, 131], F32)
        halo = sb("halo", [128, 6, 3], F32)
        AR2 = sb("AR2", [128, 4096], BF16)
        xbc = AR2[:, 0:3072].rearrange("p (j n) -> p j n", n=512)
        Mh = AR2[:, 3072:4096].rearrange("p (h n) -> p h n", n=128)
        mT = AR2[:, :].rearrange("p (k n) -> p k n", n=512)
        cva = sb("cva", [128, 512], F32)
        sig = cva
        dts = sb("dts", [128, 4, 8], F32)
        asb = sb("asb", [128, 4, 8], F32)
        btk = sb("btk", [128, 128], BF16)
        AE = sb("AE", [128, 2048], F32)
        Ah = AE[:, 0:512].rearrange("p (h n) -> p h n", n=128)
        Ee = AE[:, 512:1024].rearrange("p (h n) -> p h n", n=128)
        Gm = AE[:, 1024:1280].rearrange("p (g n) -> p g n", n=128)
        bdec = AE[:, 1280:1536].bitcast(BF16).rearrange("p (a g n) -> p a g n", a=4, g=2)
        xdt = AE[:, 1536:1792].bitcast(BF16).rearrange("p (h d) -> p h d", d=64)
        xtk = AE[:, 1792:2048].bitcast(BF16)
        macc = AE[:, :].rearrange("p (t n) -> p t n", n=512)
        hT = sb("hT", [128, 4, 64], F32)
        hTb = sb("hTb", [128, 4, 64], BF16)
        ssd8 = sb("ssd8", [128, 4, 8], F32)
        psA = [ps("psA%d" % i, [128, 512], F32) for i in range(2)]
        psT = [ps("psT%d" % i, [128, 1024], BF16) for i in range(2)]
        psS = [ps("psS%d" % i, [128, 512], F32) for i in range(2)]
        psO = [ps("psO%d" % i, [128, 512], F32) for i in range(2)]
        rr = {'A': 0, 'T': 0, 'S': 0, 'O': 0, 'W': 0, 'P': 0, 'X': 0, 'WP': 0}
        rrn = {'X': 1, 'WP': 1}

        def rot(k, n=2):
            v = rr[k]
            rr[k] = (v + 1) % rrn.get(k, n)
            return v

        ident_b = cstb[:, 0:128]
        M1f = cst[:, C_M1:C_M1 + 128]
        M2f = cst[:, C_M2:C_M2 + 128]
        S127f = cst[:, C_S127:C_S127 + 128]
        S63f = cst[:, C_S63:C_S63 + 128]
        M1b = cstb[:, 128:256]

        def smv(name, rows=128):
            o, w = SM[name]
            return small[0:rows, o:o + w]

        bar = sb("bar", [128, 2], F32)
        epst = sb("epst", [128, 1], F32)
        nhalf = sb("nhalf", [128, 8], F32)
        identf = sb("identf", [128, 64], F32)

        def barrier(keys):
            pg.op('pool', lambda e: e.memset(bar[:, 0:1], 0.0), w=['bar'] + list(keys))

        AEK = ['Ah%d' % h for h in range(8)] + ['Ee', 'Gm', 'bdec', 'xdt', 'xtk', 'AErel'] + ['macc%d' % t for t in range(4)]
        pg.op('pool', lambda e: e.memset(epst[:, :], EPS), w=['epst'])
        pg.op('pool', lambda e: e.memset(nhalf[:, :], -0.5), w=['epst'])
        pg.dma('sp', identf[0:64, :], D['ident'][0:64, 0:64], w=['identf'])
        pg.dma('sp', identf[64:128, :], D['ident'][0:64, 0:64], w=['identf'])
        pg.dma('sp', cst[:, :], D['cst'][:, :], w=['cst'])
        pg.dma('sp', AE[:, 0:128], D['ident'][:, :], w=AEK)
        pg.op('dve', lambda e: cp(e, out=cstb[:, 0:128], in_=AE[:, 0:128]), r=AEK, w=['cstb'])
        pg.op('dve', lambda e: cp(e, out=cstb[:, 128:256], in_=cst[:, C_M1:C_M1 + 128]), r=['cst'], w=['cstb'])
        for t_, k_ in ((va_c, 'va_c'), (va_a, 'va_a'), (va_m, 'va_m')):
            pg.op('pool', lambda e, t_=t_: e.memset(t_[:], 1.0), w=[k_])

        def group_wlist(l):
            wl = []
            for nm in ('c_q', 'c_k', 'c_v'):
                wl.append(('w_in', l, COL[nm], 512, 8))
            wl.append(('w_in', l, COL['c_f'], 8, 8))
            wl.append(('w_in', l, COL['c_z'], 512, 8))
            for nm in ('a_q', 'a_k', 'a_v', 'a_z', 'm_q', 'b_z'):
                wl.append(('w_in', l, COL[nm], 512, 8))
            wl.append(('w_in', l, COL['b_dt'], 8, 8))
            for half in range(2):
                wl.append(('w_in', l, COL['b_xbc'] + half * 384, 384, 8))
            for c in range(2):
                for bi, (wn, nkc) in enumerate((('w_pa', 4), ('w_pb', 4), ('w_pc', 4), ('w_pm', 2))):
                    wl.append(('w_in', l, COL['gate'] + bi * 1024 + c * 512, 512, 8, True))
                    wl.append((wn, l, c * 512, 512, nkc, True))
            for c in range(2):
                wl.append(('w_out', l, c * 512, 512, 8, True))
            return wl

        WL = []
        for l_ in range(NL):
            WL.append(('w_mkv', l_, 0, 512, 8))
            for _ in range(NQ + (1 if DO_S else 0)):
                WL += group_wlist(l_)
        wbi, prev_occ, lastocc, r3, r4 = [], [], {}, 0, 0
        for k_, ent in enumerate(WL):
            if len(ent) > 5 and ent[5]:
                b_ = r4 % 4; r4 += 1
            else:
                b_ = r3 % 3; r3 += 1; r4 = r3
            wbi.append(b_)
            prev_occ.append(lastocc.get(b_, -1))
            lastocc[b_] = k_
        wstate = {'ptr': 0, 'issued': 0}

        def load_w(dram, l, c0, n, nk=8, buf=None):
            i = wstate['ptr']
            exp = WL[i]
            assert exp[1] == l and exp[2] == c0 and exp[3] == n and exp[4] == nk and D[exp[0]] is dram, (exp, l, c0, n, nk)
            in_merge = len(exp) > 5 and exp[5]
            while wstate['issued'] < len(WL) and wstate['issued'] <= i + 3 and \
                    (wstate['issued'] <= i or prev_occ[wstate['issued']] <= i - 2) and \
                    (wbi[wstate['issued']] != 3 or in_merge):
                k = wstate['issued']
                nm, l2, c2, n2, nk2 = WL[k][0:5]
                src = D[nm][l2, :, c2:c2 + n2].rearrange("(k p) n -> p k n", p=128)
                pg.dma('pool', wbuf[wbi[k]][:, 0:nk2, 0:n2], src, w=WKEYS[wbi[k]])
                wstate['issued'] += 1
            wstate['ptr'] += 1
            return wbuf[wbi[i]], WKEYS[wbi[i]]

        def transposes(src_ap_fn, nblk, np_, dst_fn, rkeys, wkeys, evac='dve'):
            i = rot('T'); pt = psT[i]; pk = 'psT%d' % i
            for j in range(nblk):
                pg.op('pe', lambda e, j=j: e.transpose(out=pt[:, j * 128:j * 128 + np_], in_=src_ap_fn(j),
                                                       identity=ident_b[0:np_, 0:np_]),
                      r=list(rkeys) + ['cstb'], w=[pk], inc=(j == nblk - 1))
            view = pt[:, 0:nblk * 128].rearrange("p (b n) -> p b n", n=128)[:, :, 0:np_]
            pg.op(evac, lambda e: dst_fn(e, view), r=[pk], w=list(wkeys))

        def rstd_from_ss(ss_ap, rs_ap, n, rows, key):
            w_ = rs_ap.shape[-1]
            pg.op('dve', lambda e: e.tensor_scalar(out=rs_ap, in0=ss_ap, scalar1=1.0 / n, scalar2=EPS,
                                                    op0=ALU.mult, op1=ALU.add), r=[key], w=[key])
            pg.op('pool', lambda e: e.tensor_tensor(out=rs_ap, in0=rs_ap, in1=nhalf[0:rows, 0:w_], op=ALU.pow),
                  r=[key, 'epst'], w=[key])

        def head_norm(psap, pk, nh, gain_ap, out_ap, outkeys, np_, scale=None):
            n = nh * 64
            pg.op('act', lambda e: e.activation(out=sq[0:np_, 0:n], in_=psap, func=AF.Square), r=[pk], w=['sq'])
            ssv = st8[0:np_, 0, 0:nh]
            pg.op('dve', lambda e: e.tensor_reduce(out=ssv, in_=sq[0:np_, 0:n].rearrange("p (h d) -> p h d", d=64),
                                                   axis=AX.X, op=ALU.add), r=['sq'], w=['st8'])
            rstd_from_ss(ssv, ssv, 64, np_, 'st8')
            pg.op('dve', lambda e: e.tensor_tensor(
                out=f1[0:np_, 0:n].rearrange("p (h d) -> p h d", d=64),
                in0=psap.rearrange("p (h d) -> p h d", d=64),
                in1=ssv.unsqueeze(2).to_broadcast([np_, nh, 64]), op=ALU.mult), r=[pk, 'st8'], w=['f1'])
            g = gain_ap.unsqueeze(1).to_broadcast([np_, nh, 64])
            pg.op('dve', lambda e: e.tensor_tensor(
                out=out_ap.rearrange("p (h d) -> p h d", d=64),
                in0=f1[0:np_, 0:n].rearrange("p (h d) -> p h d", d=64), in1=g, op=ALU.mult),
                r=['f1', 'small'], w=list(outkeys))

        def pipe_tiles(NT, proj_fn):
            nxt = proj_fn(0)
            for t in range(NT):
                cur = nxt
                if t + 1 < NT:
                    nxt = proj_fn(t + 1)
                yield (t,) + tuple(cur)

        def proj_tm(xT, xkey, tcol, np_, wt, wkey, n, nk=8, wc0=0, xkeys=()):
            i = rot('A'); p = psA[i]; pk = 'psA%d' % i
            for kc in range(nk):
                pg.op('pe', lambda e, kc=kc: e.matmul(p[0:np_, 0:n], lhsT=xT[:, kc, tcol:tcol + np_],
                                                      rhs=wt[:, kc, wc0:wc0 + n], start=(kc == 0), stop=(kc == nk - 1)),
                      r=[xkey] + list(wkey) + list(xkeys), w=[pk], inc=(kc == nk - 1))
            return p[0:np_, 0:n], pk

        def layer_consts(l):
            pg.dma('sp', small[:, :], D['small'][l:l + 1, :].broadcast_to([128, NSM]), w=['small'])
            pg.dma('sp', cw[:, :, :], D['convw'][l], w=['cw'])
            for name in ('a_qnorm', 'c_qnorm', 'm_qnorm'):
                a = smv(name)
                pg.op('dve', lambda e, a=a: e.tensor_scalar(out=a, in0=a, scalar1=0.125, scalar2=None, op0=ALU.mult),
                      r=['small'], w=['small'])
            pg.op('act', lambda e: e.activation(out=aneg[:, :], in_=smv('b_a_log'), func=AF.Exp), r=['small'], w=['aneg'])
            pg.op('dve', lambda e: e.tensor_scalar(out=aneg[:, :], in0=aneg[:, :], scalar1=-1.0, scalar2=None, op0=ALU.mult),
                  r=['aneg'], w=['aneg'])
            barrier(AEK)
            pg.dma('sp', AE[:, 0:640], D['band'][:, :], w=AEK)
            pg.op('dve', lambda e: e.tensor_scalar(out=AE[:, 0:640], in0=AE[:, 0:640], scalar1=30000.0, scalar2=-30000.0,
                                                    op0=ALU.mult, op1=ALU.add), r=AEK, w=AEK)
            for h in range(8):
                pg.dma('sp', AE[:, 1024:1664], D['relb'][l, :, h, :], w=['AErel'])
                pg.op('dve', lambda e, h=h: e.tensor_tensor(out=expB[:, h, :], in0=AE[:, 1024:1664],
                                                            in1=AE[:, 0:640], op=ALU.add),
                      r=['AErel'] + AEK, w=['expB'])
            barrier(AEK)

        def norm_tiles(src_fn, NT, np_, gain_fn, rk_fn=None):
            for t in range(NT):
                i = rot('X')
                xk = 'xt%d' % i
                pg.dma('sp', xt[0:np_, i, :], src_fn(t), r=(rk_fn(t) if rk_fn else []), w=[xk])
                ssv = st8[0:np_, 1, 0:1]
                pg.op('act', lambda e, i=i: e.activation(out=xnb[0:np_, :], in_=xt[0:np_, i, :], func=AF.Square, accum_out=ssv),
                      r=[xk], w=['xnb', 'st8'])
                rstd_from_ss(ssv, ssv, DM, np_, 'st8')
                pg.op('dve', lambda e, i=i: e.scalar_tensor_tensor(out=xnb[0:np_, :], in0=xt[0:np_, i, :], scalar=ssv,
                                                                  in1=gain_fn(np_), op0=ALU.mult, op1=ALU.mult),
                      r=[xk, 'st8', 'small', 'f2'], w=['xnb'])
                transposes(lambda j: xnb[0:np_, j * 128:(j + 1) * 128], 8, np_,
                           lambda e, v, t=t: cp(e, out=xnT[:, :, t * np_:(t + 1) * np_], in_=v),
                           ['xnb'], ['xnT'], evac='act' if t % 2 else 'dve')

        def attend(qT_ap, NQc, qtiles, kblocks, qkeys, out_fn, acc=None):
            if acc is None:
                io = rot('O'); po = psO[io]; pok = 'psO%d' % io
                pov = po[:, 0:260].rearrange("p (j c) -> p j c", c=65)
                jbase, bank_first = 0, True
            else:
                pov, pok, jbase, bank_first = acc
            lastkb, firstkb = {}, {}
            for bi, kb in enumerate(kblocks):
                for j, (c0, nq) in enumerate(qtiles):
                    if c0 >= kb['q0']:
                        lastkb[j] = bi
                        firstkb.setdefault(j, bi)
            st = {}

            def stage_s(bi):
                kb = kblocks[bi]
                isx = rot('S'); psx = psS[isx]; psk = 'psS%d' % isx
                badd = kb.get('badd')
                pg.op('pe', lambda e: e.matmul(psx[0:kb['nk'], kb['q0']:NQc], lhsT=kb['kT'],
                                               rhs=qT_ap[:, kb['q0']:NQc], start=True, stop=(badd is None)),
                      r=list(qkeys) + list(kb['keys']), w=[psk], inc=(badd is None))
                if badd is not None:
                    pg.op('pe', lambda e: e.matmul(psx[0:kb['nk'], kb['q0']:NQc], lhsT=ident_b[:, 0:kb['nk']],
                                                   rhs=badd, start=False, stop=True),
                          r=['cstb'] + list(kb.get('mkeys', [])), w=[psk])
                st[bi] = (psx, psk)

            def stage_e(bi):
                kb = kblocks[bi]
                psx, psk = st[bi]
                ip = rot('P'); pt = PT[ip]; ptk = 'PT%d' % ip
                if kb.get('bias') is not None:
                    pg.op('act', lambda e: e.activation(out=pt[0:kb['nk'], kb['q0']:NQc], in_=psx[0:kb['nk'], kb['q0']:NQc],
                                                        func=AF.Exp, bias=kb['bias'], scale=1.0),
                          r=[psk] + list(kb.get('bkeys', [])), w=[ptk])
                else:
                    pg.op('act', lambda e: e.activation(out=pt[0:kb['nk'], kb['q0']:NQc], in_=psx[0:kb['nk'], kb['q0']:NQc],
                                                        func=AF.Exp), r=[psk], w=[ptk])
                if kb.get('diagmask'):
                    pg.op('dve', lambda e: e.tensor_tensor(
                        out=pt[0:kb['nk'], kb['q0']:kb['q0'] + kb['nk']], in0=pt[0:kb['nk'], kb['q0']:kb['q0'] + kb['nk']],
                        in1=M1b[0:kb['nk'], 0:kb['nk']], op=ALU.mult), r=[ptk, 'cstb'], w=[ptk])
                st[bi] = (pt, ptk)

            def stage_v(bi):
                kb = kblocks[bi]
                pt, ptk = st[bi]
                for j, (c0, nq) in enumerate(qtiles):
                    if c0 < kb['q0']:
                        continue
                    pg.op('pe', lambda e, j=j, c0=c0, nq=nq: e.matmul(
                        pov[0:nq, jbase + j, :], lhsT=pt[0:kb['nk'], c0:c0 + nq], rhs=kb['v'],
                        start=(bank_first and bi == 0 and j == min(firstkb)), stop=(bi == lastkb[j]), skip_group_check=True),
                        r=[ptk] + list(kb['keys']), w=[pok], inc=(c0 + nq >= NQc))

            n = len(kblocks)
            stage_s(0)
            for bi in range(n):
                if bi + 1 < n:
                    stage_s(bi + 1)
                stage_e(bi)
                stage_v(bi)
            if out_fn is not None:
                out_fn(pov, pok)
            return pov, pok

        def attn_out(h0col, NT, np_, szkey='sz'):
            def fn(pov, pok):
                rl = st8[0:np_, 2, 0:NT]
                pg.op('dve', lambda e: e.reciprocal(out=rl, in_=pov[0:np_, 0:NT, 64]), r=[pok], w=['st8'])
                tmp = f2[0:np_, 0:NT * 64].rearrange("p (j d) -> p j d", d=64)
                pg.op('dve', lambda e: e.tensor_tensor(out=tmp, in0=pov[0:np_, 0:NT, 0:64],
                                                       in1=rl.unsqueeze(2).to_broadcast([np_, NT, 64]), op=ALU.mult),
                      r=[pok, 'st8'], w=['f2'])
                pg.op('pool', lambda e: e.tensor_tensor(out=oz[0:np_, 0:NT, h0col:h0col + 64], in0=tmp,
                                                        in1=sz[0:np_, 0:NT, h0col:h0col + 64], op=ALU.mult),
                      r=['f2', szkey], w=['oz'])
            return fn

        def oz_to_T(br, NT, np_, nblk):
            for t in range(NT):
                transposes(lambda j, t=t: oz[0:np_, t, j * 128:(j + 1) * 128], nblk, np_,
                           lambda e, v, t=t: cp(e, out=ozT[br][:, 0:nblk, t * np_:(t + 1) * np_], in_=v),
                           ['oz'], ['ozT_' + br], evac='act' if t % 2 else 'dve')

        def evac_q(psap, pk, t, np_, nh, gname):
            head_norm(psap, pk, nh, smv(gname, np_), b1[0:np_, 0:nh * 64], ['b1'], np_)
            transposes(lambda j: b1[0:np_, j * 128:(j + 1) * 128], nh // 2, np_,
                       lambda e, v: cp(e, out=qT[:, 0:nh // 2, t * np_:(t + 1) * np_], in_=v),
                       ['b1'], ['qT'], evac='act')

        def evac_k(psap, pk, np_, nh, gname, out_dram, kT_dst_fn, kT_key):
            head_norm(psap, pk, nh, smv(gname, np_), f3[0:np_, 0:nh * 64], ['f3'], np_)
            if out_dram is not None:
                out_tokens.append(pg.dma('sp', out_dram, f3[0:np_, 0:nh * 64], r=['f3']))
            pg.op('dve', lambda e: cp(e, out=b1[0:np_, 0:nh * 64], in_=f3[0:np_, 0:nh * 64]), r=['f3'], w=['b1'])
            transposes(lambda j: b1[0:np_, j * 128:(j + 1) * 128], nh // 2, np_,
                       lambda e, v: cp(e, out=kT_dst_fn(), in_=v), ['b1'], [kT_key], evac='act')

        def evac_v(psap, pk, np_, nh, out_dram, va_dst, va_key):
            if out_dram is not None:
                pg.op('act', lambda e: e.activation(out=f3[0:np_, 0:nh * 64], in_=psap, func=AF.Copy), r=[pk], w=['f3'])
                out_tokens.append(pg.dma('sp', out_dram, f3[0:np_, 0:nh * 64], r=['f3']))
            pg.op('dve', lambda e: cp(e, out=va_dst, in_=psap.rearrange("p (h d) -> p h d", d=64)),
                  r=[pk], w=[va_key])

        def softplus_from(psap, pk, bias_ap, out_ap, okey, np_, neg=False):
            tmp = st8[0:np_, 3, :]
            pg.op('dve', lambda e: e.tensor_tensor(out=tmp, in0=psap, in1=bias_ap, op=ALU.add), r=[pk, 'small'], w=['st8'])
            pg.op('act', lambda e: e.activation(out=tmp, in_=tmp, func=AF.Exp, scale=(-1.0 if neg else 1.0)),
                  r=['st8'], w=['st8'])
            pg.op('dve', lambda e: e.tensor_scalar(out=tmp, in0=tmp, scalar1=1.0, scalar2=None, op0=ALU.add),
                  r=['st8'], w=['st8'])
            pg.op('act', lambda e: e.activation(out=tmp, in_=tmp, func=AF.Ln), r=['st8'], w=['st8'])
            pg.op('dve', lambda e: e.tensor_scalar(out=out_ap, in0=tmp, scalar1=(-1.0 if neg else 1.0), scalar2=None,
                                                    op0=ALU.mult), r=['st8'], w=[okey])

        def cumsum_tile(lf_ap, lfkey, np_, dst_ap, dkey, carry_ap, ckeys):
            i = rot('S'); p = psS[i]; pk = 'psS%d' % i
            pg.op('pe', lambda e: e.matmul(p[0:np_, 0:8], lhsT=M1f[0:np_, 0:np_], rhs=lf_ap, start=True,
                                           stop=(carry_ap is None)), r=[lfkey, 'cst'], w=[pk])
            if carry_ap is not None:
                pg.op('pe', lambda e: e.matmul(p[0:np_, 0:8], lhsT=carry_ap[0], rhs=carry_ap[1], start=False, stop=True),
                      r=list(ckeys) + ['cst'], w=[pk])
            pg.op('dve', lambda e: cp(e, out=dst_ap, in_=p[0:np_, 0:8]), r=[pk], w=[dkey])

        def ssd_tile(t, np_, NT):
            c0 = t * np_
            if t == 0:
                barrier(AEK)
            def ev(e, v):
                return cp(e, out=xtk[0:np_, :].rearrange("p (b n) -> p b n", n=128), in_=v[0:np_, 0:4, :])
            i = rot('T'); pt = psT[i]; pk = 'psT%d' % i
            for j in range(5):
                pg.op('pe', lambda e, j=j: e.transpose(out=pt[0:np_, j * 128:(j + 1) * 128], in_=xbc[:, j, c0:c0 + np_],
                                                       identity=ident_b[:, :]), r=['xbc', 'cstb'], w=[pk], inc=(j == 4))
            pg.op('act', lambda e: e.activation(out=xtk[0:np_, :], in_=pt[0:np_, 0:512], func=AF.Copy), r=[pk], w=['xtk'])
            pg.op('act', lambda e: e.activation(out=btk[0:np_, :], in_=pt[0:np_, 512:640], func=AF.Copy), r=[pk], w=['btk'])
            pg.op('dve', lambda e: e.tensor_tensor(out=xdt[0:np_, :, :], in0=pt[0:np_, 0:512].rearrange("p (h d) -> p h d", d=64),
                                                   in1=dts[0:np_, t, :].unsqueeze(2).to_broadcast([np_, 8, 64]), op=ALU.mult),
                  r=[pk, 'dts'], w=['xdt'])
            a_ap = asb[0:np_, t, :]
            i = rot('A'); p = psA[i]; pk2 = 'psA%d' % i
            pg.op('pe', lambda e: e.matmul(p[0:np_, 0:8], lhsT=M1f[0:np_, 0:np_], rhs=a_ap, start=True, stop=True),
                  r=['asb', 'cst'], w=[pk2], inc=False)
            pg.op('pe', lambda e: e.matmul(p[0:np_, 8:16], lhsT=M2f[0:np_, 0:np_], rhs=a_ap, start=True, stop=True),
                  r=['asb', 'cst'], w=[pk2], inc=False)
            pg.op('pe', lambda e: e.matmul(p[:, 16:24], lhsT=M1f[0:np_, :], rhs=a_ap, start=True, stop=False),
                  r=['asb', 'cst'], w=[pk2], inc=False)
            pg.op('pe', lambda e: e.matmul(p[:, 16:24], lhsT=M2f[0:np_, :], rhs=a_ap, start=False, stop=True),
                  r=['asb', 'cst'], w=[pk2])
            pg.op('act', lambda e: e.activation(out=ssd8[0:np_, 0, :], in_=p[0:np_, 0:8], func=AF.Exp), r=[pk2], w=['ssd8'])
            pg.op('act', lambda e: e.activation(out=ssd8[0:np_, 1, :], in_=p[0:np_, 8:16], func=AF.Exp), r=[pk2], w=['ssd8'])
            pg.op('act', lambda e: e.activation(out=ssd8[:, 2, :], in_=p[:, 16:24], func=AF.Exp), r=[pk2], w=['ssd8'])
            for g in range(2):
                i = rot('A'); pgm = psA[i]; pk3 = 'psA%d' % i
                pg.op('pe', lambda e, g=g, pgm=pgm: e.matmul(pgm[0:np_, 0:np_], lhsT=xbc[g * 64:(g + 1) * 64, 4, c0:c0 + np_],
                                                    rhs=xbc[g * 64:(g + 1) * 64, 5, c0:c0 + np_], start=True, stop=True),
                      r=['xbc'], w=[pk3])
                pg.op('dve', lambda e, g=g, pgm=pgm: e.tensor_tensor(
                    out=Gm[0:np_, g, 0:np_], in0=pgm[0:np_, 0:np_], in1=M1f[0:np_, 0:np_], op=ALU.mult),
                    r=[pk3, 'cst'], w=['Gm'])
            for hh in range(2):
                for h4 in range(4):
                    h = hh * 4 + h4
                    pg.op('dve', lambda e, h=h, h4=h4: e.tensor_scalar(
                        out=Ah[0:np_, h4, 0:np_], in0=M2f[0:np_, 0:np_], scalar1=asb[0:np_, t, h:h + 1], scalar2=None,
                        op0=ALU.mult), r=['cst', 'asb'], w=['Ah%d' % h4])
                psx = psS[hh]; psk = 'psS%d' % hh
                for h4 in range(4):
                    pg.op('pe', lambda e, h4=h4, psx=psx: e.matmul(
                        psx[0:np_, h4 * 128:h4 * 128 + np_], lhsT=Ah[0:np_, h4, 0:np_], rhs=M1f[0:np_, 0:np_],
                        start=True, stop=True), r=['Ah%d' % h4, 'cst'], w=[psk], inc=(h4 == 3))
                pg.op('act', lambda e, psx=psx: e.activation(
                    out=Ee[0:np_, :, 0:np_],
                    in_=psx[0:np_, :].rearrange("p (h n) -> p h n", n=128)[:, :, 0:np_], func=AF.Exp),
                    r=[psk], w=['Ee'])
                pg.op('dve', lambda e, hh=hh: e.tensor_tensor(
                    out=Mh[0:np_, hh * 4:(hh + 1) * 4, 0:np_], in0=Ee[0:np_, :, 0:np_],
                    in1=Gm[0:np_, hh, 0:np_].unsqueeze(1).to_broadcast([np_, 4, np_]), op=ALU.mult),
                    r=['Ee', 'Gm'], w=['Mh'])
            io = rot('O'); py = psO[io]; pyk = 'psO%d' % io
            io2 = rot('O'); py2 = psO[io2]; py2k = 'psO%d' % io2
            for h in range(8):
                pg.op('pe', lambda e, h=h: e.matmul(py[0:np_, h * 64:(h + 1) * 64], lhsT=Mh[0:np_, h, 0:np_],
                                                    rhs=xdt[0:np_, h, :], start=True, stop=True),
                      r=['Mh', 'xdt'], w=[pyk], inc=(h == 7))
            py3 = psS[1]; py3k = 'psS1'
            for h in range(8):
                g = h // 4
                dstp = (py2 if g == 0 else py3)
                pg.op('pe', lambda e, h=h, g=g, dstp=dstp: e.matmul(dstp[0:np_, (h % 4) * 64:(h % 4 + 1) * 64],
                                                         lhsT=xbc[g * 64:(g + 1) * 64, 5, c0:c0 + np_],
                                                         rhs=hTb[g * 64:(g + 1) * 64, h % 4, :], start=True, stop=True),
                      r=['xbc', 'hTb'], w=[py2k if g == 0 else py3k], inc=(h % 4 == 3))
            yv = f1[0:np_, :].rearrange("p (h d) -> p h d", d=64)
            for g, (dstp, dk) in enumerate(((py2, py2k), (py3, py3k))):
                pg.op('dve', lambda e, g=g, dstp=dstp: e.tensor_tensor(
                    out=yv[:, g * 4:(g + 1) * 4, :], in0=dstp[0:np_, 0:256].rearrange("p (h d) -> p h d", d=64),
                    in1=ssd8[0:np_, 0, g * 4:(g + 1) * 4].unsqueeze(2).to_broadcast([np_, 4, 64]), op=ALU.mult),
                    r=[dk, 'ssd8'], w=['f1'])
            pg.op('dve', lambda e: e.tensor_tensor(out=f1[0:np_, :], in0=f1[0:np_, :], in1=py[0:np_, :], op=ALU.add),
                  r=['f1', pyk], w=['f1'])
            pg.op('pool', lambda e: e.tensor_tensor(out=f2[0:np_, :].rearrange("p (h d) -> p h d", d=64),
                                                    in0=xtk[0:np_, :].rearrange("p (h d) -> p h d", d=64),
                                                    in1=smv('b_d', np_).unsqueeze(2).to_broadcast([np_, 8, 64]), op=ALU.mult),
                  r=['xtk', 'small'], w=['f2'])
            pg.op('dve', lambda e: e.tensor_tensor(out=f1[0:np_, :], in0=f1[0:np_, :], in1=f2[0:np_, :], op=ALU.add),
                  r=['f1', 'f2'], w=['f1'])
            pg.op('dve', lambda e: e.tensor_tensor(out=f1[0:np_, :], in0=f1[0:np_, :], in1=sz[0:np_, t, :], op=ALU.mult),
                  r=['f1', 'szb'], w=['f1'])
            ssv = st8[0:np_, 4, 0:1]
            pg.op('act', lambda e: e.activation(out=sq[0:np_, 0:512], in_=f1[0:np_, :], func=AF.Square, accum_out=ssv),
                  r=['f1'], w=['sq', 'st8'])
            rstd_from_ss(ssv, ssv, 512, np_, 'st8')
            pg.op('dve', lambda e: e.scalar_tensor_tensor(out=oz[0:np_, t, :], in0=f1[0:np_, :], scalar=ssv,
                                                          in1=smv('b_norm', np_), op0=ALU.mult, op1=ALU.mult),
                  r=['f1', 'st8', 'small'], w=['oz'])
            pg.op('dve', lambda e: e.tensor_tensor(
                out=bdec[0:np_, :, :, :].rearrange("p a g n -> p g a n"),
                in0=btk[0:np_, :].rearrange("p (g n) -> p g n", n=64).unsqueeze(2).to_broadcast([np_, 2, 4, 64]),
                in1=ssd8[0:np_, 1, :].rearrange("p (g a) -> p g a", a=4).unsqueeze(3).to_broadcast([np_, 2, 4, 64]),
                op=ALU.mult), r=['btk', 'ssd8'], w=['bdec'])
            i = rot('A'); ph = psA[i]; phk = 'psA%d' % i
            for a4 in range(4):
                pg.op('pe', lambda e, a4=a4: e.matmul(
                    ph[:, a4 * 128:(a4 + 1) * 128], lhsT=bdec[0:np_, a4, :, :].rearrange("p g n -> p (g n)"),
                    rhs=xdt[0:np_, :, :].rearrange("p (g a) d -> p a g d", a=4)[:, a4, :, :],
                    start=True, stop=True), r=['bdec', 'xdt'], w=[phk], inc=(a4 == 3))
            phv = ph[:, :].rearrange("p (a c) -> p a c", c=128)
            for g in range(2):
                sl = slice(g * 64, (g + 1) * 64)
                pg.op('dve', lambda e, g=g, sl=sl: e.tensor_tensor(
                    out=hT[sl, :, :], in0=hT[sl, :, :],
                    in1=ssd8[sl, 2, g * 4:(g + 1) * 4].unsqueeze(2).to_broadcast([64, 4, 64]), op=ALU.mult),
                    r=['hT', 'ssd8', py2k, py3k], w=['hT'])
                pg.op('dve', lambda e, g=g, sl=sl: e.tensor_tensor(
                    out=hT[sl, :, :], in0=hT[sl, :, :], in1=phv[sl, :, g * 64:(g + 1) * 64], op=ALU.add),
                    r=['hT', phk], w=['hT'])
            pg.op('act', lambda e: e.activation(out=hTb[:, :, :], in_=hT[:, :, :], func=AF.Copy), r=['hT'], w=['hTb'])

        BS = [dict(Ah=Ah, Ee=Ee, Gm=Gm, bdec=bdec, xdt=xdt, xtk=xtk, Mh=Mh, btk=btk,
                   e0=ssd8[:, 0, :], e1=ssd8[:, 1, :], e2=ssd8[:, 2, :], sfx=''),
              dict(Ah=xnb.bitcast(F32).rearrange("p (h n) -> p h n", n=128),
                   Ee=f3.rearrange("p (h n) -> p h n", n=128),
                   Gm=biasQ[:, :, :].rearrange("p a b -> p (a b)").rearrange("p (g n) -> p g n", n=128),
                   Mh=qT[:, 0:2, :].rearrange("p a n -> p (a n)").rearrange("p (h n) -> p h n", n=128),
                   bdec=qT[:, 2, :].rearrange("p (a g n) -> p a g n", a=4, g=2),
                   xdt=qT[:, 3, :].rearrange("p (h d) -> p h d", d=64),
                   xtk=PT[0][:, :], btk=PT[1][:, 0:128],
                   e0=st8[:, 5, :], e1=st8[:, 6, :], e2=st8[:, 7, :], sfx='_1')]
        SET1K = ['Ah%d_1' % h for h in range(4)] + ['Ee_1', 'Gm_1', 'bdec_1', 'xdt_1', 'xtk_1', 'Mh_1', 'btk_1', 'ssd8_1',
                                                     'qT', 'PT0', 'PT1', 'biasQ', 'xnb', 'f3', 'st8lf', 'st8c', 'st8x']

        def ssdA(t, np_, S):
            b = BS[S]; x_ = b['sfx']
            K = lambda n: n + x_
            c0 = t * np_
            i = rot('T'); pt = psT[i]; pk = 'psT%d' % i
            for j in range(5):
                pg.op('pe', lambda e, j=j: e.transpose(out=pt[0:np_, j * 128:(j + 1) * 128], in_=xbc[:, j, c0:c0 + np_],
                                                       identity=ident_b[:, :]), r=['xbc', 'cstb'], w=[pk], inc=(j == 4))
            yield
            pg.op('act', lambda e: e.activation(out=b['xtk'][0:np_, :], in_=pt[0:np_, 0:512], func=AF.Copy), r=[pk], w=[K('xtk')])
            pg.op('act', lambda e: e.activation(out=b['btk'][0:np_, :], in_=pt[0:np_, 512:640], func=AF.Copy), r=[pk], w=[K('btk')])
            yield
            pg.op('dve', lambda e: e.tensor_tensor(out=b['xdt'][0:np_, :, :], in0=pt[0:np_, 0:512].rearrange("p (h d) -> p h d", d=64),
                                                   in1=dts[0:np_, t, :].unsqueeze(2).to_broadcast([np_, 8, 64]), op=ALU.mult),
                  r=[pk, 'dts'], w=[K('xdt')])
            yield
            a_ap = asb[0:np_, t, :]
            i = rot('A'); p = psA[i]; pk2 = 'psA%d' % i
            pg.op('pe', lambda e: e.matmul(p[0:np_, 0:8], lhsT=M1f[0:np_, 0:np_], rhs=a_ap, start=True, stop=True),
                  r=['asb', 'cst'], w=[pk2], inc=False)
            pg.op('pe', lambda e: e.matmul(p[0:np_, 8:16], lhsT=M2f[0:np_, 0:np_], rhs=a_ap, start=True, stop=True),
                  r=['asb', 'cst'], w=[pk2], inc=False)
            pg.op('pe', lambda e: e.matmul(p[:, 16:24], lhsT=M1f[0:np_, :], rhs=a_ap, start=True, stop=False),
                  r=['asb', 'cst'], w=[pk2], inc=False)
            pg.op('pe', lambda e: e.matmul(p[:, 16:24], lhsT=M2f[0:np_, :], rhs=a_ap, start=False, stop=True),
                  r=['asb', 'cst'], w=[pk2])
            yield
            pg.op('act', lambda e: e.activation(out=b['e0'][0:np_, :], in_=p[0:np_, 0:8], func=AF.Exp), r=[pk2], w=[K('ssd8')])
            pg.op('act', lambda e: e.activation(out=b['e1'][0:np_, :], in_=p[0:np_, 8:16], func=AF.Exp), r=[pk2], w=[K('ssd8')])
            pg.op('act', lambda e: e.activation(out=b['e2'][:, :], in_=p[:, 16:24], func=AF.Exp), r=[pk2], w=[K('ssd8')])
            yield
            for g in range(2):
                i = rot('A'); pgm = psA[i]; pk3 = 'psA%d' % i
                pg.op('pe', lambda e, g=g, pgm=pgm: e.matmul(pgm[0:np_, 0:np_], lhsT=xbc[g * 64:(g + 1) * 64, 4, c0:c0 + np_],
                                                    rhs=xbc[g * 64:(g + 1) * 64, 5, c0:c0 + np_], start=True, stop=True),
                      r=['xbc'], w=[pk3])
                pg.op('dve', lambda e, g=g, pgm=pgm: e.tensor_tensor(
                    out=b['Gm'][0:np_, g, 0:np_], in0=pgm[0:np_, 0:np_], in1=M1f[0:np_, 0:np_], op=ALU.mult),
                    r=[pk3, 'cst'], w=[K('Gm')])
                yield
            for hh in range(2):
                for h4 in range(4):
                    h = hh * 4 + h4
                    pg.op('dve', lambda e, h=h, h4=h4: e.tensor_scalar(
                        out=b['Ah'][0:np_, h4, 0:np_], in0=M2f[0:np_, 0:np_], scalar1=asb[0:np_, t, h:h + 1], scalar2=None,
                        op0=ALU.mult), r=['cst', 'asb'], w=[K('Ah%d' % h4)])
                    if h4 % 2:
                        yield
                psx = psS[hh]; psk = 'psS%d' % hh
                for h4 in range(4):
                    pg.op('pe', lambda e, h4=h4, psx=psx: e.matmul(
                        psx[0:np_, h4 * 128:h4 * 128 + np_], lhsT=b['Ah'][0:np_, h4, 0:np_], rhs=M1f[0:np_, 0:np_],
                        start=True, stop=True), r=[K('Ah%d' % h4), 'cst'], w=[psk], inc=(h4 == 3))
                yield
                pg.op('act', lambda e, psx=psx: e.activation(
                    out=b['Ee'][0:np_, :, 0:np_],
                    in_=psx[0:np_, :].rearrange("p (h n) -> p h n", n=128)[:, :, 0:np_], func=AF.Exp),
                    r=[psk], w=[K('Ee')])
                yield
                pg.op('dve', lambda e, hh=hh: e.tensor_tensor(
                    out=b['Mh'][0:np_, hh * 4:(hh + 1) * 4, 0:np_], in0=b['Ee'][0:np_, :, 0:np_],
                    in1=b['Gm'][0:np_, hh, 0:np_].unsqueeze(1).to_broadcast([np_, 4, np_]), op=ALU.mult),
                    r=[K('Ee'), K('Gm')], w=[K('Mh')])
                yield
            pg.op('dve', lambda e: e.tensor_tensor(
                out=b['bdec'][0:np_, :, :, :].rearrange("p a g n -> p g a n"),
                in0=b['btk'][0:np_, :].rearrange("p (g n) -> p g n", n=64).unsqueeze(2).to_broadcast([np_, 2, 4, 64]),
                in1=b['e1'][0:np_, :].rearrange("p (g a) -> p g a", a=4).unsqueeze(3).to_broadcast([np_, 2, 4, 64]),
                op=ALU.mult), r=[K('btk'), K('ssd8')], w=[K('bdec')])
            yield

        def ssdB(t, np_, S):
            b = BS[S]; x_ = b['sfx']
            K = lambda n: n + x_
            c0 = t * np_
            io = rot('O'); py = psO[io]; pyk = 'psO%d' % io
            io2 = rot('O'); py2 = psO[io2]; py2k = 'psO%d' % io2
            for h in range(8):
                pg.op('pe', lambda e, h=h: e.matmul(py[0:np_, h * 64:(h + 1) * 64], lhsT=b['Mh'][0:np_, h, 0:np_],
                                                    rhs=b['xdt'][0:np_, h, :], start=True, stop=True),
                      r=[K('Mh'), K('xdt')], w=[pyk], inc=(h == 7))
            yield
            i3 = rot('A'); py3 = psA[i3]; py3k = 'psA%d' % i3
            for h in range(8):
                g = h // 4
                dstp = (py2 if g == 0 else py3)
                pg.op('pe', lambda e, h=h, g=g, dstp=dstp: e.matmul(dstp[0:np_, (h % 4) * 64:(h % 4 + 1) * 64],
                                                         lhsT=xbc[g * 64:(g + 1) * 64, 5, c0:c0 + np_],
                                                         rhs=hTb[g * 64:(g + 1) * 64, h % 4, :], start=True, stop=True),
                      r=['xbc', 'hTb'], w=[py2k if g == 0 else py3k], inc=(h % 4 == 3))
            yield
            yv = f1[0:np_, :].rearrange("p (h d) -> p h d", d=64)
            for g, (dstp, dk) in enumerate(((py2, py2k), (py3, py3k))):
                pg.op('dve', lambda e, g=g, dstp=dstp: e.tensor_tensor(
                    out=yv[:, g * 4:(g + 1) * 4, :], in0=dstp[0:np_, 0:256].rearrange("p (h d) -> p h d", d=64),
                    in1=b['e0'][0:np_, g * 4:(g + 1) * 4].unsqueeze(2).to_broadcast([np_, 4, 64]), op=ALU.mult),
                    r=[dk, K('ssd8')], w=['f1'])
                yield
            pg.op('dve', lambda e: e.tensor_tensor(out=f1[0:np_, :], in0=f1[0:np_, :], in1=py[0:np_, :], op=ALU.add),
                  r=['f1', pyk], w=['f1'])
            pg.op('pool', lambda e: e.tensor_tensor(out=f2[0:np_, :].rearrange("p (h d) -> p h d", d=64),
                                                    in0=b['xtk'][0:np_, :].rearrange("p (h d) -> p h d", d=64),
                                                    in1=smv('b_d', np_).unsqueeze(2).to_broadcast([np_, 8, 64]), op=ALU.mult),
                  r=[K('xtk'), 'small'], w=['f2'])
            yield
            pg.op('dve', lambda e: e.tensor_tensor(out=f1[0:np_, :], in0=f1[0:np_, :], in1=f2[0:np_, :], op=ALU.add),
                  r=['f1', 'f2'], w=['f1'])
            yield
            pg.op('dve', lambda e: e.tensor_tensor(out=f1[0:np_, :], in0=f1[0:np_, :], in1=sz[0:np_, t, :], op=ALU.mult),
                  r=['f1', 'szb'], w=['f1'])
            yield
            ssv = st8[0:np_, 4, 0:1]
            pg.op('act', lambda e: e.activation(out=sq[0:np_, 0:512], in_=f1[0:np_, :], func=AF.Square, accum_out=ssv),
                  r=['f1'], w=['sq', 'st8'])
            yield
            pg.op('dve', lambda e: e.tensor_scalar(out=ssv, in0=ssv, scalar1=1.0 / 512, scalar2=EPS,
                                                    op0=ALU.mult, op1=ALU.add), r=['st8'], w=['st8'])
            yield
            pg.op('pool', lambda e: e.tensor_tensor(out=ssv, in0=ssv, in1=nhalf[0:np_, 0:1], op=ALU.pow),
                  r=['st8', 'epst'], w=['st8'])
            yield
            pg.op('dve', lambda e: e.scalar_tensor_tensor(out=oz[0:np_, t, :], in0=f1[0:np_, :], scalar=ssv,
                                                          in1=smv('b_norm', np_), op0=ALU.mult, op1=ALU.mult),
                  r=['f1', 'st8', 'small'], w=['oz'])
            yield
            i = rot('A'); ph = psA[i]; phk = 'psA%d' % i
            for a4 in range(4):
                pg.op('pe', lambda e, a4=a4: e.matmul(
                    ph[:, a4 * 128:(a4 + 1) * 128], lhsT=b['bdec'][0:np_, a4, :, :].rearrange("p g n -> p (g n)"),
                    rhs=b['xdt'][0:np_, :, :].rearrange("p (g a) d -> p a g d", a=4)[:, a4, :, :],
                    start=True, stop=True), r=[K('bdec'), K('xdt')], w=[phk], inc=(a4 == 3))
            yield
            phv = ph[:, :].rearrange("p (a c) -> p a c", c=128)
            for g in range(2):
                sl = slice(g * 64, (g + 1) * 64)
                pg.op('dve', lambda e, g=g, sl=sl: e.tensor_tensor(
                    out=hT[sl, :, :], in0=hT[sl, :, :],
                    in1=b['e2'][sl, g * 4:(g + 1) * 4].unsqueeze(2).to_broadcast([64, 4, 64]), op=ALU.mult),
                    r=['hT', K('ssd8'), py2k, py3k], w=['hT'])
                yield
                pg.op('dve', lambda e, g=g, sl=sl: e.tensor_tensor(
                    out=hT[sl, :, :], in0=hT[sl, :, :], in1=phv[sl, :, g * 64:(g + 1) * 64], op=ALU.add),
                    r=['hT', phk], w=['hT'])
                yield
            pg.op('act', lambda e: e.activation(out=hTb[:, :, :], in_=hT[:, :, :], func=AF.Copy), r=['hT'], w=['hTb'])
            yield

        def ssd_pipelined(NT, np_):
            barrier(AEK + SET1K)
            for _ in ssdA(0, np_, 0):
                pass
            for t in range(NT):
                gens = [ssdB(t, np_, t % 2)]
                if t + 1 < NT:
                    gens.append(ssdA(t + 1, np_, (t + 1) % 2))
                while gens:
                    for g_ in list(gens):
                        try:
                            next(g_)
                        except StopIteration:
                            gens.remove(g_)
            barrier(AEK + SET1K)

        def process_group(l, kind, Q):
            samp = (kind == 's')
            np_ = 64 if samp else 128
            NT = 4
            NTOK = NT * np_
            xsrc = (D['xs'] if samp else D['xp']) if l == 0 else (x1s if samp else x1p)
            xdst = (D['ys'] if samp else D['yp']) if l == NL - 1 else (x1s if samp else x1p)
            row0 = 0 if samp else Q * 512

            def rows(t):
                return slice(row0 + t * np_, row0 + (t + 1) * np_)

            norm_tiles(lambda t: xsrc[rows(t), :], NT, np_, lambda n: smv('g_norm', n),
                       (lambda t: [('xd', kind, row0 + t * np_)]) if l > 0 else None)

            if cfg.get('marks'): print('MARK', kind, Q, 'C_proj', pg.nops)
            wt, wk = load_w(D['w_in'], l, COL['c_q'], 512)
            for t, p, pk in pipe_tiles(NT, lambda t: proj_tm(xnT, 'xnT', t * np_, np_, wt, wk, 512)):
                evac_q(p, pk, t, np_, 8, 'c_qnorm')
            wt, wk = load_w(D['w_in'], l, COL['c_k'], 512)
            if samp:
                kTs = sb_kTs
            for t, p, pk in pipe_tiles(NT, lambda t: proj_tm(xnT, 'xnT', t * np_, np_, wt, wk, 512)):
                if samp:
                    evac_k(p, pk, np_, 8, 'c_knorm', D['sc_k'][l, rows(t), :],
                           lambda t=t: kTs[:, 0:4, t * 64:(t + 1) * 64], 'kTs')
                else:
                    gt = Q * 4 + t
                    evac_k(p, pk, np_, 8, 'c_knorm', D['pc_k'][l, rows(t), :],
                           lambda gt=gt: kT_c[:, :, gt * 128:(gt + 1) * 128], 'kT_c')
            wt, wk = load_w(D['w_in'], l, COL['c_v'], 512)
            for t, p, pk in pipe_tiles(NT, lambda t: proj_tm(xnT, 'xnT', t * np_, np_, wt, wk, 512)):
                if samp:
                    evac_v(p, pk, np_, 8, D['sc_v'][l, rows(t), :], vas[0:np_, t, :, 0:64], 'vas')
                else:
                    evac_v(p, pk, np_, 8, D['pc_v'][l, rows(t), :], va_c[:, Q * 4 + t, :, 0:64], 'va_c')
            wt, wk = load_w(D['w_in'], l, COL['c_f'], 8)
            for t, p, pk in pipe_tiles(NT, lambda t: proj_tm(xnT, 'xnT', t * np_, np_, wt, wk, 8)):
                lf = st8[0:np_, 5, :]
                softplus_from(p, pk, smv('c_fbias', np_), lf, 'st8lf', np_, neg=True)
                if samp:
                    out_tokens.append(pg.dma('sp', D['sc_f'][l, rows(t), :], lf, r=['st8lf'], nonc=True))
                    carry = None
                    for kb in range(8):
                        pg.dma('sp', st8[:, 7, :], D['cc_f'][l, t, kb * 128:(kb + 1) * 128, :], w=['st8x'], nonc=True)
                        cumsum_tile(st8[:, 7, :], 'st8x', 128, cums[:, t, kb, :], 'cums',
                                    None if kb == 0 else (S127f, cums[:, t, kb - 1, :]), ['cums'])
                    cumsum_tile(lf, 'st8lf', 64, cums[0:64, t, 8, :], 'cums', (S127f[:, 0:64], cums[:, t, 7, :]), ['cums'])
                else:
                    gt = Q * 4 + t
                    out_tokens.append(pg.dma('sp', D['pc_f'][l, rows(t), :], lf, r=['st8lf'], nonc=True))
                    cumsum_tile(lf, 'st8lf', 128, cum[:, gt, :], 'cum',
                                None if gt == 0 else (S127f, cum[:, gt - 1, :]), ['cum'])
            wt, wk = load_w(D['w_in'], l, COL['c_z'], 512)
            for t, p, pk in pipe_tiles(NT, lambda t: proj_tm(xnT, 'xnT', t * np_, np_, wt, wk, 512)):
                pg.op('act', lambda e, p=p, t=t: e.activation(out=sz[0:np_, t, :], in_=p, func=AF.Silu), r=[pk], w=['sz'])
            if cfg.get('marks'): print('MARK', kind, Q, 'C_attn', pg.nops)
            if not samp:
                nkb = 4 * Q + 4
                i = rot('A'); pb = psA[i]; pbk = 'psA%d' % i
                pg.op('pe', lambda e: e.matmul(pb[:, 0:8], lhsT=S127f, rhs=cum[:, nkb - 1, :], start=True, stop=True),
                      r=['cum', 'cst'], w=[pbk])
                pg.op('act', lambda e: e.activation(out=st8[:, 6, :], in_=pb[:, 0:8], func=AF.Copy), r=[pbk], w=['st8c'])
                pg.op('dve', lambda e: e.tensor_tensor(out=biasQ[:, 0:nkb, :],
                                                       in0=st8[:, 6, :].unsqueeze(1).to_broadcast([128, nkb, 8]),
                                                       in1=cum[:, 0:nkb, :], op=ALU.subtract), r=['st8c', 'cum'], w=['biasQ'])
                for h in range(8):
                    hp, hb = h // 2, (h % 2) * 64
                    kbs = []
                    for kb in range(nkb):
                        dI = kb - 4 * Q
                        d = dict(kT=kT_c[hb:hb + 64, hp, kb * 128:(kb + 1) * 128], v=va_c[:, kb, h, 0:65], nk=128,
                                 bias=biasQ[:, kb, h:h + 1], bkeys=['biasQ'], q0=max(dI, 0) * 128, keys=['kT_c', 'va_c'])
                        kbs.append(d)
                    kb2 = []
                    for kb, d in enumerate(kbs):
                        dI = kb - 4 * Q
                        if dI < 0:
                            kb2.append(d)
                        else:
                            d['diag'] = True
                            kb2.append(d)
                    attend_fox(qT[hb:hb + 64, hp, :], 512, [(j * 128, 128) for j in range(4)], kb2, ['qT'],
                               attn_out(h * 64, 4, 128))
            else:
                for t in range(NT):
                    i = rot('A'); pb = psA[i]; pbk = 'psA%d' % i
                    pg.op('pe', lambda e, t=t: e.matmul(pb[:, 0:8], lhsT=S63f[0:64, :], rhs=cums[0:64, t, 8, :],
                                                        start=True, stop=True), r=['cums', 'cst'], w=[pbk])
                    pg.op('act', lambda e: e.activation(out=st8[:, 6, :], in_=pb[:, 0:8], func=AF.Copy), r=[pbk], w=['st8c'])
                    pg.op('dve', lambda e, t=t: e.tensor_tensor(out=biasQ[:, 0:9, :],
                                                                in0=st8[:, 6, :].unsqueeze(1).to_broadcast([128, 9, 8]),
                                                                in1=cums[:, t, :, :], op=ALU.subtract),
                          r=['st8c', 'cums'], w=['biasQ'])
                    load_cache_kv(D['cc_k'][l, t], D['cc_v'][l, t], 8, 8)
                    for h in range(8):
                        hp, hb = h // 2, (h % 2) * 64
                        kbs = []
                        for kb in range(8):
                            kbs.append(dict(kT=ckT[hb:hb + 64, hp, kb * 128:(kb + 1) * 128], v=cva_[:, kb, h, 0:65], nk=128,
                                            bias=biasQ[:, kb, h:h + 1], bkeys=['biasQ'], q0=0, keys=['ckT', 'cvaS']))
                        kbs.append(dict(kT=kTs[hb:hb + 64, hp, t * 64:(t + 1) * 64], v=vas[0:64, t, h, 0:65], nk=64,
                                        bias=biasQ[0:64, 8, h:h + 1], bkeys=['biasQ'], q0=0, keys=['kTs', 'vas'], diag=True))
                        attend_fox(qT[hb:hb + 64, hp, t * 64:(t + 1) * 64], 64, [(0, 64)], kbs, ['qT'],
                                   attn_out_s(h * 64, t))
            oz_to_T('c', NT, np_, 4)

            if cfg.get('marks'): print('MARK', kind, Q, 'A_proj', pg.nops)
            wt, wk = load_w(D['w_in'], l, COL['a_q'], 512)
            for t, p, pk in pipe_tiles(NT, lambda t: proj_tm(xnT, 'xnT', t * np_, np_, wt, wk, 512)):
                evac_q(p, pk, t, np_, 8, 'a_qnorm')
            wt, wk = load_w(D['w_in'], l, COL['a_k'], 512)
            for t, p, pk in pipe_tiles(NT, lambda t: proj_tm(xnT, 'xnT', t * np_, np_, wt, wk, 512)):
                if samp:
                    evac_k(p, pk, np_, 8, 'a_knorm', D['sa_k'][l, rows(t), :],
                           lambda t=t: kTs[:, 0:4, t * 64:(t + 1) * 64], 'kTs')
                else:
                    gt = Q * 4 + t
                    od = D['pa_k'][l, (gt - 28) * 128:(gt - 27) * 128, :] if gt >= 28 else None
                    evac_k(p, pk, np_, 8, 'a_knorm', od,
                           lambda gt=gt: kT_a[:, :, (gt % 8) * 128:(gt % 8 + 1) * 128], 'kT_a')
            wt, wk = load_w(D['w_in'], l, COL['a_v'], 512)
            for t, p, pk in pipe_tiles(NT, lambda t: proj_tm(xnT, 'xnT', t * np_, np_, wt, wk, 512)):
                if samp:
                    evac_v(p, pk, np_, 8, D['sa_v'][l, rows(t), :], vas[0:np_, t, :, 0:64], 'vas')
                else:
                    gt = Q * 4 + t
                    od = D['pa_v'][l, (gt - 28) * 128:(gt - 27) * 128, :] if gt >= 28 else None
                    evac_v(p, pk, np_, 8, od, va_a[:, gt % 8, :, 0:64], 'va_a')
            wt, wk = load_w(D['w_in'], l, COL['a_z'], 512)
            for t, p, pk in pipe_tiles(NT, lambda t: proj_tm(xnT, 'xnT', t * np_, np_, wt, wk, 512)):
                pg.op('act', lambda e, p=p, t=t: e.activation(out=sz[0:np_, t, :], in_=p, func=AF.Silu), r=[pk], w=['sz'])
            for t in range(NT):
                gt = Q * 4 + t
                if samp:
                    load_cache_kv(D['ca_k'][l, t], D['ca_v'][l, t], 4, 8)
                for hq in range(2):
                    acc = None
                    for h4 in range(4):
                        h = hq * 4 + h4
                        hp, hb = h // 2, (h % 2) * 64
                        kbs = []
                        if not samp:
                            for i5 in range(5):
                                gk = gt - 4 + i5
                                if gk < 0:
                                    continue
                                s8 = gk % 8
                                d_ = dict(kT=kT_a[hb:hb + 64, hp, s8 * 128:(s8 + 1) * 128], v=va_a[:, s8, h, 0:65], nk=128,
                                          bias=None, q0=0, keys=['kT_a', 'va_a'])
                                if i5 in (1, 2):
                                    d_['bias'] = expB[:, h, i5 * 128:i5 * 128 + 1]; d_['bkeys'] = ['expB']
                                else:
                                    d_['badd'] = expB[:, h, i5 * 128:(i5 + 1) * 128]; d_['mkeys'] = ['expB']
                                kbs.append(d_)
                        else:
                            for kb in range(4):
                                d_ = dict(kT=ckT[hb:hb + 64, hp, kb * 128:(kb + 1) * 128], v=cva_[:, kb, h, 0:65], nk=128,
                                          bias=None, q0=0, keys=['ckT', 'cvaS'])
                                if kb in (0, 1, 2):
                                    d_['bias'] = expB[:, h, kb * 128:kb * 128 + 1]; d_['bkeys'] = ['expB']
                                else:
                                    d_['badd'] = expB[:, h, kb * 128:kb * 128 + 64]; d_['mkeys'] = ['expB']
                                kbs.append(d_)
                            kbs.append(dict(kT=kTs[hb:hb + 64, hp, t * 64:(t + 1) * 64], v=vas[0:64, t, h, 0:65], nk=64,
                                            bias=None, badd=expB[:, h, 512:576], mkeys=['expB'], q0=0, keys=['kTs', 'vas']))
                        a2 = None if acc is None else (acc[0], acc[1], h4, False)
                        acc = attend(qT[hb:hb + 64, hp, t * np_:(t + 1) * np_], np_, [(0, np_)], kbs, ['qT'], None, acc=a2)
                    pov, pok = acc
                    rl = st8[0:np_, 2, 0:4]
                    pg.op('dve', lambda e, pov=pov: e.reciprocal(out=rl, in_=pov[0:np_, 0:4, 64]), r=[pok], w=['st8'])
                    tmp = f2[0:np_, 0:256].rearrange("p (j d) -> p j d", d=64)
                    pg.op('dve', lambda e, pov=pov, tmp=tmp: e.tensor_tensor(out=tmp, in0=pov[0:np_, 0:4, 0:64],
                                                                           in1=rl.unsqueeze(2).to_broadcast([np_, 4, 64]), op=ALU.mult),
                          r=[pok, 'st8'], w=['f2'])
                    pg.op('pool', lambda e, tmp=tmp, hq=hq, t=t: e.tensor_tensor(
                        out=oz[0:np_, t, hq * 256:(hq + 1) * 256].rearrange("p (j d) -> p j d", d=64), in0=tmp,
                        in1=sz[0:np_, t, hq * 256:(hq + 1) * 256].rearrange("p (j d) -> p j d", d=64), op=ALU.mult),
                        r=['f2', 'sz'], w=['oz'])
            oz_to_T('a', NT, np_, 4)

            if cfg.get('marks'): print('MARK', kind, Q, 'M', pg.nops)
            wt, wk = load_w(D['w_in'], l, COL['m_q'], 512)
            for t, p, pk in pipe_tiles(NT, lambda t: proj_tm(xnT, 'xnT', t * np_, np_, wt, wk, 512)):
                pg.op('act', lambda e, p=p, t=t: e.activation(out=sz[0:np_, t, 0:256], in_=p[:, 256:512], func=AF.Silu),
                      r=[pk], w=['sz'])
                evac_q(p[:, 0:256], pk, t, np_, 4, 'm_qnorm')
            if not samp:
                for h in range(4):
                    hp, hb = h // 2, (h % 2) * 64
                    kbs = [dict(kT=kT_m[hb:hb + 64, hp, kb * 128:(kb + 1) * 128], v=va_m[:, kb, h, 0:65], nk=128, bias=None,
                                q0=0, keys=['kT_m', 'va_m']) for kb in range(2)]
                    attend(qT[hb:hb + 64, hp, :], 512, [(j * 128, 128) for j in range(4)], kbs, ['qT'],
                           attn_out(h * 64, 4, 128))
            else:
                for t in range(NT):
                    load_cache_kv(D['cm_k'][l, t], D['cm_v'][l, t], 2, 4)
                    for h in range(4):
                        hp, hb = h // 2, (h % 2) * 64
                        kbs = [dict(kT=ckT[hb:hb + 64, hp, kb * 128:(kb + 1) * 128], v=cva_[:, kb, h, 0:65], nk=128, bias=None,
                                    q0=0, keys=['ckT', 'cvaS']) for kb in range(2)]
                        attend(qT[hb:hb + 64, hp, t * 64:(t + 1) * 64], 64, [(0, 64)], kbs, ['qT'], attn_out_s(h * 64, t))
            oz_to_T('m', NT, np_, 2)

            if cfg.get('marks'): print('MARK', kind, Q, 'B_proj', pg.nops)
            wt, wk = load_w(D['w_in'], l, COL['b_z'], 512)
            for t, p, pk in pipe_tiles(NT, lambda t: proj_tm(xnT, 'xnT', t * np_, np_, wt, wk, 512)):
                pg.op('act', lambda e, p=p, t=t: e.activation(out=sz[0:np_, t, :], in_=p, func=AF.Silu), r=[pk], w=['sz', 'szb'])
            wt, wk = load_w(D['w_in'], l, COL['b_dt'], 8)
            for t, p, pk in pipe_tiles(NT, lambda t: proj_tm(xnT, 'xnT', t * np_, np_, wt, wk, 8)):
                softplus_from(p, pk, smv('b_dt_bias', np_), dts[0:np_, t, :], 'dts', np_)
                pg.op('dve', lambda e, t=t: e.tensor_tensor(out=asb[0:np_, t, :], in0=dts[0:np_, t, :], in1=aneg[0:np_, :],
                                                            op=ALU.mult), r=['dts', 'aneg'], w=['asb'])
            for half in range(2):
                js = slice(half * 3, half * 3 + 3)
                if samp:
                    for t in range(NT):
                        for jj in range(3):
                            j = half * 3 + jj
                            pg.dma('sp', raw[:, jj, t, 0:3],
                                   D['sb_conv'][l, t][:, j * 128:(j + 1) * 128].rearrange("r p -> p r"), w=['raw'], nonc=True)
                else:
                    pg.op('pool', lambda e, js=js: cp(e, out=raw[:, :, 0, 0:3], in_=halo[:, js, :]), r=['halo'], w=['raw'])
                wt, wk = load_w(D['w_in'], l, COL['b_xbc'] + half * 384, 384)
                for jj in range(3):
                    i = rot('A'); p = psA[i]; pk = 'psA%d' % i
                    for kc in range(8):
                        pg.op('pe', lambda e, kc=kc, jj=jj, p=p, wt=wt: e.matmul(p[:, 0:NTOK], lhsT=wt[:, kc, jj * 128:(jj + 1) * 128],
                                                                     rhs=xnT[:, kc, 0:NTOK], start=(kc == 0), stop=(kc == 7)),
                              r=['xnT'] + list(wk), w=[pk], inc=(kc == 7))
                    pg.op('act', lambda e, jj=jj, p=p: e.activation(out=raw[:, jj, :, 3:3 + np_],
                                                             in_=p[:, 0:NTOK].rearrange("p (t n) -> p t n", n=np_), func=AF.Copy),
                          r=[pk], w=['raw'])
                if not samp:
                    for t in range(1, NT):
                        pg.op('pool', lambda e, t=t: cp(e, out=raw[:, :, t, 0:3], in_=raw[:, :, t - 1, np_:np_ + 3]),
                              r=['raw'], w=['raw'])
                    pg.op('pool', lambda e, js=js: cp(e, out=halo[:, js, :], in_=raw[:, :, NT - 1, np_:np_ + 3]),
                          r=['raw'], w=['halo'])
                    if Q == NQ - 1:
                        for jj in range(3):
                            j = half * 3 + jj
                            out_tokens.append(pg.dma('sp', D['pb_c'][l][:, j * 128:(j + 1) * 128].rearrange("r p -> p r"),
                                                     halo[:, j, :], r=['halo'], nonc=True))
                else:
                    for t in range(NT):
                        for jj in range(3):
                            j = half * 3 + jj
                            out_tokens.append(pg.dma('sp', D['sb_c'][l, t][:, j * 128:(j + 1) * 128].rearrange("r p -> p r"),
                                                     raw[:, jj, t, np_:np_ + 3], r=['raw'], nonc=True))
                for jj in range(3):
                    j = half * 3 + jj
                    cv = cva[:, 0:NTOK].rearrange("p (t n) -> p t n", n=np_)
                    pg.op('dve', lambda e, j=j, jj=jj, cv=cv: e.tensor_scalar(out=cv, in0=raw[:, jj, :, 0:np_], scalar1=cw[:, j, 0:1],
                                                                scalar2=cw[:, j, 4:5], op0=ALU.mult, op1=ALU.add),
                          r=['raw', 'cw'], w=['cva'])
                    for tap in range(1, 4):
                        pg.op('dve', lambda e, j=j, jj=jj, tap=tap, cv=cv: e.scalar_tensor_tensor(
                            out=cv, in0=raw[:, jj, :, tap:tap + np_], scalar=cw[:, j, tap:tap + 1], in1=cv,
                            op0=ALU.mult, op1=ALU.add), r=['raw', 'cw', 'cva'], w=['cva'])
                    pg.op('act', lambda e, j=j: e.activation(out=xbc[:, j, 0:NTOK], in_=cva[:, 0:NTOK], func=AF.Silu),
                          r=['cva'], w=['xbc'])
            if samp:
                for t in range(NT):
                    if not NOSTATE:
                        state_load(D['sb_ssm'][l, t])
                    ssd_tile(t, np_, NT)
                    if not NOSTATE:
                        state_store(D['sb_s'][l, t])
            else:
                ssd_pipelined(NT, np_)
            if (not samp) and Q == NQ - 1 and not NOSTATE:
                state_store(D['pb_s'][l])
            oz_to_T('b', NT, np_, 4)

            if cfg.get('marks'): print('MARK', kind, Q, 'merge', pg.nops)
            barrier(AEK)
            brs = (('a', 'w_pa', 4), ('b', 'w_pb', 4), ('c', 'w_pc', 4), ('m', 'w_pm', 2))
            for c in range(2):
                for bi, (br, wn, nkc) in enumerate(brs):
                    wg, wgk = load_w(D['w_in'], l, COL['gate'] + bi * 1024 + c * 512, 512)
                    wp, wpk = load_w(D[wn], l, c * 512, 512, nk=nkc)
                    def mproj(t, wg=wg, wgk=wgk, wp=wp, wpk=wpk, br=br, nkc=nkc):
                        p1, pk1 = proj_tm(xnT, 'xnT', t * np_, np_, wg, wgk, 512)
                        isx = rot('S'); p2 = psS[isx]; pk2 = 'psS%d' % isx
                        for kc in range(nkc):
                            pg.op('pe', lambda e, kc=kc: e.matmul(
                                p2[0:np_, :], lhsT=ozT[br][:, kc, t * np_:(t + 1) * np_], rhs=wp[:, kc, :],
                                start=(kc == 0), stop=(kc == nkc - 1)), r=['ozT_' + br] + list(wpk), w=[pk2], inc=(kc == nkc - 1))
                        return p1, pk1, p2, pk2
                    for t, p1, pk1, p2, pk2 in pipe_tiles(NT, mproj):
                        pg.op('act', lambda e, p1=p1: e.activation(out=sig[0:np_, :], in_=p1, func=AF.Sigmoid), r=[pk1], w=['cva'])
                        if bi == 0:
                            pg.op('dve', lambda e, t=t, p2=p2: e.tensor_tensor(out=macc[0:np_, t, :], in0=sig[0:np_, :],
                                                                               in1=p2[0:np_, :], op=ALU.mult),
                                  r=['cva', pk2], w=['macc%d' % t])
                        else:
                            pg.op('dve', lambda e, t=t, p2=p2: e.tensor_tensor(out=f1[0:np_, :], in0=sig[0:np_, :],
                                                                               in1=p2[0:np_, :], op=ALU.mult),
                                  r=['cva', pk2], w=['f1'])
                            pg.op('pool', lambda e, t=t: e.tensor_tensor(out=macc[0:np_, t, :], in0=macc[0:np_, t, :],
                                                                         in1=f1[0:np_, :], op=ALU.add),
                                  r=['f1', 'macc%d' % t], w=['macc%d' % t])
                for t in range(NT):
                    pg.op('act', lambda e, t=t: e.activation(out=b1[0:np_, :], in_=macc[0:np_, t, :], func=AF.Copy),
                          r=['macc%d' % t], w=['b1'])
                    transposes(lambda j: b1[0:np_, j * 128:(j + 1) * 128], 4, np_,
                               lambda e, v, t=t, c=c: cp(e, out=mT[:, c * 4:(c + 1) * 4, t * np_:(t + 1) * np_], in_=v),
                               ['b1'], ['xbc', 'Mh'], evac='dve')
            wos = [load_w(D['w_out'], l, c * 512, 512) for c in range(2)]
            for t in range(NT):
                i = rot('X'); xk = 'xt%d' % i
                pg.dma('sp', xt[0:np_, i, :], xsrc[rows(t), :], r=[('xd', kind, row0 + t * np_)] if l > 0 else [], w=[xk])
                for c in range(2):
                    wo, wok = wos[c]
                    p, pk = proj_tm(mT, 'xbc', t * np_, np_, wo, wok, 512, xkeys=['Mh'])
                    pg.op('dve', lambda e, i=i, p=p, c=c: e.tensor_tensor(out=xt[0:np_, i, c * 512:(c + 1) * 512],
                                                                          in0=xt[0:np_, i, c * 512:(c + 1) * 512], in1=p,
                                                                          op=ALU.add), r=[xk, pk], w=[xk])
                tok = pg.dma('sp', xdst[rows(t), :], xt[0:np_, i, :], r=[xk], w=[('xd', kind, row0 + t * np_)])
                out_tokens.append(tok)

        vflat = va_c[:, :, :, :].rearrange("p a h c -> p (a h c)")
        sb_kTs = vflat[:, 0:1024].rearrange("p (k n) -> p k n", n=256)
        vas = vflat[:, 1024:1024 + 2112].rearrange("p (t h c) -> p t h c", t=4, h=8)
        ckT = vflat[:, 3136:3136 + 4096].rearrange("p (k n) -> p k n", n=1024)
        cva_ = vflat[:, 7232:7232 + 4224].rearrange("p (a h c) -> p a h c", a=8, h=8)
        cums = kT_a[:, 0, 0:576].bitcast(F32).rearrange("p (t k h) -> p t k h", t=4, k=9)
        SKEYS = ['kTs', 'vas', 'ckT', 'cvaS']

        def load_cache_kv(kd, vd, nblk, nh):
            n = nh * 64
            for kb in range(nblk):
                stg, sk = ((b1, 'b1'), (xnb, 'xnb'))[kb % 2]
                pg.dma('pool', stg[:, 0:n], kd[kb * 128:(kb + 1) * 128, :], w=[sk])
                transposes(lambda j, stg=stg: stg[:, j * 128:(j + 1) * 128], nh // 2, 128,
                           lambda e, v, kb=kb: cp(e, out=ckT[:, 0:nh // 2, kb * 128:(kb + 1) * 128], in_=v),
                           [sk], ['ckT'], evac='act' if kb % 2 else 'dve')
                pg.dma('pool', cva_[:, kb, 0:nh, 0:64], vd[kb * 128:(kb + 1) * 128, :].rearrange("p (h d) -> p h d", d=64),
                       w=['cvaS'])

        def state_load(src):
            pg.dma('sp', f3[0:64, :].rearrange("p (h n) -> p h n", n=64), src.rearrange("h p n -> p h n"), w=['f3'])
            i = rot('A'); p = psA[i]; pk = 'psA%d' % i
            for h in range(8):
                pg.op('pe', lambda e, h=h: e.matmul(p[0:64, h * 64:(h + 1) * 64], lhsT=f3[0:64, h * 64:(h + 1) * 64],
                                                    rhs=identf[0:64, :], start=True, stop=True),
                      r=['f3', 'identf'], w=[pk], inc=(h == 7))
            for g in range(2):
                pg.op('act' if g else 'dve', lambda e, g=g: cp(
                    e, out=hT[g * 64:(g + 1) * 64, :, :], in_=p[0:64, g * 256:(g + 1) * 256].rearrange("p (a d) -> p a d", d=64)),
                    r=[pk], w=['hT'])
            pg.op('act', lambda e: e.activation(out=hTb[:, :, :], in_=hT[:, :, :], func=AF.Copy), r=['hT'], w=['hTb'])

        def state_store(dst):
            for g in range(2):
                i = rot('A'); p = psA[i]; pk = 'psA%d' % i
                for a in range(4):
                    pg.op('pe', lambda e, g=g, a=a, p=p: e.matmul(p[0:64, a * 64:(a + 1) * 64], lhsT=hT[g * 64:(g + 1) * 64, a, :],
                                                             rhs=identf[g * 64:(g + 1) * 64, :], start=True, stop=True),
                          r=['hT', 'identf'], w=[pk], inc=(a == 3))
                pg.op('act' if g else 'dve', lambda e, g=g, p=p: cp(e, out=f3[0:64, g * 256:(g + 1) * 256], in_=p[0:64, 0:256]),
                      r=[pk], w=['f3'])
            out_tokens.append(pg.dma('sp', dst.rearrange("h p n -> p h n"), f3[0:64, :].rearrange("p (h n) -> p h n", n=64),
                                     r=['f3']))

        def attn_out_t(h0col, t, np_):
            def fn(pov, pok):
                rl = st8[0:np_, 2, 0:1]
                pg.op('dve', lambda e: e.reciprocal(out=rl, in_=pov[0:np_, 0, 64:65]), r=[pok], w=['st8'])
                pg.op('dve', lambda e: e.scalar_tensor_tensor(out=oz[0:np_, t, h0col:h0col + 64], in0=pov[0:np_, 0, 0:64],
                                                              scalar=rl, in1=sz[0:np_, t, h0col:h0col + 64],
                                                              op0=ALU.mult, op1=ALU.mult), r=[pok, 'st8', 'sz'], w=['oz'])
            return fn

        def attn_out_s(h0col, t):
            return attn_out_t(h0col, t, 64)

        def attend_fox(qT_ap, NQc, qtiles, kblocks, qkeys, out_fn):
            for kb in kblocks:
                if kb.get('diag'):
                    kb['diagmask'] = True
            attend(qT_ap, NQc, qtiles, kblocks, qkeys, out_fn)

        def memory_kv(l):
            def src(t):
                return D['memp'][t * 128:(t + 1) * 128, :]
            barrier(AEK)
            pg.dma('sp', AE[:, 0:1024], D['mnorm'][l:l + 1, :].broadcast_to([128, 1024]), w=AEK)
            norm_tiles(src, 2, 128, lambda n: AE[0:n, 0:1024], lambda t: AEK)
            barrier(AEK)
            wt, wk = load_w(D['w_mkv'], l, 0, 512)
            for t, p, pk in pipe_tiles(2, lambda t: proj_tm(xnT, 'xnT', t * 128, 128, wt, wk, 512)):
                evac_v(p[:, 256:512], pk, 128, 4, D['pm_v'][l, t * 128:(t + 1) * 128, :], va_m[:, t, :, 0:64], 'va_m')
                evac_k(p[:, 0:256], pk, 128, 4, 'm_knorm', D['pm_k'][l, t * 128:(t + 1) * 128, :],
                       lambda t=t: kT_m[:, :, t * 128:(t + 1) * 128], 'kT_m')

        try:
          for l in range(NL):
            layer_consts(l)
            memory_kv(l)
            pg.op('pool', lambda e: e.memset(hT[:], 0.0), w=['hT'])
            pg.op('pool', lambda e: e.memset(hTb[:], 0.0), w=['hTb'])
            pg.op('pool', lambda e: e.memset(halo[:], 0.0), w=['halo'])
            for Q in range(NQ):
                process_group(l, 'p', Q)
            if DO_S:
                barrier(['va_c', 'kT_a', 'cums'] + SKEYS)
                pg.op('pool', lambda e: e.memset(vas, 1.0), w=['vas'])
                pg.op('pool', lambda e: e.memset(cva_, 1.0), w=['cvaS'])
                process_group(l, 's', 0)
                barrier(['va_c', 'kT_a', 'cums'] + SKEYS)
                pg.op('pool', lambda e: e.memset(va_c[:], 1.0), w=['va_c'])
        except StopBuild:
            pass
        pg.maxops = None
        pg.op('pe', lambda e: e.matmul(psA[0][0:1, 0:1], lhsT=cstb[:, 0:1], rhs=cstb[:, 0:1], start=True, stop=True), r=['cstb'], w=['psA0'])
        pg.op('act', lambda e: e.activation(out=bar[:, 1:2], in_=bar[:, 1:2], func=AF.Copy), r=['psA0'], w=['bar2'])
        pg.op('dve', lambda e: e.tensor_copy(out=bar[:, 1:2], in_=bar[:, 1:2]), w=['bar2'])
        pg.op('pool', lambda e: e.tensor_copy(out=bar[:, 1:2], in_=bar[:, 1:2]), w=['bar2'])
        out_tokens.append(pg.dma('sp', D['pb_c'][0, 0:1, 0:2], bar[0:1, 0:2], r=['bar2', 'bar'], w=['zz']) if False else None)
        out_tokens[:] = [t for t in out_tokens if t is not None]
        pg.op('pool', lambda e: e.memset(bar[:, 0:1], 0.0), r=['bar2'], w=['bar'])
        pg.finish(out_tokens)
        pg._wait('sp', ('c', 'pool', pg.cnt['pool'] - 1))
        print("ops:", pg.nops, "dmas:", pg.dma_n, "cnt:", pg.cnt)
    return nc


def _consts():
    c = np.zeros((128, NCST), np.float32)
    k = np.arange(128)[:, None]
    m = np.arange(128)[None, :]
    c[:, C_M1:C_M1 + 128] = (k <= m)
    c[:, C_M2:C_M2 + 128] = (k > m)
    c[:, C_S127:C_S127 + 128] = (k == 127)
    c[:, C_S63:C_S63 + 128] = (k == 63)
    band = np.ones((128, 5, 128), np.float32)
    s = np.arange(128)[:, None]
    t = np.arange(128)[None, :]
    band[:, 0, :] = 1.0 - ((s < 64) & (t >= 64))
    band[:, 4, :] = 1.0 - ((s >= 64) & (t < 64))
    ident = (k == m).astype(np.float32)
    return c, np.ascontiguousarray(band.reshape(128, 640)), ident


def _prep(inputs, cfg=None):
    f = lambda a: np.ascontiguousarray(np.asarray(a, dtype=np.float32))
    I = {k: f(v) for k, v in inputs.items()}
    small = np.concatenate([I[n].reshape(2, -1) for n in SM], axis=1)
    s = np.arange(128)[:, None]
    j = np.arange(640)[None, :]
    dist = 512 + (j % 128) - 128 * (j // 128) - s
    idx = np.clip(dist, -128, 128) + 128
    relb = np.ascontiguousarray(np.transpose(I['a_rel'][:, idx, :], (0, 1, 3, 2)))
    cwt = np.concatenate([I['b_conv_w'], I['b_conv_b'][:, None, :]], axis=1)
    convw = np.ascontiguousarray(np.transpose(cwt.reshape(2, 5, 6, 128), (0, 3, 2, 1)))
    cst, band, ident = _consts()
    maps = []
    for c in range(8):
        b = c % 4
        ss = slice(c * NS, (c + 1) * NS)
        m = dict(
            xp=I['x_prompt'][b], xs=I['x_sample'][ss].reshape(NS * TS, DM), memp=I['mem_prompt'][b],
            ca_k=I['cache_a_k'][:, ss].reshape(2, NS, 512, 512), ca_v=I['cache_a_v'][:, ss].reshape(2, NS, 512, 512),
            cc_k=I['cache_c_k'][:, ss].reshape(2, NS, 1024, 512), cc_v=I['cache_c_v'][:, ss].reshape(2, NS, 1024, 512),
            cc_f=I['cache_c_logf'][:, ss], sb_ssm=I['state_b_ssm'][:, ss], sb_conv=I['state_b_conv'][:, ss],
            cm_k=I['cache_mem_k'][:, ss].reshape(2, NS, 256, 256), cm_v=I['cache_mem_v'][:, ss].reshape(2, NS, 256, 256),
            w_in=I['w_in'], w_mkv=I['w_mkv'], w_pa=I['w_pa'], w_pb=I['w_pb'], w_pc=I['w_pc'], w_pm=I['w_pm'],
            w_out=I['w_out'], small=small, mnorm=I['m_norm'], band=band, ident=ident, relb=relb, convw=convw, cst=cst)
        maps.append({k: np.ascontiguousarray(v) for k, v in m.items()})
    return maps


_NC_CACHE = {}


def kernel(**inputs):
    cfg = {}
    key = 'full'
    if key not in _NC_CACHE:
        _NC_CACHE[key] = build(cfg)
    nc = _NC_CACHE[key]
    maps = _prep(inputs)
    res = run_bass_kernel_spmd(nc, maps, core_ids=list(range(8)))
    R = res.results
    P4 = range(4)
    st = lambda name, shape: np.stack([R[b][name] for b in P4], axis=1).reshape(shape)
    cat = lambda name: np.concatenate([R[c][name] for c in range(8)], axis=1)
    y_prompt = np.stack([R[b]['yp'] for b in P4], axis=0)
    y_sample = np.concatenate([R[c]['ys'].reshape(NS, TS, DM) for c in range(8)], axis=0)
    outs = [y_prompt, y_sample,
            st('pa_k', (2, 4, 512, 8, 64)), st('pa_v', (2, 4, 512, 8, 64)),
            st('pc_k', (2, 4, SEQ, 8, 64)), st('pc_v', (2, 4, SEQ, 8, 64)), st('pc_f', (2, 4, SEQ, 8)),
            st('pb_s', (2, 4, 8, 64, 64)), st('pb_c', (2, 4, 3, 768)),
            st('pm_k', (2, 4, 256, 4, 64)), st('pm_v', (2, 4, 256, 4, 64))]
    for name, tail in (('sa_k', (8, 64)), ('sa_v', (8, 64)), ('sc_k', (8, 64)), ('sc_v', (8, 64)), ('sc_f', (8,))):
        a = np.concatenate([R[c][name].reshape((2, NS, TS) + tail) for c in range(8)], axis=1)
        outs.append(a)
    outs.append(cat('sb_s'))
    outs.append(cat('sb_c'))
    return tuple(np.ascontiguousarray(o.astype(np.float32)) for o in outs)
```

```python
import contextlib
import numpy as np
import ml_dtypes
import concourse.bass as bass
import concourse.mybir as mybir
from concourse.bass_utils import run_bass_kernel_spmd

F32 = mybir.dt.float32
BF16 = mybir.dt.bfloat16
AF = mybir.ActivationFunctionType
ALU = mybir.AluOpType
AX = mybir.AxisListType

DM = 1024
DIN = 10000
SEQ = 4096
NS = 4
TS = 64
EPS = 1e-6
COL = dict(a_q=0, a_k=512, a_v=1024, a_z=1536, b_z=2048, b_xbc=2560, b_dt=3328,
           c_q=3336, c_k=3848, c_v=4360, c_f=4872, c_z=4880, m_q=5392, m_z=5648, gate=5904)
SM = {}
_o = 0
for _n, _w in (('g_norm', 1024), ('b_norm', 512), ('a_qnorm', 64), ('a_knorm', 64),
               ('c_qnorm', 64), ('c_knorm', 64), ('m_qnorm', 64), ('m_knorm', 64),
               ('b_dt_bias', 8), ('b_a_log', 8), ('b_d', 8), ('c_fbias', 8)):
    SM[_n] = (_o, _w)
    _o += _w
NSM = _o
C_M1, C_M2, C_S127, C_S63 = 0, 128, 256, 384
NCST = 512


def cp(e, out, in_):
    if hasattr(e, 'tensor_copy'):
        return e.tensor_copy(out=out, in_=in_)
    return e.activation(out=out, in_=in_, func=AF.Copy)


SAME_ENGINE_SYNC = True


class StopBuild(Exception):
    pass


class Prog:
    EPOCH = 24000
    ND = 48

    def __init__(self, nc, es):
        self.nc, self.es = nc, es
        self.eng = {'pe': nc.tensor, 'act': nc.scalar, 'dve': nc.vector, 'pool': nc.gpsimd, 'sp': nc.sync}
        self.cnt = {e: 0 for e in self.eng}
        self.sems = {}
        self.waited = {e: {} for e in self.eng}
        self.last_w = {}
        self.readers = {}
        self.dma_n = 0
        self.dma_sems = [es.enter_context(nc.semaphore("dq%d" % i)) for i in range(self.ND)]
        self.dma_tokens = []
        self.nops = 0
        self.maxops = None

    def _sem(self, e, epoch):
        k = (e, epoch)
        if k not in self.sems:
            self.sems[k] = self.es.enter_context(self.nc.semaphore("s_%s_%d" % (e, epoch)))
        return self.sems[k]

    def _wait(self, e, tok):
        if tok[0] == 'c':
            _, pe, c = tok
            if pe == e and (e == 'pe' or not SAME_ENGINE_SYNC):
                return
            epoch, v = divmod(c, self.EPOCH)
            key, val, sem = ('c', pe, epoch), v + 1, self._sem(pe, epoch)
        else:
            _, slot, rnd = tok
            key, val, sem = ('d', slot), 16 * (rnd + 1), self.dma_sems[slot]
        if self.waited[e].get(key, 0) >= val:
            return
        self.waited[e][key] = val
        self.eng[e].wait_ge(sem, val)

    def _deps(self, r, w):
        deps = set()
        for k in r:
            if k in self.last_w:
                deps.add(self.last_w[k])
            if isinstance(k, str) and k.startswith('ps'):
                deps.update(self.readers.get(k, ()))
        for k in w:
            if k in self.last_w:
                deps.add(self.last_w[k])
            deps.update(self.readers.get(k, ()))
        return deps

    def _reg(self, tok, r, w):
        for k in r:
            self.readers.setdefault(k, []).append(tok)
        for k in w:
            self.last_w[k] = tok
            self.readers[k] = []

    def op(self, e, fn, r=(), w=(), inc=True):
        if self.maxops is not None and self.nops >= self.maxops:
            raise StopBuild()
        for tok in self._deps(r, w):
            self._wait(e, tok)
        inst = fn(self.eng[e])
        c = self.cnt[e]
        if inc:
            epoch, _ = divmod(c, self.EPOCH)
            inst.then_inc(self._sem(e, epoch), 1)
            self.cnt[e] += 1
        self._reg(('c', e, c), r, w)
        self.nops += 1

    def dma(self, q, out, in_, r=(), w=(), nonc=False):
        if self.maxops is not None and self.nops >= self.maxops:
            raise StopBuild()
        n = self.dma_n
        self.dma_n += 1
        slot, rnd = n % self.ND, n // self.ND
        if rnd > 0:
            self._wait(q, ('d', slot, rnd - 1))
        for tok in self._deps(r, w):
            self._wait(q, tok)
        kw = {}
        if nonc:
            kw['allow_slow_non_contiguous'] = True
        inst = self.eng[q].dma_start(out=out, in_=in_, **kw)
        inst.then_inc(self.dma_sems[slot], 16)
        tok = ('d', slot, rnd)
        self._reg(tok, r, w)
        self.dma_tokens.append(tok)
        self.nops += 1
        return tok

    def finish(self, out_tokens):
        for tok in out_tokens:
            self._wait('sp', tok)
        last = {}
        for tok in self.dma_tokens:
            last[tok[1]] = tok
        for tok in last.values():
            self._wait('sp', tok)


def build(cfg):
    NL = cfg.get('layers', 2)
    NQ = cfg.get('nq', 8)
    DO_S = cfg.get('sample', True)
    NOSTATE = cfg.get('nostate', False)
    nc = bass.Bass("TRN2", target_bir_lowering=False)
    es = contextlib.ExitStack()
    D = {}

    def din(name, shape):
        D[name] = nc.dram_tensor(name, list(shape), F32, kind="ExternalInput").ap()

    def dout(name, shape):
        D[name] = nc.dram_tensor(name, list(shape), F32, kind="ExternalOutput").ap()

    din('xp', [SEQ, DM]); din('xs', [NS * TS, DM]); din('memp', [256, DM])
    din('ca_k', [2, NS, 512, 512]); din('ca_v', [2, NS, 512, 512])
    din('cc_k', [2, NS, 1024, 512]); din('cc_v', [2, NS, 1024, 512]); din('cc_f', [2, NS, 1024, 8])
    din('sb_ssm', [2, NS, 8, 64, 64]); din('sb_conv', [2, NS, 3, 768])
    din('cm_k', [2, NS, 256, 256]); din('cm_v', [2, NS, 256, 256])
    din('w_in', [2, DM, DIN]); din('w_mkv', [2, DM, 512])
    din('w_pa', [2, 512, DM]); din('w_pb', [2, 512, DM]); din('w_pc', [2, 512, DM]); din('w_pm', [2, 256, DM])
    din('w_out', [2, DM, DM])
    din('small', [2, NSM]); din('mnorm', [2, 1024]); din('relb', [2, 128, 8, 640]); din('convw', [2, 128, 6, 5]); din('cst', [128, NCST]); din('band', [128, 640]); din('ident', [128, 128])
    dout('yp', [SEQ, DM]); dout('ys', [NS * TS, DM])
    dout('pa_k', [2, 512, 512]); dout('pa_v', [2, 512, 512])
    dout('pc_k', [2, SEQ, 512]); dout('pc_v', [2, SEQ, 512]); dout('pc_f', [2, SEQ, 8])
    dout('pb_s', [2, 8, 64, 64]); dout('pb_c', [2, 3, 768])
    dout('pm_k', [2, 256, 256]); dout('pm_v', [2, 256, 256])
    dout('sa_k', [2, NS * TS, 512]); dout('sa_v', [2, NS * TS, 512])
    dout('sc_k', [2, NS * TS, 512]); dout('sc_v', [2, NS * TS, 512]); dout('sc_f', [2, NS * TS, 8])
    dout('sb_s', [2, NS, 8, 64, 64]); dout('sb_c', [2, NS, 3, 768])
    x1p = nc.dram_tensor("x1p", [SEQ, DM], F32, kind="Internal").ap()
    x1s = nc.dram_tensor("x1s", [NS * TS, DM], F32, kind="Internal").ap()

    with es:
        pg = Prog(nc, es)
        pg.maxops = cfg.get('maxops')
        out_tokens = []

        def sb(name, shape, dt):
            return es.enter_context(nc.sbuf_tensor("sb_" + name, list(shape), dt))

        def ps(name, shape, dt):
            return es.enter_context(nc.psum_tensor("ps_" + name, list(shape), dt))

        cst = sb("cst", [128, NCST], F32)
        cstb = sb("cstb", [128, 256], BF16)
        small = sb("small", [128, NSM], F32)
        expB = sb("expB", [128, 8, 640], BF16)
        cw = sb("cw", [128, 6, 5], F32)
        aneg = sb("aneg", [128, 8], F32)
        xt = sb("xt", [128, 1, DM], F32)
        W3 = sb("W3", [128, 4096], BF16)
        xnb = W3[:, 0:1024]
        sq = W3[:, 1024:2048].bitcast(F32)
        f2 = W3[:, 2048:3072].bitcast(F32)
        f3 = W3[:, 3072:4096].bitcast(F32)
        xnT = sb("xnT", [128, 8, 512], BF16)
        NWB = 3
        wbuf = [sb("wbuf%d" % i, [128, 8, 512], BF16) for i in range(NWB)]
        wbuf.append(W3[:, :].rearrange("p (k n) -> p k n", n=512))
        WKEYS = [['wbuf0'], ['wbuf1'], ['wbuf2'], ['wbuf3', 'xnb', 'sq', 'f2', 'f3']]
        kT_c = sb("kT_c", [128, 4, SEQ], BF16)
        va_c = sb("va_c", [128, 32, 8, 66], BF16)
        kT_a = sb("kT_a", [128, 4, 1024], BF16)
        va_a = sb("va_a", [128, 8, 8, 66], BF16)
        kT_m = sb("kT_m", [128, 2, 256], BF16)
        va_m = sb("va_m", [128, 2, 4, 66], BF16)
        cum = sb("cum", [128, 32, 8], F32)
        biasQ = sb("biasQ", [128, 32, 8], F32)
        qT = sb("qT", [128, 4, 512], BF16)
        sz = sb("sz", [128, 4, 512], BF16)
        oz = sb("oz", [128, 4, 512], BF16)
        ozT = {k: sb("ozT_" + k, [128, n, 512], BF16) for k, n in (('a', 4), ('b', 4), ('c', 4), ('m', 2))}
        f1 = sb("f1", [128, 512], F32)
        b1 = sb("b1", [128, 512], BF16)
        PT = [sb("PT%d" % i, [128, 512], BF16) for i in range(2)]
        st8 = sb("st8", [128, 8, 8], F32)
        raw = sb("raw", [128, 3, 4, 131], F32)
        halo = sb("halo", [128, 6, 3], F32)
        AR2 = sb("AR2", [128, 4096], BF16)
        xbc = AR2[:, 0:3072].rearrange("p (j n) -> p j n", n=512)
        Mh = AR2[:, 3072:4096].rearrange("p (h n) -> p h n", n=128)
        mT = AR2[:, :].rearrange("p (k n) -> p k n", n=512)
        cva = sb("cva", [128, 512], F32)
        sig = cva
        dts = sb("dts", [128, 4, 8], F32)
        lfs = sb("lfs", [128, 4, 8], F32)
        asb = sb("asb", [128, 4, 8], F32)
        btk = sb("btk", [128, 128], BF16)
        AE = sb("AE", [128, 2048], F32)
        Ah = AE[:, 0:512].rearrange("p (h n) -> p h n", n=128)
        Ee = AE[:, 512:1024].rearrange("p (h n) -> p h n", n=128)
        Gm = AE[:, 1024:1280].rearrange("p (g n) -> p g n", n=128)
        bdec = AE[:, 1280:1536].bitcast(BF16).rearrange("p (a g n) -> p a g n", a=4, g=2)
        xdt = AE[:, 1536:1792].bitcast(BF16).rearrange("p (h d) -> p h d", d=64)
        xtk = AE[:, 1792:2048].bitcast(BF16)
        macc = AE[:, :].rearrange("p (t n) -> p t n", n=512)
        hT = sb("hT", [128, 4, 64], F32)
        hTb = sb("hTb", [128, 4, 64], BF16)
        ssd8 = sb("ssd8", [128, 4, 8], F32)
        psA = [ps("psA%d" % i, [128, 512], F32) for i in range(2)]
        psT = [ps("psT%d" % i, [128, 1024], BF16) for i in range(2)]
        psS = [ps("psS%d" % i, [128, 512], F32) for i in range(2)]
        psO = [ps("psO%d" % i, [128, 512], F32) for i in range(2)]
        rr = {'A': 0, 'T': 0, 'S': 0, 'O': 0, 'W': 0, 'P': 0, 'X': 0, 'WP': 0}
        rrn = {'X': 1, 'WP': 1}

        def rot(k, n=2):
            v = rr[k]
            rr[k] = (v + 1) % rrn.get(k, n)
            return v

        ident_b = cstb[:, 0:128]
        M1f = cst[:, C_M1:C_M1 + 128]
        M2f = cst[:, C_M2:C_M2 + 128]
        S127f = cst[:, C_S127:C_S127 + 128]
        S63f = cst[:, C_S63:C_S63 + 128]
        M1b = cstb[:, 128:256]

        def smv(name, rows=128):
            o, w = SM[name]
            return small[0:rows, o:o + w]

        bar = sb("bar", [128, 2], F32)
        epst = sb("epst", [128, 1], F32)
        nhalf = sb("nhalf", [128, 8], F32)
        identf = sb("identf", [128, 64], F32)

        def barrier(keys):
            pg.op('pool', lambda e: e.memset(bar[:, 0:1], 0.0), w=['bar'] + list(keys))

        AEK = ['Ah%d' % h for h in range(8)] + ['Ee', 'Gm', 'bdec', 'xdt', 'xtk', 'AErel'] + ['macc%d' % t for t in range(4)]
        pg.op('pool', lambda e: e.memset(epst[:, :], EPS), w=['epst'])
        pg.op('pool', lambda e: e.memset(nhalf[:, :], -0.5), w=['epst'])
        pg.dma('sp', identf[0:64, :], D['ident'][0:64, 0:64], w=['identf'])
        pg.dma('sp', identf[64:128, :], D['ident'][0:64, 0:64], w=['identf'])
        pg.dma('sp', cst[:, :], D['cst'][:, :], w=['cst'])
        pg.dma('sp', AE[:, 0:128], D['ident'][:, :], w=AEK)
        pg.op('dve', lambda e: cp(e, out=cstb[:, 0:128], in_=AE[:, 0:128]), r=AEK, w=['cstb'])
        pg.op('dve', lambda e: cp(e, out=cstb[:, 128:256], in_=cst[:, C_M1:C_M1 + 128]), r=['cst'], w=['cstb'])
        for t_, k_ in ((va_c, 'va_c'), (va_a, 'va_a'), (va_m, 'va_m')):
            pg.op('pool', lambda e, t_=t_: e.memset(t_[:], 1.0), w=[k_])

        def group_wlist(l):
            wl = []
            for nm in ('c_q', 'c_k', 'c_v'):
                wl.append(('w_in', l, COL[nm], 512, 8))
            wl.append(('w_in', l, COL['c_f'], 8, 8))
            wl.append(('w_in', l, COL['c_z'], 512, 8))
            for nm in ('a_q', 'a_k', 'a_v', 'a_z', 'm_q', 'b_z'):
                wl.append(('w_in', l, COL[nm], 512, 8))
            wl.append(('w_in', l, COL['b_dt'], 8, 8))
            for half in range(2):
                wl.append(('w_in', l, COL['b_xbc'] + half * 384, 384, 8))
            for c in range(2):
                for bi, (wn, nkc) in enumerate((('w_pa', 4), ('w_pb', 4), ('w_pc', 4), ('w_pm', 2))):
                    wl.append(('w_in', l, COL['gate'] + bi * 1024 + c * 512, 512, 8, True))
                    wl.append((wn, l, c * 512, 512, nkc, True))
            for c in range(2):
                wl.append(('w_out', l, c * 512, 512, 8, True))
            return wl

        WL = []
        for l_ in range(NL):
            WL.append(('w_mkv', l_, 0, 512, 8))
            for _ in range(NQ + (1 if DO_S else 0)):
                WL += group_wlist(l_)
        wbi, prev_occ, lastocc, r3, r4 = [], [], {}, 0, 0
        for k_, ent in enumerate(WL):
            if len(ent) > 5 and ent[5]:
                b_ = r4 % 4; r4 += 1
            else:
                b_ = r3 % 3; r3 += 1; r4 = r3
            wbi.append(b_)
            prev_occ.append(lastocc.get(b_, -1))
            lastocc[b_] = k_
        wstate = {'ptr': 0, 'issued': 0}

        def load_w(dram, l, c0, n, nk=8, buf=None):
            i = wstate['ptr']
            exp = WL[i]
            assert exp[1] == l and exp[2] == c0 and exp[3] == n and exp[4] == nk and D[exp[0]] is dram, (exp, l, c0, n, nk)
            in_merge = len(exp) > 5 and exp[5]
            while wstate['issued'] < len(WL) and wstate['issued'] <= i + 3 and \
                    (wstate['issued'] <= i or prev_occ[wstate['issued']] <= i - 2) and \
                    (wbi[wstate['issued']] != 3 or in_merge):
                k = wstate['issued']
                nm, l2, c2, n2, nk2 = WL[k][0:5]
                src = D[nm][l2, :, c2:c2 + n2].rearrange("(k p) n -> p k n", p=128)
                pg.dma('pool', wbuf[wbi[k]][:, 0:nk2, 0:n2], src, w=WKEYS[wbi[k]])
                wstate['issued'] += 1
            wstate['ptr'] += 1
            return wbuf[wbi[i]], WKEYS[wbi[i]]

        def transposes(src_ap_fn, nblk, np_, dst_fn, rkeys, wkeys, evac='dve'):
            i = rot('T'); pt = psT[i]; pk = 'psT%d' % i
            for j in range(nblk):
                pg.op('pe', lambda e, j=j: e.transpose(out=pt[:, j * 128:j * 128 + np_], in_=src_ap_fn(j),
                                                       identity=ident_b[0:np_, 0:np_]),
                      r=list(rkeys) + ['cstb'], w=[pk], inc=(j == nblk - 1))
            view = pt[:, 0:nblk * 128].rearrange("p (b n) -> p b n", n=128)[:, :, 0:np_]
            pg.op(evac, lambda e: dst_fn(e, view), r=[pk], w=list(wkeys))

        def rstd_from_ss(ss_ap, rs_ap, n, rows, key):
            w_ = rs_ap.shape[-1]
            pg.op('dve', lambda e: e.tensor_scalar(out=rs_ap, in0=ss_ap, scalar1=1.0 / n, scalar2=EPS,
                                                    op0=ALU.mult, op1=ALU.add), r=[key], w=[key])
            pg.op('pool', lambda e: e.tensor_tensor(out=rs_ap, in0=rs_ap, in1=nhalf[0:rows, 0:w_], op=ALU.pow),
                  r=[key, 'epst'], w=[key])

        def head_norm(psap, pk, nh, gain_ap, out_ap, outkeys, np_, scale=None):
            n = nh * 64
            pg.op('act', lambda e: e.activation(out=sq[0:np_, 0:n], in_=psap, func=AF.Square), r=[pk], w=['sq'])
            ssv = st8[0:np_, 0, 0:nh]
            pg.op('dve', lambda e: e.tensor_reduce(out=ssv, in_=sq[0:np_, 0:n].rearrange("p (h d) -> p h d", d=64),
                                                   axis=AX.X, op=ALU.add), r=['sq'], w=['st8'])
            rstd_from_ss(ssv, ssv, 64, np_, 'st8')
            pg.op('dve', lambda e: e.tensor_tensor(
                out=f1[0:np_, 0:n].rearrange("p (h d) -> p h d", d=64),
                in0=psap.rearrange("p (h d) -> p h d", d=64),
                in1=ssv.unsqueeze(2).to_broadcast([np_, nh, 64]), op=ALU.mult), r=[pk, 'st8'], w=['f1'])
            g = gain_ap.unsqueeze(1).to_broadcast([np_, nh, 64])
            pg.op('dve', lambda e: e.tensor_tensor(
                out=out_ap.rearrange("p (h d) -> p h d", d=64),
                in0=f1[0:np_, 0:n].rearrange("p (h d) -> p h d", d=64), in1=g, op=ALU.mult),
                r=['f1', 'small'], w=list(outkeys))

        def pipe_tiles(NT, proj_fn):
            nxt = proj_fn(0)
            for t in range(NT):
                cur = nxt
                if t + 1 < NT:
                    nxt = proj_fn(t + 1)
                yield (t,) + tuple(cur)

        def proj_tm(xT, xkey, tcol, np_, wt, wkey, n, nk=8, wc0=0, xkeys=()):
            i = rot('A'); p = psA[i]; pk = 'psA%d' % i
            for kc in range(nk):
                pg.op('pe', lambda e, kc=kc: e.matmul(p[0:np_, 0:n], lhsT=xT[:, kc, tcol:tcol + np_],
                                                      rhs=wt[:, kc, wc0:wc0 + n], start=(kc == 0), stop=(kc == nk - 1)),
                      r=[xkey] + list(wkey) + list(xkeys), w=[pk], inc=(kc == nk - 1))
            return p[0:np_, 0:n], pk

        def layer_consts(l):
            pg.dma('sp', small[:, :], D['small'][l:l + 1, :].broadcast_to([128, NSM]), w=['small'])
            pg.dma('sp', cw[:, :, :], D['convw'][l], w=['cw'])
            for name in ('a_qnorm', 'c_qnorm', 'm_qnorm'):
                a = smv(name)
                pg.op('dve', lambda e, a=a: e.tensor_scalar(out=a, in0=a, scalar1=0.125, scalar2=None, op0=ALU.mult),
                      r=['small'], w=['small'])
            pg.op('act', lambda e: e.activation(out=aneg[:, :], in_=smv('b_a_log'), func=AF.Exp), r=['small'], w=['aneg'])
            pg.op('dve', lambda e: e.tensor_scalar(out=aneg[:, :], in0=aneg[:, :], scalar1=-1.0, scalar2=None, op0=ALU.mult),
                  r=['aneg'], w=['aneg'])
            barrier(AEK)
            pg.dma('sp', AE[:, 0:640], D['band'][:, :], w=AEK)
            pg.op('dve', lambda e: e.tensor_scalar(out=AE[:, 0:640], in0=AE[:, 0:640], scalar1=30000.0, scalar2=-30000.0,
                                                    op0=ALU.mult, op1=ALU.add), r=AEK, w=AEK)
            for h in range(8):
                pg.dma('sp', AE[:, 1024:1664], D['relb'][l, :, h, :], w=['AErel'])
                pg.op('dve', lambda e, h=h: e.tensor_tensor(out=expB[:, h, :], in0=AE[:, 1024:1664],
                                                            in1=AE[:, 0:640], op=ALU.add),
                      r=['AErel'] + AEK, w=['expB'])
            barrier(AEK)

        def norm_tiles(src_fn, NT, np_, gain_fn, rk_fn=None):
            for t in range(NT):
                i = rot('X')
                xk = 'xt%d' % i
                pg.dma('sp', xt[0:np_, i, :], src_fn(t), r=(rk_fn(t) if rk_fn else []), w=[xk])
                ssv = st8[0:np_, 1, 0:1]
                pg.op('act', lambda e, i=i: e.activation(out=xnb[0:np_, :], in_=xt[0:np_, i, :], func=AF.Square, accum_out=ssv),
                      r=[xk], w=['xnb', 'st8'])
                rstd_from_ss(ssv, ssv, DM, np_, 'st8')
                pg.op('dve', lambda e, i=i: e.scalar_tensor_tensor(out=xnb[0:np_, :], in0=xt[0:np_, i, :], scalar=ssv,
                                                                  in1=gain_fn(np_), op0=ALU.mult, op1=ALU.mult),
                      r=[xk, 'st8', 'small', 'f2'], w=['xnb'])
                transposes(lambda j: xnb[0:np_, j * 128:(j + 1) * 128], 8, np_,
                           lambda e, v, t=t: cp(e, out=xnT[:, :, t * np_:(t + 1) * np_], in_=v),
                           ['xnb'], ['xnT'], evac='act' if t % 2 else 'dve')

        def attend(qT_ap, NQc, qtiles, kblocks, qkeys, out_fn, acc=None):
            if acc is None:
                io = rot('O'); po = psO[io]; pok = 'psO%d' % io
                pov = po[:, 0:260].rearrange("p (j c) -> p j c", c=65)
                jbase, bank_first = 0, True
            else:
                pov, pok, jbase, bank_first = acc
            lastkb, firstkb = {}, {}
            for bi, kb in enumerate(kblocks):
                for j, (c0, nq) in enumerate(qtiles):
                    if c0 >= kb['q0']:
                        lastkb[j] = bi
                        firstkb.setdefault(j, bi)
            st = {}

            def stage_s(bi):
                kb = kblocks[bi]
                isx = rot('S'); psx = psS[isx]; psk = 'psS%d' % isx
                badd = kb.get('badd')
                pg.op('pe', lambda e: e.matmul(psx[0:kb['nk'], kb['q0']:NQc], lhsT=kb['kT'],
                                               rhs=qT_ap[:, kb['q0']:NQc], start=True, stop=(badd is None)),
                      r=list(qkeys) + list(kb['keys']), w=[psk], inc=(badd is None))
                if badd is not None:
                    pg.op('pe', lambda e: e.matmul(psx[0:kb['nk'], kb['q0']:NQc], lhsT=ident_b[:, 0:kb['nk']],
                                                   rhs=badd, start=False, stop=True),
                          r=['cstb'] + list(kb.get('mkeys', [])), w=[psk])
                st[bi] = (psx, psk)

            def stage_e(bi):
                kb = kblocks[bi]
                psx, psk = st[bi]
                ip = rot('P'); pt = PT[ip]; ptk = 'PT%d' % ip
                if kb.get('bias') is not None:
                    pg.op('act', lambda e: e.activation(out=pt[0:kb['nk'], kb['q0']:NQc], in_=psx[0:kb['nk'], kb['q0']:NQc],
                                                        func=AF.Exp, bias=kb['bias'], scale=1.0),
                          r=[psk] + list(kb.get('bkeys', [])), w=[ptk])
                else:
                    pg.op('act', lambda e: e.activation(out=pt[0:kb['nk'], kb['q0']:NQc], in_=psx[0:kb['nk'], kb['q0']:NQc],
                                                        func=AF.Exp), r=[psk], w=[ptk])
                if kb.get('diagmask'):
                    pg.op('dve', lambda e: e.tensor_tensor(
                        out=pt[0:kb['nk'], kb['q0']:kb['q0'] + kb['nk']], in0=pt[0:kb['nk'], kb['q0']:kb['q0'] + kb['nk']],
                        in1=M1b[0:kb['nk'], 0:kb['nk']], op=ALU.mult), r=[ptk, 'cstb'], w=[ptk])
                st[bi] = (pt, ptk)

            def stage_v(bi):
                kb = kblocks[bi]
                pt, ptk = st[bi]
                for j, (c0, nq) in enumerate(qtiles):
                    if c0 < kb['q0']:
                        continue
                    pg.op('pe', lambda e, j=j, c0=c0, nq=nq: e.matmul(
                        pov[0:nq, jbase + j, :], lhsT=pt[0:kb['nk'], c0:c0 + nq], rhs=kb['v'],
                        start=(bank_first and bi == 0 and j == min(firstkb)), stop=(bi == lastkb[j]), skip_group_check=True),
                        r=[ptk] + list(kb['keys']), w=[pok], inc=(c0 + nq >= NQc))

            n = len(kblocks)
            stage_s(0)
            for bi in range(n):
                if bi + 1 < n:
                    stage_s(bi + 1)
                stage_e(bi)
                stage_v(bi)
            if out_fn is not None:
                out_fn(pov, pok)
            return pov, pok

        def attn_out(h0col, NT, np_, szkey='sz'):
            def fn(pov, pok):
                rl = st8[0:np_, 2, 0:NT]
                pg.op('dve', lambda e: e.reciprocal(out=rl, in_=pov[0:np_, 0:NT, 64]), r=[pok], w=['st8'])
                tmp = f2[0:np_, 0:NT * 64].rearrange("p (j d) -> p j d", d=64)
                pg.op('dve', lambda e: e.tensor_tensor(out=tmp, in0=pov[0:np_, 0:NT, 0:64],
                                                       in1=rl.unsqueeze(2).to_broadcast([np_, NT, 64]), op=ALU.mult),
                      r=[pok, 'st8'], w=['f2'])
                pg.op('pool', lambda e: e.tensor_tensor(out=oz[0:np_, 0:NT, h0col:h0col + 64], in0=tmp,
                                                        in1=sz[0:np_, 0:NT, h0col:h0col + 64], op=ALU.mult),
                      r=['f2', szkey], w=['oz'])
            return fn

        def oz_to_T(br, NT, np_, nblk):
            for t in range(NT):
                transposes(lambda j, t=t: oz[0:np_, t, j * 128:(j + 1) * 128], nblk, np_,
                           lambda e, v, t=t: cp(e, out=ozT[br][:, 0:nblk, t * np_:(t + 1) * np_], in_=v),
                           ['oz'], ['ozT_' + br], evac='act' if t % 2 else 'dve')

        def evac_q(psap, pk, t, np_, nh, gname):
            head_norm(psap, pk, nh, smv(gname, np_), b1[0:np_, 0:nh * 64], ['b1'], np_)
            transposes(lambda j: b1[0:np_, j * 128:(j + 1) * 128], nh // 2, np_,
                       lambda e, v: cp(e, out=qT[:, 0:nh // 2, t * np_:(t + 1) * np_], in_=v),
                       ['b1'], ['qT'], evac='act')

        def evac_k(psap, pk, np_, nh, gname, out_dram, kT_dst_fn, kT_key):
            head_norm(psap, pk, nh, smv(gname, np_), f3[0:np_, 0:nh * 64], ['f3'], np_)
            if out_dram is not None:
                out_tokens.append(pg.dma('sp', out_dram, f3[0:np_, 0:nh * 64], r=['f3']))
            pg.op('dve', lambda e: cp(e, out=b1[0:np_, 0:nh * 64], in_=f3[0:np_, 0:nh * 64]), r=['f3'], w=['b1'])
            transposes(lambda j: b1[0:np_, j * 128:(j + 1) * 128], nh // 2, np_,
                       lambda e, v: cp(e, out=kT_dst_fn(), in_=v), ['b1'], [kT_key], evac='act')

        def evac_v(psap, pk, np_, nh, out_dram, va_dst, va_key):
            if out_dram is not None:
                pg.op('act', lambda e: e.activation(out=f3[0:np_, 0:nh * 64], in_=psap, func=AF.Copy), r=[pk], w=['f3'])
                out_tokens.append(pg.dma('sp', out_dram, f3[0:np_, 0:nh * 64], r=['f3']))
            pg.op('dve', lambda e: cp(e, out=va_dst, in_=psap.rearrange("p (h d) -> p h d", d=64)),
                  r=[pk], w=[va_key])

        def softplus_from(psap, pk, bias_ap, out_ap, okey, np_, neg=False):
            tmp = st8[0:np_, 3, :]
            pg.op('dve', lambda e: e.tensor_tensor(out=tmp, in0=psap, in1=bias_ap, op=ALU.add), r=[pk, 'small'], w=['st8'])
            pg.op('act', lambda e: e.activation(out=tmp, in_=tmp, func=AF.Exp, scale=(-1.0 if neg else 1.0)),
                  r=['st8'], w=['st8'])
            pg.op('dve', lambda e: e.tensor_scalar(out=tmp, in0=tmp, scalar1=1.0, scalar2=None, op0=ALU.add),
                  r=['st8'], w=['st8'])
            pg.op('act', lambda e: e.activation(out=tmp, in_=tmp, func=AF.Ln), r=['st8'], w=['st8'])
            pg.op('dve', lambda e: e.tensor_scalar(out=out_ap, in0=tmp, scalar1=(-1.0 if neg else 1.0), scalar2=None,
                                                    op0=ALU.mult), r=['st8'], w=[okey])

        def proj8_all(NT, np_, wt, wk):
            i = rot('A'); p = psA[i]; pk = 'psA%d' % i
            for t in range(NT):
                for kc in range(8):
                    pg.op('pe', lambda e, kc=kc, t=t: e.matmul(p[0:np_, t * 8:(t + 1) * 8], lhsT=xnT[:, kc, t * np_:(t + 1) * np_],
                                                              rhs=wt[:, kc, 0:8], start=(kc == 0), stop=(kc == 7)),
                          r=['xnT'] + list(wk), w=[pk], inc=(kc == 7 and t == NT - 1))
            return p[0:np_, 0:NT * 8].rearrange("p (t h) -> p t h", h=8), pk

        def softplus4(psv, pk, bias_ap, out_ap, okey, np_, NT, neg=False):
            tmp = st8[0:np_, 0:NT, :]
            pg.op('dve', lambda e: e.tensor_tensor(out=tmp, in0=psv, in1=bias_ap.unsqueeze(1).to_broadcast([np_, NT, 8]),
                                                   op=ALU.add), r=[pk, 'small'], w=['st8'])
            pg.op('act', lambda e: e.activation(out=tmp, in_=tmp, func=AF.Exp, scale=(-1.0 if neg else 1.0)),
                  r=['st8'], w=['st8'])
            pg.op('dve', lambda e: e.tensor_scalar(out=tmp, in0=tmp, scalar1=1.0, scalar2=None, op0=ALU.add),
                  r=['st8'], w=['st8'])
            pg.op('act', lambda e: e.activation(out=tmp, in_=tmp, func=AF.Ln), r=['st8'], w=['st8'])
            pg.op('dve', lambda e: e.tensor_scalar(out=out_ap, in0=tmp, scalar1=(-1.0 if neg else 1.0), scalar2=None,
                                                    op0=ALU.mult), r=['st8'], w=[okey])

        def cumsum_tile(lf_ap, lfkey, np_, dst_ap, dkey, carry_ap, ckeys):
            i = rot('S'); p = psS[i]; pk = 'psS%d' % i
            pg.op('pe', lambda e: e.matmul(p[0:np_, 0:8], lhsT=M1f[0:np_, 0:np_], rhs=lf_ap, start=True,
                                           stop=(carry_ap is None)), r=[lfkey, 'cst'], w=[pk])
            if carry_ap is not None:
                pg.op('pe', lambda e: e.matmul(p[0:np_, 0:8], lhsT=carry_ap[0], rhs=carry_ap[1], start=False, stop=True),
                      r=list(ckeys) + ['cst'], w=[pk])
            pg.op('dve', lambda e: cp(e, out=dst_ap, in_=p[0:np_, 0:8]), r=[pk], w=[dkey])

        def ssd_tile(t, np_, NT):
            c0 = t * np_
            if t == 0:
                barrier(AEK)
            def ev(e, v):
                return cp(e, out=xtk[0:np_, :].rearrange("p (b n) -> p b n", n=128), in_=v[0:np_, 0:4, :])
            i = rot('T'); pt = psT[i]; pk = 'psT%d' % i
            for j in range(5):
                pg.op('pe', lambda e, j=j: e.transpose(out=pt[0:np_, j * 128:(j + 1) * 128], in_=xbc[:, j, c0:c0 + np_],
                                                       identity=ident_b[:, :]), r=['xbc', 'cstb'], w=[pk], inc=(j == 4))
            pg.op('act', lambda e: e.activation(out=xtk[0:np_, :], in_=pt[0:np_, 0:512], func=AF.Copy), r=[pk], w=['xtk'])
            pg.op('act', lambda e: e.activation(out=btk[0:np_, :], in_=pt[0:np_, 512:640], func=AF.Copy), r=[pk], w=['btk'])
            pg.op('dve', lambda e: e.tensor_tensor(out=xdt[0:np_, :, :], in0=pt[0:np_, 0:512].rearrange("p (h d) -> p h d", d=64),
                                                   in1=dts[0:np_, t, :].unsqueeze(2).to_broadcast([np_, 8, 64]), op=ALU.mult),
                  r=[pk, 'dts'], w=['xdt'])
            a_ap = asb[0:np_, t, :]
            i = rot('A'); p = psA[i]; pk2 = 'psA%d' % i
            pg.op('pe', lambda e: e.matmul(p[0:np_, 0:8], lhsT=M1f[0:np_, 0:np_], rhs=a_ap, start=True, stop=True),
                  r=['asb', 'cst'], w=[pk2], inc=False)
            pg.op('pe', lambda e: e.matmul(p[0:np_, 8:16], lhsT=M2f[0:np_, 0:np_], rhs=a_ap, start=True, stop=True),
                  r=['asb', 'cst'], w=[pk2], inc=False)
            pg.op('pe', lambda e: e.matmul(p[:, 16:24], lhsT=M1f[0:np_, :], rhs=a_ap, start=True, stop=False),
                  r=['asb', 'cst'], w=[pk2], inc=False)
            pg.op('pe', lambda e: e.matmul(p[:, 16:24], lhsT=M2f[0:np_, :], rhs=a_ap, start=False, stop=True),
                  r=['asb', 'cst'], w=[pk2])
            pg.op('act', lambda e: e.activation(out=ssd8[0:np_, 0, :], in_=p[0:np_, 0:8], func=AF.Exp), r=[pk2], w=['ssd8'])
            pg.op('act', lambda e: e.activation(out=ssd8[0:np_, 1, :], in_=p[0:np_, 8:16], func=AF.Exp), r=[pk2], w=['ssd8'])
            pg.op('act', lambda e: e.activation(out=ssd8[:, 2, :], in_=p[:, 16:24], func=AF.Exp), r=[pk2], w=['ssd8'])
            for g in range(2):
                i = rot('A'); pgm = psA[i]; pk3 = 'psA%d' % i
                pg.op('pe', lambda e, g=g, pgm=pgm: e.matmul(pgm[0:np_, 0:np_], lhsT=xbc[g * 64:(g + 1) * 64, 4, c0:c0 + np_],
                                                    rhs=xbc[g * 64:(g + 1) * 64, 5, c0:c0 + np_], start=True, stop=True),
                      r=['xbc'], w=[pk3])
                pg.op('dve', lambda e, g=g, pgm=pgm: e.tensor_tensor(
                    out=Gm[0:np_, g, 0:np_], in0=pgm[0:np_, 0:np_], in1=M1f[0:np_, 0:np_], op=ALU.mult),
                    r=[pk3, 'cst'], w=['Gm'])
            for hh in range(2):
                for h4 in range(4):
                    h = hh * 4 + h4
                    pg.op('dve', lambda e, h=h, h4=h4: e.tensor_scalar(
                        out=Ah[0:np_, h4, 0:np_], in0=M2f[0:np_, 0:np_], scalar1=asb[0:np_, t, h:h + 1], scalar2=None,
                        op0=ALU.mult), r=['cst', 'asb'], w=['Ah%d' % h4])
                psx = psS[hh]; psk = 'psS%d' % hh
                for h4 in range(4):
                    pg.op('pe', lambda e, h4=h4, psx=psx: e.matmul(
                        psx[0:np_, h4 * 128:h4 * 128 + np_], lhsT=Ah[0:np_, h4, 0:np_], rhs=M1f[0:np_, 0:np_],
                        start=True, stop=True), r=['Ah%d' % h4, 'cst'], w=[psk], inc=(h4 == 3))
                pg.op('act', lambda e, psx=psx: e.activation(
                    out=Ee[0:np_, :, 0:np_],
                    in_=psx[0:np_, :].rearrange("p (h n) -> p h n", n=128)[:, :, 0:np_], func=AF.Exp),
                    r=[psk], w=['Ee'])
                pg.op('dve', lambda e, hh=hh: e.tensor_tensor(
                    out=Mh[0:np_, hh * 4:(hh + 1) * 4, 0:np_], in0=Ee[0:np_, :, 0:np_],
                    in1=Gm[0:np_, hh, 0:np_].unsqueeze(1).to_broadcast([np_, 4, np_]), op=ALU.mult),
                    r=['Ee', 'Gm'], w=['Mh'])
            io = rot('O'); py = psO[io]; pyk = 'psO%d' % io
            io2 = rot('O'); py2 = psO[io2]; py2k = 'psO%d' % io2
            for h in range(8):
                pg.op('pe', lambda e, h=h: e.matmul(py[0:np_, h * 64:(h + 1) * 64], lhsT=Mh[0:np_, h, 0:np_],
                                                    rhs=xdt[0:np_, h, :], start=True, stop=True),
                      r=['Mh', 'xdt'], w=[pyk], inc=(h == 7))
            py3 = psS[1]; py3k = 'psS1'
            for h in range(8):
                g = h // 4
                dstp = (py2 if g == 0 else py3)
                pg.op('pe', lambda e, h=h, g=g, dstp=dstp: e.matmul(dstp[0:np_, (h % 4) * 64:(h % 4 + 1) * 64],
                                                         lhsT=xbc[g * 64:(g + 1) * 64, 5, c0:c0 + np_],
                                                         rhs=hTb[g * 64:(g + 1) * 64, h % 4, :], start=True, stop=True),
                      r=['xbc', 'hTb'], w=[py2k if g == 0 else py3k], inc=(h % 4 == 3))
            yv = f1[0:np_, :].rearrange("p (h d) -> p h d", d=64)
            for g, (dstp, dk) in enumerate(((py2, py2k), (py3, py3k))):
                pg.op('dve', lambda e, g=g, dstp=dstp: e.tensor_tensor(
                    out=yv[:, g * 4:(g + 1) * 4, :], in0=dstp[0:np_, 0:256].rearrange("p (h d) -> p h d", d=64),
                    in1=ssd8[0:np_, 0, g * 4:(g + 1) * 4].unsqueeze(2).to_broadcast([np_, 4, 64]), op=ALU.mult),
                    r=[dk, 'ssd8'], w=['f1'])
            pg.op('dve', lambda e: e.tensor_tensor(out=f1[0:np_, :], in0=f1[0:np_, :], in1=py[0:np_, :], op=ALU.add),
                  r=['f1', pyk], w=['f1'])
            pg.op('pool', lambda e: e.tensor_tensor(out=f2[0:np_, :].rearrange("p (h d) -> p h d", d=64),
                                                    in0=xtk[0:np_, :].rearrange("p (h d) -> p h d", d=64),
                                                    in1=smv('b_d', np_).unsqueeze(2).to_broadcast([np_, 8, 64]), op=ALU.mult),
                  r=['xtk', 'small'], w=['f2'])
            pg.op('dve', lambda e: e.tensor_tensor(out=f1[0:np_, :], in0=f1[0:np_, :], in1=f2[0:np_, :], op=ALU.add),
                  r=['f1', 'f2'], w=['f1'])
            pg.op('dve', lambda e: e.tensor_tensor(out=f1[0:np_, :], in0=f1[0:np_, :], in1=sz[0:np_, t, :], op=ALU.mult),
                  r=['f1', 'szb'], w=['f1'])
            ssv = st8[0:np_, 4, 0:1]
            pg.op('act', lambda e: e.activation(out=sq[0:np_, 0:512], in_=f1[0:np_, :], func=AF.Square, accum_out=ssv),
                  r=['f1'], w=['sq', 'st8'])
            rstd_from_ss(ssv, ssv, 512, np_, 'st8')
            pg.op('dve', lambda e: e.scalar_tensor_tensor(out=oz[0:np_, t, :], in0=f1[0:np_, :], scalar=ssv,
                                                          in1=smv('b_norm', np_), op0=ALU.mult, op1=ALU.mult),
                  r=['f1', 'st8', 'small'], w=['oz'])
            pg.op('dve', lambda e: e.tensor_tensor(
                out=bdec[0:np_, :, :, :].rearrange("p a g n -> p g a n"),
                in0=btk[0:np_, :].rearrange("p (g n) -> p g n", n=64).unsqueeze(2).to_broadcast([np_, 2, 4, 64]),
                in1=ssd8[0:np_, 1, :].rearrange("p (g a) -> p g a", a=4).unsqueeze(3).to_broadcast([np_, 2, 4, 64]),
                op=ALU.mult), r=['btk', 'ssd8'], w=['bdec'])
            i = rot('A'); ph = psA[i]; phk = 'psA%d' % i
            for a4 in range(4):
                pg.op('pe', lambda e, a4=a4: e.matmul(
                    ph[:, a4 * 128:(a4 + 1) * 128], lhsT=bdec[0:np_, a4, :, :].rearrange("p g n -> p (g n)"),
                    rhs=xdt[0:np_, :, :].rearrange("p (g a) d -> p a g d", a=4)[:, a4, :, :],
                    start=True, stop=True), r=['bdec', 'xdt'], w=[phk], inc=(a4 == 3))
            phv = ph[:, :].rearrange("p (a c) -> p a c", c=128)
            for g in range(2):
                sl = slice(g * 64, (g + 1) * 64)
                pg.op('dve', lambda e, g=g, sl=sl: e.tensor_tensor(
                    out=hT[sl, :, :], in0=hT[sl, :, :],
                    in1=ssd8[sl, 2, g * 4:(g + 1) * 4].unsqueeze(2).to_broadcast([64, 4, 64]), op=ALU.mult),
                    r=['hT', 'ssd8', py2k, py3k], w=['hT'])
                pg.op('dve', lambda e, g=g, sl=sl: e.tensor_tensor(
                    out=hT[sl, :, :], in0=hT[sl, :, :], in1=phv[sl, :, g * 64:(g + 1) * 64], op=ALU.add),
                    r=['hT', phk], w=['hT'])
            pg.op('act', lambda e: e.activation(out=hTb[:, :, :], in_=hT[:, :, :], func=AF.Copy), r=['hT'], w=['hTb'])

        BS = [dict(Ah=Ah, Ee=Ee, Gm=Gm, bdec=bdec, xdt=xdt, xtk=xtk, Mh=Mh, btk=btk,
                   e0=ssd8[:, 0, :], e1=ssd8[:, 1, :], e2=ssd8[:, 2, :], sfx=''),
              dict(Ah=xnb.bitcast(F32).rearrange("p (h n) -> p h n", n=128),
                   Ee=f3.rearrange("p (h n) -> p h n", n=128),
                   Gm=biasQ[:, :, :].rearrange("p a b -> p (a b)").rearrange("p (g n) -> p g n", n=128),
                   Mh=qT[:, 0:2, :].rearrange("p a n -> p (a n)").rearrange("p (h n) -> p h n", n=128),
                   bdec=qT[:, 2, :].rearrange("p (a g n) -> p a g n", a=4, g=2),
                   xdt=qT[:, 3, :].rearrange("p (h d) -> p h d", d=64),
                   xtk=PT[0][:, :], btk=PT[1][:, 0:128],
                   e0=st8[:, 5, :], e1=st8[:, 6, :], e2=st8[:, 7, :], sfx='_1')]
        SET1K = ['Ah%d_1' % h for h in range(4)] + ['Ee_1', 'Gm_1', 'bdec_1', 'xdt_1', 'xtk_1', 'Mh_1', 'btk_1', 'ssd8_1',
                                                     'qT', 'PT0', 'PT1', 'biasQ', 'xnb', 'f3', 'st8lf', 'st8c', 'st8x']

        def ssdA(t, np_, S):
            b = BS[S]; x_ = b['sfx']
            K = lambda n: n + x_
            c0 = t * np_
            i = rot('T'); pt = psT[i]; pk = 'psT%d' % i
            for j in range(5):
                pg.op('pe', lambda e, j=j: e.transpose(out=pt[0:np_, j * 128:(j + 1) * 128], in_=xbc[:, j, c0:c0 + np_],
                                                       identity=ident_b[:, :]), r=['xbc', 'cstb'], w=[pk], inc=(j == 4))
            yield
            pg.op('act', lambda e: e.activation(out=b['xtk'][0:np_, :], in_=pt[0:np_, 0:512], func=AF.Copy), r=[pk], w=[K('xtk')])
            pg.op('act', lambda e: e.activation(out=b['btk'][0:np_, :], in_=pt[0:np_, 512:640], func=AF.Copy), r=[pk], w=[K('btk')])
            yield
            pg.op('dve', lambda e: e.tensor_tensor(out=b['xdt'][0:np_, :, :], in0=pt[0:np_, 0:512].rearrange("p (h d) -> p h d", d=64),
                                                   in1=dts[0:np_, t, :].unsqueeze(2).to_broadcast([np_, 8, 64]), op=ALU.mult),
                  r=[pk, 'dts'], w=[K('xdt')])
            yield
            a_ap = asb[0:np_, t, :]
            i = rot('A'); p = psA[i]; pk2 = 'psA%d' % i
            pg.op('pe', lambda e: e.matmul(p[0:np_, 0:8], lhsT=M1f[0:np_, 0:np_], rhs=a_ap, start=True, stop=True),
                  r=['asb', 'cst'], w=[pk2], inc=False)
            pg.op('pe', lambda e: e.matmul(p[0:np_, 8:16], lhsT=M2f[0:np_, 0:np_], rhs=a_ap, start=True, stop=True),
                  r=['asb', 'cst'], w=[pk2], inc=False)
            pg.op('pe', lambda e: e.matmul(p[:, 16:24], lhsT=M1f[0:np_, :], rhs=a_ap, start=True, stop=False),
                  r=['asb', 'cst'], w=[pk2], inc=False)
            pg.op('pe', lambda e: e.matmul(p[:, 16:24], lhsT=M2f[0:np_, :], rhs=a_ap, start=False, stop=True),
                  r=['asb', 'cst'], w=[pk2])
            yield
            pg.op('act', lambda e: e.activation(out=b['e0'][0:np_, :], in_=p[0:np_, 0:8], func=AF.Exp), r=[pk2], w=[K('ssd8')])
            pg.op('act', lambda e: e.activation(out=b['e1'][0:np_, :], in_=p[0:np_, 8:16], func=AF.Exp), r=[pk2], w=[K('ssd8')])
            pg.op('act', lambda e: e.activation(out=b['e2'][:, :], in_=p[:, 16:24], func=AF.Exp), r=[pk2], w=[K('ssd8')])
            yield
            for g in range(2):
                i = rot('A'); pgm = psA[i]; pk3 = 'psA%d' % i
                pg.op('pe', lambda e, g=g, pgm=pgm: e.matmul(pgm[0:np_, 0:np_], lhsT=xbc[g * 64:(g + 1) * 64, 4, c0:c0 + np_],
                                                    rhs=xbc[g * 64:(g + 1) * 64, 5, c0:c0 + np_], start=True, stop=True),
                      r=['xbc'], w=[pk3])
                pg.op('dve', lambda e, g=g, pgm=pgm: e.tensor_tensor(
                    out=b['Gm'][0:np_, g, 0:np_], in0=pgm[0:np_, 0:np_], in1=M1f[0:np_, 0:np_], op=ALU.mult),
                    r=[pk3, 'cst'], w=[K('Gm')])
                yield
            for hh in range(2):
                for h4 in range(4):
                    h = hh * 4 + h4
                    pg.op('dve', lambda e, h=h, h4=h4: e.tensor_scalar(
                        out=b['Ah'][0:np_, h4, 0:np_], in0=M2f[0:np_, 0:np_], scalar1=asb[0:np_, t, h:h + 1], scalar2=None,
                        op0=ALU.mult), r=['cst', 'asb'], w=[K('Ah%d' % h4)])
                    if h4 % 2:
                        yield
                psx = psS[hh]; psk = 'psS%d' % hh
                for h4 in range(4):
                    pg.op('pe', lambda e, h4=h4, psx=psx: e.matmul(
                        psx[0:np_, h4 * 128:h4 * 128 + np_], lhsT=b['Ah'][0:np_, h4, 0:np_], rhs=M1f[0:np_, 0:np_],
                        start=True, stop=True), r=[K('Ah%d' % h4), 'cst'], w=[psk], inc=(h4 == 3))
                yield
                pg.op('act', lambda e, psx=psx: e.activation(
                    out=b['Ee'][0:np_, :, 0:np_],
                    in_=psx[0:np_, :].rearrange("p (h n) -> p h n", n=128)[:, :, 0:np_], func=AF.Exp),
                    r=[psk], w=[K('Ee')])
                yield
                pg.op('dve', lambda e, hh=hh: e.tensor_tensor(
                    out=b['Mh'][0:np_, hh * 4:(hh + 1) * 4, 0:np_], in0=b['Ee'][0:np_, :, 0:np_],
                    in1=b['Gm'][0:np_, hh, 0:np_].unsqueeze(1).to_broadcast([np_, 4, np_]), op=ALU.mult),
                    r=[K('Ee'), K('Gm')], w=[K('Mh')])
                yield
            pg.op('dve', lambda e: e.tensor_tensor(
                out=b['bdec'][0:np_, :, :, :].rearrange("p a g n -> p g a n"),
                in0=b['btk'][0:np_, :].rearrange("p (g n) -> p g n", n=64).unsqueeze(2).to_broadcast([np_, 2, 4, 64]),
                in1=b['e1'][0:np_, :].rearrange("p (g a) -> p g a", a=4).unsqueeze(3).to_broadcast([np_, 2, 4, 64]),
                op=ALU.mult), r=[K('btk'), K('ssd8')], w=[K('bdec')])
            yield

        def ssdB(t, np_, S):
            b = BS[S]; x_ = b['sfx']
            K = lambda n: n + x_
            c0 = t * np_
            io = rot('O'); py = psO[io]; pyk = 'psO%d' % io
            io2 = rot('O'); py2 = psO[io2]; py2k = 'psO%d' % io2
            for h in range(8):
                pg.op('pe', lambda e, h=h: e.matmul(py[0:np_, h * 64:(h + 1) * 64], lhsT=b['Mh'][0:np_, h, 0:np_],
                                                    rhs=b['xdt'][0:np_, h, :], start=True, stop=True),
                      r=[K('Mh'), K('xdt')], w=[pyk], inc=(h == 7))
            yield
            i3 = rot('A'); py3 = psA[i3]; py3k = 'psA%d' % i3
            for h in range(8):
                g = h // 4
                dstp = (py2 if g == 0 else py3)
                pg.op('pe', lambda e, h=h, g=g, dstp=dstp: e.matmul(dstp[0:np_, (h % 4) * 64:(h % 4 + 1) * 64],
                                                         lhsT=xbc[g * 64:(g + 1) * 64, 5, c0:c0 + np_],
                                                         rhs=hTb[g * 64:(g + 1) * 64, h % 4, :], start=True, stop=True),
                      r=['xbc', 'hTb'], w=[py2k if g == 0 else py3k], inc=(h % 4 == 3))
            yield
            yv = f1[0:np_, :].rearrange("p (h d) -> p h d", d=64)
            for g, (dstp, dk) in enumerate(((py2, py2k), (py3, py3k))):
                pg.op('dve', lambda e, g=g, dstp=dstp: e.tensor_tensor(
                    out=yv[:, g * 4:(g + 1) * 4, :], in0=dstp[0:np_, 0:256].rearrange("p (h d) -> p h d", d=64),
                    in1=b['e0'][0:np_, g * 4:(g + 1) * 4].unsqueeze(2).to_broadcast([np_, 4, 64]), op=ALU.mult),
                    r=[dk, K('ssd8')], w=['f1'])
                yield
            pg.op('dve', lambda e: e.tensor_tensor(out=f1[0:np_, :], in0=f1[0:np_, :], in1=py[0:np_, :], op=ALU.add),
                  r=['f1', pyk], w=['f1'])
            pg.op('pool', lambda e: e.tensor_tensor(out=f2[0:np_, :].rearrange("p (h d) -> p h d", d=64),
                                                    in0=b['xtk'][0:np_, :].rearrange("p (h d) -> p h d", d=64),
                                                    in1=smv('b_d', np_).unsqueeze(2).to_broadcast([np_, 8, 64]), op=ALU.mult),
                  r=[K('xtk'), 'small'], w=['f2'])
            yield
            pg.op('dve', lambda e: e.tensor_tensor(out=f1[0:np_, :], in0=f1[0:np_, :], in1=f2[0:np_, :], op=ALU.add),
                  r=['f1', 'f2'], w=['f1'])
            yield
            pg.op('dve', lambda e: e.tensor_tensor(out=f1[0:np_, :], in0=f1[0:np_, :], in1=sz[0:np_, t, :], op=ALU.mult),
                  r=['f1', 'szb'], w=['f1'])
            yield
            ssv = st8[0:np_, 4, 0:1]
            pg.op('act', lambda e: e.activation(out=sq[0:np_, 0:512], in_=f1[0:np_, :], func=AF.Square, accum_out=ssv),
                  r=['f1'], w=['sq', 'st8'])
            yield
            pg.op('dve', lambda e: e.tensor_scalar(out=ssv, in0=ssv, scalar1=1.0 / 512, scalar2=EPS,
                                                    op0=ALU.mult, op1=ALU.add), r=['st8'], w=['st8'])
            yield
            pg.op('pool', lambda e: e.tensor_tensor(out=ssv, in0=ssv, in1=nhalf[0:np_, 0:1], op=ALU.pow),
                  r=['st8', 'epst'], w=['st8'])
            yield
            pg.op('dve', lambda e: e.scalar_tensor_tensor(out=oz[0:np_, t, :], in0=f1[0:np_, :], scalar=ssv,
                                                          in1=smv('b_norm', np_), op0=ALU.mult, op1=ALU.mult),
                  r=['f1', 'st8', 'small'], w=['oz'])
            yield
            i = rot('A'); ph = psA[i]; phk = 'psA%d' % i
            for a4 in range(4):
                pg.op('pe', lambda e, a4=a4: e.matmul(
                    ph[:, a4 * 128:(a4 + 1) * 128], lhsT=b['bdec'][0:np_, a4, :, :].rearrange("p g n -> p (g n)"),
                    rhs=b['xdt'][0:np_, :, :].rearrange("p (g a) d -> p a g d", a=4)[:, a4, :, :],
                    start=True, stop=True), r=[K('bdec'), K('xdt')], w=[phk], inc=(a4 == 3))
            yield
            phv = ph[:, :].rearrange("p (a c) -> p a c", c=128)
            for g in range(2):
                sl = slice(g * 64, (g + 1) * 64)
                pg.op('dve', lambda e, g=g, sl=sl: e.tensor_tensor(
                    out=hT[sl, :, :], in0=hT[sl, :, :],
                    in1=b['e2'][sl, g * 4:(g + 1) * 4].unsqueeze(2).to_broadcast([64, 4, 64]), op=ALU.mult),
                    r=['hT', K('ssd8'), py2k, py3k], w=['hT'])
                yield
                pg.op('dve', lambda e, g=g, sl=sl: e.tensor_tensor(
                    out=hT[sl, :, :], in0=hT[sl, :, :], in1=phv[sl, :, g * 64:(g + 1) * 64], op=ALU.add),
                    r=['hT', phk], w=['hT'])
                yield
            pg.op('act', lambda e: e.activation(out=hTb[:, :, :], in_=hT[:, :, :], func=AF.Copy), r=['hT'], w=['hTb'])
            yield

        def ssd_pipelined(NT, np_):
            barrier(AEK + SET1K)
            for _ in ssdA(0, np_, 0):
                pass
            for t in range(NT):
                gens = [ssdB(t, np_, t % 2)]
                if t + 1 < NT:
                    gens.append(ssdA(t + 1, np_, (t + 1) % 2))
                while gens:
                    for g_ in list(gens):
                        try:
                            next(g_)
                        except StopIteration:
                            gens.remove(g_)
            barrier(AEK + SET1K)

        def process_group(l, kind, Q):
            samp = (kind == 's')
            np_ = 64 if samp else 128
            NT = 4
            NTOK = NT * np_
            xsrc = (D['xs'] if samp else D['xp']) if l == 0 else (x1s if samp else x1p)
            xdst = (D['ys'] if samp else D['yp']) if l == NL - 1 else (x1s if samp else x1p)
            row0 = 0 if samp else Q * 512

            def rows(t):
                return slice(row0 + t * np_, row0 + (t + 1) * np_)

            norm_tiles(lambda t: xsrc[rows(t), :], NT, np_, lambda n: smv('g_norm', n),
                       (lambda t: [('xd', kind, row0 + t * np_)]) if l > 0 else None)

            if cfg.get('marks'): print('MARK', kind, Q, 'C_proj', pg.nops)
            wt, wk = load_w(D['w_in'], l, COL['c_q'], 512)
            for t, p, pk in pipe_tiles(NT, lambda t: proj_tm(xnT, 'xnT', t * np_, np_, wt, wk, 512)):
                evac_q(p, pk, t, np_, 8, 'c_qnorm')
            wt, wk = load_w(D['w_in'], l, COL['c_k'], 512)
            if samp:
                kTs = sb_kTs
            for t, p, pk in pipe_tiles(NT, lambda t: proj_tm(xnT, 'xnT', t * np_, np_, wt, wk, 512)):
                if samp:
                    evac_k(p, pk, np_, 8, 'c_knorm', D['sc_k'][l, rows(t), :],
                           lambda t=t: kTs[:, 0:4, t * 64:(t + 1) * 64], 'kTs')
                else:
                    gt = Q * 4 + t
                    evac_k(p, pk, np_, 8, 'c_knorm', D['pc_k'][l, rows(t), :],
                           lambda gt=gt: kT_c[:, :, gt * 128:(gt + 1) * 128], 'kT_c')
            wt, wk = load_w(D['w_in'], l, COL['c_v'], 512)
            for t, p, pk in pipe_tiles(NT, lambda t: proj_tm(xnT, 'xnT', t * np_, np_, wt, wk, 512)):
                if samp:
                    evac_v(p, pk, np_, 8, D['sc_v'][l, rows(t), :], vas[0:np_, t, :, 0:64], 'vas')
                else:
                    evac_v(p, pk, np_, 8, D['pc_v'][l, rows(t), :], va_c[:, Q * 4 + t, :, 0:64], 'va_c')
            wt, wk = load_w(D['w_in'], l, COL['c_f'], 8)
            if not samp:
                psv, pk = proj8_all(NT, np_, wt, wk)
                softplus4(psv, pk, smv('c_fbias', np_), lfs[0:np_, :, :], 'lfs', np_, NT, neg=True)
                out_tokens.append(pg.dma('sp', D['pc_f'][l, row0:row0 + NT * np_, :].rearrange("(t p) h -> p t h", p=np_),
                                         lfs[0:np_, :, :], r=['lfs'], nonc=True))
                for t in range(NT):
                    gt = Q * 4 + t
                    cumsum_tile(lfs[0:np_, t, :], 'lfs', 128, cum[:, gt, :], 'cum',
                                None if gt == 0 else (S127f, cum[:, gt - 1, :]), ['cum'])
            for t, p, pk in (pipe_tiles(NT, lambda t: proj_tm(xnT, 'xnT', t * np_, np_, wt, wk, 8)) if samp else ()):
                lf = st8[0:np_, 5, :]
                softplus_from(p, pk, smv('c_fbias', np_), lf, 'st8lf', np_, neg=True)
                if samp:
                    out_tokens.append(pg.dma('sp', D['sc_f'][l, rows(t), :], lf, r=['st8lf'], nonc=True))
                    carry = None
                    for kb in range(8):
                        pg.dma('sp', st8[:, 7, :], D['cc_f'][l, t, kb * 128:(kb + 1) * 128, :], w=['st8x'], nonc=True)
                        cumsum_tile(st8[:, 7, :], 'st8x', 128, cums[:, t, kb, :], 'cums',
                                    None if kb == 0 else (S127f, cums[:, t, kb - 1, :]), ['cums'])
                    cumsum_tile(lf, 'st8lf', 64, cums[0:64, t, 8, :], 'cums', (S127f[:, 0:64], cums[:, t, 7, :]), ['cums'])
                else:
                    gt = Q * 4 + t
                    out_tokens.append(pg.dma('sp', D['pc_f'][l, rows(t), :], lf, r=['st8lf'], nonc=True))
                    cumsum_tile(lf, 'st8lf', 128, cum[:, gt, :], 'cum',
                                None if gt == 0 else (S127f, cum[:, gt - 1, :]), ['cum'])
            wt, wk = load_w(D['w_in'], l, COL['c_z'], 512)
            for t, p, pk in pipe_tiles(NT, lambda t: proj_tm(xnT, 'xnT', t * np_, np_, wt, wk, 512)):
                pg.op('act', lambda e, p=p, t=t: e.activation(out=sz[0:np_, t, :], in_=p, func=AF.Silu), r=[pk], w=['sz'])
            if cfg.get('marks'): print('MARK', kind, Q, 'C_attn', pg.nops)
            if not samp:
                nkb = 4 * Q + 4
                i = rot('A'); pb = psA[i]; pbk = 'psA%d' % i
                pg.op('pe', lambda e: e.matmul(pb[:, 0:8], lhsT=S127f, rhs=cum[:, nkb - 1, :], start=True, stop=True),
                      r=['cum', 'cst'], w=[pbk])
                pg.op('act', lambda e: e.activation(out=st8[:, 6, :], in_=pb[:, 0:8], func=AF.Copy), r=[pbk], w=['st8c'])
                pg.op('dve', lambda e: e.tensor_tensor(out=biasQ[:, 0:nkb, :],
                                                       in0=st8[:, 6, :].unsqueeze(1).to_broadcast([128, nkb, 8]),
                                                       in1=cum[:, 0:nkb, :], op=ALU.subtract), r=['st8c', 'cum'], w=['biasQ'])
                for h in range(8):
                    hp, hb = h // 2, (h % 2) * 64
                    kbs = []
                    for kb in range(nkb):
                        dI = kb - 4 * Q
                        d = dict(kT=kT_c[hb:hb + 64, hp, kb * 128:(kb + 1) * 128], v=va_c[:, kb, h, 0:65], nk=128,
                                 bias=biasQ[:, kb, h:h + 1], bkeys=['biasQ'], q0=max(dI, 0) * 128, keys=['kT_c', 'va_c'])
                        kbs.append(d)
                    kb2 = []
                    for kb, d in enumerate(kbs):
                        dI = kb - 4 * Q
                        if dI < 0:
                            kb2.append(d)
                        else:
                            d['diag'] = True
                            kb2.append(d)
                    attend_fox(qT[hb:hb + 64, hp, :], 512, [(j * 128, 128) for j in range(4)], kb2, ['qT'],
                               attn_out(h * 64, 4, 128))
            else:
                for t in range(NT):
                    i = rot('A'); pb = psA[i]; pbk = 'psA%d' % i
                    pg.op('pe', lambda e, t=t: e.matmul(pb[:, 0:8], lhsT=S63f[0:64, :], rhs=cums[0:64, t, 8, :],
                                                        start=True, stop=True), r=['cums', 'cst'], w=[pbk])
                    pg.op('act', lambda e: e.activation(out=st8[:, 6, :], in_=pb[:, 0:8], func=AF.Copy), r=[pbk], w=['st8c'])
                    pg.op('dve', lambda e, t=t: e.tensor_tensor(out=biasQ[:, 0:9, :],
                                                                in0=st8[:, 6, :].unsqueeze(1).to_broadcast([128, 9, 8]),
                                                                in1=cums[:, t, :, :], op=ALU.subtract),
                          r=['st8c', 'cums'], w=['biasQ'])
                    load_cache_kv(D['cc_k'][l, t], D['cc_v'][l, t], 8, 8)
                    for h in range(8):
                        hp, hb = h // 2, (h % 2) * 64
                        kbs = []
                        for kb in range(8):
                            kbs.append(dict(kT=ckT[hb:hb + 64, hp, kb * 128:(kb + 1) * 128], v=cva_[:, kb, h, 0:65], nk=128,
                                            bias=biasQ[:, kb, h:h + 1], bkeys=['biasQ'], q0=0, keys=['ckT', 'cvaS']))
                        kbs.append(dict(kT=kTs[hb:hb + 64, hp, t * 64:(t + 1) * 64], v=vas[0:64, t, h, 0:65], nk=64,
                                        bias=biasQ[0:64, 8, h:h + 1], bkeys=['biasQ'], q0=0, keys=['kTs', 'vas'], diag=True))
                        attend_fox(qT[hb:hb + 64, hp, t * 64:(t + 1) * 64], 64, [(0, 64)], kbs, ['qT'],
                                   attn_out_s(h * 64, t))
            oz_to_T('c', NT, np_, 4)

            if cfg.get('marks'): print('MARK', kind, Q, 'A_proj', pg.nops)
            wt, wk = load_w(D['w_in'], l, COL['a_q'], 512)
            for t, p, pk in pipe_tiles(NT, lambda t: proj_tm(xnT, 'xnT', t * np_, np_, wt, wk, 512)):
                evac_q(p, pk, t, np_, 8, 'a_qnorm')
            wt, wk = load_w(D['w_in'], l, COL['a_k'], 512)
            for t, p, pk in pipe_tiles(NT, lambda t: proj_tm(xnT, 'xnT', t * np_, np_, wt, wk, 512)):
                if samp:
                    evac_k(p, pk, np_, 8, 'a_knorm', D['sa_k'][l, rows(t), :],
                           lambda t=t: kTs[:, 0:4, t * 64:(t + 1) * 64], 'kTs')
                else:
                    gt = Q * 4 + t
                    od = D['pa_k'][l, (gt - 28) * 128:(gt - 27) * 128, :] if gt >= 28 else None
                    evac_k(p, pk, np_, 8, 'a_knorm', od,
                           lambda gt=gt: kT_a[:, :, (gt % 8) * 128:(gt % 8 + 1) * 128], 'kT_a')
            wt, wk = load_w(D['w_in'], l, COL['a_v'], 512)
            for t, p, pk in pipe_tiles(NT, lambda t: proj_tm(xnT, 'xnT', t * np_, np_, wt, wk, 512)):
                if samp:
                    evac_v(p, pk, np_, 8, D['sa_v'][l, rows(t), :], vas[0:np_, t, :, 0:64], 'vas')
                else:
                    gt = Q * 4 + t
                    od = D['pa_v'][l, (gt - 28) * 128:(gt - 27) * 128, :] if gt >= 28 else None
                    evac_v(p, pk, np_, 8, od, va_a[:, gt % 8, :, 0:64], 'va_a')
            wt, wk = load_w(D['w_in'], l, COL['a_z'], 512)
            for t, p, pk in pipe_tiles(NT, lambda t: proj_tm(xnT, 'xnT', t * np_, np_, wt, wk, 512)):
                pg.op('act', lambda e, p=p, t=t: e.activation(out=sz[0:np_, t, :], in_=p, func=AF.Silu), r=[pk], w=['sz'])
            for t in range(NT):
                gt = Q * 4 + t
                if samp:
                    load_cache_kv(D['ca_k'][l, t], D['ca_v'][l, t], 4, 8)
                for hq in range(2):
                    acc = None
                    for h4 in range(4):
                        h = hq * 4 + h4
                        hp, hb = h // 2, (h % 2) * 64
                        kbs = []
                        if not samp:
                            for i5 in range(5):
                                gk = gt - 4 + i5
                                if gk < 0:
                                    continue
                                s8 = gk % 8
                                d_ = dict(kT=kT_a[hb:hb + 64, hp, s8 * 128:(s8 + 1) * 128], v=va_a[:, s8, h, 0:65], nk=128,
                                          bias=None, q0=0, keys=['kT_a', 'va_a'])
                                if i5 in (1, 2):
                                    d_['bias'] = expB[:, h, i5 * 128:i5 * 128 + 1]; d_['bkeys'] = ['expB']
                                else:
                                    d_['badd'] = expB[:, h, i5 * 128:(i5 + 1) * 128]; d_['mkeys'] = ['expB']
                                kbs.append(d_)
                        else:
                            for kb in range(4):
                                d_ = dict(kT=ckT[hb:hb + 64, hp, kb * 128:(kb + 1) * 128], v=cva_[:, kb, h, 0:65], nk=128,
                                          bias=None, q0=0, keys=['ckT', 'cvaS'])
                                if kb in (0, 1, 2):
                                    d_['bias'] = expB[:, h, kb * 128:kb * 128 + 1]; d_['bkeys'] = ['expB']
                                else:
                                    d_['badd'] = expB[:, h, kb * 128:kb * 128 + 64]; d_['mkeys'] = ['expB']
                                kbs.append(d_)
                            kbs.append(dict(kT=kTs[hb:hb + 64, hp, t * 64:(t + 1) * 64], v=vas[0:64, t, h, 0:65], nk=64,
                                            bias=None, badd=expB[:, h, 512:576], mkeys=['expB'], q0=0, keys=['kTs', 'vas']))
                        a2 = None if acc is None else (acc[0], acc[1], h4, False)
                        acc = attend(qT[hb:hb + 64, hp, t * np_:(t + 1) * np_], np_, [(0, np_)], kbs, ['qT'], None, acc=a2)
                    pov, pok = acc
                    rl = st8[0:np_, 2, 0:4]
                    pg.op('dve', lambda e, pov=pov: e.reciprocal(out=rl, in_=pov[0:np_, 0:4, 64]), r=[pok], w=['st8'])
                    tmp = f2[0:np_, 0:256].rearrange("p (j d) -> p j d", d=64)
                    pg.op('dve', lambda e, pov=pov, tmp=tmp: e.tensor_tensor(out=tmp, in0=pov[0:np_, 0:4, 0:64],
                                                                           in1=rl.unsqueeze(2).to_broadcast([np_, 4, 64]), op=ALU.mult),
                          r=[pok, 'st8'], w=['f2'])
                    pg.op('pool', lambda e, tmp=tmp, hq=hq, t=t: e.tensor_tensor(
                        out=oz[0:np_, t, hq * 256:(hq + 1) * 256].rearrange("p (j d) -> p j d", d=64), in0=tmp,
                        in1=sz[0:np_, t, hq * 256:(hq + 1) * 256].rearrange("p (j d) -> p j d", d=64), op=ALU.mult),
                        r=['f2', 'sz'], w=['oz'])
            oz_to_T('a', NT, np_, 4)

            if cfg.get('marks'): print('MARK', kind, Q, 'M', pg.nops)
            wt, wk = load_w(D['w_in'], l, COL['m_q'], 512)
            for t, p, pk in pipe_tiles(NT, lambda t: proj_tm(xnT, 'xnT', t * np_, np_, wt, wk, 512)):
                pg.op('act', lambda e, p=p, t=t: e.activation(out=sz[0:np_, t, 0:256], in_=p[:, 256:512], func=AF.Silu),
                      r=[pk], w=['sz'])
                evac_q(p[:, 0:256], pk, t, np_, 4, 'm_qnorm')
            if not samp:
                for h in range(4):
                    hp, hb = h // 2, (h % 2) * 64
                    kbs = [dict(kT=kT_m[hb:hb + 64, hp, kb * 128:(kb + 1) * 128], v=va_m[:, kb, h, 0:65], nk=128, bias=None,
                                q0=0, keys=['kT_m', 'va_m']) for kb in range(2)]
                    attend(qT[hb:hb + 64, hp, :], 512, [(j * 128, 128) for j in range(4)], kbs, ['qT'],
                           attn_out(h * 64, 4, 128))
            else:
                for t in range(NT):
                    load_cache_kv(D['cm_k'][l, t], D['cm_v'][l, t], 2, 4)
                    for h in range(4):
                        hp, hb = h // 2, (h % 2) * 64
                        kbs = [dict(kT=ckT[hb:hb + 64, hp, kb * 128:(kb + 1) * 128], v=cva_[:, kb, h, 0:65], nk=128, bias=None,
                                    q0=0, keys=['ckT', 'cvaS']) for kb in range(2)]
                        attend(qT[hb:hb + 64, hp, t * 64:(t + 1) * 64], 64, [(0, 64)], kbs, ['qT'], attn_out_s(h * 64, t))
            oz_to_T('m', NT, np_, 2)

            if cfg.get('marks'): print('MARK', kind, Q, 'B_proj', pg.nops)
            wt, wk = load_w(D['w_in'], l, COL['b_z'], 512)
            for t, p, pk in pipe_tiles(NT, lambda t: proj_tm(xnT, 'xnT', t * np_, np_, wt, wk, 512)):
                pg.op('act', lambda e, p=p, t=t: e.activation(out=sz[0:np_, t, :], in_=p, func=AF.Silu), r=[pk], w=['sz', 'szb'])
            wt, wk = load_w(D['w_in'], l, COL['b_dt'], 8)
            psv, pk = proj8_all(NT, np_, wt, wk)
            softplus4(psv, pk, smv('b_dt_bias', np_), dts[0:np_, :, :], 'dts', np_, NT)
            pg.op('dve', lambda e: e.tensor_tensor(out=asb[0:np_, :, :], in0=dts[0:np_, :, :],
                                                   in1=aneg[0:np_, :].unsqueeze(1).to_broadcast([np_, NT, 8]), op=ALU.mult),
                  r=['dts', 'aneg'], w=['asb'])
            for half in range(2):
                js = slice(half * 3, half * 3 + 3)
                if samp:
                    for t in range(NT):
                        for jj in range(3):
                            j = half * 3 + jj
                            pg.dma('sp', raw[:, jj, t, 0:3],
                                   D['sb_conv'][l, t][:, j * 128:(j + 1) * 128].rearrange("r p -> p r"), w=['raw'], nonc=True)
                else:
                    pg.op('pool', lambda e, js=js: cp(e, out=raw[:, :, 0, 0:3], in_=halo[:, js, :]), r=['halo'], w=['raw'])
                wt, wk = load_w(D['w_in'], l, COL['b_xbc'] + half * 384, 384)
                for jj in range(3):
                    i = rot('A'); p = psA[i]; pk = 'psA%d' % i
                    for kc in range(8):
                        pg.op('pe', lambda e, kc=kc, jj=jj, p=p, wt=wt: e.matmul(p[:, 0:NTOK], lhsT=wt[:, kc, jj * 128:(jj + 1) * 128],
                                                                     rhs=xnT[:, kc, 0:NTOK], start=(kc == 0), stop=(kc == 7)),
                              r=['xnT'] + list(wk), w=[pk], inc=(kc == 7))
                    pg.op('act', lambda e, jj=jj, p=p: e.activation(out=raw[:, jj, :, 3:3 + np_],
                                                             in_=p[:, 0:NTOK].rearrange("p (t n) -> p t n", n=np_), func=AF.Copy),
                          r=[pk], w=['raw'])
                if not samp:
                    for t in range(1, NT):
                        pg.op('pool', lambda e, t=t: cp(e, out=raw[:, :, t, 0:3], in_=raw[:, :, t - 1, np_:np_ + 3]),
                              r=['raw'], w=['raw'])
                    pg.op('pool', lambda e, js=js: cp(e, out=halo[:, js, :], in_=raw[:, :, NT - 1, np_:np_ + 3]),
                          r=['raw'], w=['halo'])
                    if Q == NQ - 1:
                        for jj in range(3):
                            j = half * 3 + jj
                            out_tokens.append(pg.dma('sp', D['pb_c'][l][:, j * 128:(j + 1) * 128].rearrange("r p -> p r"),
                                                     halo[:, j, :], r=['halo'], nonc=True))
                else:
                    for t in range(NT):
                        for jj in range(3):
                            j = half * 3 + jj
                            out_tokens.append(pg.dma('sp', D['sb_c'][l, t][:, j * 128:(j + 1) * 128].rearrange("r p -> p r"),
                                                     raw[:, jj, t, np_:np_ + 3], r=['raw'], nonc=True))
                for jj in range(3):
                    j = half * 3 + jj
                    cv = cva[:, 0:NTOK].rearrange("p (t n) -> p t n", n=np_)
                    pg.op('dve', lambda e, j=j, jj=jj, cv=cv: e.tensor_scalar(out=cv, in0=raw[:, jj, :, 0:np_], scalar1=cw[:, j, 0:1],
                                                                scalar2=cw[:, j, 4:5], op0=ALU.mult, op1=ALU.add),
                          r=['raw', 'cw'], w=['cva'])
                    for tap in range(1, 4):
                        pg.op('dve', lambda e, j=j, jj=jj, tap=tap, cv=cv: e.scalar_tensor_tensor(
                            out=cv, in0=raw[:, jj, :, tap:tap + np_], scalar=cw[:, j, tap:tap + 1], in1=cv,
                            op0=ALU.mult, op1=ALU.add), r=['raw', 'cw', 'cva'], w=['cva'])
                    pg.op('act', lambda e, j=j: e.activation(out=xbc[:, j, 0:NTOK], in_=cva[:, 0:NTOK], func=AF.Silu),
                          r=['cva'], w=['xbc'])
            if samp:
                for t in range(NT):
                    if not NOSTATE:
                        state_load(D['sb_ssm'][l, t])
                    ssd_tile(t, np_, NT)
                    if not NOSTATE:
                        state_store(D['sb_s'][l, t])
            else:
                ssd_pipelined(NT, np_)
            if (not samp) and Q == NQ - 1 and not NOSTATE:
                state_store(D['pb_s'][l])
            oz_to_T('b', NT, np_, 4)

            if cfg.get('marks'): print('MARK', kind, Q, 'merge', pg.nops)
            barrier(AEK)
            brs = (('a', 'w_pa', 4), ('b', 'w_pb', 4), ('c', 'w_pc', 4), ('m', 'w_pm', 2))
            for c in range(2):
                for bi, (br, wn, nkc) in enumerate(brs):
                    wg, wgk = load_w(D['w_in'], l, COL['gate'] + bi * 1024 + c * 512, 512)
                    wp, wpk = load_w(D[wn], l, c * 512, 512, nk=nkc)
                    def mproj(t, wg=wg, wgk=wgk, wp=wp, wpk=wpk, br=br, nkc=nkc):
                        p1, pk1 = proj_tm(xnT, 'xnT', t * np_, np_, wg, wgk, 512)
                        isx = rot('S'); p2 = psS[isx]; pk2 = 'psS%d' % isx
                        for kc in range(nkc):
                            pg.op('pe', lambda e, kc=kc: e.matmul(
                                p2[0:np_, :], lhsT=ozT[br][:, kc, t * np_:(t + 1) * np_], rhs=wp[:, kc, :],
                                start=(kc == 0), stop=(kc == nkc - 1)), r=['ozT_' + br] + list(wpk), w=[pk2], inc=(kc == nkc - 1))
                        return p1, pk1, p2, pk2
                    for t, p1, pk1, p2, pk2 in pipe_tiles(NT, mproj):
                        pg.op('act', lambda e, p1=p1: e.activation(out=sig[0:np_, :], in_=p1, func=AF.Sigmoid), r=[pk1], w=['cva'])
                        if bi == 0:
                            pg.op('dve', lambda e, t=t, p2=p2: e.tensor_tensor(out=macc[0:np_, t, :], in0=sig[0:np_, :],
                                                                               in1=p2[0:np_, :], op=ALU.mult),
                                  r=['cva', pk2], w=['macc%d' % t])
                        else:
                            pg.op('dve', lambda e, t=t, p2=p2: e.tensor_tensor(out=f1[0:np_, :], in0=sig[0:np_, :],
                                                                               in1=p2[0:np_, :], op=ALU.mult),
                                  r=['cva', pk2], w=['f1'])
                            pg.op('pool', lambda e, t=t: e.tensor_tensor(out=macc[0:np_, t, :], in0=macc[0:np_, t, :],
                                                                         in1=f1[0:np_, :], op=ALU.add),
                                  r=['f1', 'macc%d' % t], w=['macc%d' % t])
                for t in range(NT):
                    pg.op('act', lambda e, t=t: e.activation(out=b1[0:np_, :], in_=macc[0:np_, t, :], func=AF.Copy),
                          r=['macc%d' % t], w=['b1'])
                    transposes(lambda j: b1[0:np_, j * 128:(j + 1) * 128], 4, np_,
                               lambda e, v, t=t, c=c: cp(e, out=mT[:, c * 4:(c + 1) * 4, t * np_:(t + 1) * np_], in_=v),
                               ['b1'], ['xbc', 'Mh'], evac='dve')
            wos = [load_w(D['w_out'], l, c * 512, 512) for c in range(2)]
            for t in range(NT):
                i = rot('X'); xk = 'xt%d' % i
                pg.dma('sp', xt[0:np_, i, :], xsrc[rows(t), :], r=[('xd', kind, row0 + t * np_)] if l > 0 else [], w=[xk])
                for c in range(2):
                    wo, wok = wos[c]
                    p, pk = proj_tm(mT, 'xbc', t * np_, np_, wo, wok, 512, xkeys=['Mh'])
                    pg.op('dve', lambda e, i=i, p=p, c=c: e.tensor_tensor(out=xt[0:np_, i, c * 512:(c + 1) * 512],
                                                                          in0=xt[0:np_, i, c * 512:(c + 1) * 512], in1=p,
                                                                          op=ALU.add), r=[xk, pk], w=[xk])
                tok = pg.dma('sp', xdst[rows(t), :], xt[0:np_, i, :], r=[xk], w=[('xd', kind, row0 + t * np_)])
                out_tokens.append(tok)

        vflat = va_c[:, :, :, :].rearrange("p a h c -> p (a h c)")
        sb_kTs = vflat[:, 0:1024].rearrange("p (k n) -> p k n", n=256)
        vas = vflat[:, 1024:1024 + 2112].rearrange("p (t h c) -> p t h c", t=4, h=8)
        ckT = vflat[:, 3136:3136 + 4096].rearrange("p (k n) -> p k n", n=1024)
        cva_ = vflat[:, 7232:7232 + 4224].rearrange("p (a h c) -> p a h c", a=8, h=8)
        cums = kT_a[:, 0, 0:576].bitcast(F32).rearrange("p (t k h) -> p t k h", t=4, k=9)
        SKEYS = ['kTs', 'vas', 'ckT', 'cvaS']

        def load_cache_kv(kd, vd, nblk, nh):
            n = nh * 64
            for kb in range(nblk):
                stg, sk = ((b1, 'b1'), (xnb, 'xnb'))[kb % 2]
                pg.dma('pool', stg[:, 0:n], kd[kb * 128:(kb + 1) * 128, :], w=[sk])
                transposes(lambda j, stg=stg: stg[:, j * 128:(j + 1) * 128], nh // 2, 128,
                           lambda e, v, kb=kb: cp(e, out=ckT[:, 0:nh // 2, kb * 128:(kb + 1) * 128], in_=v),
                           [sk], ['ckT'], evac='act' if kb % 2 else 'dve')
                pg.dma('pool', cva_[:, kb, 0:nh, 0:64], vd[kb * 128:(kb + 1) * 128, :].rearrange("p (h d) -> p h d", d=64),
                       w=['cvaS'])

        def state_load(src):
            pg.dma('sp', f3[0:64, :].rearrange("p (h n) -> p h n", n=64), src.rearrange("h p n -> p h n"), w=['f3'])
            i = rot('A'); p = psA[i]; pk = 'psA%d' % i
            for h in range(8):
                pg.op('pe', lambda e, h=h: e.matmul(p[0:64, h * 64:(h + 1) * 64], lhsT=f3[0:64, h * 64:(h + 1) * 64],
                                                    rhs=identf[0:64, :], start=True, stop=True),
                      r=['f3', 'identf'], w=[pk], inc=(h == 7))
            for g in range(2):
                pg.op('act' if g else 'dve', lambda e, g=g: cp(
                    e, out=hT[g * 64:(g + 1) * 64, :, :], in_=p[0:64, g * 256:(g + 1) * 256].rearrange("p (a d) -> p a d", d=64)),
                    r=[pk], w=['hT'])
            pg.op('act', lambda e: e.activation(out=hTb[:, :, :], in_=hT[:, :, :], func=AF.Copy), r=['hT'], w=['hTb'])

        def state_store(dst):
            for g in range(2):
                i = rot('A'); p = psA[i]; pk = 'psA%d' % i
                for a in range(4):
                    pg.op('pe', lambda e, g=g, a=a, p=p: e.matmul(p[0:64, a * 64:(a + 1) * 64], lhsT=hT[g * 64:(g + 1) * 64, a, :],
                                                             rhs=identf[g * 64:(g + 1) * 64, :], start=True, stop=True),
                          r=['hT', 'identf'], w=[pk], inc=(a == 3))
                pg.op('act' if g else 'dve', lambda e, g=g, p=p: cp(e, out=f3[0:64, g * 256:(g + 1) * 256], in_=p[0:64, 0:256]),
                      r=[pk], w=['f3'])
            out_tokens.append(pg.dma('sp', dst.rearrange("h p n -> p h n"), f3[0:64, :].rearrange("p (h n) -> p h n", n=64),
                                     r=['f3']))

        def attn_out_t(h0col, t, np_):
            def fn(pov, pok):
                rl = st8[0:np_, 2, 0:1]
                pg.op('dve', lambda e: e.reciprocal(out=rl, in_=pov[0:np_, 0, 64:65]), r=[pok], w=['st8'])
                pg.op('dve', lambda e: e.scalar_tensor_tensor(out=oz[0:np_, t, h0col:h0col + 64], in0=pov[0:np_, 0, 0:64],
                                                              scalar=rl, in1=sz[0:np_, t, h0col:h0col + 64],
                                                              op0=ALU.mult, op1=ALU.mult), r=[pok, 'st8', 'sz'], w=['oz'])
            return fn

        def attn_out_s(h0col, t):
            return attn_out_t(h0col, t, 64)

        def attend_fox(qT_ap, NQc, qtiles, kblocks, qkeys, out_fn):
            for kb in kblocks:
                if kb.get('diag'):
                    kb['diagmask'] = True
            attend(qT_ap, NQc, qtiles, kblocks, qkeys, out_fn)

        def memory_kv(l):
            def src(t):
                return D['memp'][t * 128:(t + 1) * 128, :]
            barrier(AEK)
            pg.dma('sp', AE[:, 0:1024], D['mnorm'][l:l + 1, :].broadcast_to([128, 1024]), w=AEK)
            norm_tiles(src, 2, 128, lambda n: AE[0:n, 0:1024], lambda t: AEK)
            barrier(AEK)
            wt, wk = load_w(D['w_mkv'], l, 0, 512)
            for t, p, pk in pipe_tiles(2, lambda t: proj_tm(xnT, 'xnT', t * 128, 128, wt, wk, 512)):
                evac_v(p[:, 256:512], pk, 128, 4, D['pm_v'][l, t * 128:(t + 1) * 128, :], va_m[:, t, :, 0:64], 'va_m')
                evac_k(p[:, 0:256], pk, 128, 4, 'm_knorm', D['pm_k'][l, t * 128:(t + 1) * 128, :],
                       lambda t=t: kT_m[:, :, t * 128:(t + 1) * 128], 'kT_m')

        try:
          for l in range(NL):
            layer_consts(l)
            memory_kv(l)
            pg.op('pool', lambda e: e.memset(hT[:], 0.0), w=['hT'])
            pg.op('pool', lambda e: e.memset(hTb[:], 0.0), w=['hTb'])
            pg.op('pool', lambda e: e.memset(halo[:], 0.0), w=['halo'])
            for Q in range(NQ):
                process_group(l, 'p', Q)
            if DO_S:
                barrier(['va_c', 'kT_a', 'cums'] + SKEYS)
                pg.op('pool', lambda e: e.memset(vas, 1.0), w=['vas'])
                pg.op('pool', lambda e: e.memset(cva_, 1.0), w=['cvaS'])
                process_group(l, 's', 0)
                barrier(['va_c', 'kT_a', 'cums'] + SKEYS)
                pg.op('pool', lambda e: e.memset(va_c[:], 1.0), w=['va_c'])
        except StopBuild:
            pass
        pg.maxops = None
        pg.op('pe', lambda e: e.matmul(psA[0][0:1, 0:1], lhsT=cstb[:, 0:1], rhs=cstb[:, 0:1], start=True, stop=True), r=['cstb'], w=['psA0'])
        pg.op('act', lambda e: e.activation(out=bar[:, 1:2], in_=bar[:, 1:2], func=AF.Copy), r=['psA0'], w=['bar2'])
        pg.op('dve', lambda e: e.tensor_copy(out=bar[:, 1:2], in_=bar[:, 1:2]), w=['bar2'])
        pg.op('pool', lambda e: e.tensor_copy(out=bar[:, 1:2], in_=bar[:, 1:2]), w=['bar2'])
        out_tokens.append(pg.dma('sp', D['pb_c'][0, 0:1, 0:2], bar[0:1, 0:2], r=['bar2', 'bar'], w=['zz']) if False else None)
        out_tokens[:] = [t for t in out_tokens if t is not None]
        pg.op('pool', lambda e: e.memset(bar[:, 0:1], 0.0), r=['bar2'], w=['bar'])
        pg.finish(out_tokens)
        pg._wait('sp', ('c', 'pool', pg.cnt['pool'] - 1))
        print("ops:", pg.nops, "dmas:", pg.dma_n, "cnt:", pg.cnt)
    return nc


def _consts():
    c = np.zeros((128, NCST), np.float32)
    k = np.arange(128)[:, None]
    m = np.arange(128)[None, :]
    c[:, C_M1:C_M1 + 128] = (k <= m)
    c[:, C_M2:C_M2 + 128] = (k > m)
    c[:, C_S127:C_S127 + 128] = (k == 127)
    c[:, C_S63:C_S63 + 128] = (k == 63)
    band = np.ones((128, 5, 128), np.float32)
    s = np.arange(128)[:, None]
    t = np.arange(128)[None, :]
    band[:, 0, :] = 1.0 - ((s < 64) & (t >= 64))
    band[:, 4, :] = 1.0 - ((s >= 64) & (t < 64))
    ident = (k == m).astype(np.float32)
    return c, np.ascontiguousarray(band.reshape(128, 640)), ident


def _prep(inputs, cfg=None):
    f = lambda a: np.ascontiguousarray(np.asarray(a, dtype=np.float32))
    I = {k: f(v) for k, v in inputs.items()}
    small = np.concatenate([I[n].reshape(2, -1) for n in SM], axis=1)
    s = np.arange(128)[:, None]
    j = np.arange(640)[None, :]
    dist = 512 + (j % 128) - 128 * (j // 128) - s
    idx = np.clip(dist, -128, 128) + 128
    relb = np.ascontiguousarray(np.transpose(I['a_rel'][:, idx, :], (0, 1, 3, 2)))
    cwt = np.concatenate([I['b_conv_w'], I['b_conv_b'][:, None, :]], axis=1)
    convw = np.ascontiguousarray(np.transpose(cwt.reshape(2, 5, 6, 128), (0, 3, 2, 1)))
    cst, band, ident = _consts()
    maps = []
    for c in range(8):
        b = c % 4
        ss = slice(c * NS, (c + 1) * NS)
        m = dict(
            xp=I['x_prompt'][b], xs=I['x_sample'][ss].reshape(NS * TS, DM), memp=I['mem_prompt'][b],
            ca_k=I['cache_a_k'][:, ss].reshape(2, NS, 512, 512), ca_v=I['cache_a_v'][:, ss].reshape(2, NS, 512, 512),
            cc_k=I['cache_c_k'][:, ss].reshape(2, NS, 1024, 512), cc_v=I['cache_c_v'][:, ss].reshape(2, NS, 1024, 512),
            cc_f=I['cache_c_logf'][:, ss], sb_ssm=I['state_b_ssm'][:, ss], sb_conv=I['state_b_conv'][:, ss],
            cm_k=I['cache_mem_k'][:, ss].reshape(2, NS, 256, 256), cm_v=I['cache_mem_v'][:, ss].reshape(2, NS, 256, 256),
            w_in=I['w_in'], w_mkv=I['w_mkv'], w_pa=I['w_pa'], w_pb=I['w_pb'], w_pc=I['w_pc'], w_pm=I['w_pm'],
            w_out=I['w_out'], small=small, mnorm=I['m_norm'], band=band, ident=ident, relb=relb, convw=convw, cst=cst)
        maps.append({k: np.ascontiguousarray(v) for k, v in m.items()})
    return maps


_NC_CACHE = {}


def kernel(**inputs):
    cfg = {}
    key = 'full'
    if key not in _NC_CACHE:
        _NC_CACHE[key] = build(cfg)
    nc = _NC_CACHE[key]
    maps = _prep(inputs)
    res = run_bass_kernel_spmd(nc, maps, core_ids=list(range(8)))
    R = res.results
    P4 = range(4)
    st = lambda name, shape: np.stack([R[b][name] for b in P4], axis=1).reshape(shape)
    cat = lambda name: np.concatenate([R[c][name] for c in range(8)], axis=1)
    y_prompt = np.stack([R[b]['yp'] for b in P4], axis=0)
    y_sample = np.concatenate([R[c]['ys'].reshape(NS, TS, DM) for c in range(8)], axis=0)
    outs = [y_prompt, y_sample,
            st('pa_k', (2, 4, 512, 8, 64)), st('pa_v', (2, 4, 512, 8, 64)),
            st('pc_k', (2, 4, SEQ, 8, 64)), st('pc_v', (2, 4, SEQ, 8, 64)), st('pc_f', (2, 4, SEQ, 8)),
            st('pb_s', (2, 4, 8, 64, 64)), st('pb_c', (2, 4, 3, 768)),
            st('pm_k', (2, 4, 256, 4, 64)), st('pm_v', (2, 4, 256, 4, 64))]
    for name, tail in (('sa_k', (8, 64)), ('sa_v', (8, 64)), ('sc_k', (8, 64)), ('sc_v', (8, 64)), ('sc_f', (8,))):
        a = np.concatenate([R[c][name].reshape((2, NS, TS) + tail) for c in range(8)], axis=1)
        outs.append(a)
    outs.append(cat('sb_s'))
    outs.append(cat('sb_c'))
    return tuple(np.ascontiguousarray(o.astype(np.float32)) for o in outs)
```

```python
import contextlib
import numpy as np
import ml_dtypes
import concourse.bass as bass
import concourse.mybir as mybir
from concourse.bass_utils import run_bass_kernel_spmd

F32 = mybir.dt.float32
BF16 = mybir.dt.bfloat16
AF = mybir.ActivationFunctionType
ALU = mybir.AluOpType
AX = mybir.AxisListType

DM = 1024
DIN = 10000
SEQ = 4096
NS = 4
TS = 64
EPS = 1e-6
COL = dict(a_q=0, a_k=512, a_v=1024, a_z=1536, b_z=2048, b_xbc=2560, b_dt=3328,
           c_q=3336, c_k=3848, c_v=4360, c_f=4872, c_z=4880, m_q=5392, m_z=5648, gate=5904)
SM = {}
_o = 0
for _n, _w in (('g_norm', 1024), ('b_norm', 512), ('a_qnorm', 64), ('a_knorm', 64),
               ('c_qnorm', 64), ('c_knorm', 64), ('m_qnorm', 64), ('m_knorm', 64),
               ('b_dt_bias', 8), ('b_a_log', 8), ('b_d', 8), ('c_fbias', 8)):
    SM[_n] = (_o, _w)
    _o += _w
NSM = _o
C_M1, C_M2, C_S127, C_S63 = 0, 128, 256, 384
NCST = 512


def cp(e, out, in_):
    if hasattr(e, 'tensor_copy'):
        return e.tensor_copy(out=out, in_=in_)
    return e.activation(out=out, in_=in_, func=AF.Copy)


SAME_ENGINE_SYNC = True


class StopBuild(Exception):
    pass


class Prog:
    EPOCH = 24000
    ND = 48

    def __init__(self, nc, es):
        self.nc, self.es = nc, es
        self.eng = {'pe': nc.tensor, 'act': nc.scalar, 'dve': nc.vector, 'pool': nc.gpsimd, 'sp': nc.sync}
        self.cnt = {e: 0 for e in self.eng}
        self.sems = {}
        self.waited = {e: {} for e in self.eng}
        self.last_w = {}
        self.readers = {}
        self.dma_n = 0
        self.dma_sems = [es.enter_context(nc.semaphore("dq%d" % i)) for i in range(self.ND)]
        self.dma_tokens = []
        self.nops = 0
        self.maxops = None

    def _sem(self, e, epoch):
        k = (e, epoch)
        if k not in self.sems:
            self.sems[k] = self.es.enter_context(self.nc.semaphore("s_%s_%d" % (e, epoch)))
        return self.sems[k]

    def _wait(self, e, tok):
        if tok[0] == 'c':
            _, pe, c = tok
            if pe == e and (e == 'pe' or not SAME_ENGINE_SYNC):
                return
            epoch, v = divmod(c, self.EPOCH)
            key, val, sem = ('c', pe, epoch), v + 1, self._sem(pe, epoch)
        else:
            _, slot, rnd = tok
            key, val, sem = ('d', slot), 16 * (rnd + 1), self.dma_sems[slot]
        if self.waited[e].get(key, 0) >= val:
            return
        self.waited[e][key] = val
        self.eng[e].wait_ge(sem, val)

    def _deps(self, r, w):
        deps = set()
        for k in r:
            if k in self.last_w:
                deps.add(self.last_w[k])
            if isinstance(k, str) and k.startswith('ps'):
                deps.update(self.readers.get(k, ()))
        for k in w:
            if k in self.last_w:
                deps.add(self.last_w[k])
            deps.update(self.readers.get(k, ()))
        return deps

    def _reg(self, tok, r, w):
        for k in r:
            self.readers.setdefault(k, []).append(tok)
        for k in w:
            self.last_w[k] = tok
            self.readers[k] = []

    def op(self, e, fn, r=(), w=(), inc=True):
        if self.maxops is not None and self.nops >= self.maxops:
            raise StopBuild()
        for tok in self._deps(r, w):
            self._wait(e, tok)
        inst = fn(self.eng[e])
        c = self.cnt[e]
        if inc:
            epoch, _ = divmod(c, self.EPOCH)
            inst.then_inc(self._sem(e, epoch), 1)
            self.cnt[e] += 1
        self._reg(('c', e, c), r, w)
        self.nops += 1

    def dma(self, q, out, in_, r=(), w=(), nonc=False):
        if self.maxops is not None and self.nops >= self.maxops:
            raise StopBuild()
        n = self.dma_n
        self.dma_n += 1
        slot, rnd = n % self.ND, n // self.ND
        if rnd > 0:
            self._wait(q, ('d', slot, rnd - 1))
        for tok in self._deps(r, w):
            self._wait(q, tok)
        kw = {}
        if nonc:
            kw['allow_slow_non_contiguous'] = True
        inst = self.eng[q].dma_start(out=out, in_=in_, **kw)
        inst.then_inc(self.dma_sems[slot], 16)
        tok = ('d', slot, rnd)
        self._reg(tok, r, w)
        self.dma_tokens.append(tok)
        self.nops += 1
        return tok

    def finish(self, out_tokens):
        for tok in out_tokens:
            self._wait('sp', tok)
        last = {}
        for tok in self.dma_tokens:
            last[tok[1]] = tok
        for tok in last.values():
            self._wait('sp', tok)


def build(cfg):
    NL = cfg.get('layers', 2)
    NQ = cfg.get('nq', 8)
    DO_S = cfg.get('sample', True)
    NOSTATE = cfg.get('nostate', False)
    nc = bass.Bass("TRN2", target_bir_lowering=False)
    es = contextlib.ExitStack()
    D = {}

    def din(name, shape):
        D[name] = nc.dram_tensor(name, list(shape), F32, kind="ExternalInput").ap()

    def dout(name, shape):
        D[name] = nc.dram_tensor(name, list(shape), F32, kind="ExternalOutput").ap()

    din('xp', [SEQ, DM]); din('xs', [NS * TS, DM]); din('memp', [256, DM])
    din('ca_k', [2, NS, 512, 512]); din('ca_v', [2, NS, 512, 512])
    din('cc_k', [2, NS, 1024, 512]); din('cc_v', [2, NS, 1024, 512]); din('cc_f', [2, NS, 1024, 8])
    din('sb_ssm', [2, NS, 8, 64, 64]); din('sb_conv', [2, NS, 3, 768])
    din('cm_k', [2, NS, 256, 256]); din('cm_v', [2, NS, 256, 256])
    din('w_in', [2, DM, DIN]); din('w_mkv', [2, DM, 512])
    din('w_pa', [2, 512, DM]); din('w_pb', [2, 512, DM]); din('w_pc', [2, 512, DM]); din('w_pm', [2, 256, DM])
    din('w_out', [2, DM, DM])
    din('small', [2, NSM]); din('mnorm', [2, 1024]); din('relb', [2, 128, 8, 640]); din('convw', [2, 128, 6, 5]); din('cst', [128, NCST]); din('band', [128, 640]); din('ident', [128, 128])
    dout('yp', [SEQ, DM]); dout('ys', [NS * TS, DM])
    dout('pa_k', [2, 512, 512]); dout('pa_v', [2, 512, 512])
    dout('pc_k', [2, SEQ, 512]); dout('pc_v', [2, SEQ, 512]); dout('pc_f', [2, SEQ, 8])
    dout('pb_s', [2, 8, 64, 64]); dout('pb_c', [2, 3, 768])
    dout('pm_k', [2, 256, 256]); dout('pm_v', [2, 256, 256])
    dout('sa_k', [2, NS * TS, 512]); dout('sa_v', [2, NS * TS, 512])
    dout('sc_k', [2, NS * TS, 512]); dout('sc_v', [2, NS * TS, 512]); dout('sc_f', [2, NS * TS, 8])
    dout('sb_s', [2, NS, 8, 64, 64]); dout('sb_c', [2, NS, 3, 768])
    x1p = nc.dram_tensor("x1p", [SEQ, DM], F32, kind="Internal").ap()
    x1s = nc.dram_tensor("x1s", [NS * TS, DM], F32, kind="Internal").ap()

    with es:
        pg = Prog(nc, es)
        pg.maxops = cfg.get('maxops')
        out_tokens = []

        def sb(name, shape, dt):
            return es.enter_context(nc.sbuf_tensor("sb_" + name, list(shape), dt))

        def ps(name, shape, dt):
            return es.enter_context(nc.psum_tensor("ps_" + name, list(shape), dt))

        cst = sb("cst", [128, NCST], F32)
        cstb = sb("cstb", [128, 256], BF16)
        small = sb("small", [128, NSM], F32)
        expB = sb("expB", [128, 8, 640], BF16)
        cw = sb("cw", [128, 6, 5], F32)
        aneg = sb("aneg", [128, 8], F32)
        xt = sb("xt", [128, 1, DM], F32)
        W3 = sb("W3", [128, 4096], BF16)
        xnb = W3[:, 0:1024]
        sq = W3[:, 1024:2048].bitcast(F32)
        f2 = W3[:, 2048:3072].bitcast(F32)
        f3 = W3[:, 3072:4096].bitcast(F32)
        xnT = sb("xnT", [128, 8, 512], BF16)
        NWB = 3
        wbuf = [sb("wbuf%d" % i, [128, 8, 512], BF16) for i in range(NWB)]
        wbuf.append(W3[:, :].rearrange("p (k n) -> p k n", n=512))
        WKEYS = [['wbuf0'], ['wbuf1'], ['wbuf2'], ['wbuf3', 'xnb', 'sq', 'f2', 'f3']]
        kT_c = sb("kT_c", [128, 4, SEQ], BF16)
        va_c = sb("va_c", [128, 32, 8, 66], BF16)
        kT_a = sb("kT_a", [128, 4, 1024], BF16)
        va_a = sb("va_a", [128, 8, 8, 66], BF16)
        kT_m = sb("kT_m", [128, 2, 256], BF16)
        va_m = sb("va_m", [128, 2, 4, 66], BF16)
        cum = sb("cum", [128, 32, 8], F32)
        biasQ = sb("biasQ", [128, 32, 8], F32)
        qT = sb("qT", [128, 4, 512], BF16)
        sz = sb("sz", [128, 4, 512], BF16)
        oz = sb("oz", [128, 4, 512], BF16)
        ozT = {k: sb("ozT_" + k, [128, n, 512], BF16) for k, n in (('a', 4), ('b', 4), ('c', 4), ('m', 2))}
        f1 = sb("f1", [128, 512], F32)
        b1 = sb("b1", [128, 512], BF16)
        PT = [sb("PT%d" % i, [128, 512], BF16) for i in range(2)]
        st8 = sb("st8", [128, 8, 8], F32)
        raw = sb("raw", [128, 3, 4, 131], F32)
        halo = sb("halo", [128, 6, 3], F32)
        AR2 = sb("AR2", [128, 4096], BF16)
        xbc = AR2[:, 0:3072].rearrange("p (j n) -> p j n", n=512)
        Mh = AR2[:, 3072:4096].rearrange("p (h n) -> p h n", n=128)
        mT = AR2[:, :].rearrange("p (k n) -> p k n", n=512)
        cva = sb("cva", [128, 512], F32)
        sig = cva
        dts = sb("dts", [128, 4, 8], F32)
        lfs = sb("lfs", [128, 4, 8], F32)
        asb = sb("asb", [128, 4, 8], F32)
        btk = sb("btk", [128, 128], BF16)
        AE = sb("AE", [128, 2048], F32)
        Ah = AE[:, 0:512].rearrange("p (h n) -> p h n", n=128)
        Ee = AE[:, 512:1024].rearrange("p (h n) -> p h n", n=128)
        Gm = AE[:, 1024:1280].rearrange("p (g n) -> p g n", n=128)
        bdec = AE[:, 1280:1536].bitcast(BF16).rearrange("p (a g n) -> p a g n", a=4, g=2)
        xdt = AE[:, 1536:1792].bitcast(BF16).rearrange("p (h d) -> p h d", d=64)
        xtk = AE[:, 1792:2048].bitcast(BF16)
        macc = AE[:, :].rearrange("p (t n) -> p t n", n=512)
        hT = sb("hT", [128, 4, 64], F32)
        hTb = sb("hTb", [128, 4, 64], BF16)
        ssd8 = sb("ssd8", [128, 4, 8], F32)
        psA = [ps("psA%d" % i, [128, 512], F32) for i in range(2)]
        psT = [ps("psT%d" % i, [128, 1024], BF16) for i in range(2)]
        psS = [ps("psS%d" % i, [128, 512], F32) for i in range(2)]
        psO = [ps("psO%d" % i, [128, 512], F32) for i in range(2)]
        rr = {'A': 0, 'T': 0, 'S': 0, 'O': 0, 'W': 0, 'P': 0, 'X': 0, 'WP': 0}
        rrn = {'X': 1, 'WP': 1}

        def rot(k, n=2):
            v = rr[k]
            rr[k] = (v + 1) % rrn.get(k, n)
            return v

        ident_b = cstb[:, 0:128]
        M1f = cst[:, C_M1:C_M1 + 128]
        M2f = cst[:, C_M2:C_M2 + 128]
        S127f = cst[:, C_S127:C_S127 + 128]
        S63f = cst[:, C_S63:C_S63 + 128]
        M1b = cstb[:, 128:256]

        def smv(name, rows=128):
            o, w = SM[name]
            return small[0:rows, o:o + w]

        bar = sb("bar", [128, 2], F32)
        epst = sb("epst", [128, 1], F32)
        nhalf = sb("nhalf", [128, 8], F32)
        identf = sb("identf", [128, 64], F32)

        def barrier(keys):
            pg.op('pool', lambda e: e.memset(bar[:, 0:1], 0.0), w=['bar'] + list(keys))

        AEK = ['Ah%d' % h for h in range(8)] + ['Ee', 'Gm', 'bdec', 'xdt', 'xtk', 'AErel'] + ['macc%d' % t for t in range(4)]
        pg.op('pool', lambda e: e.memset(epst[:, :], EPS), w=['epst'])
        pg.op('pool', lambda e: e.memset(nhalf[:, :], -0.5), w=['epst'])
        pg.dma('sp', identf[0:64, :], D['ident'][0:64, 0:64], w=['identf'])
        pg.dma('sp', identf[64:128, :], D['ident'][0:64, 0:64], w=['identf'])
        pg.dma('sp', cst[:, :], D['cst'][:, :], w=['cst'])
        pg.dma('sp', AE[:, 0:128], D['ident'][:, :], w=AEK)
        pg.op('dve', lambda e: cp(e, out=cstb[:, 0:128], in_=AE[:, 0:128]), r=AEK, w=['cstb'])
        pg.op('dve', lambda e: cp(e, out=cstb[:, 128:256], in_=cst[:, C_M1:C_M1 + 128]), r=['cst'], w=['cstb'])
        for t_, k_ in ((va_c, 'va_c'), (va_a, 'va_a'), (va_m, 'va_m')):
            pg.op('pool', lambda e, t_=t_: e.memset(t_[:], 1.0), w=[k_])

        def group_wlist(l):
            wl = []
            for nm in ('c_q', 'c_k', 'c_v'):
                wl.append(('w_in', l, COL[nm], 512, 8))
            wl.append(('w_in', l, COL['c_f'], 8, 8))
            wl.append(('w_in', l, COL['c_z'], 512, 8))
            for nm in ('a_q', 'a_k', 'a_v', 'a_z', 'm_q', 'b_z'):
                wl.append(('w_in', l, COL[nm], 512, 8))
            wl.append(('w_in', l, COL['b_dt'], 8, 8))
            for half in range(2):
                wl.append(('w_in', l, COL['b_xbc'] + half * 384, 384, 8))
            for c in range(2):
                for bi, (wn, nkc) in enumerate((('w_pa', 4), ('w_pb', 4), ('w_pc', 4), ('w_pm', 2))):
                    wl.append(('w_in', l, COL['gate'] + bi * 1024 + c * 512, 512, 8, True))
                    wl.append((wn, l, c * 512, 512, nkc, True))
            for c in range(2):
                wl.append(('w_out', l, c * 512, 512, 8, True))
            return wl

        WL = []
        for l_ in range(NL):
            WL.append(('w_mkv', l_, 0, 512, 8))
            for _ in range(NQ + (1 if DO_S else 0)):
                WL += group_wlist(l_)
        wbi, prev_occ, lastocc, r3, r4 = [], [], {}, 0, 0
        for k_, ent in enumerate(WL):
            if len(ent) > 5 and ent[5]:
                b_ = r4 % 4; r4 += 1
            else:
                b_ = r3 % 3; r3 += 1; r4 = r3
            wbi.append(b_)
            prev_occ.append(lastocc.get(b_, -1))
            lastocc[b_] = k_
        wstate = {'ptr': 0, 'issued': 0}

        def load_w(dram, l, c0, n, nk=8, buf=None):
            i = wstate['ptr']
            exp = WL[i]
            assert exp[1] == l and exp[2] == c0 and exp[3] == n and exp[4] == nk and D[exp[0]] is dram, (exp, l, c0, n, nk)
            in_merge = len(exp) > 5 and exp[5]
            while wstate['issued'] < len(WL) and wstate['issued'] <= i + 3 and \
                    (wstate['issued'] <= i or prev_occ[wstate['issued']] <= i - 2) and \
                    (wbi[wstate['issued']] != 3 or in_merge):
                k = wstate['issued']
                nm, l2, c2, n2, nk2 = WL[k][0:5]
                src = D[nm][l2, :, c2:c2 + n2].rearrange("(k p) n -> p k n", p=128)
                pg.dma('pool', wbuf[wbi[k]][:, 0:nk2, 0:n2], src, w=WKEYS[wbi[k]])
                wstate['issued'] += 1
            wstate['ptr'] += 1
            return wbuf[wbi[i]], WKEYS[wbi[i]]

        def transposes(src_ap_fn, nblk, np_, dst_fn, rkeys, wkeys, evac='dve'):
            i = rot('T'); pt = psT[i]; pk = 'psT%d' % i
            for j in range(nblk):
                pg.op('pe', lambda e, j=j: e.transpose(out=pt[:, j * 128:j * 128 + np_], in_=src_ap_fn(j),
                                                       identity=ident_b[0:np_, 0:np_]),
                      r=list(rkeys) + ['cstb'], w=[pk], inc=(j == nblk - 1))
            view = pt[:, 0:nblk * 128].rearrange("p (b n) -> p b n", n=128)[:, :, 0:np_]
            pg.op(evac, lambda e: dst_fn(e, view), r=[pk], w=list(wkeys))

        def rstd_from_ss(ss_ap, rs_ap, n, rows, key):
            w_ = rs_ap.shape[-1]
            pg.op('dve', lambda e: e.tensor_scalar(out=rs_ap, in0=ss_ap, scalar1=1.0 / n, scalar2=EPS,
                                                    op0=ALU.mult, op1=ALU.add), r=[key], w=[key])
            pg.op('pool', lambda e: e.tensor_tensor(out=rs_ap, in0=rs_ap, in1=nhalf[0:rows, 0:w_], op=ALU.pow),
                  r=[key, 'epst'], w=[key])

        def head_norm(psap, pk, nh, gain_ap, out_ap, outkeys, np_, scale=None):
            n = nh * 64
            pg.op('act', lambda e: e.activation(out=sq[0:np_, 0:n], in_=psap, func=AF.Square), r=[pk], w=['sq'])
            ssv = st8[0:np_, 0, 0:nh]
            pg.op('dve', lambda e: e.tensor_reduce(out=ssv, in_=sq[0:np_, 0:n].rearrange("p (h d) -> p h d", d=64),
                                                   axis=AX.X, op=ALU.add), r=['sq'], w=['st8'])
            rstd_from_ss(ssv, ssv, 64, np_, 'st8')
            pg.op('dve', lambda e: e.tensor_tensor(
                out=f1[0:np_, 0:n].rearrange("p (h d) -> p h d", d=64),
                in0=psap.rearrange("p (h d) -> p h d", d=64),
                in1=ssv.unsqueeze(2).to_broadcast([np_, nh, 64]), op=ALU.mult), r=[pk, 'st8'], w=['f1'])
            g = gain_ap.unsqueeze(1).to_broadcast([np_, nh, 64])
            pg.op('dve', lambda e: e.tensor_tensor(
                out=out_ap.rearrange("p (h d) -> p h d", d=64),
                in0=f1[0:np_, 0:n].rearrange("p (h d) -> p h d", d=64), in1=g, op=ALU.mult),
                r=['f1', 'small'], w=list(outkeys))

        def pipe_tiles(NT, proj_fn):
            nxt = proj_fn(0)
            for t in range(NT):
                cur = nxt
                if t + 1 < NT:
                    nxt = proj_fn(t + 1)
                yield (t,) + tuple(cur)

        def proj_tm(xT, xkey, tcol, np_, wt, wkey, n, nk=8, wc0=0, xkeys=()):
            i = rot('A'); p = psA[i]; pk = 'psA%d' % i
            for kc in range(nk):
                pg.op('pe', lambda e, kc=kc: e.matmul(p[0:np_, 0:n], lhsT=xT[:, kc, tcol:tcol + np_],
                                                      rhs=wt[:, kc, wc0:wc0 + n], start=(kc == 0), stop=(kc == nk - 1)),
                      r=[xkey] + list(wkey) + list(xkeys), w=[pk], inc=(kc == nk - 1))
            return p[0:np_, 0:n], pk

        def layer_consts(l):
            pg.dma('sp', small[:, :], D['small'][l:l + 1, :].broadcast_to([128, NSM]), w=['small'])
            pg.dma('sp', cw[:, :, :], D['convw'][l], w=['cw'])
            for name in ('a_qnorm', 'c_qnorm', 'm_qnorm'):
                a = smv(name)
                pg.op('dve', lambda e, a=a: e.tensor_scalar(out=a, in0=a, scalar1=0.125, scalar2=None, op0=ALU.mult),
                      r=['small'], w=['small'])
            pg.op('act', lambda e: e.activation(out=aneg[:, :], in_=smv('b_a_log'), func=AF.Exp), r=['small'], w=['aneg'])
            pg.op('dve', lambda e: e.tensor_scalar(out=aneg[:, :], in0=aneg[:, :], scalar1=-1.0, scalar2=None, op0=ALU.mult),
                  r=['aneg'], w=['aneg'])
            barrier(AEK)
            pg.dma('sp', AE[:, 0:640], D['band'][:, :], w=AEK)
            pg.op('dve', lambda e: e.tensor_scalar(out=AE[:, 0:640], in0=AE[:, 0:640], scalar1=30000.0, scalar2=-30000.0,
                                                    op0=ALU.mult, op1=ALU.add), r=AEK, w=AEK)
            for h in range(8):
                pg.dma('sp', AE[:, 1024:1664], D['relb'][l, :, h, :], w=['AErel'])
                pg.op('dve', lambda e, h=h: e.tensor_tensor(out=expB[:, h, :], in0=AE[:, 1024:1664],
                                                            in1=AE[:, 0:640], op=ALU.add),
                      r=['AErel'] + AEK, w=['expB'])
            barrier(AEK)

        def norm_tiles(src_fn, NT, np_, gain_fn, rk_fn=None):
            rawflat = raw[:, :, :, :].rearrange("p a b c -> p (a b c)")
            sqj = sq.bitcast(BF16)
            stg = [(xt[0:np_, 0, :], 'xt0'), (rawflat[0:np_, 0:DM], 'raw')]

            def ld(t):
                xa, xk = stg[t % 2]
                pg.dma('sp', xa, src_fn(t), r=(rk_fn(t) if rk_fn else []), w=[xk])
            ld(0)
            for t in range(NT):
                xa, xk = stg[t % 2]
                if t + 1 < NT:
                    ld(t + 1)
                ssv = st8[0:np_, 1, t % 2:t % 2 + 1]
                pg.op('act', lambda e, xa=xa, ssv=ssv: e.activation(out=sqj[0:np_, :], in_=xa, func=AF.Square, accum_out=ssv),
                      r=[xk], w=['sq', 'st8'])
                rstd_from_ss(ssv, ssv, DM, np_, 'st8')
                pg.op('dve', lambda e, xa=xa, ssv=ssv: e.scalar_tensor_tensor(out=xnb[0:np_, :], in0=xa, scalar=ssv,
                                                                  in1=gain_fn(np_), op0=ALU.mult, op1=ALU.mult),
                      r=[xk, 'st8', 'small', 'f2'], w=['xnb'])
                transposes(lambda j: xnb[0:np_, j * 128:(j + 1) * 128], 8, np_,
                           lambda e, v, t=t: cp(e, out=xnT[:, :, t * np_:(t + 1) * np_], in_=v),
                           ['xnb'], ['xnT'], evac='act' if t % 2 else 'dve')

        def attend(qT_ap, NQc, qtiles, kblocks, qkeys, out_fn, acc=None):
            if acc is None:
                io = rot('O'); po = psO[io]; pok = 'psO%d' % io
                pov = po[:, 0:260].rearrange("p (j c) -> p j c", c=65)
                jbase, bank_first = 0, True
            else:
                pov, pok, jbase, bank_first = acc
            lastkb, firstkb = {}, {}
            for bi, kb in enumerate(kblocks):
                for j, (c0, nq) in enumerate(qtiles):
                    if c0 >= kb['q0']:
                        lastkb[j] = bi
                        firstkb.setdefault(j, bi)
            st = {}

            def stage_s(bi):
                kb = kblocks[bi]
                isx = rot('S'); psx = psS[isx]; psk = 'psS%d' % isx
                badd = kb.get('badd')
                pg.op('pe', lambda e: e.matmul(psx[0:kb['nk'], kb['q0']:NQc], lhsT=kb['kT'],
                                               rhs=qT_ap[:, kb['q0']:NQc], start=True, stop=(badd is None)),
                      r=list(qkeys) + list(kb['keys']), w=[psk], inc=(badd is None))
                if badd is not None:
                    pg.op('pe', lambda e: e.matmul(psx[0:kb['nk'], kb['q0']:NQc], lhsT=ident_b[:, 0:kb['nk']],
                                                   rhs=badd, start=False, stop=True),
                          r=['cstb'] + list(kb.get('mkeys', [])), w=[psk])
                st[bi] = (psx, psk)

            def stage_e(bi):
                kb = kblocks[bi]
                psx, psk = st[bi]
                ip = rot('P'); pt = PT[ip]; ptk = 'PT%d' % ip
                if kb.get('bias') is not None:
                    pg.op('act', lambda e: e.activation(out=pt[0:kb['nk'], kb['q0']:NQc], in_=psx[0:kb['nk'], kb['q0']:NQc],
                                                        func=AF.Exp, bias=kb['bias'], scale=1.0),
                          r=[psk] + list(kb.get('bkeys', [])), w=[ptk])
                else:
                    pg.op('act', lambda e: e.activation(out=pt[0:kb['nk'], kb['q0']:NQc], in_=psx[0:kb['nk'], kb['q0']:NQc],
                                                        func=AF.Exp), r=[psk], w=[ptk])
                if kb.get('diagmask'):
                    pg.op('dve', lambda e: e.tensor_tensor(
                        out=pt[0:kb['nk'], kb['q0']:kb['q0'] + kb['nk']], in0=pt[0:kb['nk'], kb['q0']:kb['q0'] + kb['nk']],
                        in1=M1b[0:kb['nk'], 0:kb['nk']], op=ALU.mult), r=[ptk, 'cstb'], w=[ptk])
                st[bi] = (pt, ptk)

            def stage_v(bi):
                kb = kblocks[bi]
                pt, ptk = st[bi]
                for j, (c0, nq) in enumerate(qtiles):
                    if c0 < kb['q0']:
                        continue
                    pg.op('pe', lambda e, j=j, c0=c0, nq=nq: e.matmul(
                        pov[0:nq, jbase + j, :], lhsT=pt[0:kb['nk'], c0:c0 + nq], rhs=kb['v'],
                        start=(bank_first and bi == 0 and j == min(firstkb)), stop=(bi == lastkb[j]), skip_group_check=True),
                        r=[ptk] + list(kb['keys']), w=[pok], inc=(c0 + nq >= NQc))

            n = len(kblocks)
            stage_s(0)
            for bi in range(n):
                if bi + 1 < n:
                    stage_s(bi + 1)
                stage_e(bi)
                stage_v(bi)
            if out_fn is not None:
                out_fn(pov, pok)
            return pov, pok

        def attn_out(h0col, NT, np_, szkey='sz'):
            def fn(pov, pok):
                rl = st8[0:np_, 2, 0:NT]
                pg.op('dve', lambda e: e.reciprocal(out=rl, in_=pov[0:np_, 0:NT, 64]), r=[pok], w=['st8'])
                tmp = f2[0:np_, 0:NT * 64].rearrange("p (j d) -> p j d", d=64)
                pg.op('dve', lambda e: e.tensor_tensor(out=tmp, in0=pov[0:np_, 0:NT, 0:64],
                                                       in1=rl.unsqueeze(2).to_broadcast([np_, NT, 64]), op=ALU.mult),
                      r=[pok, 'st8'], w=['f2'])
                pg.op('pool', lambda e: e.tensor_tensor(out=oz[0:np_, 0:NT, h0col:h0col + 64], in0=tmp,
                                                        in1=sz[0:np_, 0:NT, h0col:h0col + 64], op=ALU.mult),
                      r=['f2', szkey], w=['oz'])
            return fn

        def oz_to_T(br, NT, np_, nblk):
            for t in range(NT):
                transposes(lambda j, t=t: oz[0:np_, t, j * 128:(j + 1) * 128], nblk, np_,
                           lambda e, v, t=t: cp(e, out=ozT[br][:, 0:nblk, t * np_:(t + 1) * np_], in_=v),
                           ['oz'], ['ozT_' + br], evac='act' if t % 2 else 'dve')

        def evac_q(psap, pk, t, np_, nh, gname):
            head_norm(psap, pk, nh, smv(gname, np_), b1[0:np_, 0:nh * 64], ['b1'], np_)
            transposes(lambda j: b1[0:np_, j * 128:(j + 1) * 128], nh // 2, np_,
                       lambda e, v: cp(e, out=qT[:, 0:nh // 2, t * np_:(t + 1) * np_], in_=v),
                       ['b1'], ['qT'], evac='act')

        def evac_k(psap, pk, np_, nh, gname, out_dram, kT_dst_fn, kT_key):
            head_norm(psap, pk, nh, smv(gname, np_), f3[0:np_, 0:nh * 64], ['f3'], np_)
            if out_dram is not None:
                out_tokens.append(pg.dma('sp', out_dram, f3[0:np_, 0:nh * 64], r=['f3']))
            pg.op('dve', lambda e: cp(e, out=b1[0:np_, 0:nh * 64], in_=f3[0:np_, 0:nh * 64]), r=['f3'], w=['b1'])
            transposes(lambda j: b1[0:np_, j * 128:(j + 1) * 128], nh // 2, np_,
                       lambda e, v: cp(e, out=kT_dst_fn(), in_=v), ['b1'], [kT_key], evac='act')

        def evac_v(psap, pk, np_, nh, out_dram, va_dst, va_key):
            if out_dram is not None:
                pg.op('act', lambda e: e.activation(out=f3[0:np_, 0:nh * 64], in_=psap, func=AF.Copy), r=[pk], w=['f3'])
                out_tokens.append(pg.dma('sp', out_dram, f3[0:np_, 0:nh * 64], r=['f3']))
            pg.op('dve', lambda e: cp(e, out=va_dst, in_=psap.rearrange("p (h d) -> p h d", d=64)),
                  r=[pk], w=[va_key])

        def softplus_from(psap, pk, bias_ap, out_ap, okey, np_, neg=False):
            tmp = st8[0:np_, 3, :]
            pg.op('dve', lambda e: e.tensor_tensor(out=tmp, in0=psap, in1=bias_ap, op=ALU.add), r=[pk, 'small'], w=['st8'])
            pg.op('act', lambda e: e.activation(out=tmp, in_=tmp, func=AF.Exp, scale=(-1.0 if neg else 1.0)),
                  r=['st8'], w=['st8'])
            pg.op('dve', lambda e: e.tensor_scalar(out=tmp, in0=tmp, scalar1=1.0, scalar2=None, op0=ALU.add),
                  r=['st8'], w=['st8'])
            pg.op('act', lambda e: e.activation(out=tmp, in_=tmp, func=AF.Ln), r=['st8'], w=['st8'])
            pg.op('dve', lambda e: e.tensor_scalar(out=out_ap, in0=tmp, scalar1=(-1.0 if neg else 1.0), scalar2=None,
                                                    op0=ALU.mult), r=['st8'], w=[okey])

        def proj8_all(NT, np_, wt, wk):
            i = rot('A'); p = psA[i]; pk = 'psA%d' % i
            for t in range(NT):
                for kc in range(8):
                    pg.op('pe', lambda e, kc=kc, t=t: e.matmul(p[0:np_, t * 8:(t + 1) * 8], lhsT=xnT[:, kc, t * np_:(t + 1) * np_],
                                                              rhs=wt[:, kc, 0:8], start=(kc == 0), stop=(kc == 7)),
                          r=['xnT'] + list(wk), w=[pk], inc=(kc == 7 and t == NT - 1))
            return p[0:np_, 0:NT * 8].rearrange("p (t h) -> p t h", h=8), pk

        def softplus4(psv, pk, bias_ap, out_ap, okey, np_, NT, neg=False):
            tmp = st8[0:np_, 0:NT, :]
            pg.op('dve', lambda e: e.tensor_tensor(out=tmp, in0=psv, in1=bias_ap.unsqueeze(1).to_broadcast([np_, NT, 8]),
                                                   op=ALU.add), r=[pk, 'small'], w=['st8'])
            pg.op('act', lambda e: e.activation(out=tmp, in_=tmp, func=AF.Exp, scale=(-1.0 if neg else 1.0)),
                  r=['st8'], w=['st8'])
            pg.op('dve', lambda e: e.tensor_scalar(out=tmp, in0=tmp, scalar1=1.0, scalar2=None, op0=ALU.add),
                  r=['st8'], w=['st8'])
            pg.op('act', lambda e: e.activation(out=tmp, in_=tmp, func=AF.Ln), r=['st8'], w=['st8'])
            pg.op('dve', lambda e: e.tensor_scalar(out=out_ap, in0=tmp, scalar1=(-1.0 if neg else 1.0), scalar2=None,
                                                    op0=ALU.mult), r=['st8'], w=[okey])

        def cumsum_tile(lf_ap, lfkey, np_, dst_ap, dkey, carry_ap, ckeys):
            i = rot('S'); p = psS[i]; pk = 'psS%d' % i
            pg.op('pe', lambda e: e.matmul(p[0:np_, 0:8], lhsT=M1f[0:np_, 0:np_], rhs=lf_ap, start=True,
                                           stop=(carry_ap is None)), r=[lfkey, 'cst'], w=[pk])
            if carry_ap is not None:
                pg.op('pe', lambda e: e.matmul(p[0:np_, 0:8], lhsT=carry_ap[0], rhs=carry_ap[1], start=False, stop=True),
                      r=list(ckeys) + ['cst'], w=[pk])
            pg.op('dve', lambda e: cp(e, out=dst_ap, in_=p[0:np_, 0:8]), r=[pk], w=[dkey])

        def ssd_tile(t, np_, NT):
            c0 = t * np_
            if t == 0:
                barrier(AEK)
            def ev(e, v):
                return cp(e, out=xtk[0:np_, :].rearrange("p (b n) -> p b n", n=128), in_=v[0:np_, 0:4, :])
            i = rot('T'); pt = psT[i]; pk = 'psT%d' % i
            for j in range(5):
                pg.op('pe', lambda e, j=j: e.transpose(out=pt[0:np_, j * 128:(j + 1) * 128], in_=xbc[:, j, c0:c0 + np_],
                                                       identity=ident_b[:, :]), r=['xbc', 'cstb'], w=[pk], inc=(j == 4))
            pg.op('act', lambda e: e.activation(out=xtk[0:np_, :], in_=pt[0:np_, 0:512], func=AF.Copy), r=[pk], w=['xtk'])
            pg.op('act', lambda e: e.activation(out=btk[0:np_, :], in_=pt[0:np_, 512:640], func=AF.Copy), r=[pk], w=['btk'])
            pg.op('dve', lambda e: e.tensor_tensor(out=xdt[0:np_, :, :], in0=pt[0:np_, 0:512].rearrange("p (h d) -> p h d", d=64),
                                                   in1=dts[0:np_, t, :].unsqueeze(2).to_broadcast([np_, 8, 64]), op=ALU.mult),
                  r=[pk, 'dts'], w=['xdt'])
            a_ap = asb[0:np_, t, :]
            i = rot('A'); p = psA[i]; pk2 = 'psA%d' % i
            pg.op('pe', lambda e: e.matmul(p[0:np_, 0:8], lhsT=M1f[0:np_, 0:np_], rhs=a_ap, start=True, stop=True),
                  r=['asb', 'cst'], w=[pk2], inc=False)
            pg.op('pe', lambda e: e.matmul(p[0:np_, 8:16], lhsT=M2f[0:np_, 0:np_], rhs=a_ap, start=True, stop=True),
                  r=['asb', 'cst'], w=[pk2], inc=False)
            pg.op('pe', lambda e: e.matmul(p[:, 16:24], lhsT=M1f[0:np_, :], rhs=a_ap, start=True, stop=False),
                  r=['asb', 'cst'], w=[pk2], inc=False)
            pg.op('pe', lambda e: e.matmul(p[:, 16:24], lhsT=M2f[0:np_, :], rhs=a_ap, start=False, stop=True),
                  r=['asb', 'cst'], w=[pk2])
            pg.op('act', lambda e: e.activation(out=ssd8[0:np_, 0, :], in_=p[0:np_, 0:8], func=AF.Exp), r=[pk2], w=['ssd8'])
            pg.op('act', lambda e: e.activation(out=ssd8[0:np_, 1, :], in_=p[0:np_, 8:16], func=AF.Exp), r=[pk2], w=['ssd8'])
            pg.op('act', lambda e: e.activation(out=ssd8[:, 2, :], in_=p[:, 16:24], func=AF.Exp), r=[pk2], w=['ssd8'])
            for g in range(2):
                i = rot('A'); pgm = psA[i]; pk3 = 'psA%d' % i
                pg.op('pe', lambda e, g=g, pgm=pgm: e.matmul(pgm[0:np_, 0:np_], lhsT=xbc[g * 64:(g + 1) * 64, 4, c0:c0 + np_],
                                                    rhs=xbc[g * 64:(g + 1) * 64, 5, c0:c0 + np_], start=True, stop=True),
                      r=['xbc'], w=[pk3])
                pg.op('dve', lambda e, g=g, pgm=pgm: e.tensor_tensor(
                    out=Gm[0:np_, g, 0:np_], in0=pgm[0:np_, 0:np_], in1=M1f[0:np_, 0:np_], op=ALU.mult),
                    r=[pk3, 'cst'], w=['Gm'])
            for hh in range(2):
                for h4 in range(4):
                    h = hh * 4 + h4
                    pg.op('dve', lambda e, h=h, h4=h4: e.tensor_scalar(
                        out=Ah[0:np_, h4, 0:np_], in0=M2f[0:np_, 0:np_], scalar1=asb[0:np_, t, h:h + 1], scalar2=None,
                        op0=ALU.mult), r=['cst', 'asb'], w=['Ah%d' % h4])
                psx = psS[hh]; psk = 'psS%d' % hh
                for h4 in range(4):
                    pg.op('pe', lambda e, h4=h4, psx=psx: e.matmul(
                        psx[0:np_, h4 * 128:h4 * 128 + np_], lhsT=Ah[0:np_, h4, 0:np_], rhs=M1f[0:np_, 0:np_],
                        start=True, stop=True), r=['Ah%d' % h4, 'cst'], w=[psk], inc=(h4 == 3))
                pg.op('act', lambda e, psx=psx: e.activation(
                    out=Ee[0:np_, :, 0:np_],
                    in_=psx[0:np_, :].rearrange("p (h n) -> p h n", n=128)[:, :, 0:np_], func=AF.Exp),
                    r=[psk], w=['Ee'])
                pg.op('dve', lambda e, hh=hh: e.tensor_tensor(
                    out=Mh[0:np_, hh * 4:(hh + 1) * 4, 0:np_], in0=Ee[0:np_, :, 0:np_],
                    in1=Gm[0:np_, hh, 0:np_].unsqueeze(1).to_broadcast([np_, 4, np_]), op=ALU.mult),
                    r=['Ee', 'Gm'], w=['Mh'])
            io = rot('O'); py = psO[io]; pyk = 'psO%d' % io
            io2 = rot('O'); py2 = psO[io2]; py2k = 'psO%d' % io2
            for h in range(8):
                pg.op('pe', lambda e, h=h: e.matmul(py[0:np_, h * 64:(h + 1) * 64], lhsT=Mh[0:np_, h, 0:np_],
                                                    rhs=xdt[0:np_, h, :], start=True, stop=True),
                      r=['Mh', 'xdt'], w=[pyk], inc=(h == 7))
            py3 = psS[1]; py3k = 'psS1'
            for h in range(8):
                g = h // 4
                dstp = (py2 if g == 0 else py3)
                pg.op('pe', lambda e, h=h, g=g, dstp=dstp: e.matmul(dstp[0:np_, (h % 4) * 64:(h % 4 + 1) * 64],
                                                         lhsT=xbc[g * 64:(g + 1) * 64, 5, c0:c0 + np_],
                                                         rhs=hTb[g * 64:(g + 1) * 64, h % 4, :], start=True, stop=True),
                      r=['xbc', 'hTb'], w=[py2k if g == 0 else py3k], inc=(h % 4 == 3))
            yv = f1[0:np_, :].rearrange("p (h d) -> p h d", d=64)
            for g, (dstp, dk) in enumerate(((py2, py2k), (py3, py3k))):
                pg.op('dve', lambda e, g=g, dstp=dstp: e.tensor_tensor(
                    out=yv[:, g * 4:(g + 1) * 4, :], in0=dstp[0:np_, 0:256].rearrange("p (h d) -> p h d", d=64),
                    in1=ssd8[0:np_, 0, g * 4:(g + 1) * 4].unsqueeze(2).to_broadcast([np_, 4, 64]), op=ALU.mult),
                    r=[dk, 'ssd8'], w=['f1'])
            pg.op('dve', lambda e: e.tensor_tensor(out=f1[0:np_, :], in0=f1[0:np_, :], in1=py[0:np_, :], op=ALU.add),
                  r=['f1', pyk], w=['f1'])
            pg.op('pool', lambda e: e.tensor_tensor(out=f2[0:np_, :].rearrange("p (h d) -> p h d", d=64),
                                                    in0=xtk[0:np_, :].rearrange("p (h d) -> p h d", d=64),
                                                    in1=smv('b_d', np_).unsqueeze(2).to_broadcast([np_, 8, 64]), op=ALU.mult),
                  r=['xtk', 'small'], w=['f2'])
            pg.op('dve', lambda e: e.tensor_tensor(out=f1[0:np_, :], in0=f1[0:np_, :], in1=f2[0:np_, :], op=ALU.add),
                  r=['f1', 'f2'], w=['f1'])
            pg.op('dve', lambda e: e.tensor_tensor(out=f1[0:np_, :], in0=f1[0:np_, :], in1=sz[0:np_, t, :], op=ALU.mult),
                  r=['f1', 'szb'], w=['f1'])
            ssv = st8[0:np_, 4, 0:1]
            pg.op('act', lambda e: e.activation(out=sq[0:np_, 0:512], in_=f1[0:np_, :], func=AF.Square, accum_out=ssv),
                  r=['f1'], w=['sq', 'st8'])
            rstd_from_ss(ssv, ssv, 512, np_, 'st8')
            pg.op('dve', lambda e: e.scalar_tensor_tensor(out=oz[0:np_, t, :], in0=f1[0:np_, :], scalar=ssv,
                                                          in1=smv('b_norm', np_), op0=ALU.mult, op1=ALU.mult),
                  r=['f1', 'st8', 'small'], w=['oz'])
            pg.op('dve', lambda e: e.tensor_tensor(
                out=bdec[0:np_, :, :, :].rearrange("p a g n -> p g a n"),
                in0=btk[0:np_, :].rearrange("p (g n) -> p g n", n=64).unsqueeze(2).to_broadcast([np_, 2, 4, 64]),
                in1=ssd8[0:np_, 1, :].rearrange("p (g a) -> p g a", a=4).unsqueeze(3).to_broadcast([np_, 2, 4, 64]),
                op=ALU.mult), r=['btk', 'ssd8'], w=['bdec'])
            i = rot('A'); ph = psA[i]; phk = 'psA%d' % i
            for a4 in range(4):
                pg.op('pe', lambda e, a4=a4: e.matmul(
                    ph[:, a4 * 128:(a4 + 1) * 128], lhsT=bdec[0:np_, a4, :, :].rearrange("p g n -> p (g n)"),
                    rhs=xdt[0:np_, :, :].rearrange("p (g a) d -> p a g d", a=4)[:, a4, :, :],
                    start=True, stop=True), r=['bdec', 'xdt'], w=[phk], inc=(a4 == 3))
            phv = ph[:, :].rearrange("p (a c) -> p a c", c=128)
            for g in range(2):
                sl = slice(g * 64, (g + 1) * 64)
                pg.op('dve', lambda e, g=g, sl=sl: e.tensor_tensor(
                    out=hT[sl, :, :], in0=hT[sl, :, :],
                    in1=ssd8[sl, 2, g * 4:(g + 1) * 4].unsqueeze(2).to_broadcast([64, 4, 64]), op=ALU.mult),
                    r=['hT', 'ssd8', py2k, py3k], w=['hT'])
                pg.op('dve', lambda e, g=g, sl=sl: e.tensor_tensor(
                    out=hT[sl, :, :], in0=hT[sl, :, :], in1=phv[sl, :, g * 64:(g + 1) * 64], op=ALU.add),
                    r=['hT', phk], w=['hT'])
            pg.op('act', lambda e: e.activation(out=hTb[:, :, :], in_=hT[:, :, :], func=AF.Copy), r=['hT'], w=['hTb'])

        BS = [dict(Ah=Ah, Ee=Ee, Gm=Gm, bdec=bdec, xdt=xdt, xtk=xtk, Mh=Mh, btk=btk,
                   e0=ssd8[:, 0, :], e1=ssd8[:, 1, :], e2=ssd8[:, 2, :], sfx=''),
              dict(Ah=xnb.bitcast(F32).rearrange("p (h n) -> p h n", n=128),
                   Ee=f3.rearrange("p (h n) -> p h n", n=128),
                   Gm=biasQ[:, :, :].rearrange("p a b -> p (a b)").rearrange("p (g n) -> p g n", n=128),
                   Mh=qT[:, 0:2, :].rearrange("p a n -> p (a n)").rearrange("p (h n) -> p h n", n=128),
                   bdec=qT[:, 2, :].rearrange("p (a g n) -> p a g n", a=4, g=2),
                   xdt=qT[:, 3, :].rearrange("p (h d) -> p h d", d=64),
                   xtk=PT[0][:, :], btk=PT[1][:, 0:128],
                   e0=st8[:, 5, :], e1=st8[:, 6, :], e2=st8[:, 7, :], sfx='_1')]
        SET1K = ['Ah%d_1' % h for h in range(4)] + ['Ee_1', 'Gm_1', 'bdec_1', 'xdt_1', 'xtk_1', 'Mh_1', 'btk_1', 'ssd8_1',
                                                     'qT', 'PT0', 'PT1', 'biasQ', 'xnb', 'f3', 'st8lf', 'st8c', 'st8x']

        def ssdA(t, np_, S):
            b = BS[S]; x_ = b['sfx']
            K = lambda n: n + x_
            c0 = t * np_
            i = rot('T'); pt = psT[i]; pk = 'psT%d' % i
            for j in range(5):
                pg.op('pe', lambda e, j=j: e.transpose(out=pt[0:np_, j * 128:(j + 1) * 128], in_=xbc[:, j, c0:c0 + np_],
                                                       identity=ident_b[:, :]), r=['xbc', 'cstb'], w=[pk], inc=(j == 4))
            yield
            pg.op('act', lambda e: e.activation(out=b['xtk'][0:np_, :], in_=pt[0:np_, 0:512], func=AF.Copy), r=[pk], w=[K('xtk')])
            pg.op('act', lambda e: e.activation(out=b['btk'][0:np_, :], in_=pt[0:np_, 512:640], func=AF.Copy), r=[pk], w=[K('btk')])
            yield
            pg.op('dve', lambda e: e.tensor_tensor(out=b['xdt'][0:np_, :, :], in0=pt[0:np_, 0:512].rearrange("p (h d) -> p h d", d=64),
                                                   in1=dts[0:np_, t, :].unsqueeze(2).to_broadcast([np_, 8, 64]), op=ALU.mult),
                  r=[pk, 'dts'], w=[K('xdt')])
            yield
            a_ap = asb[0:np_, t, :]
            i = rot('A'); p = psA[i]; pk2 = 'psA%d' % i
            pg.op('pe', lambda e: e.matmul(p[0:np_, 0:8], lhsT=M1f[0:np_, 0:np_], rhs=a_ap, start=True, stop=True),
                  r=['asb', 'cst'], w=[pk2], inc=False)
            pg.op('pe', lambda e: e.matmul(p[0:np_, 8:16], lhsT=M2f[0:np_, 0:np_], rhs=a_ap, start=True, stop=True),
                  r=['asb', 'cst'], w=[pk2], inc=False)
            pg.op('pe', lambda e: e.matmul(p[:, 16:24], lhsT=M1f[0:np_, :], rhs=a_ap, start=True, stop=False),
                  r=['asb', 'cst'], w=[pk2], inc=False)
            pg.op('pe', lambda e: e.matmul(p[:, 16:24], lhsT=M2f[0:np_, :], rhs=a_ap, start=False, stop=True),
                  r=['asb', 'cst'], w=[pk2])
            yield
            pg.op('act', lambda e: e.activation(out=b['e0'][0:np_, :], in_=p[0:np_, 0:8], func=AF.Exp), r=[pk2], w=[K('ssd8')])
            pg.op('act', lambda e: e.activation(out=b['e1'][0:np_, :], in_=p[0:np_, 8:16], func=AF.Exp), r=[pk2], w=[K('ssd8')])
            pg.op('act', lambda e: e.activation(out=b['e2'][:, :], in_=p[:, 16:24], func=AF.Exp), r=[pk2], w=[K('ssd8')])
            yield
            for g in range(2):
                i = rot('A'); pgm = psA[i]; pk3 = 'psA%d' % i
                pg.op('pe', lambda e, g=g, pgm=pgm: e.matmul(pgm[0:np_, 0:np_], lhsT=xbc[g * 64:(g + 1) * 64, 4, c0:c0 + np_],
                                                    rhs=xbc[g * 64:(g + 1) * 64, 5, c0:c0 + np_], start=True, stop=True),
                      r=['xbc'], w=[pk3])
                pg.op('dve', lambda e, g=g, pgm=pgm: e.tensor_tensor(
                    out=b['Gm'][0:np_, g, 0:np_], in0=pgm[0:np_, 0:np_], in1=M1f[0:np_, 0:np_], op=ALU.mult),
                    r=[pk3, 'cst'], w=[K('Gm')])
                yield
            for hh in range(2):
                for h4 in range(4):
                    h = hh * 4 + h4
                    pg.op('dve', lambda e, h=h, h4=h4: e.tensor_scalar(
                        out=b['Ah'][0:np_, h4, 0:np_], in0=M2f[0:np_, 0:np_], scalar1=asb[0:np_, t, h:h + 1], scalar2=None,
                        op0=ALU.mult), r=['cst', 'asb'], w=[K('Ah%d' % h4)])
                    if h4 % 2:
                        yield
                psx = psS[hh]; psk = 'psS%d' % hh
                for h4 in range(4):
                    pg.op('pe', lambda e, h4=h4, psx=psx: e.matmul(
                        psx[0:np_, h4 * 128:h4 * 128 + np_], lhsT=b['Ah'][0:np_, h4, 0:np_], rhs=M1f[0:np_, 0:np_],
                        start=True, stop=True), r=[K('Ah%d' % h4), 'cst'], w=[psk], inc=(h4 == 3))
                yield
                pg.op('act', lambda e, psx=psx: e.activation(
                    out=b['Ee'][0:np_, :, 0:np_],
                    in_=psx[0:np_, :].rearrange("p (h n) -> p h n", n=128)[:, :, 0:np_], func=AF.Exp),
                    r=[psk], w=[K('Ee')])
                yield
                pg.op('dve', lambda e, hh=hh: e.tensor_tensor(
                    out=b['Mh'][0:np_, hh * 4:(hh + 1) * 4, 0:np_], in0=b['Ee'][0:np_, :, 0:np_],
                    in1=b['Gm'][0:np_, hh, 0:np_].unsqueeze(1).to_broadcast([np_, 4, np_]), op=ALU.mult),
                    r=[K('Ee'), K('Gm')], w=[K('Mh')])
                yield
            pg.op('dve', lambda e: e.tensor_tensor(
                out=b['bdec'][0:np_, :, :, :].rearrange("p a g n -> p g a n"),
                in0=b['btk'][0:np_, :].rearrange("p (g n) -> p g n", n=64).unsqueeze(2).to_broadcast([np_, 2, 4, 64]),
                in1=b['e1'][0:np_, :].rearrange("p (g a) -> p g a", a=4).unsqueeze(3).to_broadcast([np_, 2, 4, 64]),
                op=ALU.mult), r=[K('btk'), K('ssd8')], w=[K('bdec')])
            yield

        def ssdB(t, np_, S):
            b = BS[S]; x_ = b['sfx']
            K = lambda n: n + x_
            c0 = t * np_
            io = rot('O'); py = psO[io]; pyk = 'psO%d' % io
            io2 = rot('O'); py2 = psO[io2]; py2k = 'psO%d' % io2
            for h in range(8):
                pg.op('pe', lambda e, h=h: e.matmul(py[0:np_, h * 64:(h + 1) * 64], lhsT=b['Mh'][0:np_, h, 0:np_],
                                                    rhs=b['xdt'][0:np_, h, :], start=True, stop=True),
                      r=[K('Mh'), K('xdt')], w=[pyk], inc=(h == 7))
            yield
            i3 = rot('A'); py3 = psA[i3]; py3k = 'psA%d' % i3
            for h in range(8):
                g = h // 4
                dstp = (py2 if g == 0 else py3)
                pg.op('pe', lambda e, h=h, g=g, dstp=dstp: e.matmul(dstp[0:np_, (h % 4) * 64:(h % 4 + 1) * 64],
                                                         lhsT=xbc[g * 64:(g + 1) * 64, 5, c0:c0 + np_],
                                                         rhs=hTb[g * 64:(g + 1) * 64, h % 4, :], start=True, stop=True),
                      r=['xbc', 'hTb'], w=[py2k if g == 0 else py3k], inc=(h % 4 == 3))
            yield
            yv = f1[0:np_, :].rearrange("p (h d) -> p h d", d=64)
            for g, (dstp, dk) in enumerate(((py2, py2k), (py3, py3k))):
                pg.op('dve', lambda e, g=g, dstp=dstp: e.tensor_tensor(
                    out=yv[:, g * 4:(g + 1) * 4, :], in0=dstp[0:np_, 0:256].rearrange("p (h d) -> p h d", d=64),
                    in1=b['e0'][0:np_, g * 4:(g + 1) * 4].unsqueeze(2).to_broadcast([np_, 4, 64]), op=ALU.mult),
                    r=[dk, K('ssd8')], w=['f1'])
                yield
            pg.op('dve', lambda e: e.tensor_tensor(out=f1[0:np_, :], in0=f1[0:np_, :], in1=py[0:np_, :], op=ALU.add),
                  r=['f1', pyk], w=['f1'])
            pg.op('pool', lambda e: e.tensor_tensor(out=f2[0:np_, :].rearrange("p (h d) -> p h d", d=64),
                                                    in0=b['xtk'][0:np_, :].rearrange("p (h d) -> p h d", d=64),
                                                    in1=smv('b_d', np_).unsqueeze(2).to_broadcast([np_, 8, 64]), op=ALU.mult),
                  r=[K('xtk'), 'small'], w=['f2'])
            yield
            pg.op('dve', lambda e: e.tensor_tensor(out=f1[0:np_, :], in0=f1[0:np_, :], in1=f2[0:np_, :], op=ALU.add),
                  r=['f1', 'f2'], w=['f1'])
            yield
            pg.op('dve', lambda e: e.tensor_tensor(out=f1[0:np_, :], in0=f1[0:np_, :], in1=sz[0:np_, t, :], op=ALU.mult),
                  r=['f1', 'szb'], w=['f1'])
            yield
            ssv = st8[0:np_, 4, 0:1]
            pg.op('act', lambda e: e.activation(out=sq[0:np_, 0:512], in_=f1[0:np_, :], func=AF.Square, accum_out=ssv),
                  r=['f1'], w=['sq', 'st8'])
            yield
            pg.op('dve', lambda e: e.tensor_scalar(out=ssv, in0=ssv, scalar1=1.0 / 512, scalar2=EPS,
                                                    op0=ALU.mult, op1=ALU.add), r=['st8'], w=['st8'])
            yield
            pg.op('pool', lambda e: e.tensor_tensor(out=ssv, in0=ssv, in1=nhalf[0:np_, 0:1], op=ALU.pow),
                  r=['st8', 'epst'], w=['st8'])
            yield
            pg.op('dve', lambda e: e.scalar_tensor_tensor(out=oz[0:np_, t, :], in0=f1[0:np_, :], scalar=ssv,
                                                          in1=smv('b_norm', np_), op0=ALU.mult, op1=ALU.mult),
                  r=['f1', 'st8', 'small'], w=['oz'])
            yield
            i = rot('A'); ph = psA[i]; phk = 'psA%d' % i
            for a4 in range(4):
                pg.op('pe', lambda e, a4=a4: e.matmul(
                    ph[:, a4 * 128:(a4 + 1) * 128], lhsT=b['bdec'][0:np_, a4, :, :].rearrange("p g n -> p (g n)"),
                    rhs=b['xdt'][0:np_, :, :].rearrange("p (g a) d -> p a g d", a=4)[:, a4, :, :],
                    start=True, stop=True), r=[K('bdec'), K('xdt')], w=[phk], inc=(a4 == 3))
            yield
            phv = ph[:, :].rearrange("p (a c) -> p a c", c=128)
            for g in range(2):
                sl = slice(g * 64, (g + 1) * 64)
                pg.op('dve', lambda e, g=g, sl=sl: e.tensor_tensor(
                    out=hT[sl, :, :], in0=hT[sl, :, :],
                    in1=b['e2'][sl, g * 4:(g + 1) * 4].unsqueeze(2).to_broadcast([64, 4, 64]), op=ALU.mult),
                    r=['hT', K('ssd8'), py2k, py3k], w=['hT'])
                yield
                pg.op('dve', lambda e, g=g, sl=sl: e.tensor_tensor(
                    out=hT[sl, :, :], in0=hT[sl, :, :], in1=phv[sl, :, g * 64:(g + 1) * 64], op=ALU.add),
                    r=['hT', phk], w=['hT'])
                yield
            pg.op('act', lambda e: e.activation(out=hTb[:, :, :], in_=hT[:, :, :], func=AF.Copy), r=['hT'], w=['hTb'])
            yield

        def ssd_pipelined(NT, np_):
            barrier(AEK + SET1K)
            for _ in ssdA(0, np_, 0):
                pass
            for t in range(NT):
                gens = [ssdB(t, np_, t % 2)]
                if t + 1 < NT:
                    gens.append(ssdA(t + 1, np_, (t + 1) % 2))
                while gens:
                    for g_ in list(gens):
                        try:
                            next(g_)
                        except StopIteration:
                            gens.remove(g_)
            barrier(AEK + SET1K)

        def process_group(l, kind, Q):
            samp = (kind == 's')
            np_ = 64 if samp else 128
            NT = 4
            NTOK = NT * np_
            xsrc = (D['xs'] if samp else D['xp']) if l == 0 else (x1s if samp else x1p)
            xdst = (D['ys'] if samp else D['yp']) if l == NL - 1 else (x1s if samp else x1p)
            row0 = 0 if samp else Q * 512

            def rows(t):
                return slice(row0 + t * np_, row0 + (t + 1) * np_)

            norm_tiles(lambda t: xsrc[rows(t), :], NT, np_, lambda n: smv('g_norm', n),
                       (lambda t: [('xd', kind, row0 + t * np_)]) if l > 0 else None)

            if cfg.get('marks'): print('MARK', kind, Q, 'C_proj', pg.nops)
            wt, wk = load_w(D['w_in'], l, COL['c_q'], 512)
            for t, p, pk in pipe_tiles(NT, lambda t: proj_tm(xnT, 'xnT', t * np_, np_, wt, wk, 512)):
                evac_q(p, pk, t, np_, 8, 'c_qnorm')
            wt, wk = load_w(D['w_in'], l, COL['c_k'], 512)
            if samp:
                kTs = sb_kTs
            for t, p, pk in pipe_tiles(NT, lambda t: proj_tm(xnT, 'xnT', t * np_, np_, wt, wk, 512)):
                if samp:
                    evac_k(p, pk, np_, 8, 'c_knorm', D['sc_k'][l, rows(t), :],
                           lambda t=t: kTs[:, 0:4, t * 64:(t + 1) * 64], 'kTs')
                else:
                    gt = Q * 4 + t
                    evac_k(p, pk, np_, 8, 'c_knorm', D['pc_k'][l, rows(t), :],
                           lambda gt=gt: kT_c[:, :, gt * 128:(gt + 1) * 128], 'kT_c')
            wt, wk = load_w(D['w_in'], l, COL['c_v'], 512)
            for t, p, pk in pipe_tiles(NT, lambda t: proj_tm(xnT, 'xnT', t * np_, np_, wt, wk, 512)):
                if samp:
                    evac_v(p, pk, np_, 8, D['sc_v'][l, rows(t), :], vas[0:np_, t, :, 0:64], 'vas')
                else:
                    evac_v(p, pk, np_, 8, D['pc_v'][l, rows(t), :], va_c[:, Q * 4 + t, :, 0:64], 'va_c')
            wt, wk = load_w(D['w_in'], l, COL['c_f'], 8)
            if not samp:
                psv, pk = proj8_all(NT, np_, wt, wk)
                softplus4(psv, pk, smv('c_fbias', np_), lfs[0:np_, :, :], 'lfs', np_, NT, neg=True)
                out_tokens.append(pg.dma('sp', D['pc_f'][l, row0:row0 + NT * np_, :].rearrange("(t p) h -> p t h", p=np_),
                                         lfs[0:np_, :, :], r=['lfs'], nonc=True))
                for t in range(NT):
                    gt = Q * 4 + t
                    cumsum_tile(lfs[0:np_, t, :], 'lfs', 128, cum[:, gt, :], 'cum',
                                None if gt == 0 else (S127f, cum[:, gt - 1, :]), ['cum'])
            for t, p, pk in (pipe_tiles(NT, lambda t: proj_tm(xnT, 'xnT', t * np_, np_, wt, wk, 8)) if samp else ()):
                lf = st8[0:np_, 5, :]
                softplus_from(p, pk, smv('c_fbias', np_), lf, 'st8lf', np_, neg=True)
                if samp:
                    out_tokens.append(pg.dma('sp', D['sc_f'][l, rows(t), :], lf, r=['st8lf'], nonc=True))
                    carry = None
                    for kb in range(8):
                        pg.dma('sp', st8[:, 7, :], D['cc_f'][l, t, kb * 128:(kb + 1) * 128, :], w=['st8x'], nonc=True)
                        cumsum_tile(st8[:, 7, :], 'st8x', 128, cums[:, t, kb, :], 'cums',
                                    None if kb == 0 else (S127f, cums[:, t, kb - 1, :]), ['cums'])
                    cumsum_tile(lf, 'st8lf', 64, cums[0:64, t, 8, :], 'cums', (S127f[:, 0:64], cums[:, t, 7, :]), ['cums'])
                else:
                    gt = Q * 4 + t
                    out_tokens.append(pg.dma('sp', D['pc_f'][l, rows(t), :], lf, r=['st8lf'], nonc=True))
                    cumsum_tile(lf, 'st8lf', 128, cum[:, gt, :], 'cum',
                                None if gt == 0 else (S127f, cum[:, gt - 1, :]), ['cum'])
            wt, wk = load_w(D['w_in'], l, COL['c_z'], 512)
            for t, p, pk in pipe_tiles(NT, lambda t: proj_tm(xnT, 'xnT', t * np_, np_, wt, wk, 512)):
                pg.op('act', lambda e, p=p, t=t: e.activation(out=sz[0:np_, t, :], in_=p, func=AF.Silu), r=[pk], w=['sz'])
            if cfg.get('marks'): print('MARK', kind, Q, 'C_attn', pg.nops)
            if not samp:
                nkb = 4 * Q + 4
                i = rot('A'); pb = psA[i]; pbk = 'psA%d' % i
                pg.op('pe', lambda e: e.matmul(pb[:, 0:8], lhsT=S127f, rhs=cum[:, nkb - 1, :], start=True, stop=True),
                      r=['cum', 'cst'], w=[pbk])
                pg.op('act', lambda e: e.activation(out=st8[:, 6, :], in_=pb[:, 0:8], func=AF.Copy), r=[pbk], w=['st8c'])
                pg.op('dve', lambda e: e.tensor_tensor(out=biasQ[:, 0:nkb, :],
                                                       in0=st8[:, 6, :].unsqueeze(1).to_broadcast([128, nkb, 8]),
                                                       in1=cum[:, 0:nkb, :], op=ALU.subtract), r=['st8c', 'cum'], w=['biasQ'])
                for h in range(8):
                    hp, hb = h // 2, (h % 2) * 64
                    kbs = []
                    for kb in range(nkb):
                        dI = kb - 4 * Q
                        d = dict(kT=kT_c[hb:hb + 64, hp, kb * 128:(kb + 1) * 128], v=va_c[:, kb, h, 0:65], nk=128,
                                 bias=biasQ[:, kb, h:h + 1], bkeys=['biasQ'], q0=max(dI, 0) * 128, keys=['kT_c', 'va_c'])
                        kbs.append(d)
                    kb2 = []
                    for kb, d in enumerate(kbs):
                        dI = kb - 4 * Q
                        if dI < 0:
                            kb2.append(d)
                        else:
                            d['diag'] = True
                            kb2.append(d)
                    attend_fox(qT[hb:hb + 64, hp, :], 512, [(j * 128, 128) for j in range(4)], kb2, ['qT'],
                               attn_out(h * 64, 4, 128))
            else:
                for t in range(NT):
                    i = rot('A'); pb = psA[i]; pbk = 'psA%d' % i
                    pg.op('pe', lambda e, t=t: e.matmul(pb[:, 0:8], lhsT=S63f[0:64, :], rhs=cums[0:64, t, 8, :],
                                                        start=True, stop=True), r=['cums', 'cst'], w=[pbk])
                    pg.op('act', lambda e: e.activation(out=st8[:, 6, :], in_=pb[:, 0:8], func=AF.Copy), r=[pbk], w=['st8c'])
                    pg.op('dve', lambda e, t=t: e.tensor_tensor(out=biasQ[:, 0:9, :],
                                                                in0=st8[:, 6, :].unsqueeze(1).to_broadcast([128, 9, 8]),
                                                                in1=cums[:, t, :, :], op=ALU.subtract),
                          r=['st8c', 'cums'], w=['biasQ'])
                    load_cache_kv(D['cc_k'][l, t], D['cc_v'][l, t], 8, 8)
                    for h in range(8):
                        hp, hb = h // 2, (h % 2) * 64
                        kbs = []
                        for kb in range(8):
                            kbs.append(dict(kT=ckT[hb:hb + 64, hp, kb * 128:(kb + 1) * 128], v=cva_[:, kb, h, 0:65], nk=128,
                                            bias=biasQ[:, kb, h:h + 1], bkeys=['biasQ'], q0=0, keys=['ckT', 'cvaS']))
                        kbs.append(dict(kT=kTs[hb:hb + 64, hp, t * 64:(t + 1) * 64], v=vas[0:64, t, h, 0:65], nk=64,
                                        bias=biasQ[0:64, 8, h:h + 1], bkeys=['biasQ'], q0=0, keys=['kTs', 'vas'], diag=True))
                        attend_fox(qT[hb:hb + 64, hp, t * 64:(t + 1) * 64], 64, [(0, 64)], kbs, ['qT'],
                                   attn_out_s(h * 64, t))
            oz_to_T('c', NT, np_, 4)

            if cfg.get('marks'): print('MARK', kind, Q, 'A_proj', pg.nops)
            wt, wk = load_w(D['w_in'], l, COL['a_q'], 512)
            for t, p, pk in pipe_tiles(NT, lambda t: proj_tm(xnT, 'xnT', t * np_, np_, wt, wk, 512)):
                evac_q(p, pk, t, np_, 8, 'a_qnorm')
            wt, wk = load_w(D['w_in'], l, COL['a_k'], 512)
            for t, p, pk in pipe_tiles(NT, lambda t: proj_tm(xnT, 'xnT', t * np_, np_, wt, wk, 512)):
                if samp:
                    evac_k(p, pk, np_, 8, 'a_knorm', D['sa_k'][l, rows(t), :],
                           lambda t=t: kTs[:, 0:4, t * 64:(t + 1) * 64], 'kTs')
                else:
                    gt = Q * 4 + t
                    od = D['pa_k'][l, (gt - 28) * 128:(gt - 27) * 128, :] if gt >= 28 else None
                    evac_k(p, pk, np_, 8, 'a_knorm', od,
                           lambda gt=gt: kT_a[:, :, (gt % 8) * 128:(gt % 8 + 1) * 128], 'kT_a')
            wt, wk = load_w(D['w_in'], l, COL['a_v'], 512)
            for t, p, pk in pipe_tiles(NT, lambda t: proj_tm(xnT, 'xnT', t * np_, np_, wt, wk, 512)):
                if samp:
                    evac_v(p, pk, np_, 8, D['sa_v'][l, rows(t), :], vas[0:np_, t, :, 0:64], 'vas')
                else:
                    gt = Q * 4 + t
                    od = D['pa_v'][l, (gt - 28) * 128:(gt - 27) * 128, :] if gt >= 28 else None
                    evac_v(p, pk, np_, 8, od, va_a[:, gt % 8, :, 0:64], 'va_a')
            wt, wk = load_w(D['w_in'], l, COL['a_z'], 512)
            for t, p, pk in pipe_tiles(NT, lambda t: proj_tm(xnT, 'xnT', t * np_, np_, wt, wk, 512)):
                pg.op('act', lambda e, p=p, t=t: e.activation(out=sz[0:np_, t, :], in_=p, func=AF.Silu), r=[pk], w=['sz'])
            for t in range(NT):
                gt = Q * 4 + t
                if samp:
                    load_cache_kv(D['ca_k'][l, t], D['ca_v'][l, t], 4, 8)
                for hq in range(2):
                    acc = None
                    for h4 in range(4):
                        h = hq * 4 + h4
                        hp, hb = h // 2, (h % 2) * 64
                        kbs = []
                        if not samp:
                            for i5 in range(5):
                                gk = gt - 4 + i5
                                if gk < 0:
                                    continue
                                s8 = gk % 8
                                d_ = dict(kT=kT_a[hb:hb + 64, hp, s8 * 128:(s8 + 1) * 128], v=va_a[:, s8, h, 0:65], nk=128,
                                          bias=None, q0=0, keys=['kT_a', 'va_a'])
                                if i5 in (1, 2):
                                    d_['bias'] = expB[:, h, i5 * 128:i5 * 128 + 1]; d_['bkeys'] = ['expB']
                                else:
                                    d_['badd'] = expB[:, h, i5 * 128:(i5 + 1) * 128]; d_['mkeys'] = ['expB']
                                kbs.append(d_)
                        else:
                            for kb in range(4):
                                d_ = dict(kT=ckT[hb:hb + 64, hp, kb * 128:(kb + 1) * 128], v=cva_[:, kb, h, 0:65], nk=128,
                                          bias=None, q0=0, keys=['ckT', 'cvaS'])
                                if kb in (0, 1, 2):
                                    d_['bias'] = expB[:, h, kb * 128:kb * 128 + 1]; d_['bkeys'] = ['expB']
                                else:
                                    d_['badd'] = expB[:, h, kb * 128:kb * 128 + 64]; d_['mkeys'] = ['expB']
                                kbs.append(d_)
                            kbs.append(dict(kT=kTs[hb:hb + 64, hp, t * 64:(t + 1) * 64], v=vas[0:64, t, h, 0:65], nk=64,
                                            bias=None, badd=expB[:, h, 512:576], mkeys=['expB'], q0=0, keys=['kTs', 'vas']))
                        a2 = None if acc is None else (acc[0], acc[1], h4, False)
                        acc = attend(qT[hb:hb + 64, hp, t * np_:(t + 1) * np_], np_, [(0, np_)], kbs, ['qT'], None, acc=a2)
                    pov, pok = acc
                    rl = st8[0:np_, 2, 0:4]
                    pg.op('dve', lambda e, pov=pov: e.reciprocal(out=rl, in_=pov[0:np_, 0:4, 64]), r=[pok], w=['st8'])
                    tmp = f2[0:np_, 0:256].rearrange("p (j d) -> p j d", d=64)
                    pg.op('dve', lambda e, pov=pov, tmp=tmp: e.tensor_tensor(out=tmp, in0=pov[0:np_, 0:4, 0:64],
                                                                           in1=rl.unsqueeze(2).to_broadcast([np_, 4, 64]), op=ALU.mult),
                          r=[pok, 'st8'], w=['f2'])
                    pg.op('pool', lambda e, tmp=tmp, hq=hq, t=t: e.tensor_tensor(
                        out=oz[0:np_, t, hq * 256:(hq + 1) * 256].rearrange("p (j d) -> p j d", d=64), in0=tmp,
                        in1=sz[0:np_, t, hq * 256:(hq + 1) * 256].rearrange("p (j d) -> p j d", d=64), op=ALU.mult),
                        r=['f2', 'sz'], w=['oz'])
            oz_to_T('a', NT, np_, 4)

            if cfg.get('marks'): print('MARK', kind, Q, 'M', pg.nops)
            wt, wk = load_w(D['w_in'], l, COL['m_q'], 512)
            for t, p, pk in pipe_tiles(NT, lambda t: proj_tm(xnT, 'xnT', t * np_, np_, wt, wk, 512)):
                pg.op('act', lambda e, p=p, t=t: e.activation(out=sz[0:np_, t, 0:256], in_=p[:, 256:512], func=AF.Silu),
                      r=[pk], w=['sz'])
                evac_q(p[:, 0:256], pk, t, np_, 4, 'm_qnorm')
            if not samp:
                for h in range(4):
                    hp, hb = h // 2, (h % 2) * 64
                    kbs = [dict(kT=kT_m[hb:hb + 64, hp, kb * 128:(kb + 1) * 128], v=va_m[:, kb, h, 0:65], nk=128, bias=None,
                                q0=0, keys=['kT_m', 'va_m']) for kb in range(2)]
                    attend(qT[hb:hb + 64, hp, :], 512, [(j * 128, 128) for j in range(4)], kbs, ['qT'],
                           attn_out(h * 64, 4, 128))
            else:
                for t in range(NT):
                    load_cache_kv(D['cm_k'][l, t], D['cm_v'][l, t], 2, 4)
                    for h in range(4):
                        hp, hb = h // 2, (h % 2) * 64
                        kbs = [dict(kT=ckT[hb:hb + 64, hp, kb * 128:(kb + 1) * 128], v=cva_[:, kb, h, 0:65], nk=128, bias=None,
                                    q0=0, keys=['ckT', 'cvaS']) for kb in range(2)]
                        attend(qT[hb:hb + 64, hp, t * 64:(t + 1) * 64], 64, [(0, 64)], kbs, ['qT'], attn_out_s(h * 64, t))
            oz_to_T('m', NT, np_, 2)

            if cfg.get('marks'): print('MARK', kind, Q, 'B_proj', pg.nops)
            wt, wk = load_w(D['w_in'], l, COL['b_z'], 512)
            for t, p, pk in pipe_tiles(NT, lambda t: proj_tm(xnT, 'xnT', t * np_, np_, wt, wk, 512)):
                pg.op('act', lambda e, p=p, t=t: e.activation(out=sz[0:np_, t, :], in_=p, func=AF.Silu), r=[pk], w=['sz', 'szb'])
            wt, wk = load_w(D['w_in'], l, COL['b_dt'], 8)
            psv, pk = proj8_all(NT, np_, wt, wk)
            softplus4(psv, pk, smv('b_dt_bias', np_), dts[0:np_, :, :], 'dts', np_, NT)
            pg.op('dve', lambda e: e.tensor_tensor(out=asb[0:np_, :, :], in0=dts[0:np_, :, :],
                                                   in1=aneg[0:np_, :].unsqueeze(1).to_broadcast([np_, NT, 8]), op=ALU.mult),
                  r=['dts', 'aneg'], w=['asb'])
            for half in range(2):
                js = slice(half * 3, half * 3 + 3)
                if samp:
                    for t in range(NT):
                        for jj in range(3):
                            j = half * 3 + jj
                            pg.dma('sp', raw[:, jj, t, 0:3],
                                   D['sb_conv'][l, t][:, j * 128:(j + 1) * 128].rearrange("r p -> p r"), w=['raw'], nonc=True)
                else:
                    pg.op('pool', lambda e, js=js: cp(e, out=raw[:, :, 0, 0:3], in_=halo[:, js, :]), r=['halo'], w=['raw'])
                wt, wk = load_w(D['w_in'], l, COL['b_xbc'] + half * 384, 384)
                for jj in range(3):
                    i = rot('A'); p = psA[i]; pk = 'psA%d' % i
                    for kc in range(8):
                        pg.op('pe', lambda e, kc=kc, jj=jj, p=p, wt=wt: e.matmul(p[:, 0:NTOK], lhsT=wt[:, kc, jj * 128:(jj + 1) * 128],
                                                                     rhs=xnT[:, kc, 0:NTOK], start=(kc == 0), stop=(kc == 7)),
                              r=['xnT'] + list(wk), w=[pk], inc=(kc == 7))
                    pg.op('act', lambda e, jj=jj, p=p: e.activation(out=raw[:, jj, :, 3:3 + np_],
                                                             in_=p[:, 0:NTOK].rearrange("p (t n) -> p t n", n=np_), func=AF.Copy),
                          r=[pk], w=['raw'])
                if not samp:
                    for t in range(1, NT):
                        pg.op('pool', lambda e, t=t: cp(e, out=raw[:, :, t, 0:3], in_=raw[:, :, t - 1, np_:np_ + 3]),
                              r=['raw'], w=['raw'])
                    pg.op('pool', lambda e, js=js: cp(e, out=halo[:, js, :], in_=raw[:, :, NT - 1, np_:np_ + 3]),
                          r=['raw'], w=['halo'])
                    if Q == NQ - 1:
                        for jj in range(3):
                            j = half * 3 + jj
                            out_tokens.append(pg.dma('sp', D['pb_c'][l][:, j * 128:(j + 1) * 128].rearrange("r p -> p r"),
                                                     halo[:, j, :], r=['halo'], nonc=True))
                else:
                    for t in range(NT):
                        for jj in range(3):
                            j = half * 3 + jj
                            out_tokens.append(pg.dma('sp', D['sb_c'][l, t][:, j * 128:(j + 1) * 128].rearrange("r p -> p r"),
                                                     raw[:, jj, t, np_:np_ + 3], r=['raw'], nonc=True))
                for jj in range(3):
                    j = half * 3 + jj
                    cv = cva[:, 0:NTOK].rearrange("p (t n) -> p t n", n=np_)
                    pg.op('dve', lambda e, j=j, jj=jj, cv=cv: e.tensor_scalar(out=cv, in0=raw[:, jj, :, 0:np_], scalar1=cw[:, j, 0:1],
                                                                scalar2=cw[:, j, 4:5], op0=ALU.mult, op1=ALU.add),
                          r=['raw', 'cw'], w=['cva'])
                    for tap in range(1, 4):
                        pg.op('dve', lambda e, j=j, jj=jj, tap=tap, cv=cv: e.scalar_tensor_tensor(
                            out=cv, in0=raw[:, jj, :, tap:tap + np_], scalar=cw[:, j, tap:tap + 1], in1=cv,
                            op0=ALU.mult, op1=ALU.add), r=['raw', 'cw', 'cva'], w=['cva'])
                    pg.op('act', lambda e, j=j: e.activation(out=xbc[:, j, 0:NTOK], in_=cva[:, 0:NTOK], func=AF.Silu),
                          r=['cva'], w=['xbc'])
            if samp:
                barrier(AEK + SET1K)
                for _ in ssdA(0, np_, 0):
                    pass
                for t in range(NT):
                    state_load(D['sb_ssm'][l, t])
                    gens = [ssdB(t, np_, t % 2)]
                    if t + 1 < NT:
                        gens.append(ssdA(t + 1, np_, (t + 1) % 2))
                    while gens:
                        for g_ in list(gens):
                            try:
                                next(g_)
                            except StopIteration:
                                gens.remove(g_)
                    state_store(D['sb_s'][l, t])
                barrier(AEK + SET1K)
            else:
                ssd_pipelined(NT, np_)
            if (not samp) and Q == NQ - 1 and not NOSTATE:
                state_store(D['pb_s'][l])
            oz_to_T('b', NT, np_, 4)

            if cfg.get('marks'): print('MARK', kind, Q, 'merge', pg.nops)
            barrier(AEK)
            brs = (('a', 'w_pa', 4), ('b', 'w_pb', 4), ('c', 'w_pc', 4), ('m', 'w_pm', 2))
            for c in range(2):
                for bi, (br, wn, nkc) in enumerate(brs):
                    wg, wgk = load_w(D['w_in'], l, COL['gate'] + bi * 1024 + c * 512, 512)
                    wp, wpk = load_w(D[wn], l, c * 512, 512, nk=nkc)
                    def mproj(t, wg=wg, wgk=wgk, wp=wp, wpk=wpk, br=br, nkc=nkc):
                        p1, pk1 = proj_tm(xnT, 'xnT', t * np_, np_, wg, wgk, 512)
                        isx = rot('S'); p2 = psS[isx]; pk2 = 'psS%d' % isx
                        for kc in range(nkc):
                            pg.op('pe', lambda e, kc=kc: e.matmul(
                                p2[0:np_, :], lhsT=ozT[br][:, kc, t * np_:(t + 1) * np_], rhs=wp[:, kc, :],
                                start=(kc == 0), stop=(kc == nkc - 1)), r=['ozT_' + br] + list(wpk), w=[pk2], inc=(kc == nkc - 1))
                        return p1, pk1, p2, pk2
                    for t, p1, pk1, p2, pk2 in pipe_tiles(NT, mproj):
                        pg.op('act', lambda e, p1=p1: e.activation(out=sig[0:np_, :], in_=p1, func=AF.Sigmoid), r=[pk1], w=['cva'])
                        if bi == 0:
                            pg.op('dve', lambda e, t=t, p2=p2: e.tensor_tensor(out=macc[0:np_, t, :], in0=sig[0:np_, :],
                                                                               in1=p2[0:np_, :], op=ALU.mult),
                                  r=['cva', pk2], w=['macc%d' % t])
                        else:
                            pg.op('dve', lambda e, t=t, p2=p2: e.tensor_tensor(out=f1[0:np_, :], in0=sig[0:np_, :],
                                                                               in1=p2[0:np_, :], op=ALU.mult),
                                  r=['cva', pk2], w=['f1'])
                            pg.op('pool', lambda e, t=t: e.tensor_tensor(out=macc[0:np_, t, :], in0=macc[0:np_, t, :],
                                                                         in1=f1[0:np_, :], op=ALU.add),
                                  r=['f1', 'macc%d' % t], w=['macc%d' % t])
                for t in range(NT):
                    pg.op('act', lambda e, t=t: e.activation(out=b1[0:np_, :], in_=macc[0:np_, t, :], func=AF.Copy),
                          r=['macc%d' % t], w=['b1'])
                    transposes(lambda j: b1[0:np_, j * 128:(j + 1) * 128], 4, np_,
                               lambda e, v, t=t, c=c: cp(e, out=mT[:, c * 4:(c + 1) * 4, t * np_:(t + 1) * np_], in_=v),
                               ['b1'], ['xbc', 'Mh'], evac='dve')
            wos = [load_w(D['w_out'], l, c * 512, 512) for c in range(2)]
            for t in range(NT):
                i = rot('X'); xk = 'xt%d' % i
                pg.dma('sp', xt[0:np_, i, :], xsrc[rows(t), :], r=[('xd', kind, row0 + t * np_)] if l > 0 else [], w=[xk])
                for c in range(2):
                    wo, wok = wos[c]
                    p, pk = proj_tm(mT, 'xbc', t * np_, np_, wo, wok, 512, xkeys=['Mh'])
                    pg.op('dve', lambda e, i=i, p=p, c=c: e.tensor_tensor(out=xt[0:np_, i, c * 512:(c + 1) * 512],
                                                                          in0=xt[0:np_, i, c * 512:(c + 1) * 512], in1=p,
                                                                          op=ALU.add), r=[xk, pk], w=[xk])
                tok = pg.dma('sp', xdst[rows(t), :], xt[0:np_, i, :], r=[xk], w=[('xd', kind, row0 + t * np_)])
                out_tokens.append(tok)

        vflat = va_c[:, :, :, :].rearrange("p a h c -> p (a h c)")
        sb_kTs = vflat[:, 0:1024].rearrange("p (k n) -> p k n", n=256)
        vas = vflat[:, 1024:1024 + 2112].rearrange("p (t h c) -> p t h c", t=4, h=8)
        ckT = vflat[:, 3136:3136 + 4096].rearrange("p (k n) -> p k n", n=1024)
        cva_ = vflat[:, 7232:7232 + 4224].rearrange("p (a h c) -> p a h c", a=8, h=8)
        cums = kT_a[:, 0, 0:576].bitcast(F32).rearrange("p (t k h) -> p t k h", t=4, k=9)
        SKEYS = ['kTs', 'vas', 'ckT', 'cvaS']

        def load_cache_kv(kd, vd, nblk, nh):
            n = nh * 64
            for kb in range(nblk):
                stg, sk = ((b1, 'b1'), (xnb, 'xnb'))[kb % 2]
                pg.dma('pool', stg[:, 0:n], kd[kb * 128:(kb + 1) * 128, :], w=[sk])
                transposes(lambda j, stg=stg: stg[:, j * 128:(j + 1) * 128], nh // 2, 128,
                           lambda e, v, kb=kb: cp(e, out=ckT[:, 0:nh // 2, kb * 128:(kb + 1) * 128], in_=v),
                           [sk], ['ckT'], evac='act' if kb % 2 else 'dve')
                pg.dma('pool', cva_[:, kb, 0:nh, 0:64], vd[kb * 128:(kb + 1) * 128, :].rearrange("p (h d) -> p h d", d=64),
                       w=['cvaS'])

        def state_load(src):
            pg.dma('sp', cva[0:64, :].rearrange("p (h n) -> p h n", n=64), src.rearrange("h p n -> p h n"), w=['cva'])
            i = rot('A'); p = psA[i]; pk = 'psA%d' % i
            for h in range(8):
                pg.op('pe', lambda e, h=h: e.matmul(p[0:64, h * 64:(h + 1) * 64], lhsT=cva[0:64, h * 64:(h + 1) * 64],
                                                    rhs=identf[0:64, :], start=True, stop=True),
                      r=['cva', 'identf'], w=[pk], inc=(h == 7))
            for g in range(2):
                pg.op('act' if g else 'dve', lambda e, g=g: cp(
                    e, out=hT[g * 64:(g + 1) * 64, :, :], in_=p[0:64, g * 256:(g + 1) * 256].rearrange("p (a d) -> p a d", d=64)),
                    r=[pk], w=['hT'])
            pg.op('act', lambda e: e.activation(out=hTb[:, :, :], in_=hT[:, :, :], func=AF.Copy), r=['hT'], w=['hTb'])

        def state_store(dst):
            for g in range(2):
                i = rot('A'); p = psA[i]; pk = 'psA%d' % i
                for a in range(4):
                    pg.op('pe', lambda e, g=g, a=a, p=p: e.matmul(p[0:64, a * 64:(a + 1) * 64], lhsT=hT[g * 64:(g + 1) * 64, a, :],
                                                             rhs=identf[g * 64:(g + 1) * 64, :], start=True, stop=True),
                          r=['hT', 'identf'], w=[pk], inc=(a == 3))
                pg.op('act' if g else 'dve', lambda e, g=g, p=p: cp(e, out=cva[0:64, g * 256:(g + 1) * 256], in_=p[0:64, 0:256]),
                      r=[pk], w=['cva'])
            out_tokens.append(pg.dma('sp', dst.rearrange("h p n -> p h n"), cva[0:64, :].rearrange("p (h n) -> p h n", n=64),
                                     r=['cva']))

        def attn_out_t(h0col, t, np_):
            def fn(pov, pok):
                rl = st8[0:np_, 2, 0:1]
                pg.op('dve', lambda e: e.reciprocal(out=rl, in_=pov[0:np_, 0, 64:65]), r=[pok], w=['st8'])
                pg.op('dve', lambda e: e.scalar_tensor_tensor(out=oz[0:np_, t, h0col:h0col + 64], in0=pov[0:np_, 0, 0:64],
                                                              scalar=rl, in1=sz[0:np_, t, h0col:h0col + 64],
                                                              op0=ALU.mult, op1=ALU.mult), r=[pok, 'st8', 'sz'], w=['oz'])
            return fn

        def attn_out_s(h0col, t):
            return attn_out_t(h0col, t, 64)

        def attend_fox(qT_ap, NQc, qtiles, kblocks, qkeys, out_fn):
            for kb in kblocks:
                if kb.get('diag'):
                    kb['diagmask'] = True
            attend(qT_ap, NQc, qtiles, kblocks, qkeys, out_fn)

        def memory_kv(l):
            def src(t):
                return D['memp'][t * 128:(t + 1) * 128, :]
            barrier(AEK)
            pg.dma('sp', AE[:, 0:1024], D['mnorm'][l:l + 1, :].broadcast_to([128, 1024]), w=AEK)
            norm_tiles(src, 2, 128, lambda n: AE[0:n, 0:1024], lambda t: AEK)
            barrier(AEK)
            wt, wk = load_w(D['w_mkv'], l, 0, 512)
            for t, p, pk in pipe_tiles(2, lambda t: proj_tm(xnT, 'xnT', t * 128, 128, wt, wk, 512)):
                evac_v(p[:, 256:512], pk, 128, 4, D['pm_v'][l, t * 128:(t + 1) * 128, :], va_m[:, t, :, 0:64], 'va_m')
                evac_k(p[:, 0:256], pk, 128, 4, 'm_knorm', D['pm_k'][l, t * 128:(t + 1) * 128, :],
                       lambda t=t: kT_m[:, :, t * 128:(t + 1) * 128], 'kT_m')

        try:
          for l in range(NL):
            layer_consts(l)
            memory_kv(l)
            pg.op('pool', lambda e: e.memset(hT[:], 0.0), w=['hT'])
            pg.op('pool', lambda e: e.memset(hTb[:], 0.0), w=['hTb'])
            pg.op('pool', lambda e: e.memset(halo[:], 0.0), w=['halo'])
            for Q in range(NQ):
                process_group(l, 'p', Q)
            if DO_S:
                barrier(['va_c', 'kT_a', 'cums'] + SKEYS)
                pg.op('pool', lambda e: e.memset(vas, 1.0), w=['vas'])
                pg.op('pool', lambda e: e.memset(cva_, 1.0), w=['cvaS'])
                process_group(l, 's', 0)
                barrier(['va_c', 'kT_a', 'cums'] + SKEYS)
                pg.op('pool', lambda e: e.memset(va_c[:], 1.0), w=['va_c'])
        except StopBuild:
            pass
        pg.maxops = None
        pg.op('pe', lambda e: e.matmul(psA[0][0:1, 0:1], lhsT=cstb[:, 0:1], rhs=cstb[:, 0:1], start=True, stop=True), r=['cstb'], w=['psA0'])
        pg.op('act', lambda e: e.activation(out=bar[:, 1:2], in_=bar[:, 1:2], func=AF.Copy), r=['psA0'], w=['bar2'])
        pg.op('dve', lambda e: e.tensor_copy(out=bar[:, 1:2], in_=bar[:, 1:2]), w=['bar2'])
        pg.op('pool', lambda e: e.tensor_copy(out=bar[:, 1:2], in_=bar[:, 1:2]), w=['bar2'])
        out_tokens.append(pg.dma('sp', D['pb_c'][0, 0:1, 0:2], bar[0:1, 0:2], r=['bar2', 'bar'], w=['zz']) if False else None)
        out_tokens[:] = [t for t in out_tokens if t is not None]
        pg.op('pool', lambda e: e.memset(bar[:, 0:1], 0.0), r=['bar2'], w=['bar'])
        pg.finish(out_tokens)
        pg._wait('sp', ('c', 'pool', pg.cnt['pool'] - 1))
        print("ops:", pg.nops, "dmas:", pg.dma_n, "cnt:", pg.cnt)
    return nc


def _consts():
    c = np.zeros((128, NCST), np.float32)
    k = np.arange(128)[:, None]
    m = np.arange(128)[None, :]
    c[:, C_M1:C_M1 + 128] = (k <= m)
    c[:, C_M2:C_M2 + 128] = (k > m)
    c[:, C_S127:C_S127 + 128] = (k == 127)
    c[:, C_S63:C_S63 + 128] = (k == 63)
    band = np.ones((128, 5, 128), np.float32)
    s = np.arange(128)[:, None]
    t = np.arange(128)[None, :]
    band[:, 0, :] = 1.0 - ((s < 64) & (t >= 64))
    band[:, 4, :] = 1.0 - ((s >= 64) & (t < 64))
    ident = (k == m).astype(np.float32)
    return c, np.ascontiguousarray(band.reshape(128, 640)), ident


def _prep(inputs, cfg=None):
    f = lambda a: np.ascontiguousarray(np.asarray(a, dtype=np.float32))
    I = {k: f(v) for k, v in inputs.items()}
    small = np.concatenate([I[n].reshape(2, -1) for n in SM], axis=1)
    s = np.arange(128)[:, None]
    j = np.arange(640)[None, :]
    dist = 512 + (j % 128) - 128 * (j // 128) - s
    idx = np.clip(dist, -128, 128) + 128
    relb = np.ascontiguousarray(np.transpose(I['a_rel'][:, idx, :], (0, 1, 3, 2)))
    cwt = np.concatenate([I['b_conv_w'], I['b_conv_b'][:, None, :]], axis=1)
    convw = np.ascontiguousarray(np.transpose(cwt.reshape(2, 5, 6, 128), (0, 3, 2, 1)))
    cst, band, ident = _consts()
    maps = []
    for c in range(8):
        b = c % 4
        ss = slice(c * NS, (c + 1) * NS)
        m = dict(
            xp=I['x_prompt'][b], xs=I['x_sample'][ss].reshape(NS * TS, DM), memp=I['mem_prompt'][b],
            ca_k=I['cache_a_k'][:, ss].reshape(2, NS, 512, 512), ca_v=I['cache_a_v'][:, ss].reshape(2, NS, 512, 512),
            cc_k=I['cache_c_k'][:, ss].reshape(2, NS, 1024, 512), cc_v=I['cache_c_v'][:, ss].reshape(2, NS, 1024, 512),
            cc_f=I['cache_c_logf'][:, ss], sb_ssm=I['state_b_ssm'][:, ss], sb_conv=I['state_b_conv'][:, ss],
            cm_k=I['cache_mem_k'][:, ss].reshape(2, NS, 256, 256), cm_v=I['cache_mem_v'][:, ss].reshape(2, NS, 256, 256),
            w_in=I['w_in'], w_mkv=I['w_mkv'], w_pa=I['w_pa'], w_pb=I['w_pb'], w_pc=I['w_pc'], w_pm=I['w_pm'],
            w_out=I['w_out'], small=small, mnorm=I['m_norm'], band=band, ident=ident, relb=relb, convw=convw, cst=cst)
        maps.append({k: np.ascontiguousarray(v) for k, v in m.items()})
    return maps


_NC_CACHE = {}


def kernel(**inputs):
    cfg = {}
    key = 'full'
    if key not in _NC_CACHE:
        _NC_CACHE[key] = build(cfg)
    nc = _NC_CACHE[key]
    maps = _prep(inputs)
    res = run_bass_kernel_spmd(nc, maps, core_ids=list(range(8)))
    R = res.results
    P4 = range(4)
    st = lambda name, shape: np.stack([R[b][name] for b in P4], axis=1).reshape(shape)
    cat = lambda name: np.concatenate([R[c][name] for c in range(8)], axis=1)
    y_prompt = np.stack([R[b]['yp'] for b in P4], axis=0)
    y_sample = np.concatenate([R[c]['ys'].reshape(NS, TS, DM) for c in range(8)], axis=0)
    outs = [y_prompt, y_sample,
            st('pa_k', (2, 4, 512, 8, 64)), st('pa_v', (2, 4, 512, 8, 64)),
            st('pc_k', (2, 4, SEQ, 8, 64)), st('pc_v', (2, 4, SEQ, 8, 64)), st('pc_f', (2, 4, SEQ, 8)),
            st('pb_s', (2, 4, 8, 64, 64)), st('pb_c', (2, 4, 3, 768)),
            st('pm_k', (2, 4, 256, 4, 64)), st('pm_v', (2, 4, 256, 4, 64))]
    for name, tail in (('sa_k', (8, 64)), ('sa_v', (8, 64)), ('sc_k', (8, 64)), ('sc_v', (8, 64)), ('sc_f', (8,))):
        a = np.concatenate([R[c][name].reshape((2, NS, TS) + tail) for c in range(8)], axis=1)
        outs.append(a)
    outs.append(cat('sb_s'))
    outs.append(cat('sb_c'))
    return tuple(np.ascontiguousarray(o.astype(np.float32)) for o in outs)
```

```python
import contextlib
import numpy as np
import ml_dtypes
import concourse.bass as bass
import concourse.mybir as mybir
from concourse.bass_utils import run_bass_kernel_spmd

F32 = mybir.dt.float32
BF16 = mybir.dt.bfloat16
AF = mybir.ActivationFunctionType
ALU = mybir.AluOpType
AX = mybir.AxisListType

DM = 1024
DIN = 10000
SEQ = 4096
NS = 4
TS = 64
EPS = 1e-6
COL = dict(a_q=0, a_k=512, a_v=1024, a_z=1536, b_z=2048, b_xbc=2560, b_dt=3328,
           c_q=3336, c_k=3848, c_v=4360, c_f=4872, c_z=4880, m_q=5392, m_z=5648, gate=5904)
SM = {}
_o = 0
for _n, _w in (('g_norm', 1024), ('b_norm', 512), ('a_qnorm', 64), ('a_knorm', 64),
               ('c_qnorm', 64), ('c_knorm', 64), ('m_qnorm', 64), ('m_knorm', 64),
               ('b_dt_bias', 8), ('b_a_log', 8), ('b_d', 8), ('c_fbias', 8)):
    SM[_n] = (_o, _w)
    _o += _w
NSM = _o
C_M1, C_M2, C_S127, C_S63 = 0, 128, 256, 384
NCST = 512


def cp(e, out, in_):
    if hasattr(e, 'tensor_copy'):
        return e.tensor_copy(out=out, in_=in_)
    return e.activation(out=out, in_=in_, func=AF.Copy)


SAME_ENGINE_SYNC = True


class StopBuild(Exception):
    pass


class Prog:
    EPOCH = 24000
    ND = 48

    def __init__(self, nc, es):
        self.nc, self.es = nc, es
        self.eng = {'pe': nc.tensor, 'act': nc.scalar, 'dve': nc.vector, 'pool': nc.gpsimd, 'sp': nc.sync}
        self.cnt = {e: 0 for e in self.eng}
        self.sems = {}
        self.waited = {e: {} for e in self.eng}
        self.last_w = {}
        self.readers = {}
        self.dma_n = 0
        self.dma_sems = [es.enter_context(nc.semaphore("dq%d" % i)) for i in range(self.ND)]
        self.dma_tokens = []
        self.nops = 0
        self.maxops = None

    def _sem(self, e, epoch):
        k = (e, epoch)
        if k not in self.sems:
            self.sems[k] = self.es.enter_context(self.nc.semaphore("s_%s_%d" % (e, epoch)))
        return self.sems[k]

    def _wait(self, e, tok):
        if tok[0] == 'c':
            _, pe, c = tok
            if pe == e and (e == 'pe' or not SAME_ENGINE_SYNC):
                return
            epoch, v = divmod(c, self.EPOCH)
            key, val, sem = ('c', pe, epoch), v + 1, self._sem(pe, epoch)
        else:
            _, slot, rnd = tok
            key, val, sem = ('d', slot), 16 * (rnd + 1), self.dma_sems[slot]
        if self.waited[e].get(key, 0) >= val:
            return
        self.waited[e][key] = val
        self.eng[e].wait_ge(sem, val)

    def _deps(self, r, w):
        deps = set()
        for k in r:
            if k in self.last_w:
                deps.add(self.last_w[k])
            if isinstance(k, str) and k.startswith('ps'):
                deps.update(self.readers.get(k, ()))
        for k in w:
            if k in self.last_w:
                deps.add(self.last_w[k])
            deps.update(self.readers.get(k, ()))
        return deps

    def _reg(self, tok, r, w):
        for k in r:
            self.readers.setdefault(k, []).append(tok)
        for k in w:
            self.last_w[k] = tok
            self.readers[k] = []

    def op(self, e, fn, r=(), w=(), inc=True):
        if self.maxops is not None and self.nops >= self.maxops:
            raise StopBuild()
        for tok in self._deps(r, w):
            self._wait(e, tok)
        inst = fn(self.eng[e])
        c = self.cnt[e]
        if inc:
            epoch, _ = divmod(c, self.EPOCH)
            inst.then_inc(self._sem(e, epoch), 1)
            self.cnt[e] += 1
        self._reg(('c', e, c), r, w)
        self.nops += 1

    def dma(self, q, out, in_, r=(), w=(), nonc=False):
        if self.maxops is not None and self.nops >= self.maxops:
            raise StopBuild()
        n = self.dma_n
        self.dma_n += 1
        slot, rnd = n % self.ND, n // self.ND
        if rnd > 0:
            self._wait(q, ('d', slot, rnd - 1))
        for tok in self._deps(r, w):
            self._wait(q, tok)
        kw = {}
        if nonc:
            kw['allow_slow_non_contiguous'] = True
        inst = self.eng[q].dma_start(out=out, in_=in_, **kw)
        inst.then_inc(self.dma_sems[slot], 16)
        tok = ('d', slot, rnd)
        self._reg(tok, r, w)
        self.dma_tokens.append(tok)
        self.nops += 1
        return tok

    def finish(self, out_tokens):
        for tok in out_tokens:
            self._wait('sp', tok)
        last = {}
        for tok in self.dma_tokens:
            last[tok[1]] = tok
        for tok in last.values():
            self._wait('sp', tok)


def build(cfg):
    NL = cfg.get('layers', 2)
    NQ = cfg.get('nq', 8)
    DO_S = cfg.get('sample', True)
    NOSTATE = cfg.get('nostate', False)
    nc = bass.Bass("TRN2", target_bir_lowering=False)
    es = contextlib.ExitStack()
    D = {}

    def din(name, shape):
        D[name] = nc.dram_tensor(name, list(shape), F32, kind="ExternalInput").ap()

    def dout(name, shape):
        D[name] = nc.dram_tensor(name, list(shape), F32, kind="ExternalOutput").ap()

    din('xp', [SEQ, DM]); din('xs', [NS * TS, DM]); din('memp', [256, DM])
    din('ca_k', [2, NS, 512, 512]); din('ca_v', [2, NS, 512, 512])
    din('cc_k', [2, NS, 1024, 512]); din('cc_v', [2, NS, 1024, 512]); din('cc_f', [2, NS, 1024, 8])
    din('sb_ssm', [2, NS, 8, 64, 64]); din('sb_conv', [2, NS, 3, 768])
    din('cm_k', [2, NS, 256, 256]); din('cm_v', [2, NS, 256, 256])
    din('w_in', [2, DM, DIN]); din('w_mkv', [2, DM, 512])
    din('w_pa', [2, 512, DM]); din('w_pb', [2, 512, DM]); din('w_pc', [2, 512, DM]); din('w_pm', [2, 256, DM])
    din('w_out', [2, DM, DM])
    din('small', [2, NSM]); din('mnorm', [2, 1024]); din('relb', [2, 128, 8, 640]); din('convw', [2, 128, 6, 5]); din('cst', [128, NCST]); din('band', [128, 640]); din('ident', [128, 128])
    dout('yp', [SEQ, DM]); dout('ys', [NS * TS, DM])
    dout('pa_k', [2, 512, 512]); dout('pa_v', [2, 512, 512])
    dout('pc_k', [2, SEQ, 512]); dout('pc_v', [2, SEQ, 512]); dout('pc_f', [2, SEQ, 8])
    dout('pb_s', [2, 8, 64, 64]); dout('pb_c', [2, 3, 768])
    dout('pm_k', [2, 256, 256]); dout('pm_v', [2, 256, 256])
    dout('sa_k', [2, NS * TS, 512]); dout('sa_v', [2, NS * TS, 512])
    dout('sc_k', [2, NS * TS, 512]); dout('sc_v', [2, NS * TS, 512]); dout('sc_f', [2, NS * TS, 8])
    dout('sb_s', [2, NS, 8, 64, 64]); dout('sb_c', [2, NS, 3, 768])
    x1p = nc.dram_tensor("x1p", [SEQ, DM], F32, kind="Internal").ap()
    x1s = nc.dram_tensor("x1s", [NS * TS, DM], F32, kind="Internal").ap()

    with es:
        pg = Prog(nc, es)
        pg.maxops = cfg.get('maxops')
        out_tokens = []

        def sb(name, shape, dt):
            return es.enter_context(nc.sbuf_tensor("sb_" + name, list(shape), dt))

        def ps(name, shape, dt):
            return es.enter_context(nc.psum_tensor("ps_" + name, list(shape), dt))

        cst = sb("cst", [128, NCST], F32)
        cstb = sb("cstb", [128, 256], BF16)
        small = sb("small", [128, NSM], F32)
        expB = sb("expB", [128, 8, 640], BF16)
        cw = sb("cw", [128, 6, 5], F32)
        aneg = sb("aneg", [128, 8], F32)
        xt = sb("xt", [128, 1, DM], F32)
        W3 = sb("W3", [128, 4096], BF16)
        xnb = W3[:, 0:1024]
        sq = W3[:, 1024:2048].bitcast(F32)
        f2 = W3[:, 2048:3072].bitcast(F32)
        f3 = W3[:, 3072:4096].bitcast(F32)
        xnT = sb("xnT", [128, 8, 512], BF16)
        NWB = 3
        wbuf = [sb("wbuf%d" % i, [128, 8, 512], BF16) for i in range(NWB)]
        wbuf.append(W3[:, :].rearrange("p (k n) -> p k n", n=512))
        WKEYS = [['wbuf0'], ['wbuf1'], ['wbuf2'], ['wbuf3', 'xnb', 'sq', 'f2', 'f3']]
        kT_c = sb("kT_c", [128, 4, SEQ], BF16)
        va_c = sb("va_c", [128, 32, 8, 66], BF16)
        kT_a = sb("kT_a", [128, 4, 1024], BF16)
        va_a = sb("va_a", [128, 8, 8, 66], BF16)
        kT_m = sb("kT_m", [128, 2, 256], BF16)
        va_m = sb("va_m", [128, 2, 4, 66], BF16)
        cum = sb("cum", [128, 32, 8], F32)
        biasQ = sb("biasQ", [128, 32, 8], F32)
        qT = sb("qT", [128, 4, 512], BF16)
        sz = sb("sz", [128, 4, 512], BF16)
        oz = sb("oz", [128, 4, 512], BF16)
        ozT = {k: sb("ozT_" + k, [128, n, 512], BF16) for k, n in (('a', 4), ('b', 4), ('c', 4), ('m', 2))}
        f1 = sb("f1", [128, 512], F32)
        b1 = sb("b1", [128, 512], BF16)
        PT = [sb("PT%d" % i, [128, 512], BF16) for i in range(2)]
        st8 = sb("st8", [128, 8, 8], F32)
        raw = sb("raw", [128, 3, 4, 131], F32)
        halo = sb("halo", [128, 6, 3], F32)
        AR2 = sb("AR2", [128, 4096], BF16)
        xbc = AR2[:, 0:3072].rearrange("p (j n) -> p j n", n=512)
        Mh = AR2[:, 3072:4096].rearrange("p (h n) -> p h n", n=128)
        mT = AR2[:, :].rearrange("p (k n) -> p k n", n=512)
        cva = sb("cva", [128, 512], F32)
        sig = cva
        dts = sb("dts", [128, 4, 8], F32)
        lfs = sb("lfs", [128, 4, 8], F32)
        asb = sb("asb", [128, 4, 8], F32)
        btk = sb("btk", [128, 128], BF16)
        AE = sb("AE", [128, 2048], F32)
        Ah = AE[:, 0:512].rearrange("p (h n) -> p h n", n=128)
        Ee = AE[:, 512:1024].rearrange("p (h n) -> p h n", n=128)
        Gm = AE[:, 1024:1280].rearrange("p (g n) -> p g n", n=128)
        bdec = AE[:, 1280:1536].bitcast(BF16).rearrange("p (a g n) -> p a g n", a=4, g=2)
        xdt = AE[:, 1536:1792].bitcast(BF16).rearrange("p (h d) -> p h d", d=64)
        xtk = AE[:, 1792:2048].bitcast(BF16)
        macc = AE[:, :].rearrange("p (t n) -> p t n", n=512)
        hT = sb("hT", [128, 4, 64], F32)
        hTb = sb("hTb", [128, 4, 64], BF16)
        ssd8 = sb("ssd8", [128, 4, 8], F32)
        psA = [ps("psA%d" % i, [128, 512], F32) for i in range(2)]
        psT = [ps("psT%d" % i, [128, 1024], BF16) for i in range(2)]
        psS = [ps("psS%d" % i, [128, 512], F32) for i in range(2)]
        psO = [ps("psO%d" % i, [128, 512], F32) for i in range(2)]
        rr = {'A': 0, 'T': 0, 'S': 0, 'O': 0, 'W': 0, 'P': 0, 'X': 0, 'WP': 0}
        rrn = {'X': 1, 'WP': 1}

        def rot(k, n=2):
            v = rr[k]
            rr[k] = (v + 1) % rrn.get(k, n)
            return v

        ident_b = cstb[:, 0:128]
        M1f = cst[:, C_M1:C_M1 + 128]
        M2f = cst[:, C_M2:C_M2 + 128]
        S127f = cst[:, C_S127:C_S127 + 128]
        S63f = cst[:, C_S63:C_S63 + 128]
        M1b = cstb[:, 128:256]

        def smv(name, rows=128):
            o, w = SM[name]
            return small[0:rows, o:o + w]

        bar = sb("bar", [128, 2], F32)
        epst = sb("epst", [128, 1], F32)
        nhalf = sb("nhalf", [128, 8], F32)
        gcol = sb("gcol", [128, 4], F32)
        identf = sb("identf", [128, 64], F32)

        def barrier(keys):
            pg.op('pool', lambda e: e.memset(bar[:, 0:1], 0.0), w=['bar'] + list(keys))

        AEK = ['Ah%d' % h for h in range(8)] + ['Ee', 'Gm', 'bdec', 'xdt', 'xtk', 'AErel'] + ['macc%d' % t for t in range(4)]
        pg.op('pool', lambda e: e.memset(epst[:, :], EPS), w=['epst'])
        pg.op('pool', lambda e: e.memset(nhalf[:, :], -0.5), w=['epst'])
        pg.dma('sp', identf[0:64, :], D['ident'][0:64, 0:64], w=['identf'])
        pg.dma('sp', identf[64:128, :], D['ident'][0:64, 0:64], w=['identf'])
        pg.dma('sp', cst[:, :], D['cst'][:, :], w=['cst'])
        pg.dma('sp', AE[:, 0:128], D['ident'][:, :], w=AEK)
        pg.op('dve', lambda e: cp(e, out=cstb[:, 0:128], in_=AE[:, 0:128]), r=AEK, w=['cstb'])
        pg.op('dve', lambda e: cp(e, out=cstb[:, 128:256], in_=cst[:, C_M1:C_M1 + 128]), r=['cst'], w=['cstb'])
        for t_, k_ in ((va_c, 'va_c'), (va_a, 'va_a'), (va_m, 'va_m')):
            pg.op('pool', lambda e, t_=t_: e.memset(t_[:], 1.0), w=[k_])

        def group_wlist(l):
            wl = []
            for nm in ('c_q', 'c_k', 'c_v'):
                wl.append(('w_in', l, COL[nm], 512, 8))
            wl.append(('w_in', l, COL['c_f'], 8, 8))
            wl.append(('w_in', l, COL['c_z'], 512, 8))
            for nm in ('a_q', 'a_k', 'a_v', 'a_z', 'm_q', 'b_z'):
                wl.append(('w_in', l, COL[nm], 512, 8))
            wl.append(('w_in', l, COL['b_dt'], 8, 8))
            for half in range(2):
                wl.append(('w_in', l, COL['b_xbc'] + half * 384, 384, 8))
            for c in range(2):
                for bi, (wn, nkc) in enumerate((('w_pa', 4), ('w_pb', 4), ('w_pc', 4), ('w_pm', 2))):
                    wl.append(('w_in', l, COL['gate'] + bi * 1024 + c * 512, 512, 8, True))
                    wl.append((wn, l, c * 512, 512, nkc, True))
            for c in range(2):
                wl.append(('w_out', l, c * 512, 512, 8, True))
            return wl

        WL = []
        for l_ in range(NL):
            WL.append(('w_mkv', l_, 0, 512, 8))
            for _ in range(NQ + (1 if DO_S else 0)):
                WL += group_wlist(l_)
        wbi, prev_occ, lastocc, r3, r4 = [], [], {}, 0, 0
        for k_, ent in enumerate(WL):
            if len(ent) > 5 and ent[5]:
                b_ = r4 % 4; r4 += 1
            else:
                b_ = r3 % 3; r3 += 1; r4 = r3
            wbi.append(b_)
            prev_occ.append(lastocc.get(b_, -1))
            lastocc[b_] = k_
        wstate = {'ptr': 0, 'issued': 0}

        def load_w(dram, l, c0, n, nk=8, buf=None):
            i = wstate['ptr']
            exp = WL[i]
            assert exp[1] == l and exp[2] == c0 and exp[3] == n and exp[4] == nk and D[exp[0]] is dram, (exp, l, c0, n, nk)
            in_merge = len(exp) > 5 and exp[5]
            while wstate['issued'] < len(WL) and wstate['issued'] <= i + 3 and \
                    (wstate['issued'] <= i or prev_occ[wstate['issued']] <= i - 2) and \
                    (wbi[wstate['issued']] != 3 or in_merge):
                k = wstate['issued']
                nm, l2, c2, n2, nk2 = WL[k][0:5]
                src = D[nm][l2, :, c2:c2 + n2].rearrange("(k p) n -> p k n", p=128)
                pg.dma('pool', wbuf[wbi[k]][:, 0:nk2, 0:n2], src, w=WKEYS[wbi[k]])
                wstate['issued'] += 1
            wstate['ptr'] += 1
            return wbuf[wbi[i]], WKEYS[wbi[i]]

        def transposes(src_ap_fn, nblk, np_, dst_fn, rkeys, wkeys, evac='dve'):
            i = rot('T'); pt = psT[i]; pk = 'psT%d' % i
            for j in range(nblk):
                pg.op('pe', lambda e, j=j: e.transpose(out=pt[:, j * 128:j * 128 + np_], in_=src_ap_fn(j),
                                                       identity=ident_b[0:np_, 0:np_]),
                      r=list(rkeys) + ['cstb'], w=[pk], inc=(j == nblk - 1))
            view = pt[:, 0:nblk * 128].rearrange("p (b n) -> p b n", n=128)[:, :, 0:np_]
            pg.op(evac, lambda e: dst_fn(e, view), r=[pk], w=list(wkeys))

        def rstd_from_ss(ss_ap, rs_ap, n, rows, key):
            w_ = rs_ap.shape[-1]
            pg.op('dve', lambda e: e.tensor_scalar(out=rs_ap, in0=ss_ap, scalar1=1.0 / n, scalar2=EPS,
                                                    op0=ALU.mult, op1=ALU.add), r=[key], w=[key])
            pg.op('pool', lambda e: e.tensor_tensor(out=rs_ap, in0=rs_ap, in1=nhalf[0:rows, 0:w_], op=ALU.pow),
                  r=[key, 'epst'], w=[key])

        def head_norm(psap, pk, nh, gain_ap, out_ap, outkeys, np_, scale=None):
            n = nh * 64
            pg.op('act', lambda e: e.activation(out=sq[0:np_, 0:n], in_=psap, func=AF.Square), r=[pk], w=['sq'])
            ssv = st8[0:np_, 0, 0:nh]
            pg.op('dve', lambda e: e.tensor_reduce(out=ssv, in_=sq[0:np_, 0:n].rearrange("p (h d) -> p h d", d=64),
                                                   axis=AX.X, op=ALU.add), r=['sq'], w=['st8'])
            rstd_from_ss(ssv, ssv, 64, np_, 'st8')
            o1, k1 = (f1[0:np_, 0:n], ['f1']) if gain_ap is not None else (out_ap, list(outkeys))
            pg.op('dve', lambda e: e.tensor_tensor(
                out=o1.rearrange("p (h d) -> p h d", d=64),
                in0=psap.rearrange("p (h d) -> p h d", d=64),
                in1=ssv.unsqueeze(2).to_broadcast([np_, nh, 64]), op=ALU.mult), r=[pk, 'st8'], w=k1)
            if gain_ap is None:
                return
            g = gain_ap.unsqueeze(1).to_broadcast([np_, nh, 64])
            pg.op('dve', lambda e: e.tensor_tensor(
                out=out_ap.rearrange("p (h d) -> p h d", d=64),
                in0=f1[0:np_, 0:n].rearrange("p (h d) -> p h d", d=64), in1=g, op=ALU.mult),
                r=['f1', 'small'], w=list(outkeys))

        def pipe_tiles(NT, proj_fn):
            nxt = proj_fn(0)
            for t in range(NT):
                cur = nxt
                if t + 1 < NT:
                    nxt = proj_fn(t + 1)
                yield (t,) + tuple(cur)

        def proj_tm(xT, xkey, tcol, np_, wt, wkey, n, nk=8, wc0=0, xkeys=()):
            i = rot('A'); p = psA[i]; pk = 'psA%d' % i
            for kc in range(nk):
                pg.op('pe', lambda e, kc=kc: e.matmul(p[0:np_, 0:n], lhsT=xT[:, kc, tcol:tcol + np_],
                                                      rhs=wt[:, kc, wc0:wc0 + n], start=(kc == 0), stop=(kc == nk - 1)),
                      r=[xkey] + list(wkey) + list(xkeys), w=[pk], inc=(kc == nk - 1))
            return p[0:np_, 0:n], pk

        def layer_consts(l):
            pg.dma('sp', small[:, :], D['small'][l:l + 1, :].broadcast_to([128, NSM]), w=['small'])
            pg.dma('sp', cw[:, :, :], D['convw'][l], w=['cw'])
            for j, name in enumerate(('a_qnorm', 'c_qnorm', 'm_qnorm')):
                o_, w_ = SM[name]
                for half in range(2):
                    pg.dma('sp', gcol[half * 64:(half + 1) * 64, j:j + 1],
                           D['small'][l, o_:o_ + 64].rearrange("(p o) -> p o", o=1), w=['gcol'], nonc=True)
            pg.op('dve', lambda e: e.tensor_scalar(out=gcol[:, 0:3], in0=gcol[:, 0:3], scalar1=0.125, scalar2=None, op0=ALU.mult),
                  r=['gcol'], w=['gcol'])
            pg.op('act', lambda e: e.activation(out=aneg[:, :], in_=smv('b_a_log'), func=AF.Exp), r=['small'], w=['aneg'])
            pg.op('dve', lambda e: e.tensor_scalar(out=aneg[:, :], in0=aneg[:, :], scalar1=-1.0, scalar2=None, op0=ALU.mult),
                  r=['aneg'], w=['aneg'])
            barrier(AEK)
            pg.dma('sp', AE[:, 0:640], D['band'][:, :], w=AEK)
            pg.op('dve', lambda e: e.tensor_scalar(out=AE[:, 0:640], in0=AE[:, 0:640], scalar1=30000.0, scalar2=-30000.0,
                                                    op0=ALU.mult, op1=ALU.add), r=AEK, w=AEK)
            for h in range(8):
                pg.dma('sp', AE[:, 1024:1664], D['relb'][l, :, h, :], w=['AErel'])
                pg.op('dve', lambda e, h=h: e.tensor_tensor(out=expB[:, h, :], in0=AE[:, 1024:1664],
                                                            in1=AE[:, 0:640], op=ALU.add),
                      r=['AErel'] + AEK, w=['expB'])
            barrier(AEK)

        def norm_tiles(src_fn, NT, np_, gain_fn, rk_fn=None):
            rawflat = raw[:, :, :, :].rearrange("p a b c -> p (a b c)")
            sqj = sq.bitcast(BF16)
            stg = [(xt[0:np_, 0, :], 'xt0'), (rawflat[0:np_, 0:DM], 'raw')]

            def ld(t):
                xa, xk = stg[t % 2]
                pg.dma('sp', xa, src_fn(t), r=(rk_fn(t) if rk_fn else []), w=[xk])
            ld(0)
            for t in range(NT):
                xa, xk = stg[t % 2]
                if t + 1 < NT:
                    ld(t + 1)
                ssv = st8[0:np_, 1, t % 2:t % 2 + 1]
                pg.op('act', lambda e, xa=xa, ssv=ssv: e.activation(out=sqj[0:np_, :], in_=xa, func=AF.Square, accum_out=ssv),
                      r=[xk], w=['sq', 'st8'])
                rstd_from_ss(ssv, ssv, DM, np_, 'st8')
                pg.op('dve', lambda e, xa=xa, ssv=ssv: e.scalar_tensor_tensor(out=xnb[0:np_, :], in0=xa, scalar=ssv,
                                                                  in1=gain_fn(np_), op0=ALU.mult, op1=ALU.mult),
                      r=[xk, 'st8', 'small', 'f2'], w=['xnb'])
                transposes(lambda j: xnb[0:np_, j * 128:(j + 1) * 128], 8, np_,
                           lambda e, v, t=t: cp(e, out=xnT[:, :, t * np_:(t + 1) * np_], in_=v),
                           ['xnb'], ['xnT'], evac='act' if t % 2 else 'dve')

        def attend(qT_ap, NQc, qtiles, kblocks, qkeys, out_fn, acc=None):
            if acc is None:
                io = rot('O'); po = psO[io]; pok = 'psO%d' % io
                pov = po[:, 0:260].rearrange("p (j c) -> p j c", c=65)
                jbase, bank_first = 0, True
            else:
                pov, pok, jbase, bank_first = acc
            lastkb, firstkb = {}, {}
            for bi, kb in enumerate(kblocks):
                for j, (c0, nq) in enumerate(qtiles):
                    if c0 >= kb['q0']:
                        lastkb[j] = bi
                        firstkb.setdefault(j, bi)
            st = {}

            def stage_s(bi):
                kb = kblocks[bi]
                isx = rot('S'); psx = psS[isx]; psk = 'psS%d' % isx
                badd = kb.get('badd')
                pg.op('pe', lambda e: e.matmul(psx[0:kb['nk'], kb['q0']:NQc], lhsT=kb['kT'],
                                               rhs=qT_ap[:, kb['q0']:NQc], start=True, stop=(badd is None)),
                      r=list(qkeys) + list(kb['keys']), w=[psk], inc=(badd is None))
                if badd is not None:
                    pg.op('pe', lambda e: e.matmul(psx[0:kb['nk'], kb['q0']:NQc], lhsT=ident_b[:, 0:kb['nk']],
                                                   rhs=badd, start=False, stop=True),
                          r=['cstb'] + list(kb.get('mkeys', [])), w=[psk])
                st[bi] = (psx, psk)

            def stage_e(bi):
                kb = kblocks[bi]
                psx, psk = st[bi]
                ip = rot('P'); pt = PT[ip]; ptk = 'PT%d' % ip
                if kb.get('bias') is not None:
                    pg.op('act', lambda e: e.activation(out=pt[0:kb['nk'], kb['q0']:NQc], in_=psx[0:kb['nk'], kb['q0']:NQc],
                                                        func=AF.Exp, bias=kb['bias'], scale=1.0),
                          r=[psk] + list(kb.get('bkeys', [])), w=[ptk])
                else:
                    pg.op('act', lambda e: e.activation(out=pt[0:kb['nk'], kb['q0']:NQc], in_=psx[0:kb['nk'], kb['q0']:NQc],
                                                        func=AF.Exp), r=[psk], w=[ptk])
                if kb.get('diagmask'):
                    pg.op('dve', lambda e: e.tensor_tensor(
                        out=pt[0:kb['nk'], kb['q0']:kb['q0'] + kb['nk']], in0=pt[0:kb['nk'], kb['q0']:kb['q0'] + kb['nk']],
                        in1=M1b[0:kb['nk'], 0:kb['nk']], op=ALU.mult), r=[ptk, 'cstb'], w=[ptk])
                st[bi] = (pt, ptk)

            def stage_v(bi):
                kb = kblocks[bi]
                pt, ptk = st[bi]
                for j, (c0, nq) in enumerate(qtiles):
                    if c0 < kb['q0']:
                        continue
                    pg.op('pe', lambda e, j=j, c0=c0, nq=nq: e.matmul(
                        pov[0:nq, jbase + j, :], lhsT=pt[0:kb['nk'], c0:c0 + nq], rhs=kb['v'],
                        start=(bank_first and bi == 0 and j == min(firstkb)), stop=(bi == lastkb[j]), skip_group_check=True),
                        r=[ptk] + list(kb['keys']), w=[pok], inc=(c0 + nq >= NQc))

            n = len(kblocks)
            stage_s(0)
            for bi in range(n):
                if bi + 1 < n:
                    stage_s(bi + 1)
                stage_e(bi)
                stage_v(bi)
            if out_fn is not None:
                out_fn(pov, pok)
            return pov, pok

        def attn_out(h0col, NT, np_, szkey='sz'):
            def fn(pov, pok):
                rl = st8[0:np_, 2, 0:NT]
                pg.op('dve', lambda e: e.reciprocal(out=rl, in_=pov[0:np_, 0:NT, 64]), r=[pok], w=['st8'])
                tmp = f2[0:np_, 0:NT * 64].rearrange("p (j d) -> p j d", d=64)
                pg.op('dve', lambda e: e.tensor_tensor(out=tmp, in0=pov[0:np_, 0:NT, 0:64],
                                                       in1=rl.unsqueeze(2).to_broadcast([np_, NT, 64]), op=ALU.mult),
                      r=[pok, 'st8'], w=['f2'])
                pg.op('pool', lambda e: e.tensor_tensor(out=oz[0:np_, 0:NT, h0col:h0col + 64], in0=tmp,
                                                        in1=sz[0:np_, 0:NT, h0col:h0col + 64], op=ALU.mult),
                      r=['f2', szkey], w=['oz'])
            return fn

        def oz_to_T(br, NT, np_, nblk):
            for t in range(NT):
                transposes(lambda j, t=t: oz[0:np_, t, j * 128:(j + 1) * 128], nblk, np_,
                           lambda e, v, t=t: cp(e, out=ozT[br][:, 0:nblk, t * np_:(t + 1) * np_], in_=v),
                           ['oz'], ['ozT_' + br], evac='act' if t % 2 else 'dve')

        def evac_q(psap, pk, t, np_, nh, gname):
            gj = {'a_qnorm': 0, 'c_qnorm': 1, 'm_qnorm': 2}[gname]
            head_norm(psap, pk, nh, None, b1[0:np_, 0:nh * 64], ['b1'], np_)
            transposes(lambda j: b1[0:np_, j * 128:(j + 1) * 128], nh // 2, np_,
                       lambda e, v: e.activation(out=qT[:, 0:nh // 2, t * np_:(t + 1) * np_], in_=v, func=AF.Copy,
                                                 scale=gcol[:, gj:gj + 1]),
                       ['b1', 'gcol'], ['qT'], evac='act')

        def evac_k(psap, pk, np_, nh, gname, out_dram, kT_dst_fn, kT_key):
            head_norm(psap, pk, nh, smv(gname, np_), f3[0:np_, 0:nh * 64], ['f3'], np_)
            if out_dram is not None:
                out_tokens.append(pg.dma('sp', out_dram, f3[0:np_, 0:nh * 64], r=['f3']))
            pg.op('dve', lambda e: cp(e, out=b1[0:np_, 0:nh * 64], in_=f3[0:np_, 0:nh * 64]), r=['f3'], w=['b1'])
            transposes(lambda j: b1[0:np_, j * 128:(j + 1) * 128], nh // 2, np_,
                       lambda e, v: cp(e, out=kT_dst_fn(), in_=v), ['b1'], [kT_key], evac='act')

        def evac_v(psap, pk, np_, nh, out_dram, va_dst, va_key):
            if out_dram is not None:
                pg.op('act', lambda e: e.activation(out=f3[0:np_, 0:nh * 64], in_=psap, func=AF.Copy), r=[pk], w=['f3'])
                out_tokens.append(pg.dma('sp', out_dram, f3[0:np_, 0:nh * 64], r=['f3']))
            pg.op('dve', lambda e: cp(e, out=va_dst, in_=psap.rearrange("p (h d) -> p h d", d=64)),
                  r=[pk], w=[va_key])

        def softplus_from(psap, pk, bias_ap, out_ap, okey, np_, neg=False):
            tmp = st8[0:np_, 3, :]
            pg.op('dve', lambda e: e.tensor_tensor(out=tmp, in0=psap, in1=bias_ap, op=ALU.add), r=[pk, 'small'], w=['st8'])
            pg.op('act', lambda e: e.activation(out=tmp, in_=tmp, func=AF.Exp, scale=(-1.0 if neg else 1.0)),
                  r=['st8'], w=['st8'])
            pg.op('dve', lambda e: e.tensor_scalar(out=tmp, in0=tmp, scalar1=1.0, scalar2=None, op0=ALU.add),
                  r=['st8'], w=['st8'])
            pg.op('act', lambda e: e.activation(out=tmp, in_=tmp, func=AF.Ln), r=['st8'], w=['st8'])
            pg.op('dve', lambda e: e.tensor_scalar(out=out_ap, in0=tmp, scalar1=(-1.0 if neg else 1.0), scalar2=None,
                                                    op0=ALU.mult), r=['st8'], w=[okey])

        def proj8_all(NT, np_, wt, wk):
            i = rot('A'); p = psA[i]; pk = 'psA%d' % i
            for t in range(NT):
                for kc in range(8):
                    pg.op('pe', lambda e, kc=kc, t=t: e.matmul(p[0:np_, t * 8:(t + 1) * 8], lhsT=xnT[:, kc, t * np_:(t + 1) * np_],
                                                              rhs=wt[:, kc, 0:8], start=(kc == 0), stop=(kc == 7)),
                          r=['xnT'] + list(wk), w=[pk], inc=(kc == 7 and t == NT - 1))
            return p[0:np_, 0:NT * 8].rearrange("p (t h) -> p t h", h=8), pk

        def softplus4(psv, pk, bias_ap, out_ap, okey, np_, NT, neg=False):
            tmp = st8[0:np_, 0:NT, :]
            pg.op('dve', lambda e: e.tensor_tensor(out=tmp, in0=psv, in1=bias_ap.unsqueeze(1).to_broadcast([np_, NT, 8]),
                                                   op=ALU.add), r=[pk, 'small'], w=['st8'])
            pg.op('act', lambda e: e.activation(out=tmp, in_=tmp, func=AF.Exp, scale=(-1.0 if neg else 1.0)),
                  r=['st8'], w=['st8'])
            pg.op('dve', lambda e: e.tensor_scalar(out=tmp, in0=tmp, scalar1=1.0, scalar2=None, op0=ALU.add),
                  r=['st8'], w=['st8'])
            pg.op('act', lambda e: e.activation(out=tmp, in_=tmp, func=AF.Ln), r=['st8'], w=['st8'])
            pg.op('dve', lambda e: e.tensor_scalar(out=out_ap, in0=tmp, scalar1=(-1.0 if neg else 1.0), scalar2=None,
                                                    op0=ALU.mult), r=['st8'], w=[okey])

        def cumsum_tile(lf_ap, lfkey, np_, dst_ap, dkey, carry_ap, ckeys):
            i = rot('S'); p = psS[i]; pk = 'psS%d' % i
            pg.op('pe', lambda e: e.matmul(p[0:np_, 0:8], lhsT=M1f[0:np_, 0:np_], rhs=lf_ap, start=True,
                                           stop=(carry_ap is None)), r=[lfkey, 'cst'], w=[pk])
            if carry_ap is not None:
                pg.op('pe', lambda e: e.matmul(p[0:np_, 0:8], lhsT=carry_ap[0], rhs=carry_ap[1], start=False, stop=True),
                      r=list(ckeys) + ['cst'], w=[pk])
            pg.op('dve', lambda e: cp(e, out=dst_ap, in_=p[0:np_, 0:8]), r=[pk], w=[dkey])

        def ssd_tile(t, np_, NT):
            c0 = t * np_
            if t == 0:
                barrier(AEK)
            def ev(e, v):
                return cp(e, out=xtk[0:np_, :].rearrange("p (b n) -> p b n", n=128), in_=v[0:np_, 0:4, :])
            i = rot('T'); pt = psT[i]; pk = 'psT%d' % i
            for j in range(5):
                pg.op('pe', lambda e, j=j: e.transpose(out=pt[0:np_, j * 128:(j + 1) * 128], in_=xbc[:, j, c0:c0 + np_],
                                                       identity=ident_b[:, :]), r=['xbc', 'cstb'], w=[pk], inc=(j == 4))
            pg.op('act', lambda e: e.activation(out=xtk[0:np_, :], in_=pt[0:np_, 0:512], func=AF.Copy), r=[pk], w=['xtk'])
            pg.op('act', lambda e: e.activation(out=btk[0:np_, :], in_=pt[0:np_, 512:640], func=AF.Copy), r=[pk], w=['btk'])
            pg.op('dve', lambda e: e.tensor_tensor(out=xdt[0:np_, :, :], in0=pt[0:np_, 0:512].rearrange("p (h d) -> p h d", d=64),
                                                   in1=dts[0:np_, t, :].unsqueeze(2).to_broadcast([np_, 8, 64]), op=ALU.mult),
                  r=[pk, 'dts'], w=['xdt'])
            a_ap = asb[0:np_, t, :]
            i = rot('A'); p = psA[i]; pk2 = 'psA%d' % i
            pg.op('pe', lambda e: e.matmul(p[0:np_, 0:8], lhsT=M1f[0:np_, 0:np_], rhs=a_ap, start=True, stop=True),
                  r=['asb', 'cst'], w=[pk2], inc=False)
            pg.op('pe', lambda e: e.matmul(p[0:np_, 8:16], lhsT=M2f[0:np_, 0:np_], rhs=a_ap, start=True, stop=True),
                  r=['asb', 'cst'], w=[pk2], inc=False)
            pg.op('pe', lambda e: e.matmul(p[:, 16:24], lhsT=M1f[0:np_, :], rhs=a_ap, start=True, stop=False),
                  r=['asb', 'cst'], w=[pk2], inc=False)
            pg.op('pe', lambda e: e.matmul(p[:, 16:24], lhsT=M2f[0:np_, :], rhs=a_ap, start=False, stop=True),
                  r=['asb', 'cst'], w=[pk2])
            pg.op('act', lambda e: e.activation(out=ssd8[0:np_, 0, :], in_=p[0:np_, 0:8], func=AF.Exp), r=[pk2], w=['ssd8'])
            pg.op('act', lambda e: e.activation(out=ssd8[0:np_, 1, :], in_=p[0:np_, 8:16], func=AF.Exp), r=[pk2], w=['ssd8'])
            pg.op('act', lambda e: e.activation(out=ssd8[:, 2, :], in_=p[:, 16:24], func=AF.Exp), r=[pk2], w=['ssd8'])
            for g in range(2):
                i = rot('A'); pgm = psA[i]; pk3 = 'psA%d' % i
                pg.op('pe', lambda e, g=g, pgm=pgm: e.matmul(pgm[0:np_, 0:np_], lhsT=xbc[g * 64:(g + 1) * 64, 4, c0:c0 + np_],
                                                    rhs=xbc[g * 64:(g + 1) * 64, 5, c0:c0 + np_], start=True, stop=True),
                      r=['xbc'], w=[pk3])
                pg.op('dve', lambda e, g=g, pgm=pgm: e.tensor_tensor(
                    out=Gm[0:np_, g, 0:np_], in0=pgm[0:np_, 0:np_], in1=M1f[0:np_, 0:np_], op=ALU.mult),
                    r=[pk3, 'cst'], w=['Gm'])
            for hh in range(2):
                for h4 in range(4):
                    h = hh * 4 + h4
                    pg.op('dve', lambda e, h=h, h4=h4: e.tensor_scalar(
                        out=Ah[0:np_, h4, 0:np_], in0=M2f[0:np_, 0:np_], scalar1=asb[0:np_, t, h:h + 1], scalar2=None,
                        op0=ALU.mult), r=['cst', 'asb'], w=['Ah%d' % h4])
                psx = psS[hh]; psk = 'psS%d' % hh
                for h4 in range(4):
                    pg.op('pe', lambda e, h4=h4, psx=psx: e.matmul(
                        psx[0:np_, h4 * 128:h4 * 128 + np_], lhsT=Ah[0:np_, h4, 0:np_], rhs=M1f[0:np_, 0:np_],
                        start=True, stop=True), r=['Ah%d' % h4, 'cst'], w=[psk], inc=(h4 == 3))
                pg.op('act', lambda e, psx=psx: e.activation(
                    out=Ee[0:np_, :, 0:np_],
                    in_=psx[0:np_, :].rearrange("p (h n) -> p h n", n=128)[:, :, 0:np_], func=AF.Exp),
                    r=[psk], w=['Ee'])
                pg.op('dve', lambda e, hh=hh: e.tensor_tensor(
                    out=Mh[0:np_, hh * 4:(hh + 1) * 4, 0:np_], in0=Ee[0:np_, :, 0:np_],
                    in1=Gm[0:np_, hh, 0:np_].unsqueeze(1).to_broadcast([np_, 4, np_]), op=ALU.mult),
                    r=['Ee', 'Gm'], w=['Mh'])
            io = rot('O'); py = psO[io]; pyk = 'psO%d' % io
            io2 = rot('O'); py2 = psO[io2]; py2k = 'psO%d' % io2
            for h in range(8):
                pg.op('pe', lambda e, h=h: e.matmul(py[0:np_, h * 64:(h + 1) * 64], lhsT=Mh[0:np_, h, 0:np_],
                                                    rhs=xdt[0:np_, h, :], start=True, stop=True),
                      r=['Mh', 'xdt'], w=[pyk], inc=(h == 7))
            py3 = psS[1]; py3k = 'psS1'
            for h in range(8):
                g = h // 4
                dstp = (py2 if g == 0 else py3)
                pg.op('pe', lambda e, h=h, g=g, dstp=dstp: e.matmul(dstp[0:np_, (h % 4) * 64:(h % 4 + 1) * 64],
                                                         lhsT=xbc[g * 64:(g + 1) * 64, 5, c0:c0 + np_],
                                                         rhs=hTb[g * 64:(g + 1) * 64, h % 4, :], start=True, stop=True),
                      r=['xbc', 'hTb'], w=[py2k if g == 0 else py3k], inc=(h % 4 == 3))
            yv = f1[0:np_, :].rearrange("p (h d) -> p h d", d=64)
            for g, (dstp, dk) in enumerate(((py2, py2k), (py3, py3k))):
                pg.op('dve', lambda e, g=g, dstp=dstp: e.tensor_tensor(
                    out=yv[:, g * 4:(g + 1) * 4, :], in0=dstp[0:np_, 0:256].rearrange("p (h d) -> p h d", d=64),
                    in1=ssd8[0:np_, 0, g * 4:(g + 1) * 4].unsqueeze(2).to_broadcast([np_, 4, 64]), op=ALU.mult),
                    r=[dk, 'ssd8'], w=['f1'])
            pg.op('dve', lambda e: e.tensor_tensor(out=f1[0:np_, :], in0=f1[0:np_, :], in1=py[0:np_, :], op=ALU.add),
                  r=['f1', pyk], w=['f1'])
            pg.op('pool', lambda e: e.tensor_tensor(out=f2[0:np_, :].rearrange("p (h d) -> p h d", d=64),
                                                    in0=xtk[0:np_, :].rearrange("p (h d) -> p h d", d=64),
                                                    in1=smv('b_d', np_).unsqueeze(2).to_broadcast([np_, 8, 64]), op=ALU.mult),
                  r=['xtk', 'small'], w=['f2'])
            pg.op('dve', lambda e: e.tensor_tensor(out=f1[0:np_, :], in0=f1[0:np_, :], in1=f2[0:np_, :], op=ALU.add),
                  r=['f1', 'f2'], w=['f1'])
            pg.op('dve', lambda e: e.tensor_tensor(out=f1[0:np_, :], in0=f1[0:np_, :], in1=sz[0:np_, t, :], op=ALU.mult),
                  r=['f1', 'szb'], w=['f1'])
            ssv = st8[0:np_, 4, 0:1]
            pg.op('act', lambda e: e.activation(out=sq[0:np_, 0:512], in_=f1[0:np_, :], func=AF.Square, accum_out=ssv),
                  r=['f1'], w=['sq', 'st8'])
            rstd_from_ss(ssv, ssv, 512, np_, 'st8')
            pg.op('dve', lambda e: e.scalar_tensor_tensor(out=oz[0:np_, t, :], in0=f1[0:np_, :], scalar=ssv,
                                                          in1=smv('b_norm', np_), op0=ALU.mult, op1=ALU.mult),
                  r=['f1', 'st8', 'small'], w=['oz'])
            pg.op('dve', lambda e: e.tensor_tensor(
                out=bdec[0:np_, :, :, :].rearrange("p a g n -> p g a n"),
                in0=btk[0:np_, :].rearrange("p (g n) -> p g n", n=64).unsqueeze(2).to_broadcast([np_, 2, 4, 64]),
                in1=ssd8[0:np_, 1, :].rearrange("p (g a) -> p g a", a=4).unsqueeze(3).to_broadcast([np_, 2, 4, 64]),
                op=ALU.mult), r=['btk', 'ssd8'], w=['bdec'])
            i = rot('A'); ph = psA[i]; phk = 'psA%d' % i
            for a4 in range(4):
                pg.op('pe', lambda e, a4=a4: e.matmul(
                    ph[:, a4 * 128:(a4 + 1) * 128], lhsT=bdec[0:np_, a4, :, :].rearrange("p g n -> p (g n)"),
                    rhs=xdt[0:np_, :, :].rearrange("p (g a) d -> p a g d", a=4)[:, a4, :, :],
                    start=True, stop=True), r=['bdec', 'xdt'], w=[phk], inc=(a4 == 3))
            phv = ph[:, :].rearrange("p (a c) -> p a c", c=128)
            for g in range(2):
                sl = slice(g * 64, (g + 1) * 64)
                pg.op('dve', lambda e, g=g, sl=sl: e.tensor_tensor(
                    out=hT[sl, :, :], in0=hT[sl, :, :],
                    in1=ssd8[sl, 2, g * 4:(g + 1) * 4].unsqueeze(2).to_broadcast([64, 4, 64]), op=ALU.mult),
                    r=['hT', 'ssd8', py2k, py3k], w=['hT'])
                pg.op('dve', lambda e, g=g, sl=sl: e.tensor_tensor(
                    out=hT[sl, :, :], in0=hT[sl, :, :], in1=phv[sl, :, g * 64:(g + 1) * 64], op=ALU.add),
                    r=['hT', phk], w=['hT'])
            pg.op('act', lambda e: e.activation(out=hTb[:, :, :], in_=hT[:, :, :], func=AF.Copy), r=['hT'], w=['hTb'])

        BS = [dict(Ah=Ah, Ee=Ee, Gm=Gm, bdec=bdec, xdt=xdt, xtk=xtk, Mh=Mh, btk=btk,
                   e0=ssd8[:, 0, :], e1=ssd8[:, 1, :], e2=ssd8[:, 2, :], sfx=''),
              dict(Ah=xnb.bitcast(F32).rearrange("p (h n) -> p h n", n=128),
                   Ee=f3.rearrange("p (h n) -> p h n", n=128),
                   Gm=biasQ[:, :, :].rearrange("p a b -> p (a b)").rearrange("p (g n) -> p g n", n=128),
                   Mh=qT[:, 0:2, :].rearrange("p a n -> p (a n)").rearrange("p (h n) -> p h n", n=128),
                   bdec=qT[:, 2, :].rearrange("p (a g n) -> p a g n", a=4, g=2),
                   xdt=qT[:, 3, :].rearrange("p (h d) -> p h d", d=64),
                   xtk=PT[0][:, :], btk=PT[1][:, 0:128],
                   e0=st8[:, 5, :], e1=st8[:, 6, :], e2=st8[:, 7, :], sfx='_1')]
        SET1K = ['Ah%d_1' % h for h in range(4)] + ['Ee_1', 'Gm_1', 'bdec_1', 'xdt_1', 'xtk_1', 'Mh_1', 'btk_1', 'ssd8_1',
                                                     'qT', 'PT0', 'PT1', 'biasQ', 'xnb', 'f3', 'st8lf', 'st8c', 'st8x']

        def ssdA(t, np_, S):
            b = BS[S]; x_ = b['sfx']
            K = lambda n: n + x_
            c0 = t * np_
            i = rot('T'); pt = psT[i]; pk = 'psT%d' % i
            for j in range(5):
                pg.op('pe', lambda e, j=j: e.transpose(out=pt[0:np_, j * 128:(j + 1) * 128], in_=xbc[:, j, c0:c0 + np_],
                                                       identity=ident_b[:, :]), r=['xbc', 'cstb'], w=[pk], inc=(j == 4))
            yield
            pg.op('act', lambda e: e.activation(out=b['xtk'][0:np_, :], in_=pt[0:np_, 0:512], func=AF.Copy), r=[pk], w=[K('xtk')])
            pg.op('act', lambda e: e.activation(out=b['btk'][0:np_, :], in_=pt[0:np_, 512:640], func=AF.Copy), r=[pk], w=[K('btk')])
            yield
            pg.op('dve', lambda e: e.tensor_tensor(out=b['xdt'][0:np_, :, :], in0=pt[0:np_, 0:512].rearrange("p (h d) -> p h d", d=64),
                                                   in1=dts[0:np_, t, :].unsqueeze(2).to_broadcast([np_, 8, 64]), op=ALU.mult),
                  r=[pk, 'dts'], w=[K('xdt')])
            yield
            a_ap = asb[0:np_, t, :]
            i = rot('A'); p = psA[i]; pk2 = 'psA%d' % i
            pg.op('pe', lambda e: e.matmul(p[0:np_, 0:8], lhsT=M1f[0:np_, 0:np_], rhs=a_ap, start=True, stop=True),
                  r=['asb', 'cst'], w=[pk2], inc=False)
            pg.op('pe', lambda e: e.matmul(p[0:np_, 8:16], lhsT=M2f[0:np_, 0:np_], rhs=a_ap, start=True, stop=True),
                  r=['asb', 'cst'], w=[pk2], inc=False)
            pg.op('pe', lambda e: e.matmul(p[:, 16:24], lhsT=M1f[0:np_, :], rhs=a_ap, start=True, stop=False),
                  r=['asb', 'cst'], w=[pk2], inc=False)
            pg.op('pe', lambda e: e.matmul(p[:, 16:24], lhsT=M2f[0:np_, :], rhs=a_ap, start=False, stop=True),
                  r=['asb', 'cst'], w=[pk2])
            yield
            pg.op('act', lambda e: e.activation(out=b['e0'][0:np_, :], in_=p[0:np_, 0:8], func=AF.Exp), r=[pk2], w=[K('ssd8')])
            pg.op('act', lambda e: e.activation(out=b['e1'][0:np_, :], in_=p[0:np_, 8:16], func=AF.Exp), r=[pk2], w=[K('ssd8')])
            pg.op('act', lambda e: e.activation(out=b['e2'][:, :], in_=p[:, 16:24], func=AF.Exp), r=[pk2], w=[K('ssd8')])
            yield
            for g in range(2):
                i = rot('A'); pgm = psA[i]; pk3 = 'psA%d' % i
                pg.op('pe', lambda e, g=g, pgm=pgm: e.matmul(pgm[0:np_, 0:np_], lhsT=xbc[g * 64:(g + 1) * 64, 4, c0:c0 + np_],
                                                    rhs=xbc[g * 64:(g + 1) * 64, 5, c0:c0 + np_], start=True, stop=True),
                      r=['xbc'], w=[pk3])
                pg.op('dve', lambda e, g=g, pgm=pgm: e.tensor_tensor(
                    out=b['Gm'][0:np_, g, 0:np_], in0=pgm[0:np_, 0:np_], in1=M1f[0:np_, 0:np_], op=ALU.mult),
                    r=[pk3, 'cst'], w=[K('Gm')])
                yield
            for hh in range(2):
                for h4 in range(4):
                    h = hh * 4 + h4
                    pg.op('dve', lambda e, h=h, h4=h4: e.tensor_scalar(
                        out=b['Ah'][0:np_, h4, 0:np_], in0=M2f[0:np_, 0:np_], scalar1=asb[0:np_, t, h:h + 1], scalar2=None,
                        op0=ALU.mult), r=['cst', 'asb'], w=[K('Ah%d' % h4)])
                    if h4 % 2:
                        yield
                psx = psS[hh]; psk = 'psS%d' % hh
                for h4 in range(4):
                    pg.op('pe', lambda e, h4=h4, psx=psx: e.matmul(
                        psx[0:np_, h4 * 128:h4 * 128 + np_], lhsT=b['Ah'][0:np_, h4, 0:np_], rhs=M1f[0:np_, 0:np_],
                        start=True, stop=True), r=[K('Ah%d' % h4), 'cst'], w=[psk], inc=(h4 == 3))
                yield
                pg.op('act', lambda e, psx=psx: e.activation(
                    out=b['Ee'][0:np_, :, 0:np_],
                    in_=psx[0:np_, :].rearrange("p (h n) -> p h n", n=128)[:, :, 0:np_], func=AF.Exp),
                    r=[psk], w=[K('Ee')])
                yield
                pg.op('dve', lambda e, hh=hh: e.tensor_tensor(
                    out=b['Mh'][0:np_, hh * 4:(hh + 1) * 4, 0:np_], in0=b['Ee'][0:np_, :, 0:np_],
                    in1=b['Gm'][0:np_, hh, 0:np_].unsqueeze(1).to_broadcast([np_, 4, np_]), op=ALU.mult),
                    r=[K('Ee'), K('Gm')], w=[K('Mh')])
                yield
            pg.op('dve', lambda e: e.tensor_tensor(
                out=b['bdec'][0:np_, :, :, :].rearrange("p a g n -> p g a n"),
                in0=b['btk'][0:np_, :].rearrange("p (g n) -> p g n", n=64).unsqueeze(2).to_broadcast([np_, 2, 4, 64]),
                in1=b['e1'][0:np_, :].rearrange("p (g a) -> p g a", a=4).unsqueeze(3).to_broadcast([np_, 2, 4, 64]),
                op=ALU.mult), r=[K('btk'), K('ssd8')], w=[K('bdec')])
            yield

        def ssdB(t, np_, S):
            b = BS[S]; x_ = b['sfx']
            K = lambda n: n + x_
            c0 = t * np_
            io = rot('O'); py = psO[io]; pyk = 'psO%d' % io
            io2 = rot('O'); py2 = psO[io2]; py2k = 'psO%d' % io2
            for h in range(8):
                pg.op('pe', lambda e, h=h: e.matmul(py[0:np_, h * 64:(h + 1) * 64], lhsT=b['Mh'][0:np_, h, 0:np_],
                                                    rhs=b['xdt'][0:np_, h, :], start=True, stop=True),
                      r=[K('Mh'), K('xdt')], w=[pyk], inc=(h == 7))
            yield
            i3 = rot('A'); py3 = psA[i3]; py3k = 'psA%d' % i3
            for h in range(8):
                g = h // 4
                dstp = (py2 if g == 0 else py3)
                pg.op('pe', lambda e, h=h, g=g, dstp=dstp: e.matmul(dstp[0:np_, (h % 4) * 64:(h % 4 + 1) * 64],
                                                         lhsT=xbc[g * 64:(g + 1) * 64, 5, c0:c0 + np_],
                                                         rhs=hTb[g * 64:(g + 1) * 64, h % 4, :], start=True, stop=True),
                      r=['xbc', 'hTb'], w=[py2k if g == 0 else py3k], inc=(h % 4 == 3))
            yield
            yv = f1[0:np_, :].rearrange("p (h d) -> p h d", d=64)
            for g, (dstp, dk) in enumerate(((py2, py2k), (py3, py3k))):
                pg.op('dve', lambda e, g=g, dstp=dstp: e.tensor_tensor(
                    out=yv[:, g * 4:(g + 1) * 4, :], in0=dstp[0:np_, 0:256].rearrange("p (h d) -> p h d", d=64),
                    in1=b['e0'][0:np_, g * 4:(g + 1) * 4].unsqueeze(2).to_broadcast([np_, 4, 64]), op=ALU.mult),
                    r=[dk, K('ssd8')], w=['f1'])
                yield
            pg.op('dve', lambda e: e.tensor_tensor(out=f1[0:np_, :], in0=f1[0:np_, :], in1=py[0:np_, :], op=ALU.add),
                  r=['f1', pyk], w=['f1'])
            pg.op('pool', lambda e: e.tensor_tensor(out=f2[0:np_, :].rearrange("p (h d) -> p h d", d=64),
                                                    in0=b['xtk'][0:np_, :].rearrange("p (h d) -> p h d", d=64),
                                                    in1=smv('b_d', np_).unsqueeze(2).to_broadcast([np_, 8, 64]), op=ALU.mult),
                  r=[K('xtk'), 'small'], w=['f2'])
            yield
            pg.op('dve', lambda e: e.tensor_tensor(out=f1[0:np_, :], in0=f1[0:np_, :], in1=f2[0:np_, :], op=ALU.add),
                  r=['f1', 'f2'], w=['f1'])
            yield
            pg.op('dve', lambda e: e.tensor_tensor(out=f1[0:np_, :], in0=f1[0:np_, :], in1=sz[0:np_, t, :], op=ALU.mult),
                  r=['f1', 'szb'], w=['f1'])
            yield
            ssv = st8[0:np_, 4, 0:1]
            pg.op('act', lambda e: e.activation(out=sq[0:np_, 0:512], in_=f1[0:np_, :], func=AF.Square, accum_out=ssv),
                  r=['f1'], w=['sq', 'st8'])
            yield
            pg.op('dve', lambda e: e.tensor_scalar(out=ssv, in0=ssv, scalar1=1.0 / 512, scalar2=EPS,
                                                    op0=ALU.mult, op1=ALU.add), r=['st8'], w=['st8'])
            yield
            pg.op('pool', lambda e: e.tensor_tensor(out=ssv, in0=ssv, in1=nhalf[0:np_, 0:1], op=ALU.pow),
                  r=['st8', 'epst'], w=['st8'])
            yield
            pg.op('dve', lambda e: e.scalar_tensor_tensor(out=oz[0:np_, t, :], in0=f1[0:np_, :], scalar=ssv,
                                                          in1=smv('b_norm', np_), op0=ALU.mult, op1=ALU.mult),
                  r=['f1', 'st8', 'small'], w=['oz'])
            yield
            i = rot('A'); ph = psA[i]; phk = 'psA%d' % i
            for a4 in range(4):
                pg.op('pe', lambda e, a4=a4: e.matmul(
                    ph[:, a4 * 128:(a4 + 1) * 128], lhsT=b['bdec'][0:np_, a4, :, :].rearrange("p g n -> p (g n)"),
                    rhs=b['xdt'][0:np_, :, :].rearrange("p (g a) d -> p a g d", a=4)[:, a4, :, :],
                    start=True, stop=True), r=[K('bdec'), K('xdt')], w=[phk], inc=(a4 == 3))
            yield
            phv = ph[:, :].rearrange("p (a c) -> p a c", c=128)
            for g in range(2):
                sl = slice(g * 64, (g + 1) * 64)
                pg.op('dve', lambda e, g=g, sl=sl: e.tensor_tensor(
                    out=hT[sl, :, :], in0=hT[sl, :, :],
                    in1=b['e2'][sl, g * 4:(g + 1) * 4].unsqueeze(2).to_broadcast([64, 4, 64]), op=ALU.mult),
                    r=['hT', K('ssd8'), py2k, py3k], w=['hT'])
                yield
                pg.op('dve', lambda e, g=g, sl=sl: e.tensor_tensor(
                    out=hT[sl, :, :], in0=hT[sl, :, :], in1=phv[sl, :, g * 64:(g + 1) * 64], op=ALU.add),
                    r=['hT', phk], w=['hT'])
                yield
            pg.op('act', lambda e: e.activation(out=hTb[:, :, :], in_=hT[:, :, :], func=AF.Copy), r=['hT'], w=['hTb'])
            yield

        def ssd_pipelined(NT, np_):
            barrier(AEK + SET1K)
            for _ in ssdA(0, np_, 0):
                pass
            for t in range(NT):
                gens = [ssdB(t, np_, t % 2)]
                if t + 1 < NT:
                    gens.append(ssdA(t + 1, np_, (t + 1) % 2))
                while gens:
                    for g_ in list(gens):
                        try:
                            next(g_)
                        except StopIteration:
                            gens.remove(g_)
            barrier(AEK + SET1K)

        def process_group(l, kind, Q):
            samp = (kind == 's')
            np_ = 64 if samp else 128
            NT = 4
            NTOK = NT * np_
            xsrc = (D['xs'] if samp else D['xp']) if l == 0 else (x1s if samp else x1p)
            xdst = (D['ys'] if samp else D['yp']) if l == NL - 1 else (x1s if samp else x1p)
            row0 = 0 if samp else Q * 512

            def rows(t):
                return slice(row0 + t * np_, row0 + (t + 1) * np_)

            norm_tiles(lambda t: xsrc[rows(t), :], NT, np_, lambda n: smv('g_norm', n),
                       (lambda t: [('xd', kind, row0 + t * np_)]) if l > 0 else None)

            if cfg.get('marks'): print('MARK', kind, Q, 'C_proj', pg.nops)
            wt, wk = load_w(D['w_in'], l, COL['c_q'], 512)
            for t, p, pk in pipe_tiles(NT, lambda t: proj_tm(xnT, 'xnT', t * np_, np_, wt, wk, 512)):
                evac_q(p, pk, t, np_, 8, 'c_qnorm')
            wt, wk = load_w(D['w_in'], l, COL['c_k'], 512)
            if samp:
                kTs = sb_kTs
            for t, p, pk in pipe_tiles(NT, lambda t: proj_tm(xnT, 'xnT', t * np_, np_, wt, wk, 512)):
                if samp:
                    evac_k(p, pk, np_, 8, 'c_knorm', D['sc_k'][l, rows(t), :],
                           lambda t=t: kTs[:, 0:4, t * 64:(t + 1) * 64], 'kTs')
                else:
                    gt = Q * 4 + t
                    evac_k(p, pk, np_, 8, 'c_knorm', D['pc_k'][l, rows(t), :],
                           lambda gt=gt: kT_c[:, :, gt * 128:(gt + 1) * 128], 'kT_c')
            wt, wk = load_w(D['w_in'], l, COL['c_v'], 512)
            for t, p, pk in pipe_tiles(NT, lambda t: proj_tm(xnT, 'xnT', t * np_, np_, wt, wk, 512)):
                if samp:
                    evac_v(p, pk, np_, 8, D['sc_v'][l, rows(t), :], vas[0:np_, t, :, 0:64], 'vas')
                else:
                    evac_v(p, pk, np_, 8, D['pc_v'][l, rows(t), :], va_c[:, Q * 4 + t, :, 0:64], 'va_c')
            wt, wk = load_w(D['w_in'], l, COL['c_f'], 8)
            if not samp:
                psv, pk = proj8_all(NT, np_, wt, wk)
                softplus4(psv, pk, smv('c_fbias', np_), lfs[0:np_, :, :], 'lfs', np_, NT, neg=True)
                out_tokens.append(pg.dma('sp', D['pc_f'][l, row0:row0 + NT * np_, :].rearrange("(t p) h -> p t h", p=np_),
                                         lfs[0:np_, :, :], r=['lfs'], nonc=True))
                for t in range(NT):
                    gt = Q * 4 + t
                    cumsum_tile(lfs[0:np_, t, :], 'lfs', 128, cum[:, gt, :], 'cum',
                                None if gt == 0 else (S127f, cum[:, gt - 1, :]), ['cum'])
            for t, p, pk in (pipe_tiles(NT, lambda t: proj_tm(xnT, 'xnT', t * np_, np_, wt, wk, 8)) if samp else ()):
                lf = st8[0:np_, 5, :]
                softplus_from(p, pk, smv('c_fbias', np_), lf, 'st8lf', np_, neg=True)
                if samp:
                    out_tokens.append(pg.dma('sp', D['sc_f'][l, rows(t), :], lf, r=['st8lf'], nonc=True))
                    carry = None
                    for kb in range(8):
                        pg.dma('sp', st8[:, 7, :], D['cc_f'][l, t, kb * 128:(kb + 1) * 128, :], w=['st8x'], nonc=True)
                        cumsum_tile(st8[:, 7, :], 'st8x', 128, cums[:, t, kb, :], 'cums',
                                    None if kb == 0 else (S127f, cums[:, t, kb - 1, :]), ['cums'])
                    cumsum_tile(lf, 'st8lf', 64, cums[0:64, t, 8, :], 'cums', (S127f[:, 0:64], cums[:, t, 7, :]), ['cums'])
                else:
                    gt = Q * 4 + t
                    out_tokens.append(pg.dma('sp', D['pc_f'][l, rows(t), :], lf, r=['st8lf'], nonc=True))
                    cumsum_tile(lf, 'st8lf', 128, cum[:, gt, :], 'cum',
                                None if gt == 0 else (S127f, cum[:, gt - 1, :]), ['cum'])
            wt, wk = load_w(D['w_in'], l, COL['c_z'], 512)
            for t, p, pk in pipe_tiles(NT, lambda t: proj_tm(xnT, 'xnT', t * np_, np_, wt, wk, 512)):
                pg.op('act', lambda e, p=p, t=t: e.activation(out=sz[0:np_, t, :], in_=p, func=AF.Silu), r=[pk], w=['sz'])
            if cfg.get('marks'): print('MARK', kind, Q, 'C_attn', pg.nops)
            if not samp:
                nkb = 4 * Q + 4
                i = rot('A'); pb = psA[i]; pbk = 'psA%d' % i
                pg.op('pe', lambda e: e.matmul(pb[:, 0:8], lhsT=S127f, rhs=cum[:, nkb - 1, :], start=True, stop=True),
                      r=['cum', 'cst'], w=[pbk])
                pg.op('act', lambda e: e.activation(out=st8[:, 6, :], in_=pb[:, 0:8], func=AF.Copy), r=[pbk], w=['st8c'])
                pg.op('dve', lambda e: e.tensor_tensor(out=biasQ[:, 0:nkb, :],
                                                       in0=st8[:, 6, :].unsqueeze(1).to_broadcast([128, nkb, 8]),
                                                       in1=cum[:, 0:nkb, :], op=ALU.subtract), r=['st8c', 'cum'], w=['biasQ'])
                for h in range(8):
                    hp, hb = h // 2, (h % 2) * 64
                    kbs = []
                    for kb in range(nkb):
                        dI = kb - 4 * Q
                        d = dict(kT=kT_c[hb:hb + 64, hp, kb * 128:(kb + 1) * 128], v=va_c[:, kb, h, 0:65], nk=128,
                                 bias=biasQ[:, kb, h:h + 1], bkeys=['biasQ'], q0=max(dI, 0) * 128, keys=['kT_c', 'va_c'])
                        kbs.append(d)
                    kb2 = []
                    for kb, d in enumerate(kbs):
                        dI = kb - 4 * Q
                        if dI < 0:
                            kb2.append(d)
                        else:
                            d['diag'] = True
                            kb2.append(d)
                    attend_fox(qT[hb:hb + 64, hp, :], 512, [(j * 128, 128) for j in range(4)], kb2, ['qT'],
                               attn_out(h * 64, 4, 128))
            else:
                for t in range(NT):
                    i = rot('A'); pb = psA[i]; pbk = 'psA%d' % i
                    pg.op('pe', lambda e, t=t: e.matmul(pb[:, 0:8], lhsT=S63f[0:64, :], rhs=cums[0:64, t, 8, :],
                                                        start=True, stop=True), r=['cums', 'cst'], w=[pbk])
                    pg.op('act', lambda e: e.activation(out=st8[:, 6, :], in_=pb[:, 0:8], func=AF.Copy), r=[pbk], w=['st8c'])
                    pg.op('dve', lambda e, t=t: e.tensor_tensor(out=biasQ[:, 0:9, :],
                                                                in0=st8[:, 6, :].unsqueeze(1).to_broadcast([128, 9, 8]),
                                                                in1=cums[:, t, :, :], op=ALU.subtract),
                          r=['st8c', 'cums'], w=['biasQ'])
                    load_cache_kv(D['cc_k'][l, t], D['cc_v'][l, t], 8, 8)
                    for h in range(8):
                        hp, hb = h // 2, (h % 2) * 64
                        kbs = []
                        for kb in range(8):
                            kbs.append(dict(kT=ckT[hb:hb + 64, hp, kb * 128:(kb + 1) * 128], v=cva_[:, kb, h, 0:65], nk=128,
                                            bias=biasQ[:, kb, h:h + 1], bkeys=['biasQ'], q0=0, keys=['ckT', 'cvaS']))
                        kbs.append(dict(kT=kTs[hb:hb + 64, hp, t * 64:(t + 1) * 64], v=vas[0:64, t, h, 0:65], nk=64,
                                        bias=biasQ[0:64, 8, h:h + 1], bkeys=['biasQ'], q0=0, keys=['kTs', 'vas'], diag=True))
                        attend_fox(qT[hb:hb + 64, hp, t * 64:(t + 1) * 64], 64, [(0, 64)], kbs, ['qT'],
                                   attn_out_s(h * 64, t))
            oz_to_T('c', NT, np_, 4)

            if cfg.get('marks'): print('MARK', kind, Q, 'A_proj', pg.nops)
            wt, wk = load_w(D['w_in'], l, COL['a_q'], 512)
            for t, p, pk in pipe_tiles(NT, lambda t: proj_tm(xnT, 'xnT', t * np_, np_, wt, wk, 512)):
                evac_q(p, pk, t, np_, 8, 'a_qnorm')
            wt, wk = load_w(D['w_in'], l, COL['a_k'], 512)
            for t, p, pk in pipe_tiles(NT, lambda t: proj_tm(xnT, 'xnT', t * np_, np_, wt, wk, 512)):
                if samp:
                    evac_k(p, pk, np_, 8, 'a_knorm', D['sa_k'][l, rows(t), :],
                           lambda t=t: kTs[:, 0:4, t * 64:(t + 1) * 64], 'kTs')
                else:
                    gt = Q * 4 + t
                    od = D['pa_k'][l, (gt - 28) * 128:(gt - 27) * 128, :] if gt >= 28 else None
                    evac_k(p, pk, np_, 8, 'a_knorm', od,
                           lambda gt=gt: kT_a[:, :, (gt % 8) * 128:(gt % 8 + 1) * 128], 'kT_a')
            wt, wk = load_w(D['w_in'], l, COL['a_v'], 512)
            for t, p, pk in pipe_tiles(NT, lambda t: proj_tm(xnT, 'xnT', t * np_, np_, wt, wk, 512)):
                if samp:
                    evac_v(p, pk, np_, 8, D['sa_v'][l, rows(t), :], vas[0:np_, t, :, 0:64], 'vas')
                else:
                    gt = Q * 4 + t
                    od = D['pa_v'][l, (gt - 28) * 128:(gt - 27) * 128, :] if gt >= 28 else None
                    evac_v(p, pk, np_, 8, od, va_a[:, gt % 8, :, 0:64], 'va_a')
            wt, wk = load_w(D['w_in'], l, COL['a_z'], 512)
            for t, p, pk in pipe_tiles(NT, lambda t: proj_tm(xnT, 'xnT', t * np_, np_, wt, wk, 512)):
                pg.op('act', lambda e, p=p, t=t: e.activation(out=sz[0:np_, t, :], in_=p, func=AF.Silu), r=[pk], w=['sz'])
            for t in range(NT):
                gt = Q * 4 + t
                if samp:
                    load_cache_kv(D['ca_k'][l, t], D['ca_v'][l, t], 4, 8)
                for hq in range(2):
                    acc = None
                    for h4 in range(4):
                        h = hq * 4 + h4
                        hp, hb = h // 2, (h % 2) * 64
                        kbs = []
                        if not samp:
                            for i5 in range(5):
                                gk = gt - 4 + i5
                                if gk < 0:
                                    continue
                                s8 = gk % 8
                                d_ = dict(kT=kT_a[hb:hb + 64, hp, s8 * 128:(s8 + 1) * 128], v=va_a[:, s8, h, 0:65], nk=128,
                                          bias=None, q0=0, keys=['kT_a', 'va_a'])
                                if i5 in (1, 2):
                                    d_['bias'] = expB[:, h, i5 * 128:i5 * 128 + 1]; d_['bkeys'] = ['expB']
                                else:
                                    d_['badd'] = expB[:, h, i5 * 128:(i5 + 1) * 128]; d_['mkeys'] = ['expB']
                                kbs.append(d_)
                        else:
                            for kb in range(4):
                                d_ = dict(kT=ckT[hb:hb + 64, hp, kb * 128:(kb + 1) * 128], v=cva_[:, kb, h, 0:65], nk=128,
                                          bias=None, q0=0, keys=['ckT', 'cvaS'])
                                if kb in (0, 1, 2):
                                    d_['bias'] = expB[:, h, kb * 128:kb * 128 + 1]; d_['bkeys'] = ['expB']
                                else:
                                    d_['badd'] = expB[:, h, kb * 128:kb * 128 + 64]; d_['mkeys'] = ['expB']
                                kbs.append(d_)
                            kbs.append(dict(kT=kTs[hb:hb + 64, hp, t * 64:(t + 1) * 64], v=vas[0:64, t, h, 0:65], nk=64,
                                            bias=None, badd=expB[:, h, 512:576], mkeys=['expB'], q0=0, keys=['kTs', 'vas']))
                        a2 = None if acc is None else (acc[0], acc[1], h4, False)
                        acc = attend(qT[hb:hb + 64, hp, t * np_:(t + 1) * np_], np_, [(0, np_)], kbs, ['qT'], None, acc=a2)
                    pov, pok = acc
                    rl = st8[0:np_, 2, 0:4]
                    pg.op('dve', lambda e, pov=pov: e.reciprocal(out=rl, in_=pov[0:np_, 0:4, 64]), r=[pok], w=['st8'])
                    tmp = f2[0:np_, 0:256].rearrange("p (j d) -> p j d", d=64)
                    pg.op('dve', lambda e, pov=pov, tmp=tmp: e.tensor_tensor(out=tmp, in0=pov[0:np_, 0:4, 0:64],
                                                                           in1=rl.unsqueeze(2).to_broadcast([np_, 4, 64]), op=ALU.mult),
                          r=[pok, 'st8'], w=['f2'])
                    pg.op('pool', lambda e, tmp=tmp, hq=hq, t=t: e.tensor_tensor(
                        out=oz[0:np_, t, hq * 256:(hq + 1) * 256].rearrange("p (j d) -> p j d", d=64), in0=tmp,
                        in1=sz[0:np_, t, hq * 256:(hq + 1) * 256].rearrange("p (j d) -> p j d", d=64), op=ALU.mult),
                        r=['f2', 'sz'], w=['oz'])
            oz_to_T('a', NT, np_, 4)

            if cfg.get('marks'): print('MARK', kind, Q, 'M', pg.nops)
            wt, wk = load_w(D['w_in'], l, COL['m_q'], 512)
            for t, p, pk in pipe_tiles(NT, lambda t: proj_tm(xnT, 'xnT', t * np_, np_, wt, wk, 512)):
                pg.op('act', lambda e, p=p, t=t: e.activation(out=sz[0:np_, t, 0:256], in_=p[:, 256:512], func=AF.Silu),
                      r=[pk], w=['sz'])
                evac_q(p[:, 0:256], pk, t, np_, 4, 'm_qnorm')
            if not samp:
                for h in range(4):
                    hp, hb = h // 2, (h % 2) * 64
                    kbs = [dict(kT=kT_m[hb:hb + 64, hp, kb * 128:(kb + 1) * 128], v=va_m[:, kb, h, 0:65], nk=128, bias=None,
                                q0=0, keys=['kT_m', 'va_m']) for kb in range(2)]
                    attend(qT[hb:hb + 64, hp, :], 512, [(j * 128, 128) for j in range(4)], kbs, ['qT'],
                           attn_out(h * 64, 4, 128))
            else:
                for t in range(NT):
                    load_cache_kv(D['cm_k'][l, t], D['cm_v'][l, t], 2, 4)
                    for h in range(4):
                        hp, hb = h // 2, (h % 2) * 64
                        kbs = [dict(kT=ckT[hb:hb + 64, hp, kb * 128:(kb + 1) * 128], v=cva_[:, kb, h, 0:65], nk=128, bias=None,
                                    q0=0, keys=['ckT', 'cvaS']) for kb in range(2)]
                        attend(qT[hb:hb + 64, hp, t * 64:(t + 1) * 64], 64, [(0, 64)], kbs, ['qT'], attn_out_s(h * 64, t))
            oz_to_T('m', NT, np_, 2)

            if cfg.get('marks'): print('MARK', kind, Q, 'B_proj', pg.nops)
            wt, wk = load_w(D['w_in'], l, COL['b_z'], 512)
            for t, p, pk in pipe_tiles(NT, lambda t: proj_tm(xnT, 'xnT', t * np_, np_, wt, wk, 512)):
                pg.op('act', lambda e, p=p, t=t: e.activation(out=sz[0:np_, t, :], in_=p, func=AF.Silu), r=[pk], w=['sz', 'szb'])
            wt, wk = load_w(D['w_in'], l, COL['b_dt'], 8)
            psv, pk = proj8_all(NT, np_, wt, wk)
            softplus4(psv, pk, smv('b_dt_bias', np_), dts[0:np_, :, :], 'dts', np_, NT)
            pg.op('dve', lambda e: e.tensor_tensor(out=asb[0:np_, :, :], in0=dts[0:np_, :, :],
                                                   in1=aneg[0:np_, :].unsqueeze(1).to_broadcast([np_, NT, 8]), op=ALU.mult),
                  r=['dts', 'aneg'], w=['asb'])
            for half in range(2):
                js = slice(half * 3, half * 3 + 3)
                if samp:
                    for t in range(NT):
                        for jj in range(3):
                            j = half * 3 + jj
                            pg.dma('sp', raw[:, jj, t, 0:3],
                                   D['sb_conv'][l, t][:, j * 128:(j + 1) * 128].rearrange("r p -> p r"), w=['raw'], nonc=True)
                else:
                    pg.op('pool', lambda e, js=js: cp(e, out=raw[:, :, 0, 0:3], in_=halo[:, js, :]), r=['halo'], w=['raw'])
                wt, wk = load_w(D['w_in'], l, COL['b_xbc'] + half * 384, 384)
                for jj in range(3):
                    i = rot('A'); p = psA[i]; pk = 'psA%d' % i
                    for kc in range(8):
                        pg.op('pe', lambda e, kc=kc, jj=jj, p=p, wt=wt: e.matmul(p[:, 0:NTOK], lhsT=wt[:, kc, jj * 128:(jj + 1) * 128],
                                                                     rhs=xnT[:, kc, 0:NTOK], start=(kc == 0), stop=(kc == 7)),
                              r=['xnT'] + list(wk), w=[pk], inc=(kc == 7))
                    pg.op('act', lambda e, jj=jj, p=p: e.activation(out=raw[:, jj, :, 3:3 + np_],
                                                             in_=p[:, 0:NTOK].rearrange("p (t n) -> p t n", n=np_), func=AF.Copy),
                          r=[pk], w=['raw'])
                if not samp:
                    for t in range(1, NT):
                        pg.op('pool', lambda e, t=t: cp(e, out=raw[:, :, t, 0:3], in_=raw[:, :, t - 1, np_:np_ + 3]),
                              r=['raw'], w=['raw'])
                    pg.op('pool', lambda e, js=js: cp(e, out=halo[:, js, :], in_=raw[:, :, NT - 1, np_:np_ + 3]),
                          r=['raw'], w=['halo'])
                    if Q == NQ - 1:
                        for jj in range(3):
                            j = half * 3 + jj
                            out_tokens.append(pg.dma('sp', D['pb_c'][l][:, j * 128:(j + 1) * 128].rearrange("r p -> p r"),
                                                     halo[:, j, :], r=['halo'], nonc=True))
                else:
                    for t in range(NT):
                        for jj in range(3):
                            j = half * 3 + jj
                            out_tokens.append(pg.dma('sp', D['sb_c'][l, t][:, j * 128:(j + 1) * 128].rearrange("r p -> p r"),
                                                     raw[:, jj, t, np_:np_ + 3], r=['raw'], nonc=True))
                for jj in range(3):
                    j = half * 3 + jj
                    cv = cva[:, 0:NTOK].rearrange("p (t n) -> p t n", n=np_)
                    pg.op('dve', lambda e, j=j, jj=jj, cv=cv: e.tensor_scalar(out=cv, in0=raw[:, jj, :, 0:np_], scalar1=cw[:, j, 0:1],
                                                                scalar2=cw[:, j, 4:5], op0=ALU.mult, op1=ALU.add),
                          r=['raw', 'cw'], w=['cva'])
                    for tap in range(1, 4):
                        pg.op('dve', lambda e, j=j, jj=jj, tap=tap, cv=cv: e.scalar_tensor_tensor(
                            out=cv, in0=raw[:, jj, :, tap:tap + np_], scalar=cw[:, j, tap:tap + 1], in1=cv,
                            op0=ALU.mult, op1=ALU.add), r=['raw', 'cw', 'cva'], w=['cva'])
                    pg.op('act', lambda e, j=j: e.activation(out=xbc[:, j, 0:NTOK], in_=cva[:, 0:NTOK], func=AF.Silu),
                          r=['cva'], w=['xbc'])
            if samp:
                barrier(AEK + SET1K)
                for _ in ssdA(0, np_, 0):
                    pass
                for t in range(NT):
                    state_load(D['sb_ssm'][l, t])
                    gens = [ssdB(t, np_, t % 2)]
                    if t + 1 < NT:
                        gens.append(ssdA(t + 1, np_, (t + 1) % 2))
                    while gens:
                        for g_ in list(gens):
                            try:
                                next(g_)
                            except StopIteration:
                                gens.remove(g_)
                    state_store(D['sb_s'][l, t])
                barrier(AEK + SET1K)
            else:
                ssd_pipelined(NT, np_)
            if (not samp) and Q == NQ - 1 and not NOSTATE:
                state_store(D['pb_s'][l])
            oz_to_T('b', NT, np_, 4)

            if cfg.get('marks'): print('MARK', kind, Q, 'merge', pg.nops)
            barrier(AEK)
            brs = (('a', 'w_pa', 4), ('b', 'w_pb', 4), ('c', 'w_pc', 4), ('m', 'w_pm', 2))
            for c in range(2):
                for bi, (br, wn, nkc) in enumerate(brs):
                    wg, wgk = load_w(D['w_in'], l, COL['gate'] + bi * 1024 + c * 512, 512)
                    wp, wpk = load_w(D[wn], l, c * 512, 512, nk=nkc)
                    def mproj(t, wg=wg, wgk=wgk, wp=wp, wpk=wpk, br=br, nkc=nkc):
                        p1, pk1 = proj_tm(xnT, 'xnT', t * np_, np_, wg, wgk, 512)
                        isx = rot('S'); p2 = psS[isx]; pk2 = 'psS%d' % isx
                        for kc in range(nkc):
                            pg.op('pe', lambda e, kc=kc: e.matmul(
                                p2[0:np_, :], lhsT=ozT[br][:, kc, t * np_:(t + 1) * np_], rhs=wp[:, kc, :],
                                start=(kc == 0), stop=(kc == nkc - 1)), r=['ozT_' + br] + list(wpk), w=[pk2], inc=(kc == nkc - 1))
                        return p1, pk1, p2, pk2
                    for t, p1, pk1, p2, pk2 in pipe_tiles(NT, mproj):
                        pg.op('act', lambda e, p1=p1: e.activation(out=sig[0:np_, :], in_=p1, func=AF.Sigmoid), r=[pk1], w=['cva'])
                        if bi == 0:
                            pg.op('dve', lambda e, t=t, p2=p2: e.tensor_tensor(out=macc[0:np_, t, :], in0=sig[0:np_, :],
                                                                               in1=p2[0:np_, :], op=ALU.mult),
                                  r=['cva', pk2], w=['macc%d' % t])
                        else:
                            pg.op('dve', lambda e, t=t, p2=p2: e.tensor_tensor(out=f1[0:np_, :], in0=sig[0:np_, :],
                                                                               in1=p2[0:np_, :], op=ALU.mult),
                                  r=['cva', pk2], w=['f1'])
                            pg.op('pool', lambda e, t=t: e.tensor_tensor(out=macc[0:np_, t, :], in0=macc[0:np_, t, :],
                                                                         in1=f1[0:np_, :], op=ALU.add),
                                  r=['f1', 'macc%d' % t], w=['macc%d' % t])
                for t in range(NT):
                    pg.op('act', lambda e, t=t: e.activation(out=b1[0:np_, :], in_=macc[0:np_, t, :], func=AF.Copy),
                          r=['macc%d' % t], w=['b1'])
                    transposes(lambda j: b1[0:np_, j * 128:(j + 1) * 128], 4, np_,
                               lambda e, v, t=t, c=c: cp(e, out=mT[:, c * 4:(c + 1) * 4, t * np_:(t + 1) * np_], in_=v),
                               ['b1'], ['xbc', 'Mh'], evac='dve')
            wos = [load_w(D['w_out'], l, c * 512, 512) for c in range(2)]
            for t in range(NT):
                i = rot('X'); xk = 'xt%d' % i
                pg.dma('sp', xt[0:np_, i, :], xsrc[rows(t), :], r=[('xd', kind, row0 + t * np_)] if l > 0 else [], w=[xk])
                for c in range(2):
                    wo, wok = wos[c]
                    p, pk = proj_tm(mT, 'xbc', t * np_, np_, wo, wok, 512, xkeys=['Mh'])
                    pg.op('dve', lambda e, i=i, p=p, c=c: e.tensor_tensor(out=xt[0:np_, i, c * 512:(c + 1) * 512],
                                                                          in0=xt[0:np_, i, c * 512:(c + 1) * 512], in1=p,
                                                                          op=ALU.add), r=[xk, pk], w=[xk])
                tok = pg.dma('sp', xdst[rows(t), :], xt[0:np_, i, :], r=[xk], w=[('xd', kind, row0 + t * np_)])
                out_tokens.append(tok)

        vflat = va_c[:, :, :, :].rearrange("p a h c -> p (a h c)")
        sb_kTs = vflat[:, 0:1024].rearrange("p (k n) -> p k n", n=256)
        vas = vflat[:, 1024:1024 + 2112].rearrange("p (t h c) -> p t h c", t=4, h=8)
        ckT = vflat[:, 3136:3136 + 4096].rearrange("p (k n) -> p k n", n=1024)
        cva_ = vflat[:, 7232:7232 + 4224].rearrange("p (a h c) -> p a h c", a=8, h=8)
        cums = kT_a[:, 0, 0:576].bitcast(F32).rearrange("p (t k h) -> p t k h", t=4, k=9)
        SKEYS = ['kTs', 'vas', 'ckT', 'cvaS']

        def load_cache_kv(kd, vd, nblk, nh):
            n = nh * 64
            for kb in range(nblk):
                stg, sk = ((b1, 'b1'), (xnb, 'xnb'))[kb % 2]
                pg.dma('pool', stg[:, 0:n], kd[kb * 128:(kb + 1) * 128, :], w=[sk])
                transposes(lambda j, stg=stg: stg[:, j * 128:(j + 1) * 128], nh // 2, 128,
                           lambda e, v, kb=kb: cp(e, out=ckT[:, 0:nh // 2, kb * 128:(kb + 1) * 128], in_=v),
                           [sk], ['ckT'], evac='act' if kb % 2 else 'dve')
                pg.dma('pool', cva_[:, kb, 0:nh, 0:64], vd[kb * 128:(kb + 1) * 128, :].rearrange("p (h d) -> p h d", d=64),
                       w=['cvaS'])

        def state_load(src):
            pg.dma('sp', cva[0:64, :].rearrange("p (h n) -> p h n", n=64), src.rearrange("h p n -> p h n"), w=['cva'])
            i = rot('A'); p = psA[i]; pk = 'psA%d' % i
            for h in range(8):
                pg.op('pe', lambda e, h=h: e.matmul(p[0:64, h * 64:(h + 1) * 64], lhsT=cva[0:64, h * 64:(h + 1) * 64],
                                                    rhs=identf[0:64, :], start=True, stop=True),
                      r=['cva', 'identf'], w=[pk], inc=(h == 7))
            for g in range(2):
                pg.op('act' if g else 'dve', lambda e, g=g: cp(
                    e, out=hT[g * 64:(g + 1) * 64, :, :], in_=p[0:64, g * 256:(g + 1) * 256].rearrange("p (a d) -> p a d", d=64)),
                    r=[pk], w=['hT'])
            pg.op('act', lambda e: e.activation(out=hTb[:, :, :], in_=hT[:, :, :], func=AF.Copy), r=['hT'], w=['hTb'])

        def state_store(dst):
            for g in range(2):
                i = rot('A'); p = psA[i]; pk = 'psA%d' % i
                for a in range(4):
                    pg.op('pe', lambda e, g=g, a=a, p=p: e.matmul(p[0:64, a * 64:(a + 1) * 64], lhsT=hT[g * 64:(g + 1) * 64, a, :],
                                                             rhs=identf[g * 64:(g + 1) * 64, :], start=True, stop=True),
                          r=['hT', 'identf'], w=[pk], inc=(a == 3))
                pg.op('act' if g else 'dve', lambda e, g=g, p=p: cp(e, out=cva[0:64, g * 256:(g + 1) * 256], in_=p[0:64, 0:256]),
                      r=[pk], w=['cva'])
            out_tokens.append(pg.dma('sp', dst.rearrange("h p n -> p h n"), cva[0:64, :].rearrange("p (h n) -> p h n", n=64),
                                     r=['cva']))

        def attn_out_t(h0col, t, np_):
            def fn(pov, pok):
                rl = st8[0:np_, 2, 0:1]
                pg.op('dve', lambda e: e.reciprocal(out=rl, in_=pov[0:np_, 0, 64:65]), r=[pok], w=['st8'])
                pg.op('dve', lambda e: e.scalar_tensor_tensor(out=oz[0:np_, t, h0col:h0col + 64], in0=pov[0:np_, 0, 0:64],
                                                              scalar=rl, in1=sz[0:np_, t, h0col:h0col + 64],
                                                              op0=ALU.mult, op1=ALU.mult), r=[pok, 'st8', 'sz'], w=['oz'])
            return fn

        def attn_out_s(h0col, t):
            return attn_out_t(h0col, t, 64)

        def attend_fox(qT_ap, NQc, qtiles, kblocks, qkeys, out_fn):
            for kb in kblocks:
                if kb.get('diag'):
                    kb['diagmask'] = True
            attend(qT_ap, NQc, qtiles, kblocks, qkeys, out_fn)

        def memory_kv(l):
            def src(t):
                return D['memp'][t * 128:(t + 1) * 128, :]
            barrier(AEK)
            pg.dma('sp', AE[:, 0:1024], D['mnorm'][l:l + 1, :].broadcast_to([128, 1024]), w=AEK)
            norm_tiles(src, 2, 128, lambda n: AE[0:n, 0:1024], lambda t: AEK)
            barrier(AEK)
            wt, wk = load_w(D['w_mkv'], l, 0, 512)
            for t, p, pk in pipe_tiles(2, lambda t: proj_tm(xnT, 'xnT', t * 128, 128, wt, wk, 512)):
                evac_v(p[:, 256:512], pk, 128, 4, D['pm_v'][l, t * 128:(t + 1) * 128, :], va_m[:, t, :, 0:64], 'va_m')
                evac_k(p[:, 0:256], pk, 128, 4, 'm_knorm', D['pm_k'][l, t * 128:(t + 1) * 128, :],
                       lambda t=t: kT_m[:, :, t * 128:(t + 1) * 128], 'kT_m')

        try:
          for l in range(NL):
            layer_consts(l)
            memory_kv(l)
            pg.op('pool', lambda e: e.memset(hT[:], 0.0), w=['hT'])
            pg.op('pool', lambda e: e.memset(hTb[:], 0.0), w=['hTb'])
            pg.op('pool', lambda e: e.memset(halo[:], 0.0), w=['halo'])
            for Q in range(NQ):
                process_group(l, 'p', Q)
            if DO_S:
                barrier(['va_c', 'kT_a', 'cums'] + SKEYS)
                pg.op('pool', lambda e: e.memset(vas, 1.0), w=['vas'])
                pg.op('pool', lambda e: e.memset(cva_, 1.0), w=['cvaS'])
                process_group(l, 's', 0)
                barrier(['va_c', 'kT_a', 'cums'] + SKEYS)
                pg.op('pool', lambda e: e.memset(va_c[:], 1.0), w=['va_c'])
        except StopBuild:
            pass
        pg.maxops = None
        pg.op('pe', lambda e: e.matmul(psA[0][0:1, 0:1], lhsT=cstb[:, 0:1], rhs=cstb[:, 0:1], start=True, stop=True), r=['cstb'], w=['psA0'])
        pg.op('act', lambda e: e.activation(out=bar[:, 1:2], in_=bar[:, 1:2], func=AF.Copy), r=['psA0'], w=['bar2'])
        pg.op('dve', lambda e: e.tensor_copy(out=bar[:, 1:2], in_=bar[:, 1:2]), w=['bar2'])
        pg.op('pool', lambda e: e.tensor_copy(out=bar[:, 1:2], in_=bar[:, 1:2]), w=['bar2'])
        out_tokens.append(pg.dma('sp', D['pb_c'][0, 0:1, 0:2], bar[0:1, 0:2], r=['bar2', 'bar'], w=['zz']) if False else None)
        out_tokens[:] = [t for t in out_tokens if t is not None]
        pg.op('pool', lambda e: e.memset(bar[:, 0:1], 0.0), r=['bar2'], w=['bar'])
        pg.finish(out_tokens)
        pg._wait('sp', ('c', 'pool', pg.cnt['pool'] - 1))
        print("ops:", pg.nops, "dmas:", pg.dma_n, "cnt:", pg.cnt)
    return nc


def _consts():
    c = np.zeros((128, NCST), np.float32)
    k = np.arange(128)[:, None]
    m = np.arange(128)[None, :]
    c[:, C_M1:C_M1 + 128] = (k <= m)
    c[:, C_M2:C_M2 + 128] = (k > m)
    c[:, C_S127:C_S127 + 128] = (k == 127)
    c[:, C_S63:C_S63 + 128] = (k == 63)
    band = np.ones((128, 5, 128), np.float32)
    s = np.arange(128)[:, None]
    t = np.arange(128)[None, :]
    band[:, 0, :] = 1.0 - ((s < 64) & (t >= 64))
    band[:, 4, :] = 1.0 - ((s >= 64) & (t < 64))
    ident = (k == m).astype(np.float32)
    return c, np.ascontiguousarray(band.reshape(128, 640)), ident


def _prep(inputs, cfg=None):
    f = lambda a: np.ascontiguousarray(np.asarray(a, dtype=np.float32))
    I = {k: f(v) for k, v in inputs.items()}
    small = np.concatenate([I[n].reshape(2, -1) for n in SM], axis=1)
    s = np.arange(128)[:, None]
    j = np.arange(640)[None, :]
    dist = 512 + (j % 128) - 128 * (j // 128) - s
    idx = np.clip(dist, -128, 128) + 128
    relb = np.ascontiguousarray(np.transpose(I['a_rel'][:, idx, :], (0, 1, 3, 2)))
    cwt = np.concatenate([I['b_conv_w'], I['b_conv_b'][:, None, :]], axis=1)
    convw = np.ascontiguousarray(np.transpose(cwt.reshape(2, 5, 6, 128), (0, 3, 2, 1)))
    cst, band, ident = _consts()
    maps = []
    for c in range(8):
        b = c % 4
        ss = slice(c * NS, (c + 1) * NS)
        m = dict(
            xp=I['x_prompt'][b], xs=I['x_sample'][ss].reshape(NS * TS, DM), memp=I['mem_prompt'][b],
            ca_k=I['cache_a_k'][:, ss].reshape(2, NS, 512, 512), ca_v=I['cache_a_v'][:, ss].reshape(2, NS, 512, 512),
            cc_k=I['cache_c_k'][:, ss].reshape(2, NS, 1024, 512), cc_v=I['cache_c_v'][:, ss].reshape(2, NS, 1024, 512),
            cc_f=I['cache_c_logf'][:, ss], sb_ssm=I['state_b_ssm'][:, ss], sb_conv=I['state_b_conv'][:, ss],
            cm_k=I['cache_mem_k'][:, ss].reshape(2, NS, 256, 256), cm_v=I['cache_mem_v'][:, ss].reshape(2, NS, 256, 256),
            w_in=I['w_in'], w_mkv=I['w_mkv'], w_pa=I['w_pa'], w_pb=I['w_pb'], w_pc=I['w_pc'], w_pm=I['w_pm'],
            w_out=I['w_out'], small=small, mnorm=I['m_norm'], band=band, ident=ident, relb=relb, convw=convw, cst=cst)
        maps.append({k: np.ascontiguousarray(v) for k, v in m.items()})
    return maps


_NC_CACHE = {}


def kernel(**inputs):
    cfg = {}
    key = 'full'
    if key not in _NC_CACHE:
        _NC_CACHE[key] = build(cfg)
    nc = _NC_CACHE[key]
    maps = _prep(inputs)
    res = run_bass_kernel_spmd(nc, maps, core_ids=list(range(8)))
    R = res.results
    P4 = range(4)
    st = lambda name, shape: np.stack([R[b][name] for b in P4], axis=1).reshape(shape)
    cat = lambda name: np.concatenate([R[c][name] for c in range(8)], axis=1)
    y_prompt = np.stack([R[b]['yp'] for b in P4], axis=0)
    y_sample = np.concatenate([R[c]['ys'].reshape(NS, TS, DM) for c in range(8)], axis=0)
    outs = [y_prompt, y_sample,
            st('pa_k', (2, 4, 512, 8, 64)), st('pa_v', (2, 4, 512, 8, 64)),
            st('pc_k', (2, 4, SEQ, 8, 64)), st('pc_v', (2, 4, SEQ, 8, 64)), st('pc_f', (2, 4, SEQ, 8)),
            st('pb_s', (2, 4, 8, 64, 64)), st('pb_c', (2, 4, 3, 768)),
            st('pm_k', (2, 4, 256, 4, 64)), st('pm_v', (2, 4, 256, 4, 64))]
    for name, tail in (('sa_k', (8, 64)), ('sa_v', (8, 64)), ('sc_k', (8, 64)), ('sc_v', (8, 64)), ('sc_f', (8,))):
        a = np.concatenate([R[c][name].reshape((2, NS, TS) + tail) for c in range(8)], axis=1)
        outs.append(a)
    outs.append(cat('sb_s'))
    outs.append(cat('sb_c'))
    return tuple(np.ascontiguousarray(o.astype(np.float32)) for o in outs)
```

```python
import contextlib
import numpy as np
import ml_dtypes
import concourse.bass as bass
import concourse.mybir as mybir
from concourse.bass_utils import run_bass_kernel_spmd

F32 = mybir.dt.float32
BF16 = mybir.dt.bfloat16
AF = mybir.ActivationFunctionType
ALU = mybir.AluOpType
AX = mybir.AxisListType

DM = 1024
DIN = 10000
SEQ = 4096
NS = 4
TS = 64
EPS = 1e-6
COL = dict(a_q=0, a_k=512, a_v=1024, a_z=1536, b_z=2048, b_xbc=2560, b_dt=3328,
           c_q=3336, c_k=3848, c_v=4360, c_f=4872, c_z=4880, m_q=5392, m_z=5648, gate=5904)
SM = {}
_o = 0
for _n, _w in (('g_norm', 1024), ('b_norm', 512), ('a_qnorm', 64), ('a_knorm', 64),
               ('c_qnorm', 64), ('c_knorm', 64), ('m_qnorm', 64), ('m_knorm', 64),
               ('b_dt_bias', 8), ('b_a_log', 8), ('b_d', 8), ('c_fbias', 8)):
    SM[_n] = (_o, _w)
    _o += _w
NSM = _o
C_M1, C_M2, C_S127, C_S63 = 0, 128, 256, 384
NCST = 512


def cp(e, out, in_):
    if hasattr(e, 'tensor_copy'):
        return e.tensor_copy(out=out, in_=in_)
    return e.activation(out=out, in_=in_, func=AF.Copy)


SAME_ENGINE_SYNC = True


class StopBuild(Exception):
    pass


class Prog:
    EPOCH = 24000
    ND = 48

    def __init__(self, nc, es):
        self.nc, self.es = nc, es
        self.eng = {'pe': nc.tensor, 'act': nc.scalar, 'dve': nc.vector, 'pool': nc.gpsimd, 'sp': nc.sync}
        self.cnt = {e: 0 for e in self.eng}
        self.sems = {}
        self.waited = {e: {} for e in self.eng}
        self.last_w = {}
        self.readers = {}
        self.dma_n = 0
        self.dma_sems = [es.enter_context(nc.semaphore("dq%d" % i)) for i in range(self.ND)]
        self.dma_tokens = []
        self.nops = 0
        self.maxops = None

    def _sem(self, e, epoch):
        k = (e, epoch)
        if k not in self.sems:
            self.sems[k] = self.es.enter_context(self.nc.semaphore("s_%s_%d" % (e, epoch)))
        return self.sems[k]

    def _wait(self, e, tok):
        if tok[0] == 'c':
            _, pe, c = tok
            if pe == e and (e == 'pe' or not SAME_ENGINE_SYNC):
                return
            epoch, v = divmod(c, self.EPOCH)
            key, val, sem = ('c', pe, epoch), v + 1, self._sem(pe, epoch)
        else:
            _, slot, rnd = tok
            key, val, sem = ('d', slot), 16 * (rnd + 1), self.dma_sems[slot]
        if self.waited[e].get(key, 0) >= val:
            return
        self.waited[e][key] = val
        self.eng[e].wait_ge(sem, val)

    def _deps(self, r, w):
        deps = set()
        for k in r:
            if k in self.last_w:
                deps.add(self.last_w[k])
            if isinstance(k, str) and k.startswith('ps'):
                deps.update(self.readers.get(k, ()))
        for k in w:
            if k in self.last_w:
                deps.add(self.last_w[k])
            deps.update(self.readers.get(k, ()))
        return deps

    def _reg(self, tok, r, w):
        for k in r:
            self.readers.setdefault(k, []).append(tok)
        for k in w:
            self.last_w[k] = tok
            self.readers[k] = []

    def op(self, e, fn, r=(), w=(), inc=True):
        if self.maxops is not None and self.nops >= self.maxops:
            raise StopBuild()
        for tok in self._deps(r, w):
            self._wait(e, tok)
        inst = fn(self.eng[e])
        c = self.cnt[e]
        if inc:
            epoch, _ = divmod(c, self.EPOCH)
            inst.then_inc(self._sem(e, epoch), 1)
            self.cnt[e] += 1
        self._reg(('c', e, c), r, w)
        self.nops += 1

    def dma(self, q, out, in_, r=(), w=(), nonc=False):
        if self.maxops is not None and self.nops >= self.maxops:
            raise StopBuild()
        n = self.dma_n
        self.dma_n += 1
        slot, rnd = n % self.ND, n // self.ND
        if rnd > 0:
            self._wait(q, ('d', slot, rnd - 1))
        for tok in self._deps(r, w):
            self._wait(q, tok)
        kw = {}
        if nonc:
            kw['allow_slow_non_contiguous'] = True
        inst = self.eng[q].dma_start(out=out, in_=in_, **kw)
        inst.then_inc(self.dma_sems[slot], 16)
        tok = ('d', slot, rnd)
        self._reg(tok, r, w)
        self.dma_tokens.append(tok)
        self.nops += 1
        return tok

    def finish(self, out_tokens):
        for tok in out_tokens:
            self._wait('sp', tok)
        last = {}
        for tok in self.dma_tokens:
            last[tok[1]] = tok
        for tok in last.values():
            self._wait('sp', tok)


def build(cfg):
    NL = cfg.get('layers', 2)
    NQ = cfg.get('nq', 8)
    DO_S = cfg.get('sample', True)
    NOSTATE = cfg.get('nostate', False)
    nc = bass.Bass("TRN2", target_bir_lowering=False)
    es = contextlib.ExitStack()
    D = {}

    def din(name, shape):
        D[name] = nc.dram_tensor(name, list(shape), F32, kind="ExternalInput").ap()

    def dout(name, shape):
        D[name] = nc.dram_tensor(name, list(shape), F32, kind="ExternalOutput").ap()

    din('xp', [SEQ, DM]); din('xs', [NS * TS, DM]); din('memp', [256, DM])
    din('ca_k', [2, NS, 512, 512]); din('ca_v', [2, NS, 512, 512])
    din('cc_k', [2, NS, 1024, 512]); din('cc_v', [2, NS, 1024, 512]); din('cc_f', [2, NS, 1024, 8])
    din('sb_ssm', [2, NS, 8, 64, 64]); din('sb_conv', [2, NS, 3, 768])
    din('cm_k', [2, NS, 256, 256]); din('cm_v', [2, NS, 256, 256])
    din('w_in', [2, DM, DIN]); din('w_mkv', [2, DM, 512])
    din('w_pa', [2, 512, DM]); din('w_pb', [2, 512, DM]); din('w_pc', [2, 512, DM]); din('w_pm', [2, 256, DM])
    din('w_out', [2, DM, DM])
    din('small', [2, NSM]); din('mnorm', [2, 1024]); din('relb', [2, 128, 8, 640]); din('convw', [2, 128, 6, 5]); din('cst', [128, NCST]); din('band', [128, 640]); din('ident', [128, 128])
    dout('yp', [SEQ, DM]); dout('ys', [NS * TS, DM])
    dout('pa_k', [2, 512, 512]); dout('pa_v', [2, 512, 512])
    dout('pc_k', [2, SEQ, 512]); dout('pc_v', [2, SEQ, 512]); dout('pc_f', [2, SEQ, 8])
    dout('pb_s', [2, 8, 64, 64]); dout('pb_c', [2, 3, 768])
    dout('pm_k', [2, 256, 256]); dout('pm_v', [2, 256, 256])
    dout('sa_k', [2, NS * TS, 512]); dout('sa_v', [2, NS * TS, 512])
    dout('sc_k', [2, NS * TS, 512]); dout('sc_v', [2, NS * TS, 512]); dout('sc_f', [2, NS * TS, 8])
    dout('sb_s', [2, NS, 8, 64, 64]); dout('sb_c', [2, NS, 3, 768])
    x1p = nc.dram_tensor("x1p", [SEQ, DM], F32, kind="Internal").ap()
    x1s = nc.dram_tensor("x1s", [NS * TS, DM], F32, kind="Internal").ap()

    with es:
        pg = Prog(nc, es)
        pg.maxops = cfg.get('maxops')
        out_tokens = []

        def sb(name, shape, dt):
            return es.enter_context(nc.sbuf_tensor("sb_" + name, list(shape), dt))

        def ps(name, shape, dt):
            return es.enter_context(nc.psum_tensor("ps_" + name, list(shape), dt))

        cst = sb("cst", [128, NCST], F32)
        cstb = sb("cstb", [128, 256], BF16)
        small = sb("small", [128, NSM], F32)
        expB = sb("expB", [128, 8, 640], BF16)
        cw = sb("cw", [128, 6, 5], F32)
        aneg = sb("aneg", [128, 8], F32)
        xt = sb("xt", [128, 1, DM], F32)
        W3 = sb("W3", [128, 4096], BF16)
        xnb = W3[:, 0:1024]
        sq = W3[:, 1024:2048].bitcast(F32)
        f2 = W3[:, 2048:3072].bitcast(F32)
        f3 = W3[:, 3072:4096].bitcast(F32)
        xnT = sb("xnT", [128, 8, 512], BF16)
        NWB = 3
        wbuf = [sb("wbuf%d" % i, [128, 8, 512], BF16) for i in range(NWB)]
        wbuf.append(W3[:, :].rearrange("p (k n) -> p k n", n=512))
        WKEYS = [['wbuf0'], ['wbuf1'], ['wbuf2'], ['wbuf3', 'xnb', 'sq', 'f2', 'f3']]
        kT_c = sb("kT_c", [128, 4, SEQ], BF16)
        va_c = sb("va_c", [128, 32, 8, 66], BF16)
        kT_a = sb("kT_a", [128, 4, 1024], BF16)
        va_a = sb("va_a", [128, 8, 8, 66], BF16)
        kT_m = sb("kT_m", [128, 2, 256], BF16)
        va_m = sb("va_m", [128, 2, 4, 66], BF16)
        cum = sb("cum", [128, 32, 8], F32)
        biasQ = sb("biasQ", [128, 32, 8], F32)
        qT = sb("qT", [128, 4, 512], BF16)
        sz = sb("sz", [128, 4, 512], BF16)
        oz = sb("oz", [128, 4, 512], BF16)
        ozT = {k: sb("ozT_" + k, [128, n, 512], BF16) for k, n in (('a', 4), ('b', 4), ('c', 4), ('m', 2))}
        f1 = sb("f1", [128, 512], F32)
        b1 = sb("b1", [128, 512], BF16)
        PT = [sb("PT%d" % i, [128, 512], BF16) for i in range(2)]
        st8 = sb("st8", [128, 8, 8], F32)
        raw = sb("raw", [128, 3, 4, 131], F32)
        halo = sb("halo", [128, 6, 3], F32)
        AR2 = sb("AR2", [128, 4096], BF16)
        xbc = AR2[:, 0:3072].rearrange("p (j n) -> p j n", n=512)
        Mh = AR2[:, 3072:4096].rearrange("p (h n) -> p h n", n=128)
        mT = AR2[:, :].rearrange("p (k n) -> p k n", n=512)
        cva = sb("cva", [128, 512], F32)
        sig = cva
        dts = sb("dts", [128, 4, 8], F32)
        lfs = sb("lfs", [128, 4, 8], F32)
        asb = sb("asb", [128, 4, 8], F32)
        btk = sb("btk", [128, 128], BF16)
        AE = sb("AE", [128, 2048], F32)
        Ah = AE[:, 0:512].rearrange("p (h n) -> p h n", n=128)
        Ee = AE[:, 512:1024].rearrange("p (h n) -> p h n", n=128)
        Gm = AE[:, 1024:1280].rearrange("p (g n) -> p g n", n=128)
        bdec = AE[:, 1280:1536].bitcast(BF16).rearrange("p (a g n) -> p a g n", a=4, g=2)
        xdt = AE[:, 1536:1792].bitcast(BF16).rearrange("p (h d) -> p h d", d=64)
        xtk = AE[:, 1792:2048].bitcast(BF16)
        macc = AE[:, :].rearrange("p (t n) -> p t n", n=512)
        hT = sb("hT", [128, 4, 64], F32)
        hTb = sb("hTb", [128, 4, 64], BF16)
        ssd8 = sb("ssd8", [128, 4, 8], F32)
        psA = [ps("psA%d" % i, [128, 512], F32) for i in range(2)]
        psT = [ps("psT%d" % i, [128, 1024], BF16) for i in range(2)]
        psS = [ps("psS%d" % i, [128, 512], F32) for i in range(2)]
        psO = [ps("psO%d" % i, [128, 512], F32) for i in range(2)]
        rr = {'A': 0, 'T': 0, 'S': 0, 'O': 0, 'W': 0, 'P': 0, 'X': 0, 'WP': 0}
        rrn = {'X': 1, 'WP': 1}

        def rot(k, n=2):
            v = rr[k]
            rr[k] = (v + 1) % rrn.get(k, n)
            return v

        ident_b = cstb[:, 0:128]
        M1f = cst[:, C_M1:C_M1 + 128]
        M2f = cst[:, C_M2:C_M2 + 128]
        S127f = cst[:, C_S127:C_S127 + 128]
        S63f = cst[:, C_S63:C_S63 + 128]
        M1b = cstb[:, 128:256]

        def smv(name, rows=128):
            o, w = SM[name]
            return small[0:rows, o:o + w]

        bar = sb("bar", [128, 2], F32)
        epst = sb("epst", [128, 1], F32)
        nhalf = sb("nhalf", [128, 8], F32)
        identf = sb("identf", [128, 64], F32)

        def barrier(keys):
            pg.op('pool', lambda e: e.memset(bar[:, 0:1], 0.0), w=['bar'] + list(keys))

        AEK = ['Ah%d' % h for h in range(8)] + ['Ee', 'Gm', 'bdec', 'xdt', 'xtk', 'AErel'] + ['macc%d' % t for t in range(4)]
        pg.op('pool', lambda e: e.memset(epst[:, :], EPS), w=['epst'])
        pg.op('pool', lambda e: e.memset(nhalf[:, :], -0.5), w=['epst'])
        pg.dma('sp', identf[0:64, :], D['ident'][0:64, 0:64], w=['identf'])
        pg.dma('sp', identf[64:128, :], D['ident'][0:64, 0:64], w=['identf'])
        pg.dma('sp', cst[:, :], D['cst'][:, :], w=['cst'])
        pg.dma('sp', AE[:, 0:128], D['ident'][:, :], w=AEK)
        pg.op('dve', lambda e: cp(e, out=cstb[:, 0:128], in_=AE[:, 0:128]), r=AEK, w=['cstb'])
        pg.op('dve', lambda e: cp(e, out=cstb[:, 128:256], in_=cst[:, C_M1:C_M1 + 128]), r=['cst'], w=['cstb'])
        for t_, k_ in ((va_c, 'va_c'), (va_a, 'va_a'), (va_m, 'va_m')):
            pg.op('pool', lambda e, t_=t_: e.memset(t_[:], 1.0), w=[k_])

        def group_wlist(l):
            wl = []
            for nm in ('c_q', 'c_k', 'c_v'):
                wl.append(('w_in', l, COL[nm], 512, 8))
            wl.append(('w_in', l, COL['c_f'], 8, 8))
            wl.append(('w_in', l, COL['c_z'], 512, 8))
            for nm in ('a_q', 'a_k', 'a_v', 'a_z', 'm_q', 'b_z'):
                wl.append(('w_in', l, COL[nm], 512, 8))
            wl.append(('w_in', l, COL['b_dt'], 8, 8))
            for half in range(2):
                wl.append(('w_in', l, COL['b_xbc'] + half * 384, 384, 8))
            for c in range(2):
                for bi, (wn, nkc) in enumerate((('w_pa', 4), ('w_pb', 4), ('w_pc', 4), ('w_pm', 2))):
                    wl.append(('w_in', l, COL['gate'] + bi * 1024 + c * 512, 512, 8, True))
                    wl.append((wn, l, c * 512, 512, nkc, True))
            for c in range(2):
                wl.append(('w_out', l, c * 512, 512, 8, True))
            return wl

        WL = []
        for l_ in range(NL):
            WL.append(('w_mkv', l_, 0, 512, 8))
            for _ in range(NQ + (1 if DO_S else 0)):
                WL += group_wlist(l_)
        wbi, prev_occ, lastocc, r3, r4 = [], [], {}, 0, 0
        for k_, ent in enumerate(WL):
            if len(ent) > 5 and ent[5]:
                b_ = r4 % 4; r4 += 1
            else:
                b_ = r3 % 3; r3 += 1; r4 = r3
            wbi.append(b_)
            prev_occ.append(lastocc.get(b_, -1))
            lastocc[b_] = k_
        wstate = {'ptr': 0, 'issued': 0}

        def load_w(dram, l, c0, n, nk=8, buf=None):
            i = wstate['ptr']
            exp = WL[i]
            assert exp[1] == l and exp[2] == c0 and exp[3] == n and exp[4] == nk and D[exp[0]] is dram, (exp, l, c0, n, nk)
            in_merge = len(exp) > 5 and exp[5]
            while wstate['issued'] < len(WL) and wstate['issued'] <= i + 3 and \
                    (wstate['issued'] <= i or prev_occ[wstate['issued']] <= i - 2) and \
                    (wbi[wstate['issued']] != 3 or in_merge):
                k = wstate['issued']
                nm, l2, c2, n2, nk2 = WL[k][0:5]
                src = D[nm][l2, :, c2:c2 + n2].rearrange("(k p) n -> p k n", p=128)
                pg.dma('pool', wbuf[wbi[k]][:, 0:nk2, 0:n2], src, w=WKEYS[wbi[k]])
                wstate['issued'] += 1
            wstate['ptr'] += 1
            return wbuf[wbi[i]], WKEYS[wbi[i]]

        def transposes(src_ap_fn, nblk, np_, dst_fn, rkeys, wkeys, evac='dve'):
            i = rot('T'); pt = psT[i]; pk = 'psT%d' % i
            for j in range(nblk):
                pg.op('pe', lambda e, j=j: e.transpose(out=pt[:, j * 128:j * 128 + np_], in_=src_ap_fn(j),
                                                       identity=ident_b[0:np_, 0:np_]),
                      r=list(rkeys) + ['cstb'], w=[pk], inc=(j == nblk - 1))
            view = pt[:, 0:nblk * 128].rearrange("p (b n) -> p b n", n=128)[:, :, 0:np_]
            pg.op(evac, lambda e: dst_fn(e, view), r=[pk], w=list(wkeys))

        def rstd_from_ss(ss_ap, rs_ap, n, rows, key):
            w_ = rs_ap.shape[-1]
            pg.op('dve', lambda e: e.tensor_scalar(out=rs_ap, in0=ss_ap, scalar1=1.0 / n, scalar2=EPS,
                                                    op0=ALU.mult, op1=ALU.add), r=[key], w=[key])
            pg.op('pool', lambda e: e.tensor_tensor(out=rs_ap, in0=rs_ap, in1=nhalf[0:rows, 0:w_], op=ALU.pow),
                  r=[key, 'epst'], w=[key])

        def head_norm(psap, pk, nh, gain_ap, out_ap, outkeys, np_, scale=None):
            n = nh * 64
            pg.op('act', lambda e: e.activation(out=sq[0:np_, 0:n], in_=psap, func=AF.Square), r=[pk], w=['sq'])
            ssv = st8[0:np_, 0, 0:nh]
            pg.op('dve', lambda e: e.tensor_reduce(out=ssv, in_=sq[0:np_, 0:n].rearrange("p (h d) -> p h d", d=64),
                                                   axis=AX.X, op=ALU.add), r=['sq'], w=['st8'])
            rstd_from_ss(ssv, ssv, 64, np_, 'st8')
            pg.op('dve', lambda e: e.tensor_tensor(
                out=f1[0:np_, 0:n].rearrange("p (h d) -> p h d", d=64),
                in0=psap.rearrange("p (h d) -> p h d", d=64),
                in1=ssv.unsqueeze(2).to_broadcast([np_, nh, 64]), op=ALU.mult), r=[pk, 'st8'], w=['f1'])
            g = gain_ap.unsqueeze(1).to_broadcast([np_, nh, 64])
            pg.op('dve', lambda e: e.tensor_tensor(
                out=out_ap.rearrange("p (h d) -> p h d", d=64),
                in0=f1[0:np_, 0:n].rearrange("p (h d) -> p h d", d=64), in1=g, op=ALU.mult),
                r=['f1', 'small'], w=list(outkeys))

        def pipe_tiles(NT, proj_fn):
            nxt = proj_fn(0)
            for t in range(NT):
                cur = nxt
                if t + 1 < NT:
                    nxt = proj_fn(t + 1)
                yield (t,) + tuple(cur)

        def proj_tm(xT, xkey, tcol, np_, wt, wkey, n, nk=8, wc0=0, xkeys=()):
            i = rot('A'); p = psA[i]; pk = 'psA%d' % i
            for kc in range(nk):
                pg.op('pe', lambda e, kc=kc: e.matmul(p[0:np_, 0:n], lhsT=xT[:, kc, tcol:tcol + np_],
                                                      rhs=wt[:, kc, wc0:wc0 + n], start=(kc == 0), stop=(kc == nk - 1)),
                      r=[xkey] + list(wkey) + list(xkeys), w=[pk], inc=(kc == nk - 1))
            return p[0:np_, 0:n], pk

        def layer_consts(l):
            pg.dma('sp', small[:, :], D['small'][l:l + 1, :].broadcast_to([128, NSM]), w=['small'])
            pg.dma('sp', cw[:, :, :], D['convw'][l], w=['cw'])
            for name in ('a_qnorm', 'c_qnorm', 'm_qnorm'):
                a = smv(name)
                pg.op('dve', lambda e, a=a: e.tensor_scalar(out=a, in0=a, scalar1=0.125, scalar2=None, op0=ALU.mult),
                      r=['small'], w=['small'])
            pg.op('act', lambda e: e.activation(out=aneg[:, :], in_=smv('b_a_log'), func=AF.Exp), r=['small'], w=['aneg'])
            pg.op('dve', lambda e: e.tensor_scalar(out=aneg[:, :], in0=aneg[:, :], scalar1=-1.0, scalar2=None, op0=ALU.mult),
                  r=['aneg'], w=['aneg'])
            barrier(AEK)
            pg.dma('sp', AE[:, 0:640], D['band'][:, :], w=AEK)
            pg.op('dve', lambda e: e.tensor_scalar(out=AE[:, 0:640], in0=AE[:, 0:640], scalar1=30000.0, scalar2=-30000.0,
                                                    op0=ALU.mult, op1=ALU.add), r=AEK, w=AEK)
            for h in range(8):
                pg.dma('sp', AE[:, 1024:1664], D['relb'][l, :, h, :], w=['AErel'])
                pg.op('dve', lambda e, h=h: e.tensor_tensor(out=expB[:, h, :], in0=AE[:, 1024:1664],
                                                            in1=AE[:, 0:640], op=ALU.add),
                      r=['AErel'] + AEK, w=['expB'])
            barrier(AEK)

        def norm_tiles(src_fn, NT, np_, gain_fn, rk_fn=None):
            rawflat = raw[:, :, :, :].rearrange("p a b c -> p (a b c)")
            sqj = sq.bitcast(BF16)
            stg = [(xt[0:np_, 0, :], 'xt0'), (rawflat[0:np_, 0:DM], 'raw')]

            def ld(t):
                xa, xk = stg[t % 2]
                pg.dma('sp', xa, src_fn(t), r=(rk_fn(t) if rk_fn else []), w=[xk])
            ld(0)
            for t in range(NT):
                xa, xk = stg[t % 2]
                if t + 1 < NT:
                    ld(t + 1)
                ssv = st8[0:np_, 1, t % 2:t % 2 + 1]
                pg.op('act', lambda e, xa=xa, ssv=ssv: e.activation(out=sqj[0:np_, :], in_=xa, func=AF.Square, accum_out=ssv),
                      r=[xk], w=['sq', 'st8'])
                rstd_from_ss(ssv, ssv, DM, np_, 'st8')
                pg.op('dve', lambda e, xa=xa, ssv=ssv: e.scalar_tensor_tensor(out=xnb[0:np_, :], in0=xa, scalar=ssv,
                                                                  in1=gain_fn(np_), op0=ALU.mult, op1=ALU.mult),
                      r=[xk, 'st8', 'small', 'f2'], w=['xnb'])
                transposes(lambda j: xnb[0:np_, j * 128:(j + 1) * 128], 8, np_,
                           lambda e, v, t=t: cp(e, out=xnT[:, :, t * np_:(t + 1) * np_], in_=v),
                           ['xnb'], ['xnT'], evac='act' if t % 2 else 'dve')

        def attend(qT_ap, NQc, qtiles, kblocks, qkeys, out_fn, acc=None):
            if acc is None:
                io = rot('O'); po = psO[io]; pok = 'psO%d' % io
                pov = po[:, 0:260].rearrange("p (j c) -> p j c", c=65)
                jbase, bank_first = 0, True
            else:
                pov, pok, jbase, bank_first = acc
            lastkb, firstkb = {}, {}
            for bi, kb in enumerate(kblocks):
                for j, (c0, nq) in enumerate(qtiles):
                    if c0 >= kb['q0']:
                        lastkb[j] = bi
                        firstkb.setdefault(j, bi)
            st = {}

            def stage_s(bi):
                kb = kblocks[bi]
                isx = rot('S'); psx = psS[isx]; psk = 'psS%d' % isx
                badd = kb.get('badd')
                pg.op('pe', lambda e: e.matmul(psx[0:kb['nk'], kb['q0']:NQc], lhsT=kb['kT'],
                                               rhs=qT_ap[:, kb['q0']:NQc], start=True, stop=(badd is None)),
                      r=list(qkeys) + list(kb['keys']), w=[psk], inc=(badd is None))
                if badd is not None:
                    pg.op('pe', lambda e: e.matmul(psx[0:kb['nk'], kb['q0']:NQc], lhsT=ident_b[:, 0:kb['nk']],
                                                   rhs=badd, start=False, stop=True),
                          r=['cstb'] + list(kb.get('mkeys', [])), w=[psk])
                st[bi] = (psx, psk)

            def stage_e(bi):
                kb = kblocks[bi]
                psx, psk = st[bi]
                ip = rot('P'); pt = PT[ip]; ptk = 'PT%d' % ip
                if kb.get('bias') is not None:
                    pg.op('act', lambda e: e.activation(out=pt[0:kb['nk'], kb['q0']:NQc], in_=psx[0:kb['nk'], kb['q0']:NQc],
                                                        func=AF.Exp, bias=kb['bias'], scale=1.0),
                          r=[psk] + list(kb.get('bkeys', [])), w=[ptk])
                else:
                    pg.op('act', lambda e: e.activation(out=pt[0:kb['nk'], kb['q0']:NQc], in_=psx[0:kb['nk'], kb['q0']:NQc],
                                                        func=AF.Exp), r=[psk], w=[ptk])
                if kb.get('diagmask'):
                    pg.op('dve', lambda e: e.tensor_tensor(
                        out=pt[0:kb['nk'], kb['q0']:kb['q0'] + kb['nk']], in0=pt[0:kb['nk'], kb['q0']:kb['q0'] + kb['nk']],
                        in1=M1b[0:kb['nk'], 0:kb['nk']], op=ALU.mult), r=[ptk, 'cstb'], w=[ptk])
                st[bi] = (pt, ptk)

            def stage_v(bi):
                kb = kblocks[bi]
                pt, ptk = st[bi]
                for j, (c0, nq) in enumerate(qtiles):
                    if c0 < kb['q0']:
                        continue
                    pg.op('pe', lambda e, j=j, c0=c0, nq=nq: e.matmul(
                        pov[0:nq, jbase + j, :], lhsT=pt[0:kb['nk'], c0:c0 + nq], rhs=kb['v'],
                        start=(bank_first and bi == 0 and j == min(firstkb)), stop=(bi == lastkb[j]), skip_group_check=True),
                        r=[ptk] + list(kb['keys']), w=[pok], inc=(c0 + nq >= NQc))

            n = len(kblocks)
            stage_s(0)
            for bi in range(n):
                if bi + 1 < n:
                    stage_s(bi + 1)
                stage_e(bi)
                stage_v(bi)
            if out_fn is not None:
                out_fn(pov, pok)
            return pov, pok

        def attn_out(h0col, NT, np_, szkey='sz'):
            def fn(pov, pok):
                rl = st8[0:np_, 2, 0:NT]
                pg.op('dve', lambda e: e.reciprocal(out=rl, in_=pov[0:np_, 0:NT, 64]), r=[pok], w=['st8'])
                tmp = f2[0:np_, 0:NT * 64].rearrange("p (j d) -> p j d", d=64)
                pg.op('dve', lambda e: e.tensor_tensor(out=tmp, in0=pov[0:np_, 0:NT, 0:64],
                                                       in1=rl.unsqueeze(2).to_broadcast([np_, NT, 64]), op=ALU.mult),
                      r=[pok, 'st8'], w=['f2'])
                pg.op('pool', lambda e: e.tensor_tensor(out=oz[0:np_, 0:NT, h0col:h0col + 64], in0=tmp,
                                                        in1=sz[0:np_, 0:NT, h0col:h0col + 64], op=ALU.mult),
                      r=['f2', szkey], w=['oz'])
            return fn

        def oz_to_T(br, NT, np_, nblk):
            for t in range(NT):
                transposes(lambda j, t=t: oz[0:np_, t, j * 128:(j + 1) * 128], nblk, np_,
                           lambda e, v, t=t: cp(e, out=ozT[br][:, 0:nblk, t * np_:(t + 1) * np_], in_=v),
                           ['oz'], ['ozT_' + br], evac='act' if t % 2 else 'dve')

        def evac_q(psap, pk, t, np_, nh, gname):
            head_norm(psap, pk, nh, smv(gname, np_), b1[0:np_, 0:nh * 64], ['b1'], np_)
            transposes(lambda j: b1[0:np_, j * 128:(j + 1) * 128], nh // 2, np_,
                       lambda e, v: cp(e, out=qT[:, 0:nh // 2, t * np_:(t + 1) * np_], in_=v),
                       ['b1'], ['qT'], evac='act')

        def evac_k(psap, pk, np_, nh, gname, out_dram, kT_dst_fn, kT_key):
            head_norm(psap, pk, nh, smv(gname, np_), f3[0:np_, 0:nh * 64], ['f3'], np_)
            if out_dram is not None:
                out_tokens.append(pg.dma('sp', out_dram, f3[0:np_, 0:nh * 64], r=['f3']))
            pg.op('dve', lambda e: cp(e, out=b1[0:np_, 0:nh * 64], in_=f3[0:np_, 0:nh * 64]), r=['f3'], w=['b1'])
            transposes(lambda j: b1[0:np_, j * 128:(j + 1) * 128], nh // 2, np_,
                       lambda e, v: cp(e, out=kT_dst_fn(), in_=v), ['b1'], [kT_key], evac='act')

        def evac_v(psap, pk, np_, nh, out_dram, va_dst, va_key):
            if out_dram is not None:
                pg.op('act', lambda e: e.activation(out=f3[0:np_, 0:nh * 64], in_=psap, func=AF.Copy), r=[pk], w=['f3'])
                out_tokens.append(pg.dma('sp', out_dram, f3[0:np_, 0:nh * 64], r=['f3']))
            pg.op('dve', lambda e: cp(e, out=va_dst, in_=psap.rearrange("p (h d) -> p h d", d=64)),
                  r=[pk], w=[va_key])

        def softplus_from(psap, pk, bias_ap, out_ap, okey, np_, neg=False):
            tmp = st8[0:np_, 3, :]
            pg.op('dve', lambda e: e.tensor_tensor(out=tmp, in0=psap, in1=bias_ap, op=ALU.add), r=[pk, 'small'], w=['st8'])
            pg.op('act', lambda e: e.activation(out=tmp, in_=tmp, func=AF.Exp, scale=(-1.0 if neg else 1.0)),
                  r=['st8'], w=['st8'])
            pg.op('dve', lambda e: e.tensor_scalar(out=tmp, in0=tmp, scalar1=1.0, scalar2=None, op0=ALU.add),
                  r=['st8'], w=['st8'])
            pg.op('act', lambda e: e.activation(out=tmp, in_=tmp, func=AF.Ln), r=['st8'], w=['st8'])
            pg.op('dve', lambda e: e.tensor_scalar(out=out_ap, in0=tmp, scalar1=(-1.0 if neg else 1.0), scalar2=None,
                                                    op0=ALU.mult), r=['st8'], w=[okey])

        def proj8_all(NT, np_, wt, wk):
            i = rot('A'); p = psA[i]; pk = 'psA%d' % i
            for t in range(NT):
                for kc in range(8):
                    pg.op('pe', lambda e, kc=kc, t=t: e.matmul(p[0:np_, t * 8:(t + 1) * 8], lhsT=xnT[:, kc, t * np_:(t + 1) * np_],
                                                              rhs=wt[:, kc, 0:8], start=(kc == 0), stop=(kc == 7)),
                          r=['xnT'] + list(wk), w=[pk], inc=(kc == 7 and t == NT - 1))
            return p[0:np_, 0:NT * 8].rearrange("p (t h) -> p t h", h=8), pk

        def softplus4(psv, pk, bias_ap, out_ap, okey, np_, NT, neg=False):
            tmp = st8[0:np_, 0:NT, :]
            pg.op('dve', lambda e: e.tensor_tensor(out=tmp, in0=psv, in1=bias_ap.unsqueeze(1).to_broadcast([np_, NT, 8]),
                                                   op=ALU.add), r=[pk, 'small'], w=['st8'])
            pg.op('act', lambda e: e.activation(out=tmp, in_=tmp, func=AF.Exp, scale=(-1.0 if neg else 1.0)),
                  r=['st8'], w=['st8'])
            pg.op('dve', lambda e: e.tensor_scalar(out=tmp, in0=tmp, scalar1=1.0, scalar2=None, op0=ALU.add),
                  r=['st8'], w=['st8'])
            pg.op('act', lambda e: e.activation(out=tmp, in_=tmp, func=AF.Ln), r=['st8'], w=['st8'])
            pg.op('dve', lambda e: e.tensor_scalar(out=out_ap, in0=tmp, scalar1=(-1.0 if neg else 1.0), scalar2=None,
                                                    op0=ALU.mult), r=['st8'], w=[okey])

        def cumsum_tile(lf_ap, lfkey, np_, dst_ap, dkey, carry_ap, ckeys):
            i = rot('S'); p = psS[i]; pk = 'psS%d' % i
            pg.op('pe', lambda e: e.matmul(p[0:np_, 0:8], lhsT=M1f[0:np_, 0:np_], rhs=lf_ap, start=True,
                                           stop=(carry_ap is None)), r=[lfkey, 'cst'], w=[pk])
            if carry_ap is not None:
                pg.op('pe', lambda e: e.matmul(p[0:np_, 0:8], lhsT=carry_ap[0], rhs=carry_ap[1], start=False, stop=True),
                      r=list(ckeys) + ['cst'], w=[pk])
            pg.op('dve', lambda e: cp(e, out=dst_ap, in_=p[0:np_, 0:8]), r=[pk], w=[dkey])

        def ssd_tile(t, np_, NT):
            c0 = t * np_
            if t == 0:
                barrier(AEK)
            def ev(e, v):
                return cp(e, out=xtk[0:np_, :].rearrange("p (b n) -> p b n", n=128), in_=v[0:np_, 0:4, :])
            i = rot('T'); pt = psT[i]; pk = 'psT%d' % i
            for j in range(5):
                pg.op('pe', lambda e, j=j: e.transpose(out=pt[0:np_, j * 128:(j + 1) * 128], in_=xbc[:, j, c0:c0 + np_],
                                                       identity=ident_b[:, :]), r=['xbc', 'cstb'], w=[pk], inc=(j == 4))
            pg.op('act', lambda e: e.activation(out=xtk[0:np_, :], in_=pt[0:np_, 0:512], func=AF.Copy), r=[pk], w=['xtk'])
            pg.op('act', lambda e: e.activation(out=btk[0:np_, :], in_=pt[0:np_, 512:640], func=AF.Copy), r=[pk], w=['btk'])
            pg.op('dve', lambda e: e.tensor_tensor(out=xdt[0:np_, :, :], in0=pt[0:np_, 0:512].rearrange("p (h d) -> p h d", d=64),
                                                   in1=dts[0:np_, t, :].unsqueeze(2).to_broadcast([np_, 8, 64]), op=ALU.mult),
                  r=[pk, 'dts'], w=['xdt'])
            a_ap = asb[0:np_, t, :]
            i = rot('A'); p = psA[i]; pk2 = 'psA%d' % i
            pg.op('pe', lambda e: e.matmul(p[0:np_, 0:8], lhsT=M1f[0:np_, 0:np_], rhs=a_ap, start=True, stop=True),
                  r=['asb', 'cst'], w=[pk2], inc=False)
            pg.op('pe', lambda e: e.matmul(p[0:np_, 8:16], lhsT=M2f[0:np_, 0:np_], rhs=a_ap, start=True, stop=True),
                  r=['asb', 'cst'], w=[pk2], inc=False)
            pg.op('pe', lambda e: e.matmul(p[:, 16:24], lhsT=M1f[0:np_, :], rhs=a_ap, start=True, stop=False),
                  r=['asb', 'cst'], w=[pk2], inc=False)
            pg.op('pe', lambda e: e.matmul(p[:, 16:24], lhsT=M2f[0:np_, :], rhs=a_ap, start=False, stop=True),
                  r=['asb', 'cst'], w=[pk2])
            pg.op('act', lambda e: e.activation(out=ssd8[0:np_, 0, :], in_=p[0:np_, 0:8], func=AF.Exp), r=[pk2], w=['ssd8'])
            pg.op('act', lambda e: e.activation(out=ssd8[0:np_, 1, :], in_=p[0:np_, 8:16], func=AF.Exp), r=[pk2], w=['ssd8'])
            pg.op('act', lambda e: e.activation(out=ssd8[:, 2, :], in_=p[:, 16:24], func=AF.Exp), r=[pk2], w=['ssd8'])
            for g in range(2):
                i = rot('A'); pgm = psA[i]; pk3 = 'psA%d' % i
                pg.op('pe', lambda e, g=g, pgm=pgm: e.matmul(pgm[0:np_, 0:np_], lhsT=xbc[g * 64:(g + 1) * 64, 4, c0:c0 + np_],
                                                    rhs=xbc[g * 64:(g + 1) * 64, 5, c0:c0 + np_], start=True, stop=True),
                      r=['xbc'], w=[pk3])
                pg.op('dve', lambda e, g=g, pgm=pgm: e.tensor_tensor(
                    out=Gm[0:np_, g, 0:np_], in0=pgm[0:np_, 0:np_], in1=M1f[0:np_, 0:np_], op=ALU.mult),
                    r=[pk3, 'cst'], w=['Gm'])
            for hh in range(2):
                for h4 in range(4):
                    h = hh * 4 + h4
                    pg.op('dve', lambda e, h=h, h4=h4: e.tensor_scalar(
                        out=Ah[0:np_, h4, 0:np_], in0=M2f[0:np_, 0:np_], scalar1=asb[0:np_, t, h:h + 1], scalar2=None,
                        op0=ALU.mult), r=['cst', 'asb'], w=['Ah%d' % h4])
                psx = psS[hh]; psk = 'psS%d' % hh
                for h4 in range(4):
                    pg.op('pe', lambda e, h4=h4, psx=psx: e.matmul(
                        psx[0:np_, h4 * 128:h4 * 128 + np_], lhsT=Ah[0:np_, h4, 0:np_], rhs=M1f[0:np_, 0:np_],
                        start=True, stop=True), r=['Ah%d' % h4, 'cst'], w=[psk], inc=(h4 == 3))
                pg.op('act', lambda e, psx=psx: e.activation(
                    out=Ee[0:np_, :, 0:np_],
                    in_=psx[0:np_, :].rearrange("p (h n) -> p h n", n=128)[:, :, 0:np_], func=AF.Exp),
                    r=[psk], w=['Ee'])
                pg.op('dve', lambda e, hh=hh: e.tensor_tensor(
                    out=Mh[0:np_, hh * 4:(hh + 1) * 4, 0:np_], in0=Ee[0:np_, :, 0:np_],
                    in1=Gm[0:np_, hh, 0:np_].unsqueeze(1).to_broadcast([np_, 4, np_]), op=ALU.mult),
                    r=['Ee', 'Gm'], w=['Mh'])
            io = rot('O'); py = psO[io]; pyk = 'psO%d' % io
            io2 = rot('O'); py2 = psO[io2]; py2k = 'psO%d' % io2
            for h in range(8):
                pg.op('pe', lambda e, h=h: e.matmul(py[0:np_, h * 64:(h + 1) * 64], lhsT=Mh[0:np_, h, 0:np_],
                                                    rhs=xdt[0:np_, h, :], start=True, stop=True),
                      r=['Mh', 'xdt'], w=[pyk], inc=(h == 7))
            py3 = psS[1]; py3k = 'psS1'
            for h in range(8):
                g = h // 4
                dstp = (py2 if g == 0 else py3)
                pg.op('pe', lambda e, h=h, g=g, dstp=dstp: e.matmul(dstp[0:np_, (h % 4) * 64:(h % 4 + 1) * 64],
                                                         lhsT=xbc[g * 64:(g + 1) * 64, 5, c0:c0 + np_],
                                                         rhs=hTb[g * 64:(g + 1) * 64, h % 4, :], start=True, stop=True),
                      r=['xbc', 'hTb'], w=[py2k if g == 0 else py3k], inc=(h % 4 == 3))
            yv = f1[0:np_, :].rearrange("p (h d) -> p h d", d=64)
            for g, (dstp, dk) in enumerate(((py2, py2k), (py3, py3k))):
                pg.op('dve', lambda e, g=g, dstp=dstp: e.tensor_tensor(
                    out=yv[:, g * 4:(g + 1) * 4, :], in0=dstp[0:np_, 0:256].rearrange("p (h d) -> p h d", d=64),
                    in1=ssd8[0:np_, 0, g * 4:(g + 1) * 4].unsqueeze(2).to_broadcast([np_, 4, 64]), op=ALU.mult),
                    r=[dk, 'ssd8'], w=['f1'])
            pg.op('dve', lambda e: e.tensor_tensor(out=f1[0:np_, :], in0=f1[0:np_, :], in1=py[0:np_, :], op=ALU.add),
                  r=['f1', pyk], w=['f1'])
            pg.op('pool', lambda e: e.tensor_tensor(out=f2[0:np_, :].rearrange("p (h d) -> p h d", d=64),
                                                    in0=xtk[0:np_, :].rearrange("p (h d) -> p h d", d=64),
                                                    in1=smv('b_d', np_).unsqueeze(2).to_broadcast([np_, 8, 64]), op=ALU.mult),
                  r=['xtk', 'small'], w=['f2'])
            pg.op('dve', lambda e: e.tensor_tensor(out=f1[0:np_, :], in0=f1[0:np_, :], in1=f2[0:np_, :], op=ALU.add),
                  r=['f1', 'f2'], w=['f1'])
            pg.op('dve', lambda e: e.tensor_tensor(out=f1[0:np_, :], in0=f1[0:np_, :], in1=sz[0:np_, t, :], op=ALU.mult),
                  r=['f1', 'szb'], w=['f1'])
            ssv = st8[0:np_, 4, 0:1]
            pg.op('act', lambda e: e.activation(out=sq[0:np_, 0:512], in_=f1[0:np_, :], func=AF.Square, accum_out=ssv),
                  r=['f1'], w=['sq', 'st8'])
            rstd_from_ss(ssv, ssv, 512, np_, 'st8')
            pg.op('dve', lambda e: e.scalar_tensor_tensor(out=oz[0:np_, t, :], in0=f1[0:np_, :], scalar=ssv,
                                                          in1=smv('b_norm', np_), op0=ALU.mult, op1=ALU.mult),
                  r=['f1', 'st8', 'small'], w=['oz'])
            pg.op('dve', lambda e: e.tensor_tensor(
                out=bdec[0:np_, :, :, :].rearrange("p a g n -> p g a n"),
                in0=btk[0:np_, :].rearrange("p (g n) -> p g n", n=64).unsqueeze(2).to_broadcast([np_, 2, 4, 64]),
                in1=ssd8[0:np_, 1, :].rearrange("p (g a) -> p g a", a=4).unsqueeze(3).to_broadcast([np_, 2, 4, 64]),
                op=ALU.mult), r=['btk', 'ssd8'], w=['bdec'])
            i = rot('A'); ph = psA[i]; phk = 'psA%d' % i
            for a4 in range(4):
                pg.op('pe', lambda e, a4=a4: e.matmul(
                    ph[:, a4 * 128:(a4 + 1) * 128], lhsT=bdec[0:np_, a4, :, :].rearrange("p g n -> p (g n)"),
                    rhs=xdt[0:np_, :, :].rearrange("p (g a) d -> p a g d", a=4)[:, a4, :, :],
                    start=True, stop=True), r=['bdec', 'xdt'], w=[phk], inc=(a4 == 3))
            phv = ph[:, :].rearrange("p (a c) -> p a c", c=128)
            for g in range(2):
                sl = slice(g * 64, (g + 1) * 64)
                pg.op('dve', lambda e, g=g, sl=sl: e.tensor_tensor(
                    out=hT[sl, :, :], in0=hT[sl, :, :],
                    in1=ssd8[sl, 2, g * 4:(g + 1) * 4].unsqueeze(2).to_broadcast([64, 4, 64]), op=ALU.mult),
                    r=['hT', 'ssd8', py2k, py3k], w=['hT'])
                pg.op('dve', lambda e, g=g, sl=sl: e.tensor_tensor(
                    out=hT[sl, :, :], in0=hT[sl, :, :], in1=phv[sl, :, g * 64:(g + 1) * 64], op=ALU.add),
                    r=['hT', phk], w=['hT'])
            pg.op('act', lambda e: e.activation(out=hTb[:, :, :], in_=hT[:, :, :], func=AF.Copy), r=['hT'], w=['hTb'])

        BS = [dict(Ah=Ah, Ee=Ee, Gm=Gm, bdec=bdec, xdt=xdt, xtk=xtk, Mh=Mh, btk=btk,
                   e0=ssd8[:, 0, :], e1=ssd8[:, 1, :], e2=ssd8[:, 2, :], sfx=''),
              dict(Ah=xnb.bitcast(F32).rearrange("p (h n) -> p h n", n=128),
                   Ee=f3.rearrange("p (h n) -> p h n", n=128),
                   Gm=biasQ[:, :, :].rearrange("p a b -> p (a b)").rearrange("p (g n) -> p g n", n=128),
                   Mh=qT[:, 0:2, :].rearrange("p a n -> p (a n)").rearrange("p (h n) -> p h n", n=128),
                   bdec=qT[:, 2, :].rearrange("p (a g n) -> p a g n", a=4, g=2),
                   xdt=qT[:, 3, :].rearrange("p (h d) -> p h d", d=64),
                   xtk=PT[0][:, :], btk=PT[1][:, 0:128],
                   e0=st8[:, 5, :], e1=st8[:, 6, :], e2=st8[:, 7, :], sfx='_1')]
        SET1K = ['Ah%d_1' % h for h in range(4)] + ['Ee_1', 'Gm_1', 'bdec_1', 'xdt_1', 'xtk_1', 'Mh_1', 'btk_1', 'ssd8_1',
                                                     'qT', 'PT0', 'PT1', 'biasQ', 'xnb', 'f3', 'st8lf', 'st8c', 'st8x']

        def ssdA(t, np_, S):
            b = BS[S]; x_ = b['sfx']
            K = lambda n: n + x_
            c0 = t * np_
            i = rot('T'); pt = psT[i]; pk = 'psT%d' % i
            for j in range(5):
                pg.op('pe', lambda e, j=j: e.transpose(out=pt[0:np_, j * 128:(j + 1) * 128], in_=xbc[:, j, c0:c0 + np_],
                                                       identity=ident_b[:, :]), r=['xbc', 'cstb'], w=[pk], inc=(j == 4))
            yield
            pg.op('act', lambda e: e.activation(out=b['xtk'][0:np_, :], in_=pt[0:np_, 0:512], func=AF.Copy), r=[pk], w=[K('xtk')])
            pg.op('act', lambda e: e.activation(out=b['btk'][0:np_, :], in_=pt[0:np_, 512:640], func=AF.Copy), r=[pk], w=[K('btk')])
            yield
            pg.op('dve', lambda e: e.tensor_tensor(out=b['xdt'][0:np_, :, :], in0=pt[0:np_, 0:512].rearrange("p (h d) -> p h d", d=64),
                                                   in1=dts[0:np_, t, :].unsqueeze(2).to_broadcast([np_, 8, 64]), op=ALU.mult),
                  r=[pk, 'dts'], w=[K('xdt')])
            yield
            a_ap = asb[0:np_, t, :]
            i = rot('A'); p = psA[i]; pk2 = 'psA%d' % i
            pg.op('pe', lambda e: e.matmul(p[0:np_, 0:8], lhsT=M1f[0:np_, 0:np_], rhs=a_ap, start=True, stop=True),
                  r=['asb', 'cst'], w=[pk2], inc=False)
            pg.op('pe', lambda e: e.matmul(p[0:np_, 8:16], lhsT=M2f[0:np_, 0:np_], rhs=a_ap, start=True, stop=True),
                  r=['asb', 'cst'], w=[pk2], inc=False)
            pg.op('pe', lambda e: e.matmul(p[:, 16:24], lhsT=M1f[0:np_, :], rhs=a_ap, start=True, stop=False),
                  r=['asb', 'cst'], w=[pk2], inc=False)
            pg.op('pe', lambda e: e.matmul(p[:, 16:24], lhsT=M2f[0:np_, :], rhs=a_ap, start=False, stop=True),
                  r=['asb', 'cst'], w=[pk2])
            yield
            pg.op('act', lambda e: e.activation(out=b['e0'][0:np_, :], in_=p[0:np_, 0:8], func=AF.Exp), r=[pk2], w=[K('ssd8')])
            pg.op('act', lambda e: e.activation(out=b['e1'][0:np_, :], in_=p[0:np_, 8:16], func=AF.Exp), r=[pk2], w=[K('ssd8')])
            pg.op('act', lambda e: e.activation(out=b['e2'][:, :], in_=p[:, 16:24], func=AF.Exp), r=[pk2], w=[K('ssd8')])
            yield
            for g in range(2):
                i = rot('A'); pgm = psA[i]; pk3 = 'psA%d' % i
                pg.op('pe', lambda e, g=g, pgm=pgm: e.matmul(pgm[0:np_, 0:np_], lhsT=xbc[g * 64:(g + 1) * 64, 4, c0:c0 + np_],
                                                    rhs=xbc[g * 64:(g + 1) * 64, 5, c0:c0 + np_], start=True, stop=True),
                      r=['xbc'], w=[pk3])
                pg.op('dve', lambda e, g=g, pgm=pgm: e.tensor_tensor(
                    out=b['Gm'][0:np_, g, 0:np_], in0=pgm[0:np_, 0:np_], in1=M1f[0:np_, 0:np_], op=ALU.mult),
                    r=[pk3, 'cst'], w=[K('Gm')])
                yield
            for hh in range(2):
                for h4 in range(4):
                    h = hh * 4 + h4
                    pg.op('dve', lambda e, h=h, h4=h4: e.tensor_scalar(
                        out=b['Ah'][0:np_, h4, 0:np_], in0=M2f[0:np_, 0:np_], scalar1=asb[0:np_, t, h:h + 1], scalar2=None,
                        op0=ALU.mult), r=['cst', 'asb'], w=[K('Ah%d' % h4)])
                    if h4 % 2:
                        yield
                psx = psS[hh]; psk = 'psS%d' % hh
                for h4 in range(4):
                    pg.op('pe', lambda e, h4=h4, psx=psx: e.matmul(
                        psx[0:np_, h4 * 128:h4 * 128 + np_], lhsT=b['Ah'][0:np_, h4, 0:np_], rhs=M1f[0:np_, 0:np_],
                        start=True, stop=True), r=[K('Ah%d' % h4), 'cst'], w=[psk], inc=(h4 == 3))
                yield
                pg.op('act', lambda e, psx=psx: e.activation(
                    out=b['Ee'][0:np_, :, 0:np_],
                    in_=psx[0:np_, :].rearrange("p (h n) -> p h n", n=128)[:, :, 0:np_], func=AF.Exp),
                    r=[psk], w=[K('Ee')])
                yield
                pg.op('dve', lambda e, hh=hh: e.tensor_tensor(
                    out=b['Mh'][0:np_, hh * 4:(hh + 1) * 4, 0:np_], in0=b['Ee'][0:np_, :, 0:np_],
                    in1=b['Gm'][0:np_, hh, 0:np_].unsqueeze(1).to_broadcast([np_, 4, np_]), op=ALU.mult),
                    r=[K('Ee'), K('Gm')], w=[K('Mh')])
                yield
            pg.op('dve', lambda e: e.tensor_tensor(
                out=b['bdec'][0:np_, :, :, :].rearrange("p a g n -> p g a n"),
                in0=b['btk'][0:np_, :].rearrange("p (g n) -> p g n", n=64).unsqueeze(2).to_broadcast([np_, 2, 4, 64]),
                in1=b['e1'][0:np_, :].rearrange("p (g a) -> p g a", a=4).unsqueeze(3).to_broadcast([np_, 2, 4, 64]),
                op=ALU.mult), r=[K('btk'), K('ssd8')], w=[K('bdec')])
            yield

        def ssdB(t, np_, S):
            b = BS[S]; x_ = b['sfx']
            K = lambda n: n + x_
            c0 = t * np_
            io = rot('O'); py = psO[io]; pyk = 'psO%d' % io
            io2 = rot('O'); py2 = psO[io2]; py2k = 'psO%d' % io2
            for h in range(8):
                pg.op('pe', lambda e, h=h: e.matmul(py[0:np_, h * 64:(h + 1) * 64], lhsT=b['Mh'][0:np_, h, 0:np_],
                                                    rhs=b['xdt'][0:np_, h, :], start=True, stop=True),
                      r=[K('Mh'), K('xdt')], w=[pyk], inc=(h == 7))
            yield
            i3 = rot('A'); py3 = psA[i3]; py3k = 'psA%d' % i3
            for h in range(8):
                g = h // 4
                dstp = (py2 if g == 0 else py3)
                pg.op('pe', lambda e, h=h, g=g, dstp=dstp: e.matmul(dstp[0:np_, (h % 4) * 64:(h % 4 + 1) * 64],
                                                         lhsT=xbc[g * 64:(g + 1) * 64, 5, c0:c0 + np_],
                                                         rhs=hTb[g * 64:(g + 1) * 64, h % 4, :], start=True, stop=True),
                      r=['xbc', 'hTb'], w=[py2k if g == 0 else py3k], inc=(h % 4 == 3))
            yield
            yv = f1[0:np_, :].rearrange("p (h d) -> p h d", d=64)
            for g, (dstp, dk) in enumerate(((py2, py2k), (py3, py3k))):
                pg.op('dve', lambda e, g=g, dstp=dstp: e.tensor_tensor(
                    out=yv[:, g * 4:(g + 1) * 4, :], in0=dstp[0:np_, 0:256].rearrange("p (h d) -> p h d", d=64),
                    in1=b['e0'][0:np_, g * 4:(g + 1) * 4].unsqueeze(2).to_broadcast([np_, 4, 64]), op=ALU.mult),
                    r=[dk, K('ssd8')], w=['f1'])
                yield
            pg.op('dve', lambda e: e.tensor_tensor(out=f1[0:np_, :], in0=f1[0:np_, :], in1=py[0:np_, :], op=ALU.add),
                  r=['f1', pyk], w=['f1'])
            pg.op('pool', lambda e: e.tensor_tensor(out=f2[0:np_, :].rearrange("p (h d) -> p h d", d=64),
                                                    in0=b['xtk'][0:np_, :].rearrange("p (h d) -> p h d", d=64),
                                                    in1=smv('b_d', np_).unsqueeze(2).to_broadcast([np_, 8, 64]), op=ALU.mult),
                  r=[K('xtk'), 'small'], w=['f2'])
            yield
            pg.op('dve', lambda e: e.tensor_tensor(out=f1[0:np_, :], in0=f1[0:np_, :], in1=f2[0:np_, :], op=ALU.add),
                  r=['f1', 'f2'], w=['f1'])
            yield
            pg.op('dve', lambda e: e.tensor_tensor(out=f1[0:np_, :], in0=f1[0:np_, :], in1=sz[0:np_, t, :], op=ALU.mult),
                  r=['f1', 'szb'], w=['f1'])
            yield
            ssv = st8[0:np_, 4, 0:1]
            pg.op('act', lambda e: e.activation(out=sq[0:np_, 0:512], in_=f1[0:np_, :], func=AF.Square, accum_out=ssv),
                  r=['f1'], w=['sq', 'st8'])
            yield
            pg.op('dve', lambda e: e.tensor_scalar(out=ssv, in0=ssv, scalar1=1.0 / 512, scalar2=EPS,
                                                    op0=ALU.mult, op1=ALU.add), r=['st8'], w=['st8'])
            yield
            pg.op('pool', lambda e: e.tensor_tensor(out=ssv, in0=ssv, in1=nhalf[0:np_, 0:1], op=ALU.pow),
                  r=['st8', 'epst'], w=['st8'])
            yield
            pg.op('dve', lambda e: e.scalar_tensor_tensor(out=oz[0:np_, t, :], in0=f1[0:np_, :], scalar=ssv,
                                                          in1=smv('b_norm', np_), op0=ALU.mult, op1=ALU.mult),
                  r=['f1', 'st8', 'small'], w=['oz'])
            yield
            i = rot('A'); ph = psA[i]; phk = 'psA%d' % i
            for a4 in range(4):
                pg.op('pe', lambda e, a4=a4: e.matmul(
                    ph[:, a4 * 128:(a4 + 1) * 128], lhsT=b['bdec'][0:np_, a4, :, :].rearrange("p g n -> p (g n)"),
                    rhs=b['xdt'][0:np_, :, :].rearrange("p (g a) d -> p a g d", a=4)[:, a4, :, :],
                    start=True, stop=True), r=[K('bdec'), K('xdt')], w=[phk], inc=(a4 == 3))
            yield
            phv = ph[:, :].rearrange("p (a c) -> p a c", c=128)
            for g in range(2):
                sl = slice(g * 64, (g + 1) * 64)
                pg.op('dve', lambda e, g=g, sl=sl: e.tensor_tensor(
                    out=hT[sl, :, :], in0=hT[sl, :, :],
                    in1=b['e2'][sl, g * 4:(g + 1) * 4].unsqueeze(2).to_broadcast([64, 4, 64]), op=ALU.mult),
                    r=['hT', K('ssd8'), py2k, py3k], w=['hT'])
                yield
                pg.op('dve', lambda e, g=g, sl=sl: e.tensor_tensor(
                    out=hT[sl, :, :], in0=hT[sl, :, :], in1=phv[sl, :, g * 64:(g + 1) * 64], op=ALU.add),
                    r=['hT', phk], w=['hT'])
                yield
            pg.op('act', lambda e: e.activation(out=hTb[:, :, :], in_=hT[:, :, :], func=AF.Copy), r=['hT'], w=['hTb'])
            yield

        def ssd_pipelined(NT, np_):
            barrier(AEK + SET1K)
            for _ in ssdA(0, np_, 0):
                pass
            for t in range(NT):
                gens = [ssdB(t, np_, t % 2)]
                if t + 1 < NT:
                    gens.append(ssdA(t + 1, np_, (t + 1) % 2))
                while gens:
                    for g_ in list(gens):
                        try:
                            next(g_)
                        except StopIteration:
                            gens.remove(g_)
            barrier(AEK + SET1K)

        def process_group(l, kind, Q):
            samp = (kind == 's')
            np_ = 64 if samp else 128
            NT = 4
            NTOK = NT * np_
            xsrc = (D['xs'] if samp else D['xp']) if l == 0 else (x1s if samp else x1p)
            xdst = (D['ys'] if samp else D['yp']) if l == NL - 1 else (x1s if samp else x1p)
            row0 = 0 if samp else Q * 512

            def rows(t):
                return slice(row0 + t * np_, row0 + (t + 1) * np_)

            norm_tiles(lambda t: xsrc[rows(t), :], NT, np_, lambda n: smv('g_norm', n),
                       (lambda t: [('xd', kind, row0 + t * np_)]) if l > 0 else None)

            if cfg.get('marks'): print('MARK', kind, Q, 'C_proj', pg.nops)
            wt, wk = load_w(D['w_in'], l, COL['c_q'], 512)
            for t, p, pk in pipe_tiles(NT, lambda t: proj_tm(xnT, 'xnT', t * np_, np_, wt, wk, 512)):
                evac_q(p, pk, t, np_, 8, 'c_qnorm')
            wt, wk = load_w(D['w_in'], l, COL['c_k'], 512)
            if samp:
                kTs = sb_kTs
            for t, p, pk in pipe_tiles(NT, lambda t: proj_tm(xnT, 'xnT', t * np_, np_, wt, wk, 512)):
                if samp:
                    evac_k(p, pk, np_, 8, 'c_knorm', D['sc_k'][l, rows(t), :],
                           lambda t=t: kTs[:, 0:4, t * 64:(t + 1) * 64], 'kTs')
                else:
                    gt = Q * 4 + t
                    evac_k(p, pk, np_, 8, 'c_knorm', D['pc_k'][l, rows(t), :],
                           lambda gt=gt: kT_c[:, :, gt * 128:(gt + 1) * 128], 'kT_c')
            wt, wk = load_w(D['w_in'], l, COL['c_v'], 512)
            for t, p, pk in pipe_tiles(NT, lambda t: proj_tm(xnT, 'xnT', t * np_, np_, wt, wk, 512)):
                if samp:
                    evac_v(p, pk, np_, 8, D['sc_v'][l, rows(t), :], vas[0:np_, t, :, 0:64], 'vas')
                else:
                    evac_v(p, pk, np_, 8, D['pc_v'][l, rows(t), :], va_c[:, Q * 4 + t, :, 0:64], 'va_c')
            wt, wk = load_w(D['w_in'], l, COL['c_f'], 8)
            if not samp:
                psv, pk = proj8_all(NT, np_, wt, wk)
                softplus4(psv, pk, smv('c_fbias', np_), lfs[0:np_, :, :], 'lfs', np_, NT, neg=True)
                out_tokens.append(pg.dma('sp', D['pc_f'][l, row0:row0 + NT * np_, :].rearrange("(t p) h -> p t h", p=np_),
                                         lfs[0:np_, :, :], r=['lfs'], nonc=True))
                for t in range(NT):
                    gt = Q * 4 + t
                    cumsum_tile(lfs[0:np_, t, :], 'lfs', 128, cum[:, gt, :], 'cum',
                                None if gt == 0 else (S127f, cum[:, gt - 1, :]), ['cum'])
            for t, p, pk in (pipe_tiles(NT, lambda t: proj_tm(xnT, 'xnT', t * np_, np_, wt, wk, 8)) if samp else ()):
                lf = st8[0:np_, 5, :]
                softplus_from(p, pk, smv('c_fbias', np_), lf, 'st8lf', np_, neg=True)
                if samp:
                    out_tokens.append(pg.dma('sp', D['sc_f'][l, rows(t), :], lf, r=['st8lf'], nonc=True))
                    stg8 = cum[:, 0:8, :].rearrange("p k h -> p (k h)")
                    totb = cum[:, 8:16, :]
                    car = cum[:, 16:24, :]
                    pg.dma('sp', cum[:, 0:8, :], D['cc_f'][l, t].rearrange("(k p) h -> p k h", p=128), w=['cum'], nonc=True)
                    i1 = rot('S'); q1 = psS[i1]; q1k = 'psS%d' % i1
                    pg.op('pe', lambda e, q1=q1: e.matmul(q1[:, 0:64], lhsT=M1f, rhs=stg8, start=True, stop=True),
                          r=['cum', 'cst'], w=[q1k])
                    loc = cums[:, t, 0:8, :]
                    pg.op('dve', lambda e, q1=q1, loc=loc: cp(e, out=loc, in_=q1[:, 0:64].rearrange("p (k h) -> p k h", h=8)),
                          r=[q1k], w=['cums'])
                    i2 = rot('S'); q2 = psS[i2]; q2k = 'psS%d' % i2
                    pg.op('pe', lambda e, q2=q2, loc=loc: e.matmul(q2[:, 0:64], lhsT=S127f, rhs=loc.rearrange("p k h -> p (k h)"),
                                                                 start=True, stop=True), r=['cums', 'cst'], w=[q2k])
                    pg.op('act', lambda e, q2=q2: e.activation(out=totb, in_=q2[:, 0:64].rearrange("p (k h) -> p k h", h=8),
                                                              func=AF.Copy), r=[q2k], w=['cum'])
                    pg.op('pool', lambda e: e.memset(car[:, 0, :], 0.0), w=['cum'])
                    for kb in range(1, 8):
                        pg.op('dve', lambda e, kb=kb: e.tensor_tensor(out=car[:, kb, :], in0=car[:, kb - 1, :], in1=totb[:, kb - 1, :],
                                                                      op=ALU.add), r=['cum'], w=['cum'])
                    pg.op('dve', lambda e, loc=loc: e.tensor_tensor(out=loc, in0=loc, in1=car, op=ALU.add), r=['cums', 'cum'], w=['cums'])
                    cumsum_tile(lf, 'st8lf', 64, cums[0:64, t, 8, :], 'cums', (S127f[:, 0:64], cums[:, t, 7, :]), ['cums'])
                else:
                    gt = Q * 4 + t
                    out_tokens.append(pg.dma('sp', D['pc_f'][l, rows(t), :], lf, r=['st8lf'], nonc=True))
                    cumsum_tile(lf, 'st8lf', 128, cum[:, gt, :], 'cum',
                                None if gt == 0 else (S127f, cum[:, gt - 1, :]), ['cum'])
            wt, wk = load_w(D['w_in'], l, COL['c_z'], 512)
            for t, p, pk in pipe_tiles(NT, lambda t: proj_tm(xnT, 'xnT', t * np_, np_, wt, wk, 512)):
                pg.op('act', lambda e, p=p, t=t: e.activation(out=sz[0:np_, t, :], in_=p, func=AF.Silu), r=[pk], w=['sz'])
            if cfg.get('marks'): print('MARK', kind, Q, 'C_attn', pg.nops)
            if not samp:
                nkb = 4 * Q + 4
                i = rot('A'); pb = psA[i]; pbk = 'psA%d' % i
                pg.op('pe', lambda e: e.matmul(pb[:, 0:8], lhsT=S127f, rhs=cum[:, nkb - 1, :], start=True, stop=True),
                      r=['cum', 'cst'], w=[pbk])
                pg.op('act', lambda e: e.activation(out=st8[:, 6, :], in_=pb[:, 0:8], func=AF.Copy), r=[pbk], w=['st8c'])
                pg.op('dve', lambda e: e.tensor_tensor(out=biasQ[:, 0:nkb, :],
                                                       in0=st8[:, 6, :].unsqueeze(1).to_broadcast([128, nkb, 8]),
                                                       in1=cum[:, 0:nkb, :], op=ALU.subtract), r=['st8c', 'cum'], w=['biasQ'])
                for h in range(8):
                    hp, hb = h // 2, (h % 2) * 64
                    kbs = []
                    for kb in range(nkb):
                        dI = kb - 4 * Q
                        d = dict(kT=kT_c[hb:hb + 64, hp, kb * 128:(kb + 1) * 128], v=va_c[:, kb, h, 0:65], nk=128,
                                 bias=biasQ[:, kb, h:h + 1], bkeys=['biasQ'], q0=max(dI, 0) * 128, keys=['kT_c', 'va_c'])
                        kbs.append(d)
                    kb2 = []
                    for kb, d in enumerate(kbs):
                        dI = kb - 4 * Q
                        if dI < 0:
                            kb2.append(d)
                        else:
                            d['diag'] = True
                            kb2.append(d)
                    attend_fox(qT[hb:hb + 64, hp, :], 512, [(j * 128, 128) for j in range(4)], kb2, ['qT'],
                               attn_out(h * 64, 4, 128))
            else:
                for t in range(NT):
                    i = rot('A'); pb = psA[i]; pbk = 'psA%d' % i
                    pg.op('pe', lambda e, t=t: e.matmul(pb[:, 0:8], lhsT=S63f[0:64, :], rhs=cums[0:64, t, 8, :],
                                                        start=True, stop=True), r=['cums', 'cst'], w=[pbk])
                    pg.op('act', lambda e: e.activation(out=st8[:, 6, :], in_=pb[:, 0:8], func=AF.Copy), r=[pbk], w=['st8c'])
                    pg.op('dve', lambda e, t=t: e.tensor_tensor(out=biasQ[:, 0:9, :],
                                                                in0=st8[:, 6, :].unsqueeze(1).to_broadcast([128, 9, 8]),
                                                                in1=cums[:, t, :, :], op=ALU.subtract),
                          r=['st8c', 'cums'], w=['biasQ'])
                    load_cache_kv(D['cc_k'][l, t], D['cc_v'][l, t], 8, 8)
                    for h in range(8):
                        hp, hb = h // 2, (h % 2) * 64
                        kbs = []
                        for kb in range(8):
                            kbs.append(dict(kT=ckT[hb:hb + 64, hp, kb * 128:(kb + 1) * 128], v=cva_[:, kb, h, 0:65], nk=128,
                                            bias=biasQ[:, kb, h:h + 1], bkeys=['biasQ'], q0=0, keys=['ckT', 'cvaS']))
                        kbs.append(dict(kT=kTs[hb:hb + 64, hp, t * 64:(t + 1) * 64], v=vas[0:64, t, h, 0:65], nk=64,
                                        bias=biasQ[0:64, 8, h:h + 1], bkeys=['biasQ'], q0=0, keys=['kTs', 'vas'], diag=True))
                        attend_fox(qT[hb:hb + 64, hp, t * 64:(t + 1) * 64], 64, [(0, 64)], kbs, ['qT'],
                                   attn_out_s(h * 64, t))
            oz_to_T('c', NT, np_, 4)

            if cfg.get('marks'): print('MARK', kind, Q, 'A_proj', pg.nops)
            wt, wk = load_w(D['w_in'], l, COL['a_q'], 512)
            for t, p, pk in pipe_tiles(NT, lambda t: proj_tm(xnT, 'xnT', t * np_, np_, wt, wk, 512)):
                evac_q(p, pk, t, np_, 8, 'a_qnorm')
            wt, wk = load_w(D['w_in'], l, COL['a_k'], 512)
            for t, p, pk in pipe_tiles(NT, lambda t: proj_tm(xnT, 'xnT', t * np_, np_, wt, wk, 512)):
                if samp:
                    evac_k(p, pk, np_, 8, 'a_knorm', D['sa_k'][l, rows(t), :],
                           lambda t=t: kTs[:, 0:4, t * 64:(t + 1) * 64], 'kTs')
                else:
                    gt = Q * 4 + t
                    od = D['pa_k'][l, (gt - 28) * 128:(gt - 27) * 128, :] if gt >= 28 else None
                    evac_k(p, pk, np_, 8, 'a_knorm', od,
                           lambda gt=gt: kT_a[:, :, (gt % 8) * 128:(gt % 8 + 1) * 128], 'kT_a')
            wt, wk = load_w(D['w_in'], l, COL['a_v'], 512)
            for t, p, pk in pipe_tiles(NT, lambda t: proj_tm(xnT, 'xnT', t * np_, np_, wt, wk, 512)):
                if samp:
                    evac_v(p, pk, np_, 8, D['sa_v'][l, rows(t), :], vas[0:np_, t, :, 0:64], 'vas')
                else:
                    gt = Q * 4 + t
                    od = D['pa_v'][l, (gt - 28) * 128:(gt - 27) * 128, :] if gt >= 28 else None
                    evac_v(p, pk, np_, 8, od, va_a[:, gt % 8, :, 0:64], 'va_a')
            wt, wk = load_w(D['w_in'], l, COL['a_z'], 512)
            for t, p, pk in pipe_tiles(NT, lambda t: proj_tm(xnT, 'xnT', t * np_, np_, wt, wk, 512)):
                pg.op('act', lambda e, p=p, t=t: e.activation(out=sz[0:np_, t, :], in_=p, func=AF.Silu), r=[pk], w=['sz'])
            for t in range(NT):
                gt = Q * 4 + t
                if samp:
                    load_cache_kv(D['ca_k'][l, t], D['ca_v'][l, t], 4, 8)
                for hq in range(2):
                    acc = None
                    for h4 in range(4):
                        h = hq * 4 + h4
                        hp, hb = h // 2, (h % 2) * 64
                        kbs = []
                        if not samp:
                            for i5 in range(5):
                                gk = gt - 4 + i5
                                if gk < 0:
                                    continue
                                s8 = gk % 8
                                d_ = dict(kT=kT_a[hb:hb + 64, hp, s8 * 128:(s8 + 1) * 128], v=va_a[:, s8, h, 0:65], nk=128,
                                          bias=None, q0=0, keys=['kT_a', 'va_a'])
                                if i5 in (1, 2):
                                    d_['bias'] = expB[:, h, i5 * 128:i5 * 128 + 1]; d_['bkeys'] = ['expB']
                                else:
                                    d_['badd'] = expB[:, h, i5 * 128:(i5 + 1) * 128]; d_['mkeys'] = ['expB']
                                kbs.append(d_)
                        else:
                            for kb in range(4):
                                d_ = dict(kT=ckT[hb:hb + 64, hp, kb * 128:(kb + 1) * 128], v=cva_[:, kb, h, 0:65], nk=128,
                                          bias=None, q0=0, keys=['ckT', 'cvaS'])
                                if kb in (0, 1, 2):
                                    d_['bias'] = expB[:, h, kb * 128:kb * 128 + 1]; d_['bkeys'] = ['expB']
                                else:
                                    d_['badd'] = expB[:, h, kb * 128:kb * 128 + 64]; d_['mkeys'] = ['expB']
                                kbs.append(d_)
                            kbs.append(dict(kT=kTs[hb:hb + 64, hp, t * 64:(t + 1) * 64], v=vas[0:64, t, h, 0:65], nk=64,
                                            bias=None, badd=expB[:, h, 512:576], mkeys=['expB'], q0=0, keys=['kTs', 'vas']))
                        a2 = None if acc is None else (acc[0], acc[1], h4, False)
                        acc = attend(qT[hb:hb + 64, hp, t * np_:(t + 1) * np_], np_, [(0, np_)], kbs, ['qT'], None, acc=a2)
                    pov, pok = acc
                    rl = st8[0:np_, 2, 0:4]
                    pg.op('dve', lambda e, pov=pov: e.reciprocal(out=rl, in_=pov[0:np_, 0:4, 64]), r=[pok], w=['st8'])
                    tmp = f2[0:np_, 0:256].rearrange("p (j d) -> p j d", d=64)
                    pg.op('dve', lambda e, pov=pov, tmp=tmp: e.tensor_tensor(out=tmp, in0=pov[0:np_, 0:4, 0:64],
                                                                           in1=rl.unsqueeze(2).to_broadcast([np_, 4, 64]), op=ALU.mult),
                          r=[pok, 'st8'], w=['f2'])
                    pg.op('pool', lambda e, tmp=tmp, hq=hq, t=t: e.tensor_tensor(
                        out=oz[0:np_, t, hq * 256:(hq + 1) * 256].rearrange("p (j d) -> p j d", d=64), in0=tmp,
                        in1=sz[0:np_, t, hq * 256:(hq + 1) * 256].rearrange("p (j d) -> p j d", d=64), op=ALU.mult),
                        r=['f2', 'sz'], w=['oz'])
            oz_to_T('a', NT, np_, 4)

            if cfg.get('marks'): print('MARK', kind, Q, 'M', pg.nops)
            wt, wk = load_w(D['w_in'], l, COL['m_q'], 512)
            for t, p, pk in pipe_tiles(NT, lambda t: proj_tm(xnT, 'xnT', t * np_, np_, wt, wk, 512)):
                pg.op('act', lambda e, p=p, t=t: e.activation(out=sz[0:np_, t, 0:256], in_=p[:, 256:512], func=AF.Silu),
                      r=[pk], w=['sz'])
                evac_q(p[:, 0:256], pk, t, np_, 4, 'm_qnorm')
            if not samp:
                for h in range(4):
                    hp, hb = h // 2, (h % 2) * 64
                    kbs = [dict(kT=kT_m[hb:hb + 64, hp, kb * 128:(kb + 1) * 128], v=va_m[:, kb, h, 0:65], nk=128, bias=None,
                                q0=0, keys=['kT_m', 'va_m']) for kb in range(2)]
                    attend(qT[hb:hb + 64, hp, :], 512, [(j * 128, 128) for j in range(4)], kbs, ['qT'],
                           attn_out(h * 64, 4, 128))
            else:
                for t in range(NT):
                    load_cache_kv(D['cm_k'][l, t], D['cm_v'][l, t], 2, 4)
                    for h in range(4):
                        hp, hb = h // 2, (h % 2) * 64
                        kbs = [dict(kT=ckT[hb:hb + 64, hp, kb * 128:(kb + 1) * 128], v=cva_[:, kb, h, 0:65], nk=128, bias=None,
                                    q0=0, keys=['ckT', 'cvaS']) for kb in range(2)]
                        attend(qT[hb:hb + 64, hp, t * 64:(t + 1) * 64], 64, [(0, 64)], kbs, ['qT'], attn_out_s(h * 64, t))
            oz_to_T('m', NT, np_, 2)

            if cfg.get('marks'): print('MARK', kind, Q, 'B_proj', pg.nops)
            wt, wk = load_w(D['w_in'], l, COL['b_z'], 512)
            for t, p, pk in pipe_tiles(NT, lambda t: proj_tm(xnT, 'xnT', t * np_, np_, wt, wk, 512)):
                pg.op('act', lambda e, p=p, t=t: e.activation(out=sz[0:np_, t, :], in_=p, func=AF.Silu), r=[pk], w=['sz', 'szb'])
            wt, wk = load_w(D['w_in'], l, COL['b_dt'], 8)
            psv, pk = proj8_all(NT, np_, wt, wk)
            softplus4(psv, pk, smv('b_dt_bias', np_), dts[0:np_, :, :], 'dts', np_, NT)
            pg.op('dve', lambda e: e.tensor_tensor(out=asb[0:np_, :, :], in0=dts[0:np_, :, :],
                                                   in1=aneg[0:np_, :].unsqueeze(1).to_broadcast([np_, NT, 8]), op=ALU.mult),
                  r=['dts', 'aneg'], w=['asb'])
            for half in range(2):
                js = slice(half * 3, half * 3 + 3)
                if samp:
                    for t in range(NT):
                        for jj in range(3):
                            j = half * 3 + jj
                            pg.dma('sp', raw[:, jj, t, 0:3],
                                   D['sb_conv'][l, t][:, j * 128:(j + 1) * 128].rearrange("r p -> p r"), w=['raw'], nonc=True)
                else:
                    pg.op('pool', lambda e, js=js: cp(e, out=raw[:, :, 0, 0:3], in_=halo[:, js, :]), r=['halo'], w=['raw'])
                wt, wk = load_w(D['w_in'], l, COL['b_xbc'] + half * 384, 384)
                for jj in range(3):
                    i = rot('A'); p = psA[i]; pk = 'psA%d' % i
                    for kc in range(8):
                        pg.op('pe', lambda e, kc=kc, jj=jj, p=p, wt=wt: e.matmul(p[:, 0:NTOK], lhsT=wt[:, kc, jj * 128:(jj + 1) * 128],
                                                                     rhs=xnT[:, kc, 0:NTOK], start=(kc == 0), stop=(kc == 7)),
                              r=['xnT'] + list(wk), w=[pk], inc=(kc == 7))
                    pg.op('act', lambda e, jj=jj, p=p: e.activation(out=raw[:, jj, :, 3:3 + np_],
                                                             in_=p[:, 0:NTOK].rearrange("p (t n) -> p t n", n=np_), func=AF.Copy),
                          r=[pk], w=['raw'])
                if not samp:
                    for t in range(1, NT):
                        pg.op('pool', lambda e, t=t: cp(e, out=raw[:, :, t, 0:3], in_=raw[:, :, t - 1, np_:np_ + 3]),
                              r=['raw'], w=['raw'])
                    pg.op('pool', lambda e, js=js: cp(e, out=halo[:, js, :], in_=raw[:, :, NT - 1, np_:np_ + 3]),
                          r=['raw'], w=['halo'])
                    if Q == NQ - 1:
                        for jj in range(3):
                            j = half * 3 + jj
                            out_tokens.append(pg.dma('sp', D['pb_c'][l][:, j * 128:(j + 1) * 128].rearrange("r p -> p r"),
                                                     halo[:, j, :], r=['halo'], nonc=True))
                else:
                    for t in range(NT):
                        for jj in range(3):
                            j = half * 3 + jj
                            out_tokens.append(pg.dma('sp', D['sb_c'][l, t][:, j * 128:(j + 1) * 128].rearrange("r p -> p r"),
                                                     raw[:, jj, t, np_:np_ + 3], r=['raw'], nonc=True))
                for jj in range(3):
                    j = half * 3 + jj
                    cv = cva[:, 0:NTOK].rearrange("p (t n) -> p t n", n=np_)
                    pg.op('dve', lambda e, j=j, jj=jj, cv=cv: e.tensor_scalar(out=cv, in0=raw[:, jj, :, 0:np_], scalar1=cw[:, j, 0:1],
                                                                scalar2=cw[:, j, 4:5], op0=ALU.mult, op1=ALU.add),
                          r=['raw', 'cw'], w=['cva'])
                    for tap in range(1, 4):
                        pg.op('dve', lambda e, j=j, jj=jj, tap=tap, cv=cv: e.scalar_tensor_tensor(
                            out=cv, in0=raw[:, jj, :, tap:tap + np_], scalar=cw[:, j, tap:tap + 1], in1=cv,
                            op0=ALU.mult, op1=ALU.add), r=['raw', 'cw', 'cva'], w=['cva'])
                    pg.op('act', lambda e, j=j: e.activation(out=xbc[:, j, 0:NTOK], in_=cva[:, 0:NTOK], func=AF.Silu),
                          r=['cva'], w=['xbc'])
            if samp:
                barrier(AEK + SET1K)
                for _ in ssdA(0, np_, 0):
                    pass
                for t in range(NT):
                    state_load(D['sb_ssm'][l, t])
                    gens = [ssdB(t, np_, t % 2)]
                    if t + 1 < NT:
                        gens.append(ssdA(t + 1, np_, (t + 1) % 2))
                    while gens:
                        for g_ in list(gens):
                            try:
                                next(g_)
                            except StopIteration:
                                gens.remove(g_)
                    state_store(D['sb_s'][l, t])
                barrier(AEK + SET1K)
            else:
                ssd_pipelined(NT, np_)
            if (not samp) and Q == NQ - 1 and not NOSTATE:
                state_store(D['pb_s'][l])
            oz_to_T('b', NT, np_, 4)

            if cfg.get('marks'): print('MARK', kind, Q, 'merge', pg.nops)
            barrier(AEK)
            brs = (('a', 'w_pa', 4), ('b', 'w_pb', 4), ('c', 'w_pc', 4), ('m', 'w_pm', 2))
            for c in range(2):
                for bi, (br, wn, nkc) in enumerate(brs):
                    wg, wgk = load_w(D['w_in'], l, COL['gate'] + bi * 1024 + c * 512, 512)
                    wp, wpk = load_w(D[wn], l, c * 512, 512, nk=nkc)
                    def mproj(t, wg=wg, wgk=wgk, wp=wp, wpk=wpk, br=br, nkc=nkc):
                        p1, pk1 = proj_tm(xnT, 'xnT', t * np_, np_, wg, wgk, 512)
                        isx = rot('S'); p2 = psS[isx]; pk2 = 'psS%d' % isx
                        for kc in range(nkc):
                            pg.op('pe', lambda e, kc=kc: e.matmul(
                                p2[0:np_, :], lhsT=ozT[br][:, kc, t * np_:(t + 1) * np_], rhs=wp[:, kc, :],
                                start=(kc == 0), stop=(kc == nkc - 1)), r=['ozT_' + br] + list(wpk), w=[pk2], inc=(kc == nkc - 1))
                        return p1, pk1, p2, pk2
                    for t, p1, pk1, p2, pk2 in pipe_tiles(NT, mproj):
                        pg.op('act', lambda e, p1=p1: e.activation(out=sig[0:np_, :], in_=p1, func=AF.Sigmoid), r=[pk1], w=['cva'])
                        if bi == 0:
                            pg.op('dve', lambda e, t=t, p2=p2: e.tensor_tensor(out=macc[0:np_, t, :], in0=sig[0:np_, :],
                                                                               in1=p2[0:np_, :], op=ALU.mult),
                                  r=['cva', pk2], w=['macc%d' % t])
                        else:
                            pg.op('dve', lambda e, t=t, p2=p2: e.tensor_tensor(out=f1[0:np_, :], in0=sig[0:np_, :],
                                                                               in1=p2[0:np_, :], op=ALU.mult),
                                  r=['cva', pk2], w=['f1'])
                            pg.op('pool', lambda e, t=t: e.tensor_tensor(out=macc[0:np_, t, :], in0=macc[0:np_, t, :],
                                                                         in1=f1[0:np_, :], op=ALU.add),
                                  r=['f1', 'macc%d' % t], w=['macc%d' % t])
                for t in range(NT):
                    pg.op('act', lambda e, t=t: e.activation(out=b1[0:np_, :], in_=macc[0:np_, t, :], func=AF.Copy),
                          r=['macc%d' % t], w=['b1'])
                    transposes(lambda j: b1[0:np_, j * 128:(j + 1) * 128], 4, np_,
                               lambda e, v, t=t, c=c: cp(e, out=mT[:, c * 4:(c + 1) * 4, t * np_:(t + 1) * np_], in_=v),
                               ['b1'], ['xbc', 'Mh'], evac='dve')
            wos = [load_w(D['w_out'], l, c * 512, 512) for c in range(2)]
            for t in range(NT):
                i = rot('X'); xk = 'xt%d' % i
                pg.dma('sp', xt[0:np_, i, :], xsrc[rows(t), :], r=[('xd', kind, row0 + t * np_)] if l > 0 else [], w=[xk])
                for c in range(2):
                    wo, wok = wos[c]
                    p, pk = proj_tm(mT, 'xbc', t * np_, np_, wo, wok, 512, xkeys=['Mh'])
                    pg.op('dve', lambda e, i=i, p=p, c=c: e.tensor_tensor(out=xt[0:np_, i, c * 512:(c + 1) * 512],
                                                                          in0=xt[0:np_, i, c * 512:(c + 1) * 512], in1=p,
                                                                          op=ALU.add), r=[xk, pk], w=[xk])
                tok = pg.dma('sp', xdst[rows(t), :], xt[0:np_, i, :], r=[xk], w=[('xd', kind, row0 + t * np_)])
                out_tokens.append(tok)

        vflat = va_c[:, :, :, :].rearrange("p a h c -> p (a h c)")
        sb_kTs = vflat[:, 0:1024].rearrange("p (k n) -> p k n", n=256)
        vas = vflat[:, 1024:1024 + 2112].rearrange("p (t h c) -> p t h c", t=4, h=8)
        ckT = vflat[:, 3136:3136 + 4096].rearrange("p (k n) -> p k n", n=1024)
        cva_ = vflat[:, 7232:7232 + 4224].rearrange("p (a h c) -> p a h c", a=8, h=8)
        cums = kT_a[:, 0, 0:576].bitcast(F32).rearrange("p (t k h) -> p t k h", t=4, k=9)
        SKEYS = ['kTs', 'vas', 'ckT', 'cvaS']

        def load_cache_kv(kd, vd, nblk, nh):
            n = nh * 64
            for kb in range(nblk):
                stg, sk = ((b1, 'b1'), (xnb, 'xnb'))[kb % 2]
                pg.dma('pool', stg[:, 0:n], kd[kb * 128:(kb + 1) * 128, :], w=[sk])
                transposes(lambda j, stg=stg: stg[:, j * 128:(j + 1) * 128], nh // 2, 128,
                           lambda e, v, kb=kb: cp(e, out=ckT[:, 0:nh // 2, kb * 128:(kb + 1) * 128], in_=v),
                           [sk], ['ckT'], evac='act' if kb % 2 else 'dve')
                pg.dma('pool', cva_[:, kb, 0:nh, 0:64], vd[kb * 128:(kb + 1) * 128, :].rearrange("p (h d) -> p h d", d=64),
                       w=['cvaS'])

        def state_load(src):
            pg.dma('sp', cva[0:64, :].rearrange("p (h n) -> p h n", n=64), src.rearrange("h p n -> p h n"), w=['cva'])
            i = rot('A'); p = psA[i]; pk = 'psA%d' % i
            for h in range(8):
                pg.op('pe', lambda e, h=h: e.matmul(p[0:64, h * 64:(h + 1) * 64], lhsT=cva[0:64, h * 64:(h + 1) * 64],
                                                    rhs=identf[0:64, :], start=True, stop=True),
                      r=['cva', 'identf'], w=[pk], inc=(h == 7))
            for g in range(2):
                pg.op('act' if g else 'dve', lambda e, g=g: cp(
                    e, out=hT[g * 64:(g + 1) * 64, :, :], in_=p[0:64, g * 256:(g + 1) * 256].rearrange("p (a d) -> p a d", d=64)),
                    r=[pk], w=['hT'])
            pg.op('act', lambda e: e.activation(out=hTb[:, :, :], in_=hT[:, :, :], func=AF.Copy), r=['hT'], w=['hTb'])

        def state_store(dst):
            for g in range(2):
                i = rot('A'); p = psA[i]; pk = 'psA%d' % i
                for a in range(4):
                    pg.op('pe', lambda e, g=g, a=a, p=p: e.matmul(p[0:64, a * 64:(a + 1) * 64], lhsT=hT[g * 64:(g + 1) * 64, a, :],
                                                             rhs=identf[g * 64:(g + 1) * 64, :], start=True, stop=True),
                          r=['hT', 'identf'], w=[pk], inc=(a == 3))
                pg.op('act' if g else 'dve', lambda e, g=g, p=p: cp(e, out=cva[0:64, g * 256:(g + 1) * 256], in_=p[0:64, 0:256]),
                      r=[pk], w=['cva'])
            out_tokens.append(pg.dma('sp', dst.rearrange("h p n -> p h n"), cva[0:64, :].rearrange("p (h n) -> p h n", n=64),
                                     r=['cva']))

        def attn_out_t(h0col, t, np_):
            def fn(pov, pok):
                rl = st8[0:np_, 2, 0:1]
                pg.op('dve', lambda e: e.reciprocal(out=rl, in_=pov[0:np_, 0, 64:65]), r=[pok], w=['st8'])
                pg.op('dve', lambda e: e.scalar_tensor_tensor(out=oz[0:np_, t, h0col:h0col + 64], in0=pov[0:np_, 0, 0:64],
                                                              scalar=rl, in1=sz[0:np_, t, h0col:h0col + 64],
                                                              op0=ALU.mult, op1=ALU.mult), r=[pok, 'st8', 'sz'], w=['oz'])
            return fn

        def attn_out_s(h0col, t):
            return attn_out_t(h0col, t, 64)

        def attend_fox(qT_ap, NQc, qtiles, kblocks, qkeys, out_fn):
            for kb in kblocks:
                if kb.get('diag'):
                    kb['diagmask'] = True
            attend(qT_ap, NQc, qtiles, kblocks, qkeys, out_fn)

        def memory_kv(l):
            def src(t):
                return D['memp'][t * 128:(t + 1) * 128, :]
            barrier(AEK)
            pg.dma('sp', AE[:, 0:1024], D['mnorm'][l:l + 1, :].broadcast_to([128, 1024]), w=AEK)
            norm_tiles(src, 2, 128, lambda n: AE[0:n, 0:1024], lambda t: AEK)
            barrier(AEK)
            wt, wk = load_w(D['w_mkv'], l, 0, 512)
            for t, p, pk in pipe_tiles(2, lambda t: proj_tm(xnT, 'xnT', t * 128, 128, wt, wk, 512)):
                evac_v(p[:, 256:512], pk, 128, 4, D['pm_v'][l, t * 128:(t + 1) * 128, :], va_m[:, t, :, 0:64], 'va_m')
                evac_k(p[:, 0:256], pk, 128, 4, 'm_knorm', D['pm_k'][l, t * 128:(t + 1) * 128, :],
                       lambda t=t: kT_m[:, :, t * 128:(t + 1) * 128], 'kT_m')

        try:
          for l in range(NL):
            layer_consts(l)
            memory_kv(l)
            pg.op('pool', lambda e: e.memset(hT[:], 0.0), w=['hT'])
            pg.op('pool', lambda e: e.memset(hTb[:], 0.0), w=['hTb'])
            pg.op('pool', lambda e: e.memset(halo[:], 0.0), w=['halo'])
            for Q in range(NQ):
                process_group(l, 'p', Q)
            if DO_S:
                barrier(['va_c', 'kT_a', 'cums'] + SKEYS)
                pg.op('pool', lambda e: e.memset(vas, 1.0), w=['vas'])
                pg.op('pool', lambda e: e.memset(cva_, 1.0), w=['cvaS'])
                process_group(l, 's', 0)
                barrier(['va_c', 'kT_a', 'cums'] + SKEYS)
                pg.op('pool', lambda e: e.memset(va_c[:], 1.0), w=['va_c'])
        except StopBuild:
            pass
        pg.maxops = None
        pg.op('pe', lambda e: e.matmul(psA[0][0:1, 0:1], lhsT=cstb[:, 0:1], rhs=cstb[:, 0:1], start=True, stop=True), r=['cstb'], w=['psA0'])
        pg.op('act', lambda e: e.activation(out=bar[:, 1:2], in_=bar[:, 1:2], func=AF.Copy), r=['psA0'], w=['bar2'])
        pg.op('dve', lambda e: e.tensor_copy(out=bar[:, 1:2], in_=bar[:, 1:2]), w=['bar2'])
        pg.op('pool', lambda e: e.tensor_copy(out=bar[:, 1:2], in_=bar[:, 1:2]), w=['bar2'])
        out_tokens.append(pg.dma('sp', D['pb_c'][0, 0:1, 0:2], bar[0:1, 0:2], r=['bar2', 'bar'], w=['zz']) if False else None)
        out_tokens[:] = [t for t in out_tokens if t is not None]
        pg.op('pool', lambda e: e.memset(bar[:, 0:1], 0.0), r=['bar2'], w=['bar'])
        pg.finish(out_tokens)
        pg._wait('sp', ('c', 'pool', pg.cnt['pool'] - 1))
        print("ops:", pg.nops, "dmas:", pg.dma_n, "cnt:", pg.cnt)
    return nc


def _consts():
    c = np.zeros((128, NCST), np.float32)
    k = np.arange(128)[:, None]
    m = np.arange(128)[None, :]
    c[:, C_M1:C_M1 + 128] = (k <= m)
    c[:, C_M2:C_M2 + 128] = (k > m)
    c[:, C_S127:C_S127 + 128] = (k == 127)
    c[:, C_S63:C_S63 + 128] = (k == 63)
    band = np.ones((128, 5, 128), np.float32)
    s = np.arange(128)[:, None]
    t = np.arange(128)[None, :]
    band[:, 0, :] = 1.0 - ((s < 64) & (t >= 64))
    band[:, 4, :] = 1.0 - ((s >= 64) & (t < 64))
    ident = (k == m).astype(np.float32)
    return c, np.ascontiguousarray(band.reshape(128, 640)), ident


def _prep(inputs, cfg=None):
    f = lambda a: np.ascontiguousarray(np.asarray(a, dtype=np.float32))
    I = {k: f(v) for k, v in inputs.items()}
    small = np.concatenate([I[n].reshape(2, -1) for n in SM], axis=1)
    s = np.arange(128)[:, None]
    j = np.arange(640)[None, :]
    dist = 512 + (j % 128) - 128 * (j // 128) - s
    idx = np.clip(dist, -128, 128) + 128
    relb = np.ascontiguousarray(np.transpose(I['a_rel'][:, idx, :], (0, 1, 3, 2)))
    cwt = np.concatenate([I['b_conv_w'], I['b_conv_b'][:, None, :]], axis=1)
    convw = np.ascontiguousarray(np.transpose(cwt.reshape(2, 5, 6, 128), (0, 3, 2, 1)))
    cst, band, ident = _consts()
    maps = []
    for c in range(8):
        b = c % 4
        ss = slice(c * NS, (c + 1) * NS)
        m = dict(
            xp=I['x_prompt'][b], xs=I['x_sample'][ss].reshape(NS * TS, DM), memp=I['mem_prompt'][b],
            ca_k=I['cache_a_k'][:, ss].reshape(2, NS, 512, 512), ca_v=I['cache_a_v'][:, ss].reshape(2, NS, 512, 512),
            cc_k=I['cache_c_k'][:, ss].reshape(2, NS, 1024, 512), cc_v=I['cache_c_v'][:, ss].reshape(2, NS, 1024, 512),
            cc_f=I['cache_c_logf'][:, ss], sb_ssm=I['state_b_ssm'][:, ss], sb_conv=I['state_b_conv'][:, ss],
            cm_k=I['cache_mem_k'][:, ss].reshape(2, NS, 256, 256), cm_v=I['cache_mem_v'][:, ss].reshape(2, NS, 256, 256),
            w_in=I['w_in'], w_mkv=I['w_mkv'], w_pa=I['w_pa'], w_pb=I['w_pb'], w_pc=I['w_pc'], w_pm=I['w_pm'],
            w_out=I['w_out'], small=small, mnorm=I['m_norm'], band=band, ident=ident, relb=relb, convw=convw, cst=cst)
        maps.append({k: np.ascontiguousarray(v) for k, v in m.items()})
    return maps


_NC_CACHE = {}


def kernel(**inputs):
    cfg = {}
    key = 'full'
    if key not in _NC_CACHE:
        _NC_CACHE[key] = build(cfg)
    nc = _NC_CACHE[key]
    maps = _prep(inputs)
    res = run_bass_kernel_spmd(nc, maps, core_ids=list(range(8)))
    R = res.results
    P4 = range(4)
    st = lambda name, shape: np.stack([R[b][name] for b in P4], axis=1).reshape(shape)
    cat = lambda name: np.concatenate([R[c][name] for c in range(8)], axis=1)
    y_prompt = np.stack([R[b]['yp'] for b in P4], axis=0)
    y_sample = np.concatenate([R[c]['ys'].reshape(NS, TS, DM) for c in range(8)], axis=0)
    outs = [y_prompt, y_sample,
            st('pa_k', (2, 4, 512, 8, 64)), st('pa_v', (2, 4, 512, 8, 64)),
            st('pc_k', (2, 4, SEQ, 8, 64)), st('pc_v', (2, 4, SEQ, 8, 64)), st('pc_f', (2, 4, SEQ, 8)),
            st('pb_s', (2, 4, 8, 64, 64)), st('pb_c', (2, 4, 3, 768)),
            st('pm_k', (2, 4, 256, 4, 64)), st('pm_v', (2, 4, 256, 4, 64))]
    for name, tail in (('sa_k', (8, 64)), ('sa_v', (8, 64)), ('sc_k', (8, 64)), ('sc_v', (8, 64)), ('sc_f', (8,))):
        a = np.concatenate([R[c][name].reshape((2, NS, TS) + tail) for c in range(8)], axis=1)
        outs.append(a)
    outs.append(cat('sb_s'))
    outs.append(cat('sb_c'))
    return tuple(np.ascontiguousarray(o.astype(np.float32)) for o in outs)
```

```python
import contextlib
import numpy as np
import ml_dtypes
import concourse.bass as bass
import concourse.mybir as mybir
from concourse.bass_utils import run_bass_kernel_spmd

F32 = mybir.dt.float32
BF16 = mybir.dt.bfloat16
AF = mybir.ActivationFunctionType
ALU = mybir.AluOpType
AX = mybir.AxisListType

DM = 1024
DIN = 10000
SEQ = 4096
NS = 4
TS = 64
EPS = 1e-6
COL = dict(a_q=0, a_k=512, a_v=1024, a_z=1536, b_z=2048, b_xbc=2560, b_dt=3328,
           c_q=3336, c_k=3848, c_v=4360, c_f=4872, c_z=4880, m_q=5392, m_z=5648, gate=5904)
SM = {}
_o = 0
for _n, _w in (('g_norm', 1024), ('b_norm', 512), ('a_qnorm', 64), ('a_knorm', 64),
               ('c_qnorm', 64), ('c_knorm', 64), ('m_qnorm', 64), ('m_knorm', 64),
               ('b_dt_bias', 8), ('b_a_log', 8), ('b_d', 8), ('c_fbias', 8)):
    SM[_n] = (_o, _w)
    _o += _w
NSM = _o
C_M1, C_M2, C_S127, C_S63 = 0, 128, 256, 384
NCST = 512


def cp(e, out, in_):
    if hasattr(e, 'tensor_copy'):
        return e.tensor_copy(out=out, in_=in_)
    return e.activation(out=out, in_=in_, func=AF.Copy)


SAME_ENGINE_SYNC = True


class StopBuild(Exception):
    pass


class Prog:
    EPOCH = 24000
    ND = 48

    def __init__(self, nc, es):
        self.nc, self.es = nc, es
        self.eng = {'pe': nc.tensor, 'act': nc.scalar, 'dve': nc.vector, 'pool': nc.gpsimd, 'sp': nc.sync}
        self.cnt = {e: 0 for e in self.eng}
        self.sems = {}
        self.waited = {e: {} for e in self.eng}
        self.last_w = {}
        self.readers = {}
        self.dma_n = 0
        self.dma_sems = [es.enter_context(nc.semaphore("dq%d" % i)) for i in range(self.ND)]
        self.dma_tokens = []
        self.nops = 0
        self.maxops = None

    def _sem(self, e, epoch):
        k = (e, epoch)
        if k not in self.sems:
            self.sems[k] = self.es.enter_context(self.nc.semaphore("s_%s_%d" % (e, epoch)))
        return self.sems[k]

    def _wait(self, e, tok):
        if tok[0] == 'c':
            _, pe, c = tok
            if pe == e and (e == 'pe' or not SAME_ENGINE_SYNC):
                return
            epoch, v = divmod(c, self.EPOCH)
            key, val, sem = ('c', pe, epoch), v + 1, self._sem(pe, epoch)
        else:
            _, slot, rnd = tok
            key, val, sem = ('d', slot), 16 * (rnd + 1), self.dma_sems[slot]
        if self.waited[e].get(key, 0) >= val:
            return
        self.waited[e][key] = val
        self.eng[e].wait_ge(sem, val)

    def _deps(self, r, w):
        deps = set()
        for k in r:
            if k in self.last_w:
                deps.add(self.last_w[k])
            if isinstance(k, str) and k.startswith('ps'):
                deps.update(self.readers.get(k, ()))
        for k in w:
            if k in self.last_w:
                deps.add(self.last_w[k])
            deps.update(self.readers.get(k, ()))
        return deps

    def _reg(self, tok, r, w):
        for k in r:
            self.readers.setdefault(k, []).append(tok)
        for k in w:
            self.last_w[k] = tok
            self.readers[k] = []

    def op(self, e, fn, r=(), w=(), inc=True):
        if self.maxops is not None and self.nops >= self.maxops:
            raise StopBuild()
        for tok in self._deps(r, w):
            self._wait(e, tok)
        inst = fn(self.eng[e])
        c = self.cnt[e]
        if inc:
            epoch, _ = divmod(c, self.EPOCH)
            inst.then_inc(self._sem(e, epoch), 1)
            self.cnt[e] += 1
        self._reg(('c', e, c), r, w)
        self.nops += 1

    def dma(self, q, out, in_, r=(), w=(), nonc=False):
        if self.maxops is not None and self.nops >= self.maxops:
            raise StopBuild()
        n = self.dma_n
        self.dma_n += 1
        slot, rnd = n % self.ND, n // self.ND
        if rnd > 0:
            self._wait(q, ('d', slot, rnd - 1))
        for tok in self._deps(r, w):
            self._wait(q, tok)
        kw = {}
        if nonc:
            kw['allow_slow_non_contiguous'] = True
        inst = self.eng[q].dma_start(out=out, in_=in_, **kw)
        inst.then_inc(self.dma_sems[slot], 16)
        tok = ('d', slot, rnd)
        self._reg(tok, r, w)
        self.dma_tokens.append(tok)
        self.nops += 1
        return tok

    def finish(self, out_tokens):
        for tok in out_tokens:
            self._wait('sp', tok)
        last = {}
        for tok in self.dma_tokens:
            last[tok[1]] = tok
        for tok in last.values():
            self._wait('sp', tok)


def build(cfg):
    NL = cfg.get('layers', 2)
    NQ = cfg.get('nq', 8)
    DO_S = cfg.get('sample', True)
    NOSTATE = cfg.get('nostate', False)
    nc = bass.Bass("TRN2", target_bir_lowering=False)
    es = contextlib.ExitStack()
    D = {}

    def din(name, shape):
        D[name] = nc.dram_tensor(name, list(shape), F32, kind="ExternalInput").ap()

    def dout(name, shape):
        D[name] = nc.dram_tensor(name, list(shape), F32, kind="ExternalOutput").ap()

    din('xp', [SEQ, DM]); din('xs', [NS * TS, DM]); din('memp', [256, DM])
    din('ca_k', [2, NS, 512, 512]); din('ca_v', [2, NS, 512, 512])
    din('cc_k', [2, NS, 1024, 512]); din('cc_v', [2, NS, 1024, 512]); din('cc_f', [2, NS, 1024, 8])
    din('sb_ssm', [2, NS, 8, 64, 64]); din('sb_conv', [2, NS, 3, 768])
    din('cm_k', [2, NS, 256, 256]); din('cm_v', [2, NS, 256, 256])
    din('w_in', [2, DM, DIN]); din('w_mkv', [2, DM, 512])
    din('w_pa', [2, 512, DM]); din('w_pb', [2, 512, DM]); din('w_pc', [2, 512, DM]); din('w_pm', [2, 256, DM])
    din('w_out', [2, DM, DM])
    din('small', [2, NSM]); din('mnorm', [2, 1024]); din('relb', [2, 128, 8, 640]); din('convw', [2, 128, 6, 5]); din('cst', [128, NCST]); din('band', [128, 640]); din('ident', [128, 128])
    dout('yp', [SEQ, DM]); dout('ys', [NS * TS, DM])
    dout('pa_k', [2, 512, 512]); dout('pa_v', [2, 512, 512])
    dout('pc_k', [2, SEQ, 512]); dout('pc_v', [2, SEQ, 512]); dout('pc_f', [2, SEQ, 8])
    dout('pb_s', [2, 8, 64, 64]); dout('pb_c', [2, 3, 768])
    dout('pm_k', [2, 256, 256]); dout('pm_v', [2, 256, 256])
    dout('sa_k', [2, NS * TS, 512]); dout('sa_v', [2, NS * TS, 512])
    dout('sc_k', [2, NS * TS, 512]); dout('sc_v', [2, NS * TS, 512]); dout('sc_f', [2, NS * TS, 8])
    dout('sb_s', [2, NS, 8, 64, 64]); dout('sb_c', [2, NS, 3, 768])
    x1p = nc.dram_tensor("x1p", [SEQ, DM], F32, kind="Internal").ap()
    x1s = nc.dram_tensor("x1s", [NS * TS, DM], F32, kind="Internal").ap()

    with es:
        pg = Prog(nc, es)
        pg.maxops = cfg.get('maxops')
        out_tokens = []

        def sb(name, shape, dt):
            return es.enter_context(nc.sbuf_tensor("sb_" + name, list(shape), dt))

        def ps(name, shape, dt):
            return es.enter_context(nc.psum_tensor("ps_" + name, list(shape), dt))

        cst = sb("cst", [128, NCST], F32)
        cstb = sb("cstb", [128, 256], BF16)
        small = sb("small", [128, NSM], F32)
        expB = sb("expB", [128, 8, 640], BF16)
        cw = sb("cw", [128, 6, 5], F32)
        aneg = sb("aneg", [128, 8], F32)
        xt = sb("xt", [128, 1, DM], F32)
        W3 = sb("W3", [128, 4096], BF16)
        xnb = W3[:, 0:1024]
        sq = W3[:, 1024:2048].bitcast(F32)
        f2 = W3[:, 2048:3072].bitcast(F32)
        f3 = W3[:, 3072:4096].bitcast(F32)
        xnT = sb("xnT", [128, 8, 512], BF16)
        NWB = 3
        wbuf = [sb("wbuf%d" % i, [128, 8, 512], BF16) for i in range(NWB)]
        wbuf.append(W3[:, :].rearrange("p (k n) -> p k n", n=512))
        WKEYS = [['wbuf0'], ['wbuf1'], ['wbuf2'], ['wbuf3', 'xnb', 'sq', 'f2', 'f3']]
        kT_c = sb("kT_c", [128, 4, SEQ], BF16)
        va_c = sb("va_c", [128, 32, 8, 66], BF16)
        kT_a = sb("kT_a", [128, 4, 1024], BF16)
        va_a = sb("va_a", [128, 8, 8, 66], BF16)
        kT_m = sb("kT_m", [128, 2, 256], BF16)
        va_m = sb("va_m", [128, 2, 4, 66], BF16)
        cum = sb("cum", [128, 32, 8], F32)
        biasQ = sb("biasQ", [128, 32, 8], F32)
        qT = sb("qT", [128, 4, 512], BF16)
        sz = sb("sz", [128, 4, 512], BF16)
        oz = sb("oz", [128, 4, 512], BF16)
        ozT = {k: sb("ozT_" + k, [128, n, 512], BF16) for k, n in (('a', 4), ('b', 4), ('c', 4), ('m', 2))}
        f1 = sb("f1", [128, 512], F32)
        b1 = sb("b1", [128, 512], BF16)
        PT = [sb("PT%d" % i, [128, 512], BF16) for i in range(2)]
        st8 = sb("st8", [128, 8, 8], F32)
        raw = sb("raw", [128, 3, 4, 131], F32)
        halo = sb("halo", [128, 6, 3], F32)
        AR2 = sb("AR2", [128, 4096], BF16)
        xbc = AR2[:, 0:3072].rearrange("p (j n) -> p j n", n=512)
        Mh = AR2[:, 3072:4096].rearrange("p (h n) -> p h n", n=128)
        mT = AR2[:, :].rearrange("p (k n) -> p k n", n=512)
        cva = sb("cva", [128, 512], F32)
        sig = cva
        dts = sb("dts", [128, 4, 8], F32)
        lfs = sb("lfs", [128, 4, 8], F32)
        asb = sb("asb", [128, 4, 8], F32)
        btk = sb("btk", [128, 128], BF16)
        AE = sb("AE", [128, 2048], F32)
        Ah = AE[:, 0:512].rearrange("p (h n) -> p h n", n=128)
        Ee = AE[:, 512:1024].rearrange("p (h n) -> p h n", n=128)
        Gm = AE[:, 1024:1280].rearrange("p (g n) -> p g n", n=128)
        bdec = AE[:, 1280:1536].bitcast(BF16).rearrange("p (a g n) -> p a g n", a=4, g=2)
        xdt = AE[:, 1536:1792].bitcast(BF16).rearrange("p (h d) -> p h d", d=64)
        xtk = AE[:, 1792:2048].bitcast(BF16)
        macc = AE[:, :].rearrange("p (t n) -> p t n", n=512)
        hT = sb("hT", [128, 4, 64], F32)
        hTb = sb("hTb", [128, 4, 64], BF16)
        ssd8 = sb("ssd8", [128, 4, 8], F32)
        psA = [ps("psA%d" % i, [128, 512], F32) for i in range(2)]
        psT = [ps("psT%d" % i, [128, 1024], BF16) for i in range(2)]
        psS = [ps("psS%d" % i, [128, 512], F32) for i in range(2)]
        psO = [ps("psO%d" % i, [128, 512], F32) for i in range(2)]
        rr = {'A': 0, 'T': 0, 'S': 0, 'O': 0, 'W': 0, 'P': 0, 'X': 0, 'WP': 0}
        rrn = {'X': 1, 'WP': 1}

        def rot(k, n=2):
            v = rr[k]
            rr[k] = (v + 1) % rrn.get(k, n)
            return v

        ident_b = cstb[:, 0:128]
        M1f = cst[:, C_M1:C_M1 + 128]
        M2f = cst[:, C_M2:C_M2 + 128]
        S127f = cst[:, C_S127:C_S127 + 128]
        S63f = cst[:, C_S63:C_S63 + 128]
        M1b = cstb[:, 128:256]

        def smv(name, rows=128):
            o, w = SM[name]
            return small[0:rows, o:o + w]

        bar = sb("bar", [128, 2], F32)
        epst = sb("epst", [128, 1], F32)
        nhalf = sb("nhalf", [128, 8], F32)
        gcol = sb("gcol", [128, 4], F32)
        identf = sb("identf", [128, 64], F32)

        def barrier(keys):
            pg.op('pool', lambda e: e.memset(bar[:, 0:1], 0.0), w=['bar'] + list(keys))

        AEK = ['Ah%d' % h for h in range(8)] + ['Ee', 'Gm', 'bdec', 'xdt', 'xtk', 'AErel'] + ['macc%d' % t for t in range(4)]
        pg.op('pool', lambda e: e.memset(epst[:, :], EPS), w=['epst'])
        pg.op('pool', lambda e: e.memset(nhalf[:, :], -0.5), w=['epst'])
        pg.dma('sp', identf[0:64, :], D['ident'][0:64, 0:64], w=['identf'])
        pg.dma('sp', identf[64:128, :], D['ident'][0:64, 0:64], w=['identf'])
        pg.dma('sp', cst[:, :], D['cst'][:, :], w=['cst'])
        pg.dma('sp', AE[:, 0:128], D['ident'][:, :], w=AEK)
        pg.op('dve', lambda e: cp(e, out=cstb[:, 0:128], in_=AE[:, 0:128]), r=AEK, w=['cstb'])
        pg.op('dve', lambda e: cp(e, out=cstb[:, 128:256], in_=cst[:, C_M1:C_M1 + 128]), r=['cst'], w=['cstb'])
        for t_, k_ in ((va_c, 'va_c'), (va_a, 'va_a'), (va_m, 'va_m')):
            pg.op('pool', lambda e, t_=t_: e.memset(t_[:], 1.0), w=[k_])

        def group_wlist(l):
            wl = []
            for nm in ('c_q', 'c_k', 'c_v'):
                wl.append(('w_in', l, COL[nm], 512, 8))
            wl.append(('w_in', l, COL['c_f'], 8, 8))
            wl.append(('w_in', l, COL['c_z'], 512, 8))
            for nm in ('a_q', 'a_k', 'a_v', 'a_z', 'm_q', 'b_z'):
                wl.append(('w_in', l, COL[nm], 512, 8))
            wl.append(('w_in', l, COL['b_dt'], 8, 8))
            for half in range(2):
                wl.append(('w_in', l, COL['b_xbc'] + half * 384, 384, 8))
            for c in range(2):
                for bi, (wn, nkc) in enumerate((('w_pa', 4), ('w_pb', 4), ('w_pc', 4), ('w_pm', 2))):
                    wl.append(('w_in', l, COL['gate'] + bi * 1024 + c * 512, 512, 8, True))
                    wl.append((wn, l, c * 512, 512, nkc, True))
            for c in range(2):
                wl.append(('w_out', l, c * 512, 512, 8, True))
            return wl

        WL = []
        for l_ in range(NL):
            WL.append(('w_mkv', l_, 0, 512, 8))
            for _ in range(NQ + (1 if DO_S else 0)):
                WL += group_wlist(l_)
        wbi, prev_occ, lastocc, r3, r4 = [], [], {}, 0, 0
        for k_, ent in enumerate(WL):
            if len(ent) > 5 and ent[5]:
                b_ = r4 % 4; r4 += 1
            else:
                b_ = r3 % 3; r3 += 1; r4 = r3
            wbi.append(b_)
            prev_occ.append(lastocc.get(b_, -1))
            lastocc[b_] = k_
        wstate = {'ptr': 0, 'issued': 0}

        def load_w(dram, l, c0, n, nk=8, buf=None):
            i = wstate['ptr']
            exp = WL[i]
            assert exp[1] == l and exp[2] == c0 and exp[3] == n and exp[4] == nk and D[exp[0]] is dram, (exp, l, c0, n, nk)
            in_merge = len(exp) > 5 and exp[5]
            while wstate['issued'] < len(WL) and wstate['issued'] <= i + 3 and \
                    (wstate['issued'] <= i or prev_occ[wstate['issued']] <= i - 2) and \
                    (wbi[wstate['issued']] != 3 or in_merge):
                k = wstate['issued']
                nm, l2, c2, n2, nk2 = WL[k][0:5]
                src = D[nm][l2, :, c2:c2 + n2].rearrange("(k p) n -> p k n", p=128)
                pg.dma('pool', wbuf[wbi[k]][:, 0:nk2, 0:n2], src, w=WKEYS[wbi[k]])
                wstate['issued'] += 1
            wstate['ptr'] += 1
            return wbuf[wbi[i]], WKEYS[wbi[i]]

        def transposes(src_ap_fn, nblk, np_, dst_fn, rkeys, wkeys, evac='dve'):
            i = rot('T'); pt = psT[i]; pk = 'psT%d' % i
            for j in range(nblk):
                pg.op('pe', lambda e, j=j: e.transpose(out=pt[:, j * 128:j * 128 + np_], in_=src_ap_fn(j),
                                                       identity=ident_b[0:np_, 0:np_]),
                      r=list(rkeys) + ['cstb'], w=[pk], inc=(j == nblk - 1))
            view = pt[:, 0:nblk * 128].rearrange("p (b n) -> p b n", n=128)[:, :, 0:np_]
            pg.op(evac, lambda e: dst_fn(e, view), r=[pk], w=list(wkeys))

        def rstd_from_ss(ss_ap, rs_ap, n, rows, key):
            w_ = rs_ap.shape[-1]
            pg.op('dve', lambda e: e.tensor_scalar(out=rs_ap, in0=ss_ap, scalar1=1.0 / n, scalar2=EPS,
                                                    op0=ALU.mult, op1=ALU.add), r=[key], w=[key])
            pg.op('pool', lambda e: e.tensor_tensor(out=rs_ap, in0=rs_ap, in1=nhalf[0:rows, 0:w_], op=ALU.pow),
                  r=[key, 'epst'], w=[key])

        def head_norm(psap, pk, nh, gain_ap, out_ap, outkeys, np_, scale=None):
            n = nh * 64
            pg.op('act', lambda e: e.activation(out=sq[0:np_, 0:n], in_=psap, func=AF.Square), r=[pk], w=['sq'])
            ssv = st8[0:np_, 0, 0:nh]
            pg.op('dve', lambda e: e.tensor_reduce(out=ssv, in_=sq[0:np_, 0:n].rearrange("p (h d) -> p h d", d=64),
                                                   axis=AX.X, op=ALU.add), r=['sq'], w=['st8'])
            rstd_from_ss(ssv, ssv, 64, np_, 'st8')
            o1, k1 = (f1[0:np_, 0:n], ['f1']) if gain_ap is not None else (out_ap, list(outkeys))
            pg.op('dve', lambda e: e.tensor_tensor(
                out=o1.rearrange("p (h d) -> p h d", d=64),
                in0=psap.rearrange("p (h d) -> p h d", d=64),
                in1=ssv.unsqueeze(2).to_broadcast([np_, nh, 64]), op=ALU.mult), r=[pk, 'st8'], w=k1)
            if gain_ap is None:
                return
            g = gain_ap.unsqueeze(1).to_broadcast([np_, nh, 64])
            pg.op('dve', lambda e: e.tensor_tensor(
                out=out_ap.rearrange("p (h d) -> p h d", d=64),
                in0=f1[0:np_, 0:n].rearrange("p (h d) -> p h d", d=64), in1=g, op=ALU.mult),
                r=['f1', 'small'], w=list(outkeys))

        def pipe_tiles(NT, proj_fn):
            nxt = proj_fn(0)
            for t in range(NT):
                cur = nxt
                if t + 1 < NT:
                    nxt = proj_fn(t + 1)
                yield (t,) + tuple(cur)

        def proj_tm(xT, xkey, tcol, np_, wt, wkey, n, nk=8, wc0=0, xkeys=()):
            i = rot('A'); p = psA[i]; pk = 'psA%d' % i
            for kc in range(nk):
                pg.op('pe', lambda e, kc=kc: e.matmul(p[0:np_, 0:n], lhsT=xT[:, kc, tcol:tcol + np_],
                                                      rhs=wt[:, kc, wc0:wc0 + n], start=(kc == 0), stop=(kc == nk - 1)),
                      r=[xkey] + list(wkey) + list(xkeys), w=[pk], inc=(kc == nk - 1))
            return p[0:np_, 0:n], pk

        def layer_consts(l):
            pg.dma('sp', small[:, :], D['small'][l:l + 1, :].broadcast_to([128, NSM]), w=['small'])
            pg.dma('sp', cw[:, :, :], D['convw'][l], w=['cw'])
            for j, name in enumerate(('a_qnorm', 'c_qnorm', 'm_qnorm')):
                o_, w_ = SM[name]
                for half in range(2):
                    pg.dma('sp', gcol[half * 64:(half + 1) * 64, j:j + 1],
                           D['small'][l, o_:o_ + 64].rearrange("(p o) -> p o", o=1), w=['gcol'], nonc=True)
            pg.op('dve', lambda e: e.tensor_scalar(out=gcol[:, 0:3], in0=gcol[:, 0:3], scalar1=0.125, scalar2=None, op0=ALU.mult),
                  r=['gcol'], w=['gcol'])
            pg.op('act', lambda e: e.activation(out=aneg[:, :], in_=smv('b_a_log'), func=AF.Exp), r=['small'], w=['aneg'])
            pg.op('dve', lambda e: e.tensor_scalar(out=aneg[:, :], in0=aneg[:, :], scalar1=-1.0, scalar2=None, op0=ALU.mult),
                  r=['aneg'], w=['aneg'])
            barrier(AEK)
            pg.dma('sp', AE[:, 0:640], D['band'][:, :], w=AEK)
            pg.op('dve', lambda e: e.tensor_scalar(out=AE[:, 0:640], in0=AE[:, 0:640], scalar1=30000.0, scalar2=-30000.0,
                                                    op0=ALU.mult, op1=ALU.add), r=AEK, w=AEK)
            for h in range(8):
                pg.dma('sp', AE[:, 1024:1664], D['relb'][l, :, h, :], w=['AErel'])
                pg.op('dve', lambda e, h=h: e.tensor_tensor(out=expB[:, h, :], in0=AE[:, 1024:1664],
                                                            in1=AE[:, 0:640], op=ALU.add),
                      r=['AErel'] + AEK, w=['expB'])
            barrier(AEK)

        def norm_tiles(src_fn, NT, np_, gain_fn, rk_fn=None):
            rawflat = raw[:, :, :, :].rearrange("p a b c -> p (a b c)")
            sqj = sq.bitcast(BF16)
            stg = [(xt[0:np_, 0, :], 'xt0'), (rawflat[0:np_, 0:DM], 'raw')]

            def ld(t):
                xa, xk = stg[t % 2]
                pg.dma('sp', xa, src_fn(t), r=(rk_fn(t) if rk_fn else []), w=[xk])
            ld(0)
            for t in range(NT):
                xa, xk = stg[t % 2]
                if t + 1 < NT:
                    ld(t + 1)
                ssv = st8[0:np_, 1, t % 2:t % 2 + 1]
                pg.op('act', lambda e, xa=xa, ssv=ssv: e.activation(out=sqj[0:np_, :], in_=xa, func=AF.Square, accum_out=ssv),
                      r=[xk], w=['sq', 'st8'])
                rstd_from_ss(ssv, ssv, DM, np_, 'st8')
                pg.op('dve', lambda e, xa=xa, ssv=ssv: e.scalar_tensor_tensor(out=xnb[0:np_, :], in0=xa, scalar=ssv,
                                                                  in1=gain_fn(np_), op0=ALU.mult, op1=ALU.mult),
                      r=[xk, 'st8', 'small', 'f2'], w=['xnb'])
                transposes(lambda j: xnb[0:np_, j * 128:(j + 1) * 128], 8, np_,
                           lambda e, v, t=t: cp(e, out=xnT[:, :, t * np_:(t + 1) * np_], in_=v),
                           ['xnb'], ['xnT'], evac='act' if t % 2 else 'dve')

        def attend(qT_ap, NQc, qtiles, kblocks, qkeys, out_fn, acc=None):
            if acc is None:
                io = rot('O'); po = psO[io]; pok = 'psO%d' % io
                pov = po[:, 0:260].rearrange("p (j c) -> p j c", c=65)
                jbase, bank_first = 0, True
            else:
                pov, pok, jbase, bank_first = acc
            lastkb, firstkb = {}, {}
            for bi, kb in enumerate(kblocks):
                for j, (c0, nq) in enumerate(qtiles):
                    if c0 >= kb['q0']:
                        lastkb[j] = bi
                        firstkb.setdefault(j, bi)
            st = {}

            def stage_s(bi):
                kb = kblocks[bi]
                isx = rot('S'); psx = psS[isx]; psk = 'psS%d' % isx
                badd = kb.get('badd')
                pg.op('pe', lambda e: e.matmul(psx[0:kb['nk'], kb['q0']:NQc], lhsT=kb['kT'],
                                               rhs=qT_ap[:, kb['q0']:NQc], start=True, stop=(badd is None)),
                      r=list(qkeys) + list(kb['keys']), w=[psk], inc=(badd is None))
                if badd is not None:
                    pg.op('pe', lambda e: e.matmul(psx[0:kb['nk'], kb['q0']:NQc], lhsT=ident_b[:, 0:kb['nk']],
                                                   rhs=badd, start=False, stop=True),
                          r=['cstb'] + list(kb.get('mkeys', [])), w=[psk])
                st[bi] = (psx, psk)

            def stage_e(bi):
                kb = kblocks[bi]
                psx, psk = st[bi]
                ip = rot('P'); pt = PT[ip]; ptk = 'PT%d' % ip
                if kb.get('bias') is not None:
                    pg.op('act', lambda e: e.activation(out=pt[0:kb['nk'], kb['q0']:NQc], in_=psx[0:kb['nk'], kb['q0']:NQc],
                                                        func=AF.Exp, bias=kb['bias'], scale=1.0),
                          r=[psk] + list(kb.get('bkeys', [])), w=[ptk])
                else:
                    pg.op('act', lambda e: e.activation(out=pt[0:kb['nk'], kb['q0']:NQc], in_=psx[0:kb['nk'], kb['q0']:NQc],
                                                        func=AF.Exp), r=[psk], w=[ptk])
                if kb.get('diagmask'):
                    pg.op('dve', lambda e: e.tensor_tensor(
                        out=pt[0:kb['nk'], kb['q0']:kb['q0'] + kb['nk']], in0=pt[0:kb['nk'], kb['q0']:kb['q0'] + kb['nk']],
                        in1=M1b[0:kb['nk'], 0:kb['nk']], op=ALU.mult), r=[ptk, 'cstb'], w=[ptk])
                st[bi] = (pt, ptk)

            def stage_v(bi):
                kb = kblocks[bi]
                pt, ptk = st[bi]
                for j, (c0, nq) in enumerate(qtiles):
                    if c0 < kb['q0']:
                        continue
                    pg.op('pe', lambda e, j=j, c0=c0, nq=nq: e.matmul(
                        pov[0:nq, jbase + j, :], lhsT=pt[0:kb['nk'], c0:c0 + nq], rhs=kb['v'],
                        start=(bank_first and bi == 0 and j == min(firstkb)), stop=(bi == lastkb[j]), skip_group_check=True),
                        r=[ptk] + list(kb['keys']), w=[pok], inc=(c0 + nq >= NQc))

            n = len(kblocks)
            stage_s(0)
            for bi in range(n):
                if bi + 1 < n:
                    stage_s(bi + 1)
                stage_e(bi)
                stage_v(bi)
            if out_fn is not None:
                out_fn(pov, pok)
            return pov, pok

        def attn_out(h0col, NT, np_, szkey='sz'):
            def fn(pov, pok):
                rl = st8[0:np_, 2, 0:NT]
                pg.op('dve', lambda e: e.reciprocal(out=rl, in_=pov[0:np_, 0:NT, 64]), r=[pok], w=['st8'])
                tmp = f2[0:np_, 0:NT * 64].rearrange("p (j d) -> p j d", d=64)
                pg.op('dve', lambda e: e.tensor_tensor(out=tmp, in0=pov[0:np_, 0:NT, 0:64],
                                                       in1=rl.unsqueeze(2).to_broadcast([np_, NT, 64]), op=ALU.mult),
                      r=[pok, 'st8'], w=['f2'])
                pg.op('pool', lambda e: e.tensor_tensor(out=oz[0:np_, 0:NT, h0col:h0col + 64], in0=tmp,
                                                        in1=sz[0:np_, 0:NT, h0col:h0col + 64], op=ALU.mult),
                      r=['f2', szkey], w=['oz'])
            return fn

        def oz_to_T(br, NT, np_, nblk):
            for t in range(NT):
                transposes(lambda j, t=t: oz[0:np_, t, j * 128:(j + 1) * 128], nblk, np_,
                           lambda e, v, t=t: cp(e, out=ozT[br][:, 0:nblk, t * np_:(t + 1) * np_], in_=v),
                           ['oz'], ['ozT_' + br], evac='act' if t % 2 else 'dve')

        def evac_q(psap, pk, t, np_, nh, gname):
            gj = {'a_qnorm': 0, 'c_qnorm': 1, 'm_qnorm': 2}[gname]
            head_norm(psap, pk, nh, None, b1[0:np_, 0:nh * 64], ['b1'], np_)
            transposes(lambda j: b1[0:np_, j * 128:(j + 1) * 128], nh // 2, np_,
                       lambda e, v: e.activation(out=qT[:, 0:nh // 2, t * np_:(t + 1) * np_], in_=v, func=AF.Copy,
                                                 scale=gcol[:, gj:gj + 1]),
                       ['b1', 'gcol'], ['qT'], evac='act')

        def evac_k(psap, pk, np_, nh, gname, out_dram, kT_dst_fn, kT_key):
            head_norm(psap, pk, nh, smv(gname, np_), f3[0:np_, 0:nh * 64], ['f3'], np_)
            if out_dram is not None:
                out_tokens.append(pg.dma('sp', out_dram, f3[0:np_, 0:nh * 64], r=['f3']))
            pg.op('dve', lambda e: cp(e, out=b1[0:np_, 0:nh * 64], in_=f3[0:np_, 0:nh * 64]), r=['f3'], w=['b1'])
            transposes(lambda j: b1[0:np_, j * 128:(j + 1) * 128], nh // 2, np_,
                       lambda e, v: cp(e, out=kT_dst_fn(), in_=v), ['b1'], [kT_key], evac='act')

        def evac_v(psap, pk, np_, nh, out_dram, va_dst, va_key):
            if out_dram is not None:
                pg.op('act', lambda e: e.activation(out=f3[0:np_, 0:nh * 64], in_=psap, func=AF.Copy), r=[pk], w=['f3'])
                out_tokens.append(pg.dma('sp', out_dram, f3[0:np_, 0:nh * 64], r=['f3']))
            pg.op('dve', lambda e: cp(e, out=va_dst, in_=psap.rearrange("p (h d) -> p h d", d=64)),
                  r=[pk], w=[va_key])

        def softplus_from(psap, pk, bias_ap, out_ap, okey, np_, neg=False):
            tmp = st8[0:np_, 3, :]
            pg.op('dve', lambda e: e.tensor_tensor(out=tmp, in0=psap, in1=bias_ap, op=ALU.add), r=[pk, 'small'], w=['st8'])
            pg.op('act', lambda e: e.activation(out=tmp, in_=tmp, func=AF.Exp, scale=(-1.0 if neg else 1.0)),
                  r=['st8'], w=['st8'])
            pg.op('dve', lambda e: e.tensor_scalar(out=tmp, in0=tmp, scalar1=1.0, scalar2=None, op0=ALU.add),
                  r=['st8'], w=['st8'])
            pg.op('act', lambda e: e.activation(out=tmp, in_=tmp, func=AF.Ln), r=['st8'], w=['st8'])
            pg.op('dve', lambda e: e.tensor_scalar(out=out_ap, in0=tmp, scalar1=(-1.0 if neg else 1.0), scalar2=None,
                                                    op0=ALU.mult), r=['st8'], w=[okey])

        def proj8_all(NT, np_, wt, wk):
            i = rot('A'); p = psA[i]; pk = 'psA%d' % i
            for t in range(NT):
                for kc in range(8):
                    pg.op('pe', lambda e, kc=kc, t=t: e.matmul(p[0:np_, t * 8:(t + 1) * 8], lhsT=xnT[:, kc, t * np_:(t + 1) * np_],
                                                              rhs=wt[:, kc, 0:8], start=(kc == 0), stop=(kc == 7)),
                          r=['xnT'] + list(wk), w=[pk], inc=(kc == 7 and t == NT - 1))
            return p[0:np_, 0:NT * 8].rearrange("p (t h) -> p t h", h=8), pk

        def softplus4(psv, pk, bias_ap, out_ap, okey, np_, NT, neg=False):
            tmp = st8[0:np_, 0:NT, :]
            pg.op('dve', lambda e: e.tensor_tensor(out=tmp, in0=psv, in1=bias_ap.unsqueeze(1).to_broadcast([np_, NT, 8]),
                                                   op=ALU.add), r=[pk, 'small'], w=['st8'])
            pg.op('act', lambda e: e.activation(out=tmp, in_=tmp, func=AF.Exp, scale=(-1.0 if neg else 1.0)),
                  r=['st8'], w=['st8'])
            pg.op('dve', lambda e: e.tensor_scalar(out=tmp, in0=tmp, scalar1=1.0, scalar2=None, op0=ALU.add),
                  r=['st8'], w=['st8'])
            pg.op('act', lambda e: e.activation(out=tmp, in_=tmp, func=AF.Ln), r=['st8'], w=['st8'])
            pg.op('dve', lambda e: e.tensor_scalar(out=out_ap, in0=tmp, scalar1=(-1.0 if neg else 1.0), scalar2=None,
                                                    op0=ALU.mult), r=['st8'], w=[okey])

        def cumsum_tile(lf_ap, lfkey, np_, dst_ap, dkey, carry_ap, ckeys):
            i = rot('S'); p = psS[i]; pk = 'psS%d' % i
            pg.op('pe', lambda e: e.matmul(p[0:np_, 0:8], lhsT=M1f[0:np_, 0:np_], rhs=lf_ap, start=True,
                                           stop=(carry_ap is None)), r=[lfkey, 'cst'], w=[pk])
            if carry_ap is not None:
                pg.op('pe', lambda e: e.matmul(p[0:np_, 0:8], lhsT=carry_ap[0], rhs=carry_ap[1], start=False, stop=True),
                      r=list(ckeys) + ['cst'], w=[pk])
            pg.op('dve', lambda e: cp(e, out=dst_ap, in_=p[0:np_, 0:8]), r=[pk], w=[dkey])

        def ssd_tile(t, np_, NT):
            c0 = t * np_
            if t == 0:
                barrier(AEK)
            def ev(e, v):
                return cp(e, out=xtk[0:np_, :].rearrange("p (b n) -> p b n", n=128), in_=v[0:np_, 0:4, :])
            i = rot('T'); pt = psT[i]; pk = 'psT%d' % i
            for j in range(5):
                pg.op('pe', lambda e, j=j: e.transpose(out=pt[0:np_, j * 128:(j + 1) * 128], in_=xbc[:, j, c0:c0 + np_],
                                                       identity=ident_b[:, :]), r=['xbc', 'cstb'], w=[pk], inc=(j == 4))
            pg.op('act', lambda e: e.activation(out=xtk[0:np_, :], in_=pt[0:np_, 0:512], func=AF.Copy), r=[pk], w=['xtk'])
            pg.op('act', lambda e: e.activation(out=btk[0:np_, :], in_=pt[0:np_, 512:640], func=AF.Copy), r=[pk], w=['btk'])
            pg.op('dve', lambda e: e.tensor_tensor(out=xdt[0:np_, :, :], in0=pt[0:np_, 0:512].rearrange("p (h d) -> p h d", d=64),
                                                   in1=dts[0:np_, t, :].unsqueeze(2).to_broadcast([np_, 8, 64]), op=ALU.mult),
                  r=[pk, 'dts'], w=['xdt'])
            a_ap = asb[0:np_, t, :]
            i = rot('A'); p = psA[i]; pk2 = 'psA%d' % i
            pg.op('pe', lambda e: e.matmul(p[0:np_, 0:8], lhsT=M1f[0:np_, 0:np_], rhs=a_ap, start=True, stop=True),
                  r=['asb', 'cst'], w=[pk2], inc=False)
            pg.op('pe', lambda e: e.matmul(p[0:np_, 8:16], lhsT=M2f[0:np_, 0:np_], rhs=a_ap, start=True, stop=True),
                  r=['asb', 'cst'], w=[pk2], inc=False)
            pg.op('pe', lambda e: e.matmul(p[:, 16:24], lhsT=M1f[0:np_, :], rhs=a_ap, start=True, stop=False),
                  r=['asb', 'cst'], w=[pk2], inc=False)
            pg.op('pe', lambda e: e.matmul(p[:, 16:24], lhsT=M2f[0:np_, :], rhs=a_ap, start=False, stop=True),
                  r=['asb', 'cst'], w=[pk2])
            pg.op('act', lambda e: e.activation(out=ssd8[0:np_, 0, :], in_=p[0:np_, 0:8], func=AF.Exp), r=[pk2], w=['ssd8'])
            pg.op('act', lambda e: e.activation(out=ssd8[0:np_, 1, :], in_=p[0:np_, 8:16], func=AF.Exp), r=[pk2], w=['ssd8'])
            pg.op('act', lambda e: e.activation(out=ssd8[:, 2, :], in_=p[:, 16:24], func=AF.Exp), r=[pk2], w=['ssd8'])
            for g in range(2):
                i = rot('A'); pgm = psA[i]; pk3 = 'psA%d' % i
                pg.op('pe', lambda e, g=g, pgm=pgm: e.matmul(pgm[0:np_, 0:np_], lhsT=xbc[g * 64:(g + 1) * 64, 4, c0:c0 + np_],
                                                    rhs=xbc[g * 64:(g + 1) * 64, 5, c0:c0 + np_], start=True, stop=True),
                      r=['xbc'], w=[pk3])
                pg.op('dve', lambda e, g=g, pgm=pgm: e.tensor_tensor(
                    out=Gm[0:np_, g, 0:np_], in0=pgm[0:np_, 0:np_], in1=M1f[0:np_, 0:np_], op=ALU.mult),
                    r=[pk3, 'cst'], w=['Gm'])
            for hh in range(2):
                for h4 in range(4):
                    h = hh * 4 + h4
                    pg.op('dve', lambda e, h=h, h4=h4: e.tensor_scalar(
                        out=Ah[0:np_, h4, 0:np_], in0=M2f[0:np_, 0:np_], scalar1=asb[0:np_, t, h:h + 1], scalar2=None,
                        op0=ALU.mult), r=['cst', 'asb'], w=['Ah%d' % h4])
                psx = psS[hh]; psk = 'psS%d' % hh
                for h4 in range(4):
                    pg.op('pe', lambda e, h4=h4, psx=psx: e.matmul(
                        psx[0:np_, h4 * 128:h4 * 128 + np_], lhsT=Ah[0:np_, h4, 0:np_], rhs=M1f[0:np_, 0:np_],
                        start=True, stop=True), r=['Ah%d' % h4, 'cst'], w=[psk], inc=(h4 == 3))
                pg.op('act', lambda e, psx=psx: e.activation(
                    out=Ee[0:np_, :, 0:np_],
                    in_=psx[0:np_, :].rearrange("p (h n) -> p h n", n=128)[:, :, 0:np_], func=AF.Exp),
                    r=[psk], w=['Ee'])
                pg.op('dve', lambda e, hh=hh: e.tensor_tensor(
                    out=Mh[0:np_, hh * 4:(hh + 1) * 4, 0:np_], in0=Ee[0:np_, :, 0:np_],
                    in1=Gm[0:np_, hh, 0:np_].unsqueeze(1).to_broadcast([np_, 4, np_]), op=ALU.mult),
                    r=['Ee', 'Gm'], w=['Mh'])
            io = rot('O'); py = psO[io]; pyk = 'psO%d' % io
            io2 = rot('O'); py2 = psO[io2]; py2k = 'psO%d' % io2
            for h in range(8):
                pg.op('pe', lambda e, h=h: e.matmul(py[0:np_, h * 64:(h + 1) * 64], lhsT=Mh[0:np_, h, 0:np_],
                                                    rhs=xdt[0:np_, h, :], start=True, stop=True),
                      r=['Mh', 'xdt'], w=[pyk], inc=(h == 7))
            py3 = psS[1]; py3k = 'psS1'
            for h in range(8):
                g = h // 4
                dstp = (py2 if g == 0 else py3)
                pg.op('pe', lambda e, h=h, g=g, dstp=dstp: e.matmul(dstp[0:np_, (h % 4) * 64:(h % 4 + 1) * 64],
                                                         lhsT=xbc[g * 64:(g + 1) * 64, 5, c0:c0 + np_],
                                                         rhs=hTb[g * 64:(g + 1) * 64, h % 4, :], start=True, stop=True),
                      r=['xbc', 'hTb'], w=[py2k if g == 0 else py3k], inc=(h % 4 == 3))
            yv = f1[0:np_, :].rearrange("p (h d) -> p h d", d=64)
            for g, (dstp, dk) in enumerate(((py2, py2k), (py3, py3k))):
                pg.op('dve', lambda e, g=g, dstp=dstp: e.tensor_tensor(
                    out=yv[:, g * 4:(g + 1) * 4, :], in0=dstp[0:np_, 0:256].rearrange("p (h d) -> p h d", d=64),
                    in1=ssd8[0:np_, 0, g * 4:(g + 1) * 4].unsqueeze(2).to_broadcast([np_, 4, 64]), op=ALU.mult),
                    r=[dk, 'ssd8'], w=['f1'])
            pg.op('dve', lambda e: e.tensor_tensor(out=f1[0:np_, :], in0=f1[0:np_, :], in1=py[0:np_, :], op=ALU.add),
                  r=['f1', pyk], w=['f1'])
            pg.op('pool', lambda e: e.tensor_tensor(out=f2[0:np_, :].rearrange("p (h d) -> p h d", d=64),
                                                    in0=xtk[0:np_, :].rearrange("p (h d) -> p h d", d=64),
                                                    in1=smv('b_d', np_).unsqueeze(2).to_broadcast([np_, 8, 64]), op=ALU.mult),
                  r=['xtk', 'small'], w=['f2'])
            pg.op('dve', lambda e: e.tensor_tensor(out=f1[0:np_, :], in0=f1[0:np_, :], in1=f2[0:np_, :], op=ALU.add),
                  r=['f1', 'f2'], w=['f1'])
            pg.op('dve', lambda e: e.tensor_tensor(out=f1[0:np_, :], in0=f1[0:np_, :], in1=sz[0:np_, t, :], op=ALU.mult),
                  r=['f1', 'szb'], w=['f1'])
            ssv = st8[0:np_, 4, 0:1]
            pg.op('act', lambda e: e.activation(out=sq[0:np_, 0:512], in_=f1[0:np_, :], func=AF.Square, accum_out=ssv),
                  r=['f1'], w=['sq', 'st8'])
            rstd_from_ss(ssv, ssv, 512, np_, 'st8')
            pg.op('dve', lambda e: e.scalar_tensor_tensor(out=oz[0:np_, t, :], in0=f1[0:np_, :], scalar=ssv,
                                                          in1=smv('b_norm', np_), op0=ALU.mult, op1=ALU.mult),
                  r=['f1', 'st8', 'small'], w=['oz'])
            pg.op('dve', lambda e: e.tensor_tensor(
                out=bdec[0:np_, :, :, :].rearrange("p a g n -> p g a n"),
                in0=btk[0:np_, :].rearrange("p (g n) -> p g n", n=64).unsqueeze(2).to_broadcast([np_, 2, 4, 64]),
                in1=ssd8[0:np_, 1, :].rearrange("p (g a) -> p g a", a=4).unsqueeze(3).to_broadcast([np_, 2, 4, 64]),
                op=ALU.mult), r=['btk', 'ssd8'], w=['bdec'])
            i = rot('A'); ph = psA[i]; phk = 'psA%d' % i
            for a4 in range(4):
                pg.op('pe', lambda e, a4=a4: e.matmul(
                    ph[:, a4 * 128:(a4 + 1) * 128], lhsT=bdec[0:np_, a4, :, :].rearrange("p g n -> p (g n)"),
                    rhs=xdt[0:np_, :, :].rearrange("p (g a) d -> p a g d", a=4)[:, a4, :, :],
                    start=True, stop=True), r=['bdec', 'xdt'], w=[phk], inc=(a4 == 3))
            phv = ph[:, :].rearrange("p (a c) -> p a c", c=128)
            for g in range(2):
                sl = slice(g * 64, (g + 1) * 64)
                pg.op('dve', lambda e, g=g, sl=sl: e.tensor_tensor(
                    out=hT[sl, :, :], in0=hT[sl, :, :],
                    in1=ssd8[sl, 2, g * 4:(g + 1) * 4].unsqueeze(2).to_broadcast([64, 4, 64]), op=ALU.mult),
                    r=['hT', 'ssd8', py2k, py3k], w=['hT'])
                pg.op('dve', lambda e, g=g, sl=sl: e.tensor_tensor(
                    out=hT[sl, :, :], in0=hT[sl, :, :], in1=phv[sl, :, g * 64:(g + 1) * 64], op=ALU.add),
                    r=['hT', phk], w=['hT'])
            pg.op('act', lambda e: e.activation(out=hTb[:, :, :], in_=hT[:, :, :], func=AF.Copy), r=['hT'], w=['hTb'])

        BS = [dict(Ah=Ah, Ee=Ee, Gm=Gm, bdec=bdec, xdt=xdt, xtk=xtk, Mh=Mh, btk=btk,
                   e0=ssd8[:, 0, :], e1=ssd8[:, 1, :], e2=ssd8[:, 2, :], sfx=''),
              dict(Ah=xnb.bitcast(F32).rearrange("p (h n) -> p h n", n=128),
                   Ee=f3.rearrange("p (h n) -> p h n", n=128),
                   Gm=biasQ[:, :, :].rearrange("p a b -> p (a b)").rearrange("p (g n) -> p g n", n=128),
                   Mh=qT[:, 0:2, :].rearrange("p a n -> p (a n)").rearrange("p (h n) -> p h n", n=128),
                   bdec=qT[:, 2, :].rearrange("p (a g n) -> p a g n", a=4, g=2),
                   xdt=qT[:, 3, :].rearrange("p (h d) -> p h d", d=64),
                   xtk=PT[0][:, :], btk=PT[1][:, 0:128],
                   e0=st8[:, 5, :], e1=st8[:, 6, :], e2=st8[:, 7, :], sfx='_1')]
        SET1K = ['Ah%d_1' % h for h in range(4)] + ['Ee_1', 'Gm_1', 'bdec_1', 'xdt_1', 'xtk_1', 'Mh_1', 'btk_1', 'ssd8_1',
                                                     'qT', 'PT0', 'PT1', 'biasQ', 'xnb', 'f3', 'st8lf', 'st8c', 'st8x']

        def ssdA(t, np_, S):
            b = BS[S]; x_ = b['sfx']
            K = lambda n: n + x_
            c0 = t * np_
            i = rot('T'); pt = psT[i]; pk = 'psT%d' % i
            for j in range(5):
                pg.op('pe', lambda e, j=j: e.transpose(out=pt[0:np_, j * 128:(j + 1) * 128], in_=xbc[:, j, c0:c0 + np_],
                                                       identity=ident_b[:, :]), r=['xbc', 'cstb'], w=[pk], inc=(j == 4))
            yield
            pg.op('act', lambda e: e.activation(out=b['xtk'][0:np_, :], in_=pt[0:np_, 0:512], func=AF.Copy), r=[pk], w=[K('xtk')])
            pg.op('act', lambda e: e.activation(out=b['btk'][0:np_, :], in_=pt[0:np_, 512:640], func=AF.Copy), r=[pk], w=[K('btk')])
            yield
            pg.op('dve', lambda e: e.tensor_tensor(out=b['xdt'][0:np_, :, :], in0=pt[0:np_, 0:512].rearrange("p (h d) -> p h d", d=64),
                                                   in1=dts[0:np_, t, :].unsqueeze(2).to_broadcast([np_, 8, 64]), op=ALU.mult),
                  r=[pk, 'dts'], w=[K('xdt')])
            yield
            a_ap = asb[0:np_, t, :]
            i = rot('A'); p = psA[i]; pk2 = 'psA%d' % i
            pg.op('pe', lambda e: e.matmul(p[0:np_, 0:8], lhsT=M1f[0:np_, 0:np_], rhs=a_ap, start=True, stop=True),
                  r=['asb', 'cst'], w=[pk2], inc=False)
            pg.op('pe', lambda e: e.matmul(p[0:np_, 8:16], lhsT=M2f[0:np_, 0:np_], rhs=a_ap, start=True, stop=True),
                  r=['asb', 'cst'], w=[pk2], inc=False)
            pg.op('pe', lambda e: e.matmul(p[:, 16:24], lhsT=M1f[0:np_, :], rhs=a_ap, start=True, stop=False),
                  r=['asb', 'cst'], w=[pk2], inc=False)
            pg.op('pe', lambda e: e.matmul(p[:, 16:24], lhsT=M2f[0:np_, :], rhs=a_ap, start=False, stop=True),
                  r=['asb', 'cst'], w=[pk2])
            yield
            pg.op('act', lambda e: e.activation(out=b['e0'][0:np_, :], in_=p[0:np_, 0:8], func=AF.Exp), r=[pk2], w=[K('ssd8')])
            pg.op('act', lambda e: e.activation(out=b['e1'][0:np_, :], in_=p[0:np_, 8:16], func=AF.Exp), r=[pk2], w=[K('ssd8')])
            pg.op('act', lambda e: e.activation(out=b['e2'][:, :], in_=p[:, 16:24], func=AF.Exp), r=[pk2], w=[K('ssd8')])
            yield
            for g in range(2):
                i = rot('A'); pgm = psA[i]; pk3 = 'psA%d' % i
                pg.op('pe', lambda e, g=g, pgm=pgm: e.matmul(pgm[0:np_, 0:np_], lhsT=xbc[g * 64:(g + 1) * 64, 4, c0:c0 + np_],
                                                    rhs=xbc[g * 64:(g + 1) * 64, 5, c0:c0 + np_], start=True, stop=True),
                      r=['xbc'], w=[pk3])
                pg.op('dve', lambda e, g=g, pgm=pgm: e.tensor_tensor(
                    out=b['Gm'][0:np_, g, 0:np_], in0=pgm[0:np_, 0:np_], in1=M1f[0:np_, 0:np_], op=ALU.mult),
                    r=[pk3, 'cst'], w=[K('Gm')])
                yield
            for hh in range(2):
                for h4 in range(4):
                    h = hh * 4 + h4
                    pg.op('dve', lambda e, h=h, h4=h4: e.tensor_scalar(
                        out=b['Ah'][0:np_, h4, 0:np_], in0=M2f[0:np_, 0:np_], scalar1=asb[0:np_, t, h:h + 1], scalar2=None,
                        op0=ALU.mult), r=['cst', 'asb'], w=[K('Ah%d' % h4)])
                    if h4 % 2:
                        yield
                psx = psS[hh]; psk = 'psS%d' % hh
                for h4 in range(4):
                    pg.op('pe', lambda e, h4=h4, psx=psx: e.matmul(
                        psx[0:np_, h4 * 128:h4 * 128 + np_], lhsT=b['Ah'][0:np_, h4, 0:np_], rhs=M1f[0:np_, 0:np_],
                        start=True, stop=True), r=[K('Ah%d' % h4), 'cst'], w=[psk], inc=(h4 == 3))
                yield
                pg.op('act', lambda e, psx=psx: e.activation(
                    out=b['Ee'][0:np_, :, 0:np_],
                    in_=psx[0:np_, :].rearrange("p (h n) -> p h n", n=128)[:, :, 0:np_], func=AF.Exp),
                    r=[psk], w=[K('Ee')])
                yield
                pg.op('dve', lambda e, hh=hh: e.tensor_tensor(
                    out=b['Mh'][0:np_, hh * 4:(hh + 1) * 4, 0:np_], in0=b['Ee'][0:np_, :, 0:np_],
                    in1=b['Gm'][0:np_, hh, 0:np_].unsqueeze(1).to_broadcast([np_, 4, np_]), op=ALU.mult),
                    r=[K('Ee'), K('Gm')], w=[K('Mh')])
                yield
            pg.op('dve', lambda e: e.tensor_tensor(
                out=b['bdec'][0:np_, :, :, :].rearrange("p a g n -> p g a n"),
                in0=b['btk'][0:np_, :].rearrange("p (g n) -> p g n", n=64).unsqueeze(2).to_broadcast([np_, 2, 4, 64]),
                in1=b['e1'][0:np_, :].rearrange("p (g a) -> p g a", a=4).unsqueeze(3).to_broadcast([np_, 2, 4, 64]),
                op=ALU.mult), r=[K('btk'), K('ssd8')], w=[K('bdec')])
            yield

        def ssdB(t, np_, S):
            b = BS[S]; x_ = b['sfx']
            K = lambda n: n + x_
            c0 = t * np_
            io = rot('O'); py = psO[io]; pyk = 'psO%d' % io
            io2 = rot('O'); py2 = psO[io2]; py2k = 'psO%d' % io2
            for h in range(8):
                pg.op('pe', lambda e, h=h: e.matmul(py[0:np_, h * 64:(h + 1) * 64], lhsT=b['Mh'][0:np_, h, 0:np_],
                                                    rhs=b['xdt'][0:np_, h, :], start=True, stop=True),
                      r=[K('Mh'), K('xdt')], w=[pyk], inc=(h == 7))
            yield
            i3 = rot('A'); py3 = psA[i3]; py3k = 'psA%d' % i3
            for h in range(8):
                g = h // 4
                dstp = (py2 if g == 0 else py3)
                pg.op('pe', lambda e, h=h, g=g, dstp=dstp: e.matmul(dstp[0:np_, (h % 4) * 64:(h % 4 + 1) * 64],
                                                         lhsT=xbc[g * 64:(g + 1) * 64, 5, c0:c0 + np_],
                                                         rhs=hTb[g * 64:(g + 1) * 64, h % 4, :], start=True, stop=True),
                      r=['xbc', 'hTb'], w=[py2k if g == 0 else py3k], inc=(h % 4 == 3))
            yield
            yv = f1[0:np_, :].rearrange("p (h d) -> p h d", d=64)
            for g, (dstp, dk) in enumerate(((py2, py2k), (py3, py3k))):
                pg.op('dve', lambda e, g=g, dstp=dstp: e.tensor_tensor(
                    out=yv[:, g * 4:(g + 1) * 4, :], in0=dstp[0:np_, 0:256].rearrange("p (h d) -> p h d", d=64),
                    in1=b['e0'][0:np_, g * 4:(g + 1) * 4].unsqueeze(2).to_broadcast([np_, 4, 64]), op=ALU.mult),
                    r=[dk, K('ssd8')], w=['f1'])
                yield
            pg.op('dve', lambda e: e.tensor_tensor(out=f1[0:np_, :], in0=f1[0:np_, :], in1=py[0:np_, :], op=ALU.add),
                  r=['f1', pyk], w=['f1'])
            pg.op('pool', lambda e: e.tensor_tensor(out=f2[0:np_, :].rearrange("p (h d) -> p h d", d=64),
                                                    in0=b['xtk'][0:np_, :].rearrange("p (h d) -> p h d", d=64),
                                                    in1=smv('b_d', np_).unsqueeze(2).to_broadcast([np_, 8, 64]), op=ALU.mult),
                  r=[K('xtk'), 'small'], w=['f2'])
            yield
            pg.op('dve', lambda e: e.tensor_tensor(out=f1[0:np_, :], in0=f1[0:np_, :], in1=f2[0:np_, :], op=ALU.add),
                  r=['f1', 'f2'], w=['f1'])
            yield
            pg.op('dve', lambda e: e.tensor_tensor(out=f1[0:np_, :], in0=f1[0:np_, :], in1=sz[0:np_, t, :], op=ALU.mult),
                  r=['f1', 'szb'], w=['f1'])
            yield
            ssv = st8[0:np_, 4, 0:1]
            pg.op('act', lambda e: e.activation(out=sq[0:np_, 0:512], in_=f1[0:np_, :], func=AF.Square, accum_out=ssv),
                  r=['f1'], w=['sq', 'st8'])
            yield
            pg.op('dve', lambda e: e.tensor_scalar(out=ssv, in0=ssv, scalar1=1.0 / 512, scalar2=EPS,
                                                    op0=ALU.mult, op1=ALU.add), r=['st8'], w=['st8'])
            yield
            pg.op('pool', lambda e: e.tensor_tensor(out=ssv, in0=ssv, in1=nhalf[0:np_, 0:1], op=ALU.pow),
                  r=['st8', 'epst'], w=['st8'])
            yield
            pg.op('dve', lambda e: e.scalar_tensor_tensor(out=oz[0:np_, t, :], in0=f1[0:np_, :], scalar=ssv,
                                                          in1=smv('b_norm', np_), op0=ALU.mult, op1=ALU.mult),
                  r=['f1', 'st8', 'small'], w=['oz'])
            yield
            i = rot('A'); ph = psA[i]; phk = 'psA%d' % i
            for a4 in range(4):
                pg.op('pe', lambda e, a4=a4: e.matmul(
                    ph[:, a4 * 128:(a4 + 1) * 128], lhsT=b['bdec'][0:np_, a4, :, :].rearrange("p g n -> p (g n)"),
                    rhs=b['xdt'][0:np_, :, :].rearrange("p (g a) d -> p a g d", a=4)[:, a4, :, :],
                    start=True, stop=True), r=[K('bdec'), K('xdt')], w=[phk], inc=(a4 == 3))
            yield
            phv = ph[:, :].rearrange("p (a c) -> p a c", c=128)
            for g in range(2):
                sl = slice(g * 64, (g + 1) * 64)
                pg.op('dve', lambda e, g=g, sl=sl: e.tensor_tensor(
                    out=hT[sl, :, :], in0=hT[sl, :, :],
                    in1=b['e2'][sl, g * 4:(g + 1) * 4].unsqueeze(2).to_broadcast([64, 4, 64]), op=ALU.mult),
                    r=['hT', K('ssd8'), py2k, py3k], w=['hT'])
                yield
                pg.op('dve', lambda e, g=g, sl=sl: e.tensor_tensor(
                    out=hT[sl, :, :], in0=hT[sl, :, :], in1=phv[sl, :, g * 64:(g + 1) * 64], op=ALU.add),
                    r=['hT', phk], w=['hT'])
                yield
            pg.op('act', lambda e: e.activation(out=hTb[:, :, :], in_=hT[:, :, :], func=AF.Copy), r=['hT'], w=['hTb'])
            yield

        def ssd_pipelined(NT, np_):
            barrier(AEK + SET1K)
            for _ in ssdA(0, np_, 0):
                pass
            for t in range(NT):
                gens = [ssdB(t, np_, t % 2)]
                if t + 1 < NT:
                    gens.append(ssdA(t + 1, np_, (t + 1) % 2))
                while gens:
                    for g_ in list(gens):
                        try:
                            next(g_)
                        except StopIteration:
                            gens.remove(g_)
            barrier(AEK + SET1K)

        def process_group(l, kind, Q):
            samp = (kind == 's')
            np_ = 64 if samp else 128
            NT = 4
            NTOK = NT * np_
            xsrc = (D['xs'] if samp else D['xp']) if l == 0 else (x1s if samp else x1p)
            xdst = (D['ys'] if samp else D['yp']) if l == NL - 1 else (x1s if samp else x1p)
            row0 = 0 if samp else Q * 512

            def rows(t):
                return slice(row0 + t * np_, row0 + (t + 1) * np_)

            norm_tiles(lambda t: xsrc[rows(t), :], NT, np_, lambda n: smv('g_norm', n),
                       (lambda t: [('xd', kind, row0 + t * np_)]) if l > 0 else None)

            if cfg.get('marks'): print('MARK', kind, Q, 'C_proj', pg.nops)
            wt, wk = load_w(D['w_in'], l, COL['c_q'], 512)
            for t, p, pk in pipe_tiles(NT, lambda t: proj_tm(xnT, 'xnT', t * np_, np_, wt, wk, 512)):
                evac_q(p, pk, t, np_, 8, 'c_qnorm')
            wt, wk = load_w(D['w_in'], l, COL['c_k'], 512)
            if samp:
                kTs = sb_kTs
            for t, p, pk in pipe_tiles(NT, lambda t: proj_tm(xnT, 'xnT', t * np_, np_, wt, wk, 512)):
                if samp:
                    evac_k(p, pk, np_, 8, 'c_knorm', D['sc_k'][l, rows(t), :],
                           lambda t=t: kTs[:, 0:4, t * 64:(t + 1) * 64], 'kTs')
                else:
                    gt = Q * 4 + t
                    evac_k(p, pk, np_, 8, 'c_knorm', D['pc_k'][l, rows(t), :],
                           lambda gt=gt: kT_c[:, :, gt * 128:(gt + 1) * 128], 'kT_c')
            wt, wk = load_w(D['w_in'], l, COL['c_v'], 512)
            for t, p, pk in pipe_tiles(NT, lambda t: proj_tm(xnT, 'xnT', t * np_, np_, wt, wk, 512)):
                if samp:
                    evac_v(p, pk, np_, 8, D['sc_v'][l, rows(t), :], vas[0:np_, t, :, 0:64], 'vas')
                else:
                    evac_v(p, pk, np_, 8, D['pc_v'][l, rows(t), :], va_c[:, Q * 4 + t, :, 0:64], 'va_c')
            wt, wk = load_w(D['w_in'], l, COL['c_f'], 8)
            if not samp:
                psv, pk = proj8_all(NT, np_, wt, wk)
                softplus4(psv, pk, smv('c_fbias', np_), lfs[0:np_, :, :], 'lfs', np_, NT, neg=True)
                out_tokens.append(pg.dma('sp', D['pc_f'][l, row0:row0 + NT * np_, :].rearrange("(t p) h -> p t h", p=np_),
                                         lfs[0:np_, :, :], r=['lfs'], nonc=True))
                for t in range(NT):
                    gt = Q * 4 + t
                    cumsum_tile(lfs[0:np_, t, :], 'lfs', 128, cum[:, gt, :], 'cum',
                                None if gt == 0 else (S127f, cum[:, gt - 1, :]), ['cum'])
            for t, p, pk in (pipe_tiles(NT, lambda t: proj_tm(xnT, 'xnT', t * np_, np_, wt, wk, 8)) if samp else ()):
                lf = st8[0:np_, 5, :]
                softplus_from(p, pk, smv('c_fbias', np_), lf, 'st8lf', np_, neg=True)
                if samp:
                    out_tokens.append(pg.dma('sp', D['sc_f'][l, rows(t), :], lf, r=['st8lf'], nonc=True))
                    stg8 = cum[:, 0:8, :].rearrange("p k h -> p (k h)")
                    totb = cum[:, 8:16, :]
                    car = cum[:, 16:24, :]
                    pg.dma('sp', cum[:, 0:8, :], D['cc_f'][l, t].rearrange("(k p) h -> p k h", p=128), w=['cum'], nonc=True)
                    i1 = rot('S'); q1 = psS[i1]; q1k = 'psS%d' % i1
                    pg.op('pe', lambda e, q1=q1: e.matmul(q1[:, 0:64], lhsT=M1f, rhs=stg8, start=True, stop=True),
                          r=['cum', 'cst'], w=[q1k])
                    loc = cums[:, t, 0:8, :]
                    pg.op('dve', lambda e, q1=q1, loc=loc: cp(e, out=loc, in_=q1[:, 0:64].rearrange("p (k h) -> p k h", h=8)),
                          r=[q1k], w=['cums'])
                    i2 = rot('S'); q2 = psS[i2]; q2k = 'psS%d' % i2
                    pg.op('pe', lambda e, q2=q2, loc=loc: e.matmul(q2[:, 0:64], lhsT=S127f, rhs=loc.rearrange("p k h -> p (k h)"),
                                                                 start=True, stop=True), r=['cums', 'cst'], w=[q2k])
                    pg.op('act', lambda e, q2=q2: e.activation(out=totb, in_=q2[:, 0:64].rearrange("p (k h) -> p k h", h=8),
                                                              func=AF.Copy), r=[q2k], w=['cum'])
                    pg.op('pool', lambda e: e.memset(car[:, 0, :], 0.0), w=['cum'])
                    for kb in range(1, 8):
                        pg.op('dve', lambda e, kb=kb: e.tensor_tensor(out=car[:, kb, :], in0=car[:, kb - 1, :], in1=totb[:, kb - 1, :],
                                                                      op=ALU.add), r=['cum'], w=['cum'])
                    pg.op('dve', lambda e, loc=loc: e.tensor_tensor(out=loc, in0=loc, in1=car, op=ALU.add), r=['cums', 'cum'], w=['cums'])
                    cumsum_tile(lf, 'st8lf', 64, cums[0:64, t, 8, :], 'cums', (S127f[:, 0:64], cums[:, t, 7, :]), ['cums'])
                else:
                    gt = Q * 4 + t
                    out_tokens.append(pg.dma('sp', D['pc_f'][l, rows(t), :], lf, r=['st8lf'], nonc=True))
                    cumsum_tile(lf, 'st8lf', 128, cum[:, gt, :], 'cum',
                                None if gt == 0 else (S127f, cum[:, gt - 1, :]), ['cum'])
            wt, wk = load_w(D['w_in'], l, COL['c_z'], 512)
            for t, p, pk in pipe_tiles(NT, lambda t: proj_tm(xnT, 'xnT', t * np_, np_, wt, wk, 512)):
                pg.op('act', lambda e, p=p, t=t: e.activation(out=sz[0:np_, t, :], in_=p, func=AF.Silu), r=[pk], w=['sz'])
            if cfg.get('marks'): print('MARK', kind, Q, 'C_attn', pg.nops)
            if not samp:
                nkb = 4 * Q + 4
                i = rot('A'); pb = psA[i]; pbk = 'psA%d' % i
                pg.op('pe', lambda e: e.matmul(pb[:, 0:8], lhsT=S127f, rhs=cum[:, nkb - 1, :], start=True, stop=True),
                      r=['cum', 'cst'], w=[pbk])
                pg.op('act', lambda e: e.activation(out=st8[:, 6, :], in_=pb[:, 0:8], func=AF.Copy), r=[pbk], w=['st8c'])
                pg.op('dve', lambda e: e.tensor_tensor(out=biasQ[:, 0:nkb, :],
                                                       in0=st8[:, 6, :].unsqueeze(1).to_broadcast([128, nkb, 8]),
                                                       in1=cum[:, 0:nkb, :], op=ALU.subtract), r=['st8c', 'cum'], w=['biasQ'])
                for h in range(8):
                    hp, hb = h // 2, (h % 2) * 64
                    kbs = []
                    for kb in range(nkb):
                        dI = kb - 4 * Q
                        d = dict(kT=kT_c[hb:hb + 64, hp, kb * 128:(kb + 1) * 128], v=va_c[:, kb, h, 0:65], nk=128,
                                 bias=biasQ[:, kb, h:h + 1], bkeys=['biasQ'], q0=max(dI, 0) * 128, keys=['kT_c', 'va_c'])
                        kbs.append(d)
                    kb2 = []
                    for kb, d in enumerate(kbs):
                        dI = kb - 4 * Q
                        if dI < 0:
                            kb2.append(d)
                        else:
                            d['diag'] = True
                            kb2.append(d)
                    attend_fox(qT[hb:hb + 64, hp, :], 512, [(j * 128, 128) for j in range(4)], kb2, ['qT'],
                               attn_out(h * 64, 4, 128))
            else:
                for t in range(NT):
                    i = rot('A'); pb = psA[i]; pbk = 'psA%d' % i
                    pg.op('pe', lambda e, t=t: e.matmul(pb[:, 0:8], lhsT=S63f[0:64, :], rhs=cums[0:64, t, 8, :],
                                                        start=True, stop=True), r=['cums', 'cst'], w=[pbk])
                    pg.op('act', lambda e: e.activation(out=st8[:, 6, :], in_=pb[:, 0:8], func=AF.Copy), r=[pbk], w=['st8c'])
                    pg.op('dve', lambda e, t=t: e.tensor_tensor(out=biasQ[:, 0:9, :],
                                                                in0=st8[:, 6, :].unsqueeze(1).to_broadcast([128, 9, 8]),
                                                                in1=cums[:, t, :, :], op=ALU.subtract),
                          r=['st8c', 'cums'], w=['biasQ'])
                    load_cache_kv(D['cc_k'][l, t], D['cc_v'][l, t], 8, 8)
                    for h in range(8):
                        hp, hb = h // 2, (h % 2) * 64
                        kbs = []
                        for kb in range(8):
                            kbs.append(dict(kT=ckT[hb:hb + 64, hp, kb * 128:(kb + 1) * 128], v=cva_[:, kb, h, 0:65], nk=128,
                                            bias=biasQ[:, kb, h:h + 1], bkeys=['biasQ'], q0=0, keys=['ckT', 'cvaS']))
                        kbs.append(dict(kT=kTs[hb:hb + 64, hp, t * 64:(t + 1) * 64], v=vas[0:64, t, h, 0:65], nk=64,
                                        bias=biasQ[0:64, 8, h:h + 1], bkeys=['biasQ'], q0=0, keys=['kTs', 'vas'], diag=True))
                        attend_fox(qT[hb:hb + 64, hp, t * 64:(t + 1) * 64], 64, [(0, 64)], kbs, ['qT'],
                                   attn_out_s(h * 64, t))
            oz_to_T('c', NT, np_, 4)

            if cfg.get('marks'): print('MARK', kind, Q, 'A_proj', pg.nops)
            wt, wk = load_w(D['w_in'], l, COL['a_q'], 512)
            for t, p, pk in pipe_tiles(NT, lambda t: proj_tm(xnT, 'xnT', t * np_, np_, wt, wk, 512)):
                evac_q(p, pk, t, np_, 8, 'a_qnorm')
            wt, wk = load_w(D['w_in'], l, COL['a_k'], 512)
            for t, p, pk in pipe_tiles(NT, lambda t: proj_tm(xnT, 'xnT', t * np_, np_, wt, wk, 512)):
                if samp:
                    evac_k(p, pk, np_, 8, 'a_knorm', D['sa_k'][l, rows(t), :],
                           lambda t=t: kTs[:, 0:4, t * 64:(t + 1) * 64], 'kTs')
                else:
                    gt = Q * 4 + t
                    od = D['pa_k'][l, (gt - 28) * 128:(gt - 27) * 128, :] if gt >= 28 else None
                    evac_k(p, pk, np_, 8, 'a_knorm', od,
                           lambda gt=gt: kT_a[:, :, (gt % 8) * 128:(gt % 8 + 1) * 128], 'kT_a')
            wt, wk = load_w(D['w_in'], l, COL['a_v'], 512)
            for t, p, pk in pipe_tiles(NT, lambda t: proj_tm(xnT, 'xnT', t * np_, np_, wt, wk, 512)):
                if samp:
                    evac_v(p, pk, np_, 8, D['sa_v'][l, rows(t), :], vas[0:np_, t, :, 0:64], 'vas')
                else:
                    gt = Q * 4 + t
                    od = D['pa_v'][l, (gt - 28) * 128:(gt - 27) * 128, :] if gt >= 28 else None
                    evac_v(p, pk, np_, 8, od, va_a[:, gt % 8, :, 0:64], 'va_a')
            wt, wk = load_w(D['w_in'], l, COL['a_z'], 512)
            for t, p, pk in pipe_tiles(NT, lambda t: proj_tm(xnT, 'xnT', t * np_, np_, wt, wk, 512)):
                pg.op('act', lambda e, p=p, t=t: e.activation(out=sz[0:np_, t, :], in_=p, func=AF.Silu), r=[pk], w=['sz'])
            for t in range(NT):
                gt = Q * 4 + t
                if samp:
                    load_cache_kv(D['ca_k'][l, t], D['ca_v'][l, t], 4, 8)
                for hq in range(2):
                    acc = None
                    for h4 in range(4):
                        h = hq * 4 + h4
                        hp, hb = h // 2, (h % 2) * 64
                        kbs = []
                        if not samp:
                            for i5 in range(5):
                                gk = gt - 4 + i5
                                if gk < 0:
                                    continue
                                s8 = gk % 8
                                d_ = dict(kT=kT_a[hb:hb + 64, hp, s8 * 128:(s8 + 1) * 128], v=va_a[:, s8, h, 0:65], nk=128,
                                          bias=None, q0=0, keys=['kT_a', 'va_a'])
                                if i5 in (1, 2):
                                    d_['bias'] = expB[:, h, i5 * 128:i5 * 128 + 1]; d_['bkeys'] = ['expB']
                                else:
                                    d_['badd'] = expB[:, h, i5 * 128:(i5 + 1) * 128]; d_['mkeys'] = ['expB']
                                kbs.append(d_)
                        else:
                            for kb in range(4):
                                d_ = dict(kT=ckT[hb:hb + 64, hp, kb * 128:(kb + 1) * 128], v=cva_[:, kb, h, 0:65], nk=128,
                                          bias=None, q0=0, keys=['ckT', 'cvaS'])
                                if kb in (0, 1, 2):
                                    d_['bias'] = expB[:, h, kb * 128:kb * 128 + 1]; d_['bkeys'] = ['expB']
                                else:
                                    d_['badd'] = expB[:, h, kb * 128:kb * 128 + 64]; d_['mkeys'] = ['expB']
                                kbs.append(d_)
                            kbs.append(dict(kT=kTs[hb:hb + 64, hp, t * 64:(t + 1) * 64], v=vas[0:64, t, h, 0:65], nk=64,
                                            bias=None, badd=expB[:, h, 512:576], mkeys=['expB'], q0=0, keys=['kTs', 'vas']))
                        a2 = None if acc is None else (acc[0], acc[1], h4, False)
                        acc = attend(qT[hb:hb + 64, hp, t * np_:(t + 1) * np_], np_, [(0, np_)], kbs, ['qT'], None, acc=a2)
                    pov, pok = acc
                    rl = st8[0:np_, 2, 0:4]
                    pg.op('dve', lambda e, pov=pov: e.reciprocal(out=rl, in_=pov[0:np_, 0:4, 64]), r=[pok], w=['st8'])
                    tmp = f2[0:np_, 0:256].rearrange("p (j d) -> p j d", d=64)
                    pg.op('dve', lambda e, pov=pov, tmp=tmp: e.tensor_tensor(out=tmp, in0=pov[0:np_, 0:4, 0:64],
                                                                           in1=rl.unsqueeze(2).to_broadcast([np_, 4, 64]), op=ALU.mult),
                          r=[pok, 'st8'], w=['f2'])
                    pg.op('pool', lambda e, tmp=tmp, hq=hq, t=t: e.tensor_tensor(
                        out=oz[0:np_, t, hq * 256:(hq + 1) * 256].rearrange("p (j d) -> p j d", d=64), in0=tmp,
                        in1=sz[0:np_, t, hq * 256:(hq + 1) * 256].rearrange("p (j d) -> p j d", d=64), op=ALU.mult),
                        r=['f2', 'sz'], w=['oz'])
            oz_to_T('a', NT, np_, 4)

            if cfg.get('marks'): print('MARK', kind, Q, 'M', pg.nops)
            wt, wk = load_w(D['w_in'], l, COL['m_q'], 512)
            for t, p, pk in pipe_tiles(NT, lambda t: proj_tm(xnT, 'xnT', t * np_, np_, wt, wk, 512)):
                pg.op('act', lambda e, p=p, t=t: e.activation(out=sz[0:np_, t, 0:256], in_=p[:, 256:512], func=AF.Silu),
                      r=[pk], w=['sz'])
                evac_q(p[:, 0:256], pk, t, np_, 4, 'm_qnorm')
            if not samp:
                for h in range(4):
                    hp, hb = h // 2, (h % 2) * 64
                    kbs = [dict(kT=kT_m[hb:hb + 64, hp, kb * 128:(kb + 1) * 128], v=va_m[:, kb, h, 0:65], nk=128, bias=None,
                                q0=0, keys=['kT_m', 'va_m']) for kb in range(2)]
                    attend(qT[hb:hb + 64, hp, :], 512, [(j * 128, 128) for j in range(4)], kbs, ['qT'],
                           attn_out(h * 64, 4, 128))
            else:
                for t in range(NT):
                    load_cache_kv(D['cm_k'][l, t], D['cm_v'][l, t], 2, 4)
                    for h in range(4):
                        hp, hb = h // 2, (h % 2) * 64
                        kbs = [dict(kT=ckT[hb:hb + 64, hp, kb * 128:(kb + 1) * 128], v=cva_[:, kb, h, 0:65], nk=128, bias=None,
                                    q0=0, keys=['ckT', 'cvaS']) for kb in range(2)]
                        attend(qT[hb:hb + 64, hp, t * 64:(t + 1) * 64], 64, [(0, 64)], kbs, ['qT'], attn_out_s(h * 64, t))
            oz_to_T('m', NT, np_, 2)

            if cfg.get('marks'): print('MARK', kind, Q, 'B_proj', pg.nops)
            wt, wk = load_w(D['w_in'], l, COL['b_z'], 512)
            for t, p, pk in pipe_tiles(NT, lambda t: proj_tm(xnT, 'xnT', t * np_, np_, wt, wk, 512)):
                pg.op('act', lambda e, p=p, t=t: e.activation(out=sz[0:np_, t, :], in_=p, func=AF.Silu), r=[pk], w=['sz', 'szb'])
            wt, wk = load_w(D['w_in'], l, COL['b_dt'], 8)
            psv, pk = proj8_all(NT, np_, wt, wk)
            softplus4(psv, pk, smv('b_dt_bias', np_), dts[0:np_, :, :], 'dts', np_, NT)
            pg.op('dve', lambda e: e.tensor_tensor(out=asb[0:np_, :, :], in0=dts[0:np_, :, :],
                                                   in1=aneg[0:np_, :].unsqueeze(1).to_broadcast([np_, NT, 8]), op=ALU.mult),
                  r=['dts', 'aneg'], w=['asb'])
            for half in range(2):
                js = slice(half * 3, half * 3 + 3)
                if samp:
                    for t in range(NT):
                        for jj in range(3):
                            j = half * 3 + jj
                            pg.dma('sp', raw[:, jj, t, 0:3],
                                   D['sb_conv'][l, t][:, j * 128:(j + 1) * 128].rearrange("r p -> p r"), w=['raw'], nonc=True)
                else:
                    pg.op('pool', lambda e, js=js: cp(e, out=raw[:, :, 0, 0:3], in_=halo[:, js, :]), r=['halo'], w=['raw'])
                wt, wk = load_w(D['w_in'], l, COL['b_xbc'] + half * 384, 384)
                for jj in range(3):
                    i = rot('A'); p = psA[i]; pk = 'psA%d' % i
                    for kc in range(8):
                        pg.op('pe', lambda e, kc=kc, jj=jj, p=p, wt=wt: e.matmul(p[:, 0:NTOK], lhsT=wt[:, kc, jj * 128:(jj + 1) * 128],
                                                                     rhs=xnT[:, kc, 0:NTOK], start=(kc == 0), stop=(kc == 7)),
                              r=['xnT'] + list(wk), w=[pk], inc=(kc == 7))
                    pg.op('act', lambda e, jj=jj, p=p: e.activation(out=raw[:, jj, :, 3:3 + np_],
                                                             in_=p[:, 0:NTOK].rearrange("p (t n) -> p t n", n=np_), func=AF.Copy),
                          r=[pk], w=['raw'])
                if not samp:
                    for t in range(1, NT):
                        pg.op('pool', lambda e, t=t: cp(e, out=raw[:, :, t, 0:3], in_=raw[:, :, t - 1, np_:np_ + 3]),
                              r=['raw'], w=['raw'])
                    pg.op('pool', lambda e, js=js: cp(e, out=halo[:, js, :], in_=raw[:, :, NT - 1, np_:np_ + 3]),
                          r=['raw'], w=['halo'])
                    if Q == NQ - 1:
                        for jj in range(3):
                            j = half * 3 + jj
                            out_tokens.append(pg.dma('sp', D['pb_c'][l][:, j * 128:(j + 1) * 128].rearrange("r p -> p r"),
                                                     halo[:, j, :], r=['halo'], nonc=True))
                else:
                    for t in range(NT):
                        for jj in range(3):
                            j = half * 3 + jj
                            out_tokens.append(pg.dma('sp', D['sb_c'][l, t][:, j * 128:(j + 1) * 128].rearrange("r p -> p r"),
                                                     raw[:, jj, t, np_:np_ + 3], r=['raw'], nonc=True))
                for jj in range(3):
                    j = half * 3 + jj
                    cv = cva[:, 0:NTOK].rearrange("p (t n) -> p t n", n=np_)
                    pg.op('dve', lambda e, j=j, jj=jj, cv=cv: e.tensor_scalar(out=cv, in0=raw[:, jj, :, 0:np_], scalar1=cw[:, j, 0:1],
                                                                scalar2=cw[:, j, 4:5], op0=ALU.mult, op1=ALU.add),
                          r=['raw', 'cw'], w=['cva'])
                    for tap in range(1, 4):
                        pg.op('dve', lambda e, j=j, jj=jj, tap=tap, cv=cv: e.scalar_tensor_tensor(
                            out=cv, in0=raw[:, jj, :, tap:tap + np_], scalar=cw[:, j, tap:tap + 1], in1=cv,
                            op0=ALU.mult, op1=ALU.add), r=['raw', 'cw', 'cva'], w=['cva'])
                    pg.op('act', lambda e, j=j: e.activation(out=xbc[:, j, 0:NTOK], in_=cva[:, 0:NTOK], func=AF.Silu),
                          r=['cva'], w=['xbc'])
            if samp:
                barrier(AEK + SET1K)
                for _ in ssdA(0, np_, 0):
                    pass
                for t in range(NT):
                    state_load(D['sb_ssm'][l, t])
                    gens = [ssdB(t, np_, t % 2)]
                    if t + 1 < NT:
                        gens.append(ssdA(t + 1, np_, (t + 1) % 2))
                    while gens:
                        for g_ in list(gens):
                            try:
                                next(g_)
                            except StopIteration:
                                gens.remove(g_)
                    state_store(D['sb_s'][l, t])
                barrier(AEK + SET1K)
            else:
                ssd_pipelined(NT, np_)
            if (not samp) and Q == NQ - 1 and not NOSTATE:
                state_store(D['pb_s'][l])
            oz_to_T('b', NT, np_, 4)

            if cfg.get('marks'): print('MARK', kind, Q, 'merge', pg.nops)
            barrier(AEK)
            brs = (('a', 'w_pa', 4), ('b', 'w_pb', 4), ('c', 'w_pc', 4), ('m', 'w_pm', 2))
            for c in range(2):
                for bi, (br, wn, nkc) in enumerate(brs):
                    wg, wgk = load_w(D['w_in'], l, COL['gate'] + bi * 1024 + c * 512, 512)
                    wp, wpk = load_w(D[wn], l, c * 512, 512, nk=nkc)
                    def mproj(t, wg=wg, wgk=wgk, wp=wp, wpk=wpk, br=br, nkc=nkc):
                        p1, pk1 = proj_tm(xnT, 'xnT', t * np_, np_, wg, wgk, 512)
                        isx = rot('S'); p2 = psS[isx]; pk2 = 'psS%d' % isx
                        for kc in range(nkc):
                            pg.op('pe', lambda e, kc=kc: e.matmul(
                                p2[0:np_, :], lhsT=ozT[br][:, kc, t * np_:(t + 1) * np_], rhs=wp[:, kc, :],
                                start=(kc == 0), stop=(kc == nkc - 1)), r=['ozT_' + br] + list(wpk), w=[pk2], inc=(kc == nkc - 1))
                        return p1, pk1, p2, pk2
                    for t, p1, pk1, p2, pk2 in pipe_tiles(NT, mproj):
                        pg.op('act', lambda e, p1=p1: e.activation(out=sig[0:np_, :], in_=p1, func=AF.Sigmoid), r=[pk1], w=['cva'])
                        if bi == 0:
                            pg.op('dve', lambda e, t=t, p2=p2: e.tensor_tensor(out=macc[0:np_, t, :], in0=sig[0:np_, :],
                                                                               in1=p2[0:np_, :], op=ALU.mult),
                                  r=['cva', pk2], w=['macc%d' % t])
                        else:
                            pg.op('dve', lambda e, t=t, p2=p2: e.tensor_tensor(out=f1[0:np_, :], in0=sig[0:np_, :],
                                                                               in1=p2[0:np_, :], op=ALU.mult),
                                  r=['cva', pk2], w=['f1'])
                            pg.op('pool', lambda e, t=t: e.tensor_tensor(out=macc[0:np_, t, :], in0=macc[0:np_, t, :],
                                                                         in1=f1[0:np_, :], op=ALU.add),
                                  r=['f1', 'macc%d' % t], w=['macc%d' % t])
                for t in range(NT):
                    pg.op('act', lambda e, t=t: e.activation(out=b1[0:np_, :], in_=macc[0:np_, t, :], func=AF.Copy),
                          r=['macc%d' % t], w=['b1'])
                    transposes(lambda j: b1[0:np_, j * 128:(j + 1) * 128], 4, np_,
                               lambda e, v, t=t, c=c: cp(e, out=mT[:, c * 4:(c + 1) * 4, t * np_:(t + 1) * np_], in_=v),
                               ['b1'], ['xbc', 'Mh'], evac='dve')
            wos = [load_w(D['w_out'], l, c * 512, 512) for c in range(2)]
            for t in range(NT):
                i = rot('X'); xk = 'xt%d' % i
                pg.dma('sp', xt[0:np_, i, :], xsrc[rows(t), :], r=[('xd', kind, row0 + t * np_)] if l > 0 else [], w=[xk])
                for c in range(2):
                    wo, wok = wos[c]
                    p, pk = proj_tm(mT, 'xbc', t * np_, np_, wo, wok, 512, xkeys=['Mh'])
                    pg.op('dve', lambda e, i=i, p=p, c=c: e.tensor_tensor(out=xt[0:np_, i, c * 512:(c + 1) * 512],
                                                                          in0=xt[0:np_, i, c * 512:(c + 1) * 512], in1=p,
                                                                          op=ALU.add), r=[xk, pk], w=[xk])
                tok = pg.dma('sp', xdst[rows(t), :], xt[0:np_, i, :], r=[xk], w=[('xd', kind, row0 + t * np_)])
                out_tokens.append(tok)

        vflat = va_c[:, :, :, :].rearrange("p a h c -> p (a h c)")
        sb_kTs = vflat[:, 0:1024].rearrange("p (k n) -> p k n", n=256)
        vas = vflat[:, 1024:1024 + 2112].rearrange("p (t h c) -> p t h c", t=4, h=8)
        ckT = vflat[:, 3136:3136 + 4096].rearrange("p (k n) -> p k n", n=1024)
        cva_ = vflat[:, 7232:7232 + 4224].rearrange("p (a h c) -> p a h c", a=8, h=8)
        cums = kT_a[:, 0, 0:576].bitcast(F32).rearrange("p (t k h) -> p t k h", t=4, k=9)
        SKEYS = ['kTs', 'vas', 'ckT', 'cvaS']

        def load_cache_kv(kd, vd, nblk, nh):
            n = nh * 64
            for kb in range(nblk):
                stg, sk = ((b1, 'b1'), (xnb, 'xnb'))[kb % 2]
                pg.dma('pool', stg[:, 0:n], kd[kb * 128:(kb + 1) * 128, :], w=[sk])
                transposes(lambda j, stg=stg: stg[:, j * 128:(j + 1) * 128], nh // 2, 128,
                           lambda e, v, kb=kb: cp(e, out=ckT[:, 0:nh // 2, kb * 128:(kb + 1) * 128], in_=v),
                           [sk], ['ckT'], evac='act' if kb % 2 else 'dve')
                pg.dma('pool', cva_[:, kb, 0:nh, 0:64], vd[kb * 128:(kb + 1) * 128, :].rearrange("p (h d) -> p h d", d=64),
                       w=['cvaS'])

        def state_load(src):
            pg.dma('sp', cva[0:64, :].rearrange("p (h n) -> p h n", n=64), src.rearrange("h p n -> p h n"), w=['cva'])
            i = rot('A'); p = psA[i]; pk = 'psA%d' % i
            for h in range(8):
                pg.op('pe', lambda e, h=h: e.matmul(p[0:64, h * 64:(h + 1) * 64], lhsT=cva[0:64, h * 64:(h + 1) * 64],
                                                    rhs=identf[0:64, :], start=True, stop=True),
                      r=['cva', 'identf'], w=[pk], inc=(h == 7))
            for g in range(2):
                pg.op('act' if g else 'dve', lambda e, g=g: cp(
                    e, out=hT[g * 64:(g + 1) * 64, :, :], in_=p[0:64, g * 256:(g + 1) * 256].rearrange("p (a d) -> p a d", d=64)),
                    r=[pk], w=['hT'])
            pg.op('act', lambda e: e.activation(out=hTb[:, :, :], in_=hT[:, :, :], func=AF.Copy), r=['hT'], w=['hTb'])

        def state_store(dst):
            for g in range(2):
                i = rot('A'); p = psA[i]; pk = 'psA%d' % i
                for a in range(4):
                    pg.op('pe', lambda e, g=g, a=a, p=p: e.matmul(p[0:64, a * 64:(a + 1) * 64], lhsT=hT[g * 64:(g + 1) * 64, a, :],
                                                             rhs=identf[g * 64:(g + 1) * 64, :], start=True, stop=True),
                          r=['hT', 'identf'], w=[pk], inc=(a == 3))
                pg.op('act' if g else 'dve', lambda e, g=g, p=p: cp(e, out=cva[0:64, g * 256:(g + 1) * 256], in_=p[0:64, 0:256]),
                      r=[pk], w=['cva'])
            out_tokens.append(pg.dma('sp', dst.rearrange("h p n -> p h n"), cva[0:64, :].rearrange("p (h n) -> p h n", n=64),
                                     r=['cva']))

        def attn_out_t(h0col, t, np_):
            def fn(pov, pok):
                rl = st8[0:np_, 2, 0:1]
                pg.op('dve', lambda e: e.reciprocal(out=rl, in_=pov[0:np_, 0, 64:65]), r=[pok], w=['st8'])
                pg.op('dve', lambda e: e.scalar_tensor_tensor(out=oz[0:np_, t, h0col:h0col + 64], in0=pov[0:np_, 0, 0:64],
                                                              scalar=rl, in1=sz[0:np_, t, h0col:h0col + 64],
                                                              op0=ALU.mult, op1=ALU.mult), r=[pok, 'st8', 'sz'], w=['oz'])
            return fn

        def attn_out_s(h0col, t):
            return attn_out_t(h0col, t, 64)

        def attend_fox(qT_ap, NQc, qtiles, kblocks, qkeys, out_fn):
            for kb in kblocks:
                if kb.get('diag'):
                    kb['diagmask'] = True
            attend(qT_ap, NQc, qtiles, kblocks, qkeys, out_fn)

        def memory_kv(l):
            def src(t):
                return D['memp'][t * 128:(t + 1) * 128, :]
            barrier(AEK)
            pg.dma('sp', AE[:, 0:1024], D['mnorm'][l:l + 1, :].broadcast_to([128, 1024]), w=AEK)
            norm_tiles(src, 2, 128, lambda n: AE[0:n, 0:1024], lambda t: AEK)
            barrier(AEK)
            wt, wk = load_w(D['w_mkv'], l, 0, 512)
            for t, p, pk in pipe_tiles(2, lambda t: proj_tm(xnT, 'xnT', t * 128, 128, wt, wk, 512)):
                evac_v(p[:, 256:512], pk, 128, 4, D['pm_v'][l, t * 128:(t + 1) * 128, :], va_m[:, t, :, 0:64], 'va_m')
                evac_k(p[:, 0:256], pk, 128, 4, 'm_knorm', D['pm_k'][l, t * 128:(t + 1) * 128, :],
                       lambda t=t: kT_m[:, :, t * 128:(t + 1) * 128], 'kT_m')

        try:
          for l in range(NL):
            layer_consts(l)
            memory_kv(l)
            pg.op('pool', lambda e: e.memset(hT[:], 0.0), w=['hT'])
            pg.op('pool', lambda e: e.memset(hTb[:], 0.0), w=['hTb'])
            pg.op('pool', lambda e: e.memset(halo[:], 0.0), w=['halo'])
            for Q in range(NQ):
                process_group(l, 'p', Q)
            if DO_S:
                barrier(['va_c', 'kT_a', 'cums'] + SKEYS)
                pg.op('pool', lambda e: e.memset(vas, 1.0), w=['vas'])
                pg.op('pool', lambda e: e.memset(cva_, 1.0), w=['cvaS'])
                process_group(l, 's', 0)
                barrier(['va_c', 'kT_a', 'cums'] + SKEYS)
                pg.op('pool', lambda e: e.memset(va_c[:], 1.0), w=['va_c'])
        except StopBuild:
            pass
        pg.maxops = None
        pg.op('pe', lambda e: e.matmul(psA[0][0:1, 0:1], lhsT=cstb[:, 0:1], rhs=cstb[:, 0:1], start=True, stop=True), r=['cstb'], w=['psA0'])
        pg.op('act', lambda e: e.activation(out=bar[:, 1:2], in_=bar[:, 1:2], func=AF.Copy), r=['psA0'], w=['bar2'])
        pg.op('dve', lambda e: e.tensor_copy(out=bar[:, 1:2], in_=bar[:, 1:2]), w=['bar2'])
        pg.op('pool', lambda e: e.tensor_copy(out=bar[:, 1:2], in_=bar[:, 1:2]), w=['bar2'])
        out_tokens.append(pg.dma('sp', D['pb_c'][0, 0:1, 0:2], bar[0:1, 0:2], r=['bar2', 'bar'], w=['zz']) if False else None)
        out_tokens[:] = [t for t in out_tokens if t is not None]
        pg.op('pool', lambda e: e.memset(bar[:, 0:1], 0.0), r=['bar2'], w=['bar'])
        pg.finish(out_tokens)
        pg._wait('sp', ('c', 'pool', pg.cnt['pool'] - 1))
        print("ops:", pg.nops, "dmas:", pg.dma_n, "cnt:", pg.cnt)
    return nc


def _consts():
    c = np.zeros((128, NCST), np.float32)
    k = np.arange(128)[:, None]
    m = np.arange(128)[None, :]
    c[:, C_M1:C_M1 + 128] = (k <= m)
    c[:, C_M2:C_M2 + 128] = (k > m)
    c[:, C_S127:C_S127 + 128] = (k == 127)
    c[:, C_S63:C_S63 + 128] = (k == 63)
    band = np.ones((128, 5, 128), np.float32)
    s = np.arange(128)[:, None]
    t = np.arange(128)[None, :]
    band[:, 0, :] = 1.0 - ((s < 64) & (t >= 64))
    band[:, 4, :] = 1.0 - ((s >= 64) & (t < 64))
    ident = (k == m).astype(np.float32)
    return c, np.ascontiguousarray(band.reshape(128, 640)), ident


def _prep(inputs, cfg=None):
    f = lambda a: np.ascontiguousarray(np.asarray(a, dtype=np.float32))
    I = {k: f(v) for k, v in inputs.items()}
    small = np.concatenate([I[n].reshape(2, -1) for n in SM], axis=1)
    s = np.arange(128)[:, None]
    j = np.arange(640)[None, :]
    dist = 512 + (j % 128) - 128 * (j // 128) - s
    idx = np.clip(dist, -128, 128) + 128
    relb = np.ascontiguousarray(np.transpose(I['a_rel'][:, idx, :], (0, 1, 3, 2)))
    cwt = np.concatenate([I['b_conv_w'], I['b_conv_b'][:, None, :]], axis=1)
    convw = np.ascontiguousarray(np.transpose(cwt.reshape(2, 5, 6, 128), (0, 3, 2, 1)))
    cst, band, ident = _consts()
    maps = []
    for c in range(8):
        b = c % 4
        ss = slice(c * NS, (c + 1) * NS)
        m = dict(
            xp=I['x_prompt'][b], xs=I['x_sample'][ss].reshape(NS * TS, DM), memp=I['mem_prompt'][b],
            ca_k=I['cache_a_k'][:, ss].reshape(2, NS, 512, 512), ca_v=I['cache_a_v'][:, ss].reshape(2, NS, 512, 512),
            cc_k=I['cache_c_k'][:, ss].reshape(2, NS, 1024, 512), cc_v=I['cache_c_v'][:, ss].reshape(2, NS, 1024, 512),
            cc_f=I['cache_c_logf'][:, ss], sb_ssm=I['state_b_ssm'][:, ss], sb_conv=I['state_b_conv'][:, ss],
            cm_k=I['cache_mem_k'][:, ss].reshape(2, NS, 256, 256), cm_v=I['cache_mem_v'][:, ss].reshape(2, NS, 256, 256),
            w_in=I['w_in'], w_mkv=I['w_mkv'], w_pa=I['w_pa'], w_pb=I['w_pb'], w_pc=I['w_pc'], w_pm=I['w_pm'],
            w_out=I['w_out'], small=small, mnorm=I['m_norm'], band=band, ident=ident, relb=relb, convw=convw, cst=cst)
        maps.append({k: np.ascontiguousarray(v) for k, v in m.items()})
    return maps


_NC_CACHE = {}


def kernel(**inputs):
    cfg = {}
    key = 'full'
    if key not in _NC_CACHE:
        _NC_CACHE[key] = build(cfg)
    nc = _NC_CACHE[key]
    maps = _prep(inputs)
    res = run_bass_kernel_spmd(nc, maps, core_ids=list(range(8)))
    R = res.results
    P4 = range(4)
    st = lambda name, shape: np.stack([R[b][name] for b in P4], axis=1).reshape(shape)
    cat = lambda name: np.concatenate([R[c][name] for c in range(8)], axis=1)
    y_prompt = np.stack([R[b]['yp'] for b in P4], axis=0)
    y_sample = np.concatenate([R[c]['ys'].reshape(NS, TS, DM) for c in range(8)], axis=0)
    outs = [y_prompt, y_sample,
            st('pa_k', (2, 4, 512, 8, 64)), st('pa_v', (2, 4, 512, 8, 64)),
            st('pc_k', (2, 4, SEQ, 8, 64)), st('pc_v', (2, 4, SEQ, 8, 64)), st('pc_f', (2, 4, SEQ, 8)),
            st('pb_s', (2, 4, 8, 64, 64)), st('pb_c', (2, 4, 3, 768)),
            st('pm_k', (2, 4, 256, 4, 64)), st('pm_v', (2, 4, 256, 4, 64))]
    for name, tail in (('sa_k', (8, 64)), ('sa_v', (8, 64)), ('sc_k', (8, 64)), ('sc_v', (8, 64)), ('sc_f', (8,))):
        a = np.concatenate([R[c][name].reshape((2, NS, TS) + tail) for c in range(8)], axis=1)
        outs.append(a)
    outs.append(cat('sb_s'))
    outs.append(cat('sb_c'))
    return tuple(np.ascontiguousarray(o.astype(np.float32)) for o in outs)
```

```python
import contextlib
import numpy as np
import ml_dtypes
import concourse.bass as bass
import concourse.mybir as mybir
from concourse.bass_utils import run_bass_kernel_spmd

F32 = mybir.dt.float32
BF16 = mybir.dt.bfloat16
AF = mybir.ActivationFunctionType
ALU = mybir.AluOpType
AX = mybir.AxisListType

DM = 1024
DIN = 10000
SEQ = 4096
NS = 4
TS = 64
EPS = 1e-6
COL = dict(a_q=0, a_k=512, a_v=1024, a_z=1536, b_z=2048, b_xbc=2560, b_dt=3328,
           c_q=3336, c_k=3848, c_v=4360, c_f=4872, c_z=4880, m_q=5392, m_z=5648, gate=5904)
SM = {}
_o = 0
for _n, _w in (('g_norm', 1024), ('b_norm', 512), ('a_qnorm', 64), ('a_knorm', 64),
               ('c_qnorm', 64), ('c_knorm', 64), ('m_qnorm', 64), ('m_knorm', 64),
               ('b_dt_bias', 8), ('b_a_log', 8), ('b_d', 8), ('c_fbias', 8)):
    SM[_n] = (_o, _w)
    _o += _w
NSM = _o
C_M1, C_M2, C_S127, C_S63 = 0, 128, 256, 384
NCST = 512


def cp(e, out, in_):
    if hasattr(e, 'tensor_copy'):
        return e.tensor_copy(out=out, in_=in_)
    return e.activation(out=out, in_=in_, func=AF.Copy)


SAME_ENGINE_SYNC = True


class StopBuild(Exception):
    pass


class Prog:
    EPOCH = 24000
    ND = 48

    def __init__(self, nc, es):
        self.nc, self.es = nc, es
        self.eng = {'pe': nc.tensor, 'act': nc.scalar, 'dve': nc.vector, 'pool': nc.gpsimd, 'sp': nc.sync}
        self.cnt = {e: 0 for e in self.eng}
        self.sems = {}
        self.waited = {e: {} for e in self.eng}
        self.last_w = {}
        self.readers = {}
        self.dma_n = 0
        self.dma_sems = [es.enter_context(nc.semaphore("dq%d" % i)) for i in range(self.ND)]
        self.dma_tokens = []
        self.nops = 0
        self.maxops = None

    def _sem(self, e, epoch):
        k = (e, epoch)
        if k not in self.sems:
            self.sems[k] = self.es.enter_context(self.nc.semaphore("s_%s_%d" % (e, epoch)))
        return self.sems[k]

    def _wait(self, e, tok):
        if tok[0] == 'c':
            _, pe, c = tok
            if pe == e and (e == 'pe' or not SAME_ENGINE_SYNC):
                return
            epoch, v = divmod(c, self.EPOCH)
            key, val, sem = ('c', pe, epoch), v + 1, self._sem(pe, epoch)
        else:
            _, slot, rnd = tok
            key, val, sem = ('d', slot), 16 * (rnd + 1), self.dma_sems[slot]
        if self.waited[e].get(key, 0) >= val:
            return
        self.waited[e][key] = val
        self.eng[e].wait_ge(sem, val)

    def _deps(self, r, w):
        deps = set()
        for k in r:
            if k in self.last_w:
                deps.add(self.last_w[k])
            if isinstance(k, str) and k.startswith('ps'):
                deps.update(self.readers.get(k, ()))
        for k in w:
            if k in self.last_w:
                deps.add(self.last_w[k])
            deps.update(self.readers.get(k, ()))
        return deps

    def _reg(self, tok, r, w):
        for k in r:
            self.readers.setdefault(k, []).append(tok)
        for k in w:
            self.last_w[k] = tok
            self.readers[k] = []

    def op(self, e, fn, r=(), w=(), inc=True):
        if self.maxops is not None and self.nops >= self.maxops:
            raise StopBuild()
        for tok in self._deps(r, w):
            self._wait(e, tok)
        inst = fn(self.eng[e])
        c = self.cnt[e]
        if inc:
            epoch, _ = divmod(c, self.EPOCH)
            inst.then_inc(self._sem(e, epoch), 1)
            self.cnt[e] += 1
        self._reg(('c', e, c), r, w)
        self.nops += 1

    def dma(self, q, out, in_, r=(), w=(), nonc=False):
        if self.maxops is not None and self.nops >= self.maxops:
            raise StopBuild()
        n = self.dma_n
        self.dma_n += 1
        slot, rnd = n % self.ND, n // self.ND
        if rnd > 0:
            self._wait(q, ('d', slot, rnd - 1))
        for tok in self._deps(r, w):
            self._wait(q, tok)
        kw = {}
        if nonc:
            kw['allow_slow_non_contiguous'] = True
        inst = self.eng[q].dma_start(out=out, in_=in_, **kw)
        inst.then_inc(self.dma_sems[slot], 16)
        tok = ('d', slot, rnd)
        self._reg(tok, r, w)
        self.dma_tokens.append(tok)
        self.nops += 1
        return tok

    def finish(self, out_tokens):
        for tok in out_tokens:
            self._wait('sp', tok)
        last = {}
        for tok in self.dma_tokens:
            last[tok[1]] = tok
        for tok in last.values():
            self._wait('sp', tok)


def build(cfg):
    NL = cfg.get('layers', 2)
    NQ = cfg.get('nq', 8)
    DO_S = cfg.get('sample', True)
    NOSTATE = cfg.get('nostate', False)
    nc = bass.Bass("TRN2", target_bir_lowering=False)
    es = contextlib.ExitStack()
    D = {}

    def din(name, shape):
        D[name] = nc.dram_tensor(name, list(shape), F32, kind="ExternalInput").ap()

    def dout(name, shape):
        D[name] = nc.dram_tensor(name, list(shape), F32, kind="ExternalOutput").ap()

    din('xp', [SEQ, DM]); din('xs', [NS * TS, DM]); din('memp', [256, DM])
    din('ca_k', [2, NS, 512, 512]); din('ca_v', [2, NS, 512, 512])
    din('cc_k', [2, NS, 1024, 512]); din('cc_v', [2, NS, 1024, 512]); din('cc_f', [2, NS, 1024, 8])
    din('sb_ssm', [2, NS, 8, 64, 64]); din('sb_conv', [2, NS, 3, 768])
    din('cm_k', [2, NS, 256, 256]); din('cm_v', [2, NS, 256, 256])
    din('w_in', [2, DM, DIN]); din('w_mkv', [2, DM, 512])
    din('w_pa', [2, 512, DM]); din('w_pb', [2, 512, DM]); din('w_pc', [2, 512, DM]); din('w_pm', [2, 256, DM])
    din('w_out', [2, DM, DM])
    din('small', [2, NSM]); din('mnorm', [2, 1024]); din('relb', [2, 128, 8, 640]); din('convw', [2, 128, 6, 5]); din('cst', [128, NCST]); din('band', [128, 640]); din('ident', [128, 128])
    dout('yp', [SEQ, DM]); dout('ys', [NS * TS, DM])
    dout('pa_k', [2, 512, 512]); dout('pa_v', [2, 512, 512])
    dout('pc_k', [2, SEQ, 512]); dout('pc_v', [2, SEQ, 512]); dout('pc_f', [2, SEQ, 8])
    dout('pb_s', [2, 8, 64, 64]); dout('pb_c', [2, 3, 768])
    dout('pm_k', [2, 256, 256]); dout('pm_v', [2, 256, 256])
    dout('sa_k', [2, NS * TS, 512]); dout('sa_v', [2, NS * TS, 512])
    dout('sc_k', [2, NS * TS, 512]); dout('sc_v', [2, NS * TS, 512]); dout('sc_f', [2, NS * TS, 8])
    dout('sb_s', [2, NS, 8, 64, 64]); dout('sb_c', [2, NS, 3, 768])
    x1p = nc.dram_tensor("x1p", [SEQ, DM], F32, kind="Internal").ap()
    x1s = nc.dram_tensor("x1s", [NS * TS, DM], F32, kind="Internal").ap()

    with es:
        pg = Prog(nc, es)
        pg.maxops = cfg.get('maxops')
        out_tokens = []

        def sb(name, shape, dt):
            return es.enter_context(nc.sbuf_tensor("sb_" + name, list(shape), dt))

        def ps(name, shape, dt):
            return es.enter_context(nc.psum_tensor("ps_" + name, list(shape), dt))

        cst = sb("cst", [128, NCST], F32)
        cstb = sb("cstb", [128, 256], BF16)
        small = sb("small", [128, NSM], F32)
        expB = sb("expB", [128, 8, 640], BF16)
        cw = sb("cw", [128, 6, 5], F32)
        aneg = sb("aneg", [128, 8], F32)
        xt = sb("xt", [128, 1, DM], F32)
        W3 = sb("W3", [128, 4096], BF16)
        xnb = W3[:, 0:1024]
        sq = W3[:, 1024:2048].bitcast(F32)
        f2 = W3[:, 2048:3072].bitcast(F32)
        f3 = W3[:, 3072:4096].bitcast(F32)
        xnT = sb("xnT", [128, 8, 512], BF16)
        NWB = 3
        wbuf = [sb("wbuf%d" % i, [128, 8, 512], BF16) for i in range(NWB)]
        wbuf.append(W3[:, :].rearrange("p (k n) -> p k n", n=512))
        WKEYS = [['wbuf0'], ['wbuf1'], ['wbuf2'], ['wbuf3', 'xnb', 'sq', 'f2', 'f3']]
        kT_c = sb("kT_c", [128, 4, SEQ], BF16)
        va_c = sb("va_c", [128, 32, 8, 66], BF16)
        kT_a = sb("kT_a", [128, 4, 1024], BF16)
        va_a = sb("va_a", [128, 8, 8, 66], BF16)
        kT_m = sb("kT_m", [128, 2, 256], BF16)
        va_m = sb("va_m", [128, 2, 4, 66], BF16)
        cum = sb("cum", [128, 32, 8], F32)
        biasQ = sb("biasQ", [128, 32, 8], F32)
        qT = sb("qT", [128, 4, 512], BF16)
        sz = sb("sz", [128, 4, 512], BF16)
        oz = sb("oz", [128, 4, 512], BF16)
        ozT = {k: sb("ozT_" + k, [128, n, 512], BF16) for k, n in (('a', 4), ('b', 4), ('c', 4), ('m', 2))}
        f1 = sb("f1", [128, 512], F32)
        b1 = sb("b1", [128, 512], BF16)
        PT = [sb("PT%d" % i, [128, 512], BF16) for i in range(2)]
        st8 = sb("st8", [128, 8, 8], F32)
        raw = sb("raw", [128, 3, 4, 131], F32)
        halo = sb("halo", [128, 6, 3], F32)
        AR2 = sb("AR2", [128, 4096], BF16)
        xbc = AR2[:, 0:3072].rearrange("p (j n) -> p j n", n=512)
        Mh = AR2[:, 3072:4096].rearrange("p (h n) -> p h n", n=128)
        mT = AR2[:, :].rearrange("p (k n) -> p k n", n=512)
        cva = sb("cva", [128, 512], F32)
        sig = cva
        dts = sb("dts", [128, 4, 8], F32)
        lfs = sb("lfs", [128, 4, 8], F32)
        asb = sb("asb", [128, 4, 8], F32)
        btk = sb("btk", [128, 128], BF16)
        AE = sb("AE", [128, 2048], F32)
        Ah = AE[:, 0:512].rearrange("p (h n) -> p h n", n=128)
        Ee = AE[:, 512:1024].rearrange("p (h n) -> p h n", n=128)
        Gm = AE[:, 1024:1280].rearrange("p (g n) -> p g n", n=128)
        bdec = AE[:, 1280:1536].bitcast(BF16).rearrange("p (a g n) -> p a g n", a=4, g=2)
        xdt = AE[:, 1536:1792].bitcast(BF16).rearrange("p (h d) -> p h d", d=64)
        xtk = AE[:, 1792:2048].bitcast(BF16)
        macc = AE[:, :].rearrange("p (t n) -> p t n", n=512)
        hT = sb("hT", [128, 4, 64], F32)
        hTb = sb("hTb", [128, 4, 64], BF16)
        ssd8 = sb("ssd8", [128, 4, 8], F32)
        psA = [ps("psA%d" % i, [128, 512], F32) for i in range(2)]
        psT = [ps("psT%d" % i, [128, 1024], BF16) for i in range(2)]
        psS = [ps("psS%d" % i, [128, 512], F32) for i in range(2)]
        psO = [ps("psO%d" % i, [128, 512], F32) for i in range(2)]
        rr = {'A': 0, 'T': 0, 'S': 0, 'O': 0, 'W': 0, 'P': 0, 'X': 0, 'WP': 0}
        rrn = {'X': 1, 'WP': 1}

        def rot(k, n=2):
            v = rr[k]
            rr[k] = (v + 1) % rrn.get(k, n)
            return v

        ident_b = cstb[:, 0:128]
        M1f = cst[:, C_M1:C_M1 + 128]
        M2f = cst[:, C_M2:C_M2 + 128]
        S127f = cst[:, C_S127:C_S127 + 128]
        S63f = cst[:, C_S63:C_S63 + 128]
        M1b = cstb[:, 128:256]

        def smv(name, rows=128):
            o, w = SM[name]
            return small[0:rows, o:o + w]

        bar = sb("bar", [128, 2], F32)
        epst = sb("epst", [128, 1], F32)
        nhalf = sb("nhalf", [128, 8], F32)
        gcol = sb("gcol", [128, 4], F32)
        identf = sb("identf", [128, 64], F32)

        def barrier(keys):
            pg.op('pool', lambda e: e.memset(bar[:, 0:1], 0.0), w=['bar'] + list(keys))

        AEK = ['Ah%d' % h for h in range(8)] + ['Ee', 'Gm', 'bdec', 'xdt', 'xtk', 'AErel'] + ['macc%d' % t for t in range(4)]
        pg.op('pool', lambda e: e.memset(epst[:, :], EPS), w=['epst'])
        pg.op('pool', lambda e: e.memset(nhalf[:, :], -0.5), w=['epst'])
        pg.dma('sp', identf[0:64, :], D['ident'][0:64, 0:64], w=['identf'])
        pg.dma('sp', identf[64:128, :], D['ident'][0:64, 0:64], w=['identf'])
        pg.dma('sp', cst[:, :], D['cst'][:, :], w=['cst'])
        pg.dma('sp', AE[:, 0:128], D['ident'][:, :], w=AEK)
        pg.op('dve', lambda e: cp(e, out=cstb[:, 0:128], in_=AE[:, 0:128]), r=AEK, w=['cstb'])
        pg.op('dve', lambda e: cp(e, out=cstb[:, 128:256], in_=cst[:, C_M1:C_M1 + 128]), r=['cst'], w=['cstb'])
        for t_, k_ in ((va_c, 'va_c'), (va_a, 'va_a'), (va_m, 'va_m')):
            pg.op('pool', lambda e, t_=t_: e.memset(t_[:], 1.0), w=[k_])

        def group_wlist(l):
            wl = []
            for nm in ('c_q', 'c_k', 'c_v'):
                wl.append(('w_in', l, COL[nm], 512, 8))
            wl.append(('w_in', l, COL['c_f'], 8, 8))
            wl.append(('w_in', l, COL['c_z'], 512, 8))
            for nm in ('a_q', 'a_k', 'a_v', 'a_z', 'm_q', 'b_z'):
                wl.append(('w_in', l, COL[nm], 512, 8))
            wl.append(('w_in', l, COL['b_dt'], 8, 8))
            for half in range(2):
                wl.append(('w_in', l, COL['b_xbc'] + half * 384, 384, 8))
            for c in range(2):
                for bi, (wn, nkc) in enumerate((('w_pa', 4), ('w_pb', 4), ('w_pc', 4), ('w_pm', 2))):
                    wl.append(('w_in', l, COL['gate'] + bi * 1024 + c * 512, 512, 8, True))
                    wl.append((wn, l, c * 512, 512, nkc, True))
            for c in range(2):
                wl.append(('w_out', l, c * 512, 512, 8, True))
            return wl

        WL = []
        for l_ in range(NL):
            WL.append(('w_mkv', l_, 0, 512, 8))
            for _ in range(NQ + (1 if DO_S else 0)):
                WL += group_wlist(l_)
        wbi, prev_occ, lastocc, r3, r4 = [], [], {}, 0, 0
        for k_, ent in enumerate(WL):
            if len(ent) > 5 and ent[5]:
                b_ = r4 % 4; r4 += 1
            else:
                b_ = r3 % 3; r3 += 1; r4 = r3
            wbi.append(b_)
            prev_occ.append(lastocc.get(b_, -1))
            lastocc[b_] = k_
        wstate = {'ptr': 0, 'issued': 0}

        def load_w(dram, l, c0, n, nk=8, buf=None):
            i = wstate['ptr']
            exp = WL[i]
            assert exp[1] == l and exp[2] == c0 and exp[3] == n and exp[4] == nk and D[exp[0]] is dram, (exp, l, c0, n, nk)
            in_merge = len(exp) > 5 and exp[5]
            while wstate['issued'] < len(WL) and wstate['issued'] <= i + 3 and \
                    (wstate['issued'] <= i or prev_occ[wstate['issued']] <= i - 2) and \
                    (wbi[wstate['issued']] != 3 or in_merge):
                k = wstate['issued']
                nm, l2, c2, n2, nk2 = WL[k][0:5]
                src = D[nm][l2, :, c2:c2 + n2].rearrange("(k p) n -> p k n", p=128)
                pg.dma('pool', wbuf[wbi[k]][:, 0:nk2, 0:n2], src, w=WKEYS[wbi[k]])
                wstate['issued'] += 1
            wstate['ptr'] += 1
            return wbuf[wbi[i]], WKEYS[wbi[i]]

        def transposes(src_ap_fn, nblk, np_, dst_fn, rkeys, wkeys, evac='dve'):
            i = rot('T'); pt = psT[i]; pk = 'psT%d' % i
            for j in range(nblk):
                pg.op('pe', lambda e, j=j: e.transpose(out=pt[:, j * 128:j * 128 + np_], in_=src_ap_fn(j),
                                                       identity=ident_b[0:np_, 0:np_]),
                      r=list(rkeys) + ['cstb'], w=[pk], inc=(j == nblk - 1))
            view = pt[:, 0:nblk * 128].rearrange("p (b n) -> p b n", n=128)[:, :, 0:np_]
            pg.op(evac, lambda e: dst_fn(e, view), r=[pk], w=list(wkeys))

        def rstd_from_ss(ss_ap, rs_ap, n, rows, key):
            w_ = rs_ap.shape[-1]
            pg.op('dve', lambda e: e.tensor_scalar(out=rs_ap, in0=ss_ap, scalar1=1.0 / n, scalar2=EPS,
                                                    op0=ALU.mult, op1=ALU.add), r=[key], w=[key])
            pg.op('pool', lambda e: e.tensor_tensor(out=rs_ap, in0=rs_ap, in1=nhalf[0:rows, 0:w_], op=ALU.pow),
                  r=[key, 'epst'], w=[key])

        def head_norm(psap, pk, nh, gain_ap, out_ap, outkeys, np_, scale=None):
            n = nh * 64
            pg.op('act', lambda e: e.activation(out=sq[0:np_, 0:n], in_=psap, func=AF.Square), r=[pk], w=['sq'])
            ssv = st8[0:np_, 0, 0:nh]
            pg.op('dve', lambda e: e.tensor_reduce(out=ssv, in_=sq[0:np_, 0:n].rearrange("p (h d) -> p h d", d=64),
                                                   axis=AX.X, op=ALU.add), r=['sq'], w=['st8'])
            rstd_from_ss(ssv, ssv, 64, np_, 'st8')
            o1, k1 = (f1[0:np_, 0:n], ['f1']) if gain_ap is not None else (out_ap, list(outkeys))
            pg.op('dve', lambda e: e.tensor_tensor(
                out=o1.rearrange("p (h d) -> p h d", d=64),
                in0=psap.rearrange("p (h d) -> p h d", d=64),
                in1=ssv.unsqueeze(2).to_broadcast([np_, nh, 64]), op=ALU.mult), r=[pk, 'st8'], w=k1)
            if gain_ap is None:
                return
            g = gain_ap.unsqueeze(1).to_broadcast([np_, nh, 64])
            pg.op('dve', lambda e: e.tensor_tensor(
                out=out_ap.rearrange("p (h d) -> p h d", d=64),
                in0=f1[0:np_, 0:n].rearrange("p (h d) -> p h d", d=64), in1=g, op=ALU.mult),
                r=['f1', 'small'], w=list(outkeys))

        def pipe_tiles(NT, proj_fn):
            nxt = proj_fn(0)
            for t in range(NT):
                cur = nxt
                if t + 1 < NT:
                    nxt = proj_fn(t + 1)
                yield (t,) + tuple(cur)

        def proj_tm(xT, xkey, tcol, np_, wt, wkey, n, nk=8, wc0=0, xkeys=()):
            i = rot('A'); p = psA[i]; pk = 'psA%d' % i
            for kc in range(nk):
                pg.op('pe', lambda e, kc=kc: e.matmul(p[0:np_, 0:n], lhsT=xT[:, kc, tcol:tcol + np_],
                                                      rhs=wt[:, kc, wc0:wc0 + n], start=(kc == 0), stop=(kc == nk - 1)),
                      r=[xkey] + list(wkey) + list(xkeys), w=[pk], inc=(kc == nk - 1))
            return p[0:np_, 0:n], pk

        def layer_consts(l):
            pg.dma('sp', small[:, :], D['small'][l:l + 1, :].broadcast_to([128, NSM]), w=['small'])
            pg.dma('sp', cw[:, :, :], D['convw'][l], w=['cw'])
            for j, name in enumerate(('a_qnorm', 'c_qnorm', 'm_qnorm')):
                o_, w_ = SM[name]
                for half in range(2):
                    pg.dma('sp', gcol[half * 64:(half + 1) * 64, j:j + 1],
                           D['small'][l, o_:o_ + 64].rearrange("(p o) -> p o", o=1), w=['gcol'], nonc=True)
            pg.op('dve', lambda e: e.tensor_scalar(out=gcol[:, 0:3], in0=gcol[:, 0:3], scalar1=0.125, scalar2=None, op0=ALU.mult),
                  r=['gcol'], w=['gcol'])
            pg.op('act', lambda e: e.activation(out=aneg[:, :], in_=smv('b_a_log'), func=AF.Exp), r=['small'], w=['aneg'])
            pg.op('dve', lambda e: e.tensor_scalar(out=aneg[:, :], in0=aneg[:, :], scalar1=-1.0, scalar2=None, op0=ALU.mult),
                  r=['aneg'], w=['aneg'])
            barrier(AEK)
            pg.dma('sp', AE[:, 0:640], D['band'][:, :], w=AEK)
            pg.op('dve', lambda e: e.tensor_scalar(out=AE[:, 0:640], in0=AE[:, 0:640], scalar1=30000.0, scalar2=-30000.0,
                                                    op0=ALU.mult, op1=ALU.add), r=AEK, w=AEK)
            for h in range(8):
                pg.dma('sp', AE[:, 1024:1664], D['relb'][l, :, h, :], w=['AErel'])
                pg.op('dve', lambda e, h=h: e.tensor_tensor(out=expB[:, h, :], in0=AE[:, 1024:1664],
                                                            in1=AE[:, 0:640], op=ALU.add),
                      r=['AErel'] + AEK, w=['expB'])
            barrier(AEK)

        def norm_tiles(src_fn, NT, np_, gain_fn, rk_fn=None):
            rawflat = raw[:, :, :, :].rearrange("p a b c -> p (a b c)")
            sqj = sq.bitcast(BF16)
            stg = [(xt[0:np_, 0, :], 'xt0'), (rawflat[0:np_, 0:DM], 'raw')]

            def ld(t):
                xa, xk = stg[t % 2]
                pg.dma('sp', xa, src_fn(t), r=(rk_fn(t) if rk_fn else []), w=[xk])
            ld(0)
            for t in range(NT):
                xa, xk = stg[t % 2]
                if t + 1 < NT:
                    ld(t + 1)
                ssv = st8[0:np_, 1, t % 2:t % 2 + 1]
                pg.op('act', lambda e, xa=xa, ssv=ssv: e.activation(out=sqj[0:np_, :], in_=xa, func=AF.Square, accum_out=ssv),
                      r=[xk], w=['sq', 'st8'])
                rstd_from_ss(ssv, ssv, DM, np_, 'st8')
                pg.op('dve', lambda e, xa=xa, ssv=ssv: e.scalar_tensor_tensor(out=xnb[0:np_, :], in0=xa, scalar=ssv,
                                                                  in1=gain_fn(np_), op0=ALU.mult, op1=ALU.mult),
                      r=[xk, 'st8', 'small', 'f2'], w=['xnb'])
                transposes(lambda j: xnb[0:np_, j * 128:(j + 1) * 128], 8, np_,
                           lambda e, v, t=t: cp(e, out=xnT[:, :, t * np_:(t + 1) * np_], in_=v),
                           ['xnb'], ['xnT'], evac='act' if t % 2 else 'dve')

        def attend(qT_ap, NQc, qtiles, kblocks, qkeys, out_fn, acc=None):
            if acc is None:
                io = rot('O'); po = psO[io]; pok = 'psO%d' % io
                pov = po[:, 0:260].rearrange("p (j c) -> p j c", c=65)
                jbase, bank_first = 0, True
            else:
                pov, pok, jbase, bank_first = acc
            lastkb, firstkb = {}, {}
            for bi, kb in enumerate(kblocks):
                for j, (c0, nq) in enumerate(qtiles):
                    if c0 >= kb['q0']:
                        lastkb[j] = bi
                        firstkb.setdefault(j, bi)
            st = {}

            def stage_s(bi):
                kb = kblocks[bi]
                isx = rot('S'); psx = psS[isx]; psk = 'psS%d' % isx
                badd = kb.get('badd')
                pg.op('pe', lambda e: e.matmul(psx[0:kb['nk'], kb['q0']:NQc], lhsT=kb['kT'],
                                               rhs=qT_ap[:, kb['q0']:NQc], start=True, stop=(badd is None)),
                      r=list(qkeys) + list(kb['keys']), w=[psk], inc=(badd is None))
                if badd is not None:
                    pg.op('pe', lambda e: e.matmul(psx[0:kb['nk'], kb['q0']:NQc], lhsT=ident_b[:, 0:kb['nk']],
                                                   rhs=badd, start=False, stop=True),
                          r=['cstb'] + list(kb.get('mkeys', [])), w=[psk])
                st[bi] = (psx, psk)

            def stage_e(bi):
                kb = kblocks[bi]
                psx, psk = st[bi]
                ip = rot('P'); pt = PT[ip]; ptk = 'PT%d' % ip
                if kb.get('bias') is not None:
                    pg.op('act', lambda e: e.activation(out=pt[0:kb['nk'], kb['q0']:NQc], in_=psx[0:kb['nk'], kb['q0']:NQc],
                                                        func=AF.Exp, bias=kb['bias'], scale=1.0),
                          r=[psk] + list(kb.get('bkeys', [])), w=[ptk])
                else:
                    pg.op('act', lambda e: e.activation(out=pt[0:kb['nk'], kb['q0']:NQc], in_=psx[0:kb['nk'], kb['q0']:NQc],
                                                        func=AF.Exp), r=[psk], w=[ptk])
                if kb.get('diagmask'):
                    pg.op('dve', lambda e: e.tensor_tensor(
                        out=pt[0:kb['nk'], kb['q0']:kb['q0'] + kb['nk']], in0=pt[0:kb['nk'], kb['q0']:kb['q0'] + kb['nk']],
                        in1=M1b[0:kb['nk'], 0:kb['nk']], op=ALU.mult), r=[ptk, 'cstb'], w=[ptk])
                st[bi] = (pt, ptk)

            def stage_v(bi):
                kb = kblocks[bi]
                pt, ptk = st[bi]
                for j, (c0, nq) in enumerate(qtiles):
                    if c0 < kb['q0']:
                        continue
                    pg.op('pe', lambda e, j=j, c0=c0, nq=nq: e.matmul(
                        pov[0:nq, jbase + j, :], lhsT=pt[0:kb['nk'], c0:c0 + nq], rhs=kb['v'],
                        start=(bank_first and bi == 0 and j == min(firstkb)), stop=(bi == lastkb[j]), skip_group_check=True),
                        r=[ptk] + list(kb['keys']), w=[pok], inc=(c0 + nq >= NQc))

            n = len(kblocks)
            stage_s(0)
            for bi in range(n):
                if bi + 1 < n:
                    stage_s(bi + 1)
                stage_e(bi)
                stage_v(bi)
            if out_fn is not None:
                out_fn(pov, pok)
            return pov, pok

        def attn_out(h0col, NT, np_, szkey='sz'):
            def fn(pov, pok):
                rl = st8[0:np_, 2, 0:NT]
                pg.op('dve', lambda e: e.reciprocal(out=rl, in_=pov[0:np_, 0:NT, 64]), r=[pok], w=['st8'])
                tmp = f2[0:np_, 0:NT * 64].rearrange("p (j d) -> p j d", d=64)
                pg.op('dve', lambda e: e.tensor_tensor(out=tmp, in0=pov[0:np_, 0:NT, 0:64],
                                                       in1=rl.unsqueeze(2).to_broadcast([np_, NT, 64]), op=ALU.mult),
                      r=[pok, 'st8'], w=['f2'])
                pg.op('dve', lambda e: e.tensor_tensor(out=oz[0:np_, 0:NT, h0col:h0col + 64], in0=tmp,
                                                        in1=sz[0:np_, 0:NT, h0col:h0col + 64], op=ALU.mult),
                      r=['f2', szkey], w=['oz'])
            return fn

        def oz_to_T(br, NT, np_, nblk):
            for t in range(NT):
                transposes(lambda j, t=t: oz[0:np_, t, j * 128:(j + 1) * 128], nblk, np_,
                           lambda e, v, t=t: cp(e, out=ozT[br][:, 0:nblk, t * np_:(t + 1) * np_], in_=v),
                           ['oz'], ['ozT_' + br], evac='act' if t % 2 else 'dve')

        def evac_q(psap, pk, t, np_, nh, gname):
            gj = {'a_qnorm': 0, 'c_qnorm': 1, 'm_qnorm': 2}[gname]
            head_norm(psap, pk, nh, None, b1[0:np_, 0:nh * 64], ['b1'], np_)
            transposes(lambda j: b1[0:np_, j * 128:(j + 1) * 128], nh // 2, np_,
                       lambda e, v: e.activation(out=qT[:, 0:nh // 2, t * np_:(t + 1) * np_], in_=v, func=AF.Copy,
                                                 scale=gcol[:, gj:gj + 1]),
                       ['b1', 'gcol'], ['qT'], evac='act')

        def evac_k(psap, pk, np_, nh, gname, out_dram, kT_dst_fn, kT_key):
            head_norm(psap, pk, nh, smv(gname, np_), f3[0:np_, 0:nh * 64], ['f3'], np_)
            if out_dram is not None:
                out_tokens.append(pg.dma('sp', out_dram, f3[0:np_, 0:nh * 64], r=['f3']))
            pg.op('dve', lambda e: cp(e, out=b1[0:np_, 0:nh * 64], in_=f3[0:np_, 0:nh * 64]), r=['f3'], w=['b1'])
            transposes(lambda j: b1[0:np_, j * 128:(j + 1) * 128], nh // 2, np_,
                       lambda e, v: cp(e, out=kT_dst_fn(), in_=v), ['b1'], [kT_key], evac='act')

        def evac_v(psap, pk, np_, nh, out_dram, va_dst, va_key):
            if out_dram is not None:
                pg.op('act', lambda e: e.activation(out=f3[0:np_, 0:nh * 64], in_=psap, func=AF.Copy), r=[pk], w=['f3'])
                out_tokens.append(pg.dma('sp', out_dram, f3[0:np_, 0:nh * 64], r=['f3']))
            pg.op('dve', lambda e: cp(e, out=va_dst, in_=psap.rearrange("p (h d) -> p h d", d=64)),
                  r=[pk], w=[va_key])

        def softplus_from(psap, pk, bias_ap, out_ap, okey, np_, neg=False):
            tmp = st8[0:np_, 3, :]
            pg.op('dve', lambda e: e.tensor_tensor(out=tmp, in0=psap, in1=bias_ap, op=ALU.add), r=[pk, 'small'], w=['st8'])
            pg.op('act', lambda e: e.activation(out=tmp, in_=tmp, func=AF.Exp, scale=(-1.0 if neg else 1.0)),
                  r=['st8'], w=['st8'])
            pg.op('dve', lambda e: e.tensor_scalar(out=tmp, in0=tmp, scalar1=1.0, scalar2=None, op0=ALU.add),
                  r=['st8'], w=['st8'])
            pg.op('act', lambda e: e.activation(out=tmp, in_=tmp, func=AF.Ln), r=['st8'], w=['st8'])
            pg.op('dve', lambda e: e.tensor_scalar(out=out_ap, in0=tmp, scalar1=(-1.0 if neg else 1.0), scalar2=None,
                                                    op0=ALU.mult), r=['st8'], w=[okey])

        def proj8_all(NT, np_, wt, wk):
            i = rot('A'); p = psA[i]; pk = 'psA%d' % i
            for t in range(NT):
                for kc in range(8):
                    pg.op('pe', lambda e, kc=kc, t=t: e.matmul(p[0:np_, t * 8:(t + 1) * 8], lhsT=xnT[:, kc, t * np_:(t + 1) * np_],
                                                              rhs=wt[:, kc, 0:8], start=(kc == 0), stop=(kc == 7)),
                          r=['xnT'] + list(wk), w=[pk], inc=(kc == 7 and t == NT - 1))
            return p[0:np_, 0:NT * 8].rearrange("p (t h) -> p t h", h=8), pk

        def softplus4(psv, pk, bias_ap, out_ap, okey, np_, NT, neg=False):
            tmp = st8[0:np_, 0:NT, :]
            pg.op('dve', lambda e: e.tensor_tensor(out=tmp, in0=psv, in1=bias_ap.unsqueeze(1).to_broadcast([np_, NT, 8]),
                                                   op=ALU.add), r=[pk, 'small'], w=['st8'])
            pg.op('act', lambda e: e.activation(out=tmp, in_=tmp, func=AF.Exp, scale=(-1.0 if neg else 1.0)),
                  r=['st8'], w=['st8'])
            pg.op('dve', lambda e: e.tensor_scalar(out=tmp, in0=tmp, scalar1=1.0, scalar2=None, op0=ALU.add),
                  r=['st8'], w=['st8'])
            pg.op('act', lambda e: e.activation(out=tmp, in_=tmp, func=AF.Ln), r=['st8'], w=['st8'])
            pg.op('dve', lambda e: e.tensor_scalar(out=out_ap, in0=tmp, scalar1=(-1.0 if neg else 1.0), scalar2=None,
                                                    op0=ALU.mult), r=['st8'], w=[okey])

        def cumsum_tile(lf_ap, lfkey, np_, dst_ap, dkey, carry_ap, ckeys):
            i = rot('S'); p = psS[i]; pk = 'psS%d' % i
            pg.op('pe', lambda e: e.matmul(p[0:np_, 0:8], lhsT=M1f[0:np_, 0:np_], rhs=lf_ap, start=True,
                                           stop=(carry_ap is None)), r=[lfkey, 'cst'], w=[pk])
            if carry_ap is not None:
                pg.op('pe', lambda e: e.matmul(p[0:np_, 0:8], lhsT=carry_ap[0], rhs=carry_ap[1], start=False, stop=True),
                      r=list(ckeys) + ['cst'], w=[pk])
            pg.op('dve', lambda e: cp(e, out=dst_ap, in_=p[0:np_, 0:8]), r=[pk], w=[dkey])

        def ssd_tile(t, np_, NT):
            c0 = t * np_
            if t == 0:
                barrier(AEK)
            def ev(e, v):
                return cp(e, out=xtk[0:np_, :].rearrange("p (b n) -> p b n", n=128), in_=v[0:np_, 0:4, :])
            i = rot('T'); pt = psT[i]; pk = 'psT%d' % i
            for j in range(5):
                pg.op('pe', lambda e, j=j: e.transpose(out=pt[0:np_, j * 128:(j + 1) * 128], in_=xbc[:, j, c0:c0 + np_],
                                                       identity=ident_b[:, :]), r=['xbc', 'cstb'], w=[pk], inc=(j == 4))
            pg.op('act', lambda e: e.activation(out=xtk[0:np_, :], in_=pt[0:np_, 0:512], func=AF.Copy), r=[pk], w=['xtk'])
            pg.op('act', lambda e: e.activation(out=btk[0:np_, :], in_=pt[0:np_, 512:640], func=AF.Copy), r=[pk], w=['btk'])
            pg.op('dve', lambda e: e.tensor_tensor(out=xdt[0:np_, :, :], in0=pt[0:np_, 0:512].rearrange("p (h d) -> p h d", d=64),
                                                   in1=dts[0:np_, t, :].unsqueeze(2).to_broadcast([np_, 8, 64]), op=ALU.mult),
                  r=[pk, 'dts'], w=['xdt'])
            a_ap = asb[0:np_, t, :]
            i = rot('A'); p = psA[i]; pk2 = 'psA%d' % i
            pg.op('pe', lambda e: e.matmul(p[0:np_, 0:8], lhsT=M1f[0:np_, 0:np_], rhs=a_ap, start=True, stop=True),
                  r=['asb', 'cst'], w=[pk2], inc=False)
            pg.op('pe', lambda e: e.matmul(p[0:np_, 8:16], lhsT=M2f[0:np_, 0:np_], rhs=a_ap, start=True, stop=True),
                  r=['asb', 'cst'], w=[pk2], inc=False)
            pg.op('pe', lambda e: e.matmul(p[:, 16:24], lhsT=M1f[0:np_, :], rhs=a_ap, start=True, stop=False),
                  r=['asb', 'cst'], w=[pk2], inc=False)
            pg.op('pe', lambda e: e.matmul(p[:, 16:24], lhsT=M2f[0:np_, :], rhs=a_ap, start=False, stop=True),
                  r=['asb', 'cst'], w=[pk2])
            pg.op('act', lambda e: e.activation(out=ssd8[0:np_, 0, :], in_=p[0:np_, 0:8], func=AF.Exp), r=[pk2], w=['ssd8'])
            pg.op('act', lambda e: e.activation(out=ssd8[0:np_, 1, :], in_=p[0:np_, 8:16], func=AF.Exp), r=[pk2], w=['ssd8'])
            pg.op('act', lambda e: e.activation(out=ssd8[:, 2, :], in_=p[:, 16:24], func=AF.Exp), r=[pk2], w=['ssd8'])
            for g in range(2):
                i = rot('A'); pgm = psA[i]; pk3 = 'psA%d' % i
                pg.op('pe', lambda e, g=g, pgm=pgm: e.matmul(pgm[0:np_, 0:np_], lhsT=xbc[g * 64:(g + 1) * 64, 4, c0:c0 + np_],
                                                    rhs=xbc[g * 64:(g + 1) * 64, 5, c0:c0 + np_], start=True, stop=True),
                      r=['xbc'], w=[pk3])
                pg.op('dve', lambda e, g=g, pgm=pgm: e.tensor_tensor(
                    out=Gm[0:np_, g, 0:np_], in0=pgm[0:np_, 0:np_], in1=M1f[0:np_, 0:np_], op=ALU.mult),
                    r=[pk3, 'cst'], w=['Gm'])
            for hh in range(2):
                for h4 in range(4):
                    h = hh * 4 + h4
                    pg.op('dve', lambda e, h=h, h4=h4: e.tensor_scalar(
                        out=Ah[0:np_, h4, 0:np_], in0=M2f[0:np_, 0:np_], scalar1=asb[0:np_, t, h:h + 1], scalar2=None,
                        op0=ALU.mult), r=['cst', 'asb'], w=['Ah%d' % h4])
                psx = psS[hh]; psk = 'psS%d' % hh
                for h4 in range(4):
                    pg.op('pe', lambda e, h4=h4, psx=psx: e.matmul(
                        psx[0:np_, h4 * 128:h4 * 128 + np_], lhsT=Ah[0:np_, h4, 0:np_], rhs=M1f[0:np_, 0:np_],
                        start=True, stop=True), r=['Ah%d' % h4, 'cst'], w=[psk], inc=(h4 == 3))
                pg.op('act', lambda e, psx=psx: e.activation(
                    out=Ee[0:np_, :, 0:np_],
                    in_=psx[0:np_, :].rearrange("p (h n) -> p h n", n=128)[:, :, 0:np_], func=AF.Exp),
                    r=[psk], w=['Ee'])
                pg.op('dve', lambda e, hh=hh: e.tensor_tensor(
                    out=Mh[0:np_, hh * 4:(hh + 1) * 4, 0:np_], in0=Ee[0:np_, :, 0:np_],
                    in1=Gm[0:np_, hh, 0:np_].unsqueeze(1).to_broadcast([np_, 4, np_]), op=ALU.mult),
                    r=['Ee', 'Gm'], w=['Mh'])
            io = rot('O'); py = psO[io]; pyk = 'psO%d' % io
            io2 = rot('O'); py2 = psO[io2]; py2k = 'psO%d' % io2
            for h in range(8):
                pg.op('pe', lambda e, h=h: e.matmul(py[0:np_, h * 64:(h + 1) * 64], lhsT=Mh[0:np_, h, 0:np_],
                                                    rhs=xdt[0:np_, h, :], start=True, stop=True),
                      r=['Mh', 'xdt'], w=[pyk], inc=(h == 7))
            py3 = psS[1]; py3k = 'psS1'
            for h in range(8):
                g = h // 4
                dstp = (py2 if g == 0 else py3)
                pg.op('pe', lambda e, h=h, g=g, dstp=dstp: e.matmul(dstp[0:np_, (h % 4) * 64:(h % 4 + 1) * 64],
                                                         lhsT=xbc[g * 64:(g + 1) * 64, 5, c0:c0 + np_],
                                                         rhs=hTb[g * 64:(g + 1) * 64, h % 4, :], start=True, stop=True),
                      r=['xbc', 'hTb'], w=[py2k if g == 0 else py3k], inc=(h % 4 == 3))
            yv = f1[0:np_, :].rearrange("p (h d) -> p h d", d=64)
            for g, (dstp, dk) in enumerate(((py2, py2k), (py3, py3k))):
                pg.op('dve', lambda e, g=g, dstp=dstp: e.tensor_tensor(
                    out=yv[:, g * 4:(g + 1) * 4, :], in0=dstp[0:np_, 0:256].rearrange("p (h d) -> p h d", d=64),
                    in1=ssd8[0:np_, 0, g * 4:(g + 1) * 4].unsqueeze(2).to_broadcast([np_, 4, 64]), op=ALU.mult),
                    r=[dk, 'ssd8'], w=['f1'])
            pg.op('dve', lambda e: e.tensor_tensor(out=f1[0:np_, :], in0=f1[0:np_, :], in1=py[0:np_, :], op=ALU.add),
                  r=['f1', pyk], w=['f1'])
            pg.op('pool', lambda e: e.tensor_tensor(out=f2[0:np_, :].rearrange("p (h d) -> p h d", d=64),
                                                    in0=xtk[0:np_, :].rearrange("p (h d) -> p h d", d=64),
                                                    in1=smv('b_d', np_).unsqueeze(2).to_broadcast([np_, 8, 64]), op=ALU.mult),
                  r=['xtk', 'small'], w=['f2'])
            pg.op('dve', lambda e: e.tensor_tensor(out=f1[0:np_, :], in0=f1[0:np_, :], in1=f2[0:np_, :], op=ALU.add),
                  r=['f1', 'f2'], w=['f1'])
            pg.op('dve', lambda e: e.tensor_tensor(out=f1[0:np_, :], in0=f1[0:np_, :], in1=sz[0:np_, t, :], op=ALU.mult),
                  r=['f1', 'szb'], w=['f1'])
            ssv = st8[0:np_, 4, 0:1]
            pg.op('act', lambda e: e.activation(out=sq[0:np_, 0:512], in_=f1[0:np_, :], func=AF.Square, accum_out=ssv),
                  r=['f1'], w=['sq', 'st8'])
            rstd_from_ss(ssv, ssv, 512, np_, 'st8')
            pg.op('dve', lambda e: e.scalar_tensor_tensor(out=oz[0:np_, t, :], in0=f1[0:np_, :], scalar=ssv,
                                                          in1=smv('b_norm', np_), op0=ALU.mult, op1=ALU.mult),
                  r=['f1', 'st8', 'small'], w=['oz'])
            pg.op('dve', lambda e: e.tensor_tensor(
                out=bdec[0:np_, :, :, :].rearrange("p a g n -> p g a n"),
                in0=btk[0:np_, :].rearrange("p (g n) -> p g n", n=64).unsqueeze(2).to_broadcast([np_, 2, 4, 64]),
                in1=ssd8[0:np_, 1, :].rearrange("p (g a) -> p g a", a=4).unsqueeze(3).to_broadcast([np_, 2, 4, 64]),
                op=ALU.mult), r=['btk', 'ssd8'], w=['bdec'])
            i = rot('A'); ph = psA[i]; phk = 'psA%d' % i
            for a4 in range(4):
                pg.op('pe', lambda e, a4=a4: e.matmul(
                    ph[:, a4 * 128:(a4 + 1) * 128], lhsT=bdec[0:np_, a4, :, :].rearrange("p g n -> p (g n)"),
                    rhs=xdt[0:np_, :, :].rearrange("p (g a) d -> p a g d", a=4)[:, a4, :, :],
                    start=True, stop=True), r=['bdec', 'xdt'], w=[phk], inc=(a4 == 3))
            phv = ph[:, :].rearrange("p (a c) -> p a c", c=128)
            for g in range(2):
                sl = slice(g * 64, (g + 1) * 64)
                pg.op('dve', lambda e, g=g, sl=sl: e.tensor_tensor(
                    out=hT[sl, :, :], in0=hT[sl, :, :],
                    in1=ssd8[sl, 2, g * 4:(g + 1) * 4].unsqueeze(2).to_broadcast([64, 4, 64]), op=ALU.mult),
                    r=['hT', 'ssd8', py2k, py3k], w=['hT'])
                pg.op('dve', lambda e, g=g, sl=sl: e.tensor_tensor(
                    out=hT[sl, :, :], in0=hT[sl, :, :], in1=phv[sl, :, g * 64:(g + 1) * 64], op=ALU.add),
                    r=['hT', phk], w=['hT'])
            pg.op('act', lambda e: e.activation(out=hTb[:, :, :], in_=hT[:, :, :], func=AF.Copy), r=['hT'], w=['hTb'])

        BS = [dict(Ah=Ah, Ee=Ee, Gm=Gm, bdec=bdec, xdt=xdt, xtk=xtk, Mh=Mh, btk=btk,
                   e0=ssd8[:, 0, :], e1=ssd8[:, 1, :], e2=ssd8[:, 2, :], sfx=''),
              dict(Ah=xnb.bitcast(F32).rearrange("p (h n) -> p h n", n=128),
                   Ee=f3.rearrange("p (h n) -> p h n", n=128),
                   Gm=biasQ[:, :, :].rearrange("p a b -> p (a b)").rearrange("p (g n) -> p g n", n=128),
                   Mh=qT[:, 0:2, :].rearrange("p a n -> p (a n)").rearrange("p (h n) -> p h n", n=128),
                   bdec=qT[:, 2, :].rearrange("p (a g n) -> p a g n", a=4, g=2),
                   xdt=qT[:, 3, :].rearrange("p (h d) -> p h d", d=64),
                   xtk=PT[0][:, :], btk=PT[1][:, 0:128],
                   e0=st8[:, 5, :], e1=st8[:, 6, :], e2=st8[:, 7, :], sfx='_1')]
        SET1K = ['Ah%d_1' % h for h in range(4)] + ['Ee_1', 'Gm_1', 'bdec_1', 'xdt_1', 'xtk_1', 'Mh_1', 'btk_1', 'ssd8_1',
                                                     'qT', 'PT0', 'PT1', 'biasQ', 'xnb', 'f3', 'st8lf', 'st8c', 'st8x']

        def ssdA(t, np_, S):
            b = BS[S]; x_ = b['sfx']
            K = lambda n: n + x_
            c0 = t * np_
            i = rot('T'); pt = psT[i]; pk = 'psT%d' % i
            for j in range(5):
                pg.op('pe', lambda e, j=j: e.transpose(out=pt[0:np_, j * 128:(j + 1) * 128], in_=xbc[:, j, c0:c0 + np_],
                                                       identity=ident_b[:, :]), r=['xbc', 'cstb'], w=[pk], inc=(j == 4))
            yield
            pg.op('act', lambda e: e.activation(out=b['xtk'][0:np_, :], in_=pt[0:np_, 0:512], func=AF.Copy), r=[pk], w=[K('xtk')])
            pg.op('act', lambda e: e.activation(out=b['btk'][0:np_, :], in_=pt[0:np_, 512:640], func=AF.Copy), r=[pk], w=[K('btk')])
            yield
            pg.op('dve', lambda e: e.tensor_tensor(out=b['xdt'][0:np_, :, :], in0=pt[0:np_, 0:512].rearrange("p (h d) -> p h d", d=64),
                                                   in1=dts[0:np_, t, :].unsqueeze(2).to_broadcast([np_, 8, 64]), op=ALU.mult),
                  r=[pk, 'dts'], w=[K('xdt')])
            yield
            a_ap = asb[0:np_, t, :]
            i = rot('A'); p = psA[i]; pk2 = 'psA%d' % i
            pg.op('pe', lambda e: e.matmul(p[0:np_, 0:8], lhsT=M1f[0:np_, 0:np_], rhs=a_ap, start=True, stop=True),
                  r=['asb', 'cst'], w=[pk2], inc=False)
            pg.op('pe', lambda e: e.matmul(p[0:np_, 8:16], lhsT=M2f[0:np_, 0:np_], rhs=a_ap, start=True, stop=True),
                  r=['asb', 'cst'], w=[pk2], inc=False)
            pg.op('pe', lambda e: e.matmul(p[:, 16:24], lhsT=M1f[0:np_, :], rhs=a_ap, start=True, stop=False),
                  r=['asb', 'cst'], w=[pk2], inc=False)
            pg.op('pe', lambda e: e.matmul(p[:, 16:24], lhsT=M2f[0:np_, :], rhs=a_ap, start=False, stop=True),
                  r=['asb', 'cst'], w=[pk2])
            yield
            pg.op('act', lambda e: e.activation(out=b['e0'][0:np_, :], in_=p[0:np_, 0:8], func=AF.Exp), r=[pk2], w=[K('ssd8')])
            pg.op('act', lambda e: e.activation(out=b['e1'][0:np_, :], in_=p[0:np_, 8:16], func=AF.Exp), r=[pk2], w=[K('ssd8')])
            pg.op('act', lambda e: e.activation(out=b['e2'][:, :], in_=p[:, 16:24], func=AF.Exp), r=[pk2], w=[K('ssd8')])
            yield
            for g in range(2):
                i = rot('A'); pgm = psA[i]; pk3 = 'psA%d' % i
                pg.op('pe', lambda e, g=g, pgm=pgm: e.matmul(pgm[0:np_, 0:np_], lhsT=xbc[g * 64:(g + 1) * 64, 4, c0:c0 + np_],
                                                    rhs=xbc[g * 64:(g + 1) * 64, 5, c0:c0 + np_], start=True, stop=True),
                      r=['xbc'], w=[pk3])
                pg.op('dve', lambda e, g=g, pgm=pgm: e.tensor_tensor(
                    out=b['Gm'][0:np_, g, 0:np_], in0=pgm[0:np_, 0:np_], in1=M1f[0:np_, 0:np_], op=ALU.mult),
                    r=[pk3, 'cst'], w=[K('Gm')])
                yield
            for hh in range(2):
                for h4 in range(4):
                    h = hh * 4 + h4
                    pg.op('dve', lambda e, h=h, h4=h4: e.tensor_scalar(
                        out=b['Ah'][0:np_, h4, 0:np_], in0=M2f[0:np_, 0:np_], scalar1=asb[0:np_, t, h:h + 1], scalar2=None,
                        op0=ALU.mult), r=['cst', 'asb'], w=[K('Ah%d' % h4)])
                    if h4 % 2:
                        yield
                psx = psS[hh]; psk = 'psS%d' % hh
                for h4 in range(4):
                    pg.op('pe', lambda e, h4=h4, psx=psx: e.matmul(
                        psx[0:np_, h4 * 128:h4 * 128 + np_], lhsT=b['Ah'][0:np_, h4, 0:np_], rhs=M1f[0:np_, 0:np_],
                        start=True, stop=True), r=[K('Ah%d' % h4), 'cst'], w=[psk], inc=(h4 == 3))
                yield
                pg.op('act', lambda e, psx=psx: e.activation(
                    out=b['Ee'][0:np_, :, 0:np_],
                    in_=psx[0:np_, :].rearrange("p (h n) -> p h n", n=128)[:, :, 0:np_], func=AF.Exp),
                    r=[psk], w=[K('Ee')])
                yield
                pg.op('dve', lambda e, hh=hh: e.tensor_tensor(
                    out=b['Mh'][0:np_, hh * 4:(hh + 1) * 4, 0:np_], in0=b['Ee'][0:np_, :, 0:np_],
                    in1=b['Gm'][0:np_, hh, 0:np_].unsqueeze(1).to_broadcast([np_, 4, np_]), op=ALU.mult),
                    r=[K('Ee'), K('Gm')], w=[K('Mh')])
                yield
            pg.op('dve', lambda e: e.tensor_tensor(
                out=b['bdec'][0:np_, :, :, :].rearrange("p a g n -> p g a n"),
                in0=b['btk'][0:np_, :].rearrange("p (g n) -> p g n", n=64).unsqueeze(2).to_broadcast([np_, 2, 4, 64]),
                in1=b['e1'][0:np_, :].rearrange("p (g a) -> p g a", a=4).unsqueeze(3).to_broadcast([np_, 2, 4, 64]),
                op=ALU.mult), r=[K('btk'), K('ssd8')], w=[K('bdec')])
            yield

        def ssdB(t, np_, S):
            b = BS[S]; x_ = b['sfx']
            K = lambda n: n + x_
            c0 = t * np_
            io = rot('O'); py = psO[io]; pyk = 'psO%d' % io
            io2 = rot('O'); py2 = psO[io2]; py2k = 'psO%d' % io2
            for h in range(8):
                pg.op('pe', lambda e, h=h: e.matmul(py[0:np_, h * 64:(h + 1) * 64], lhsT=b['Mh'][0:np_, h, 0:np_],
                                                    rhs=b['xdt'][0:np_, h, :], start=True, stop=True),
                      r=[K('Mh'), K('xdt')], w=[pyk], inc=(h == 7))
            yield
            i3 = rot('A'); py3 = psA[i3]; py3k = 'psA%d' % i3
            for h in range(8):
                g = h // 4
                dstp = (py2 if g == 0 else py3)
                pg.op('pe', lambda e, h=h, g=g, dstp=dstp: e.matmul(dstp[0:np_, (h % 4) * 64:(h % 4 + 1) * 64],
                                                         lhsT=xbc[g * 64:(g + 1) * 64, 5, c0:c0 + np_],
                                                         rhs=hTb[g * 64:(g + 1) * 64, h % 4, :], start=True, stop=True),
                      r=['xbc', 'hTb'], w=[py2k if g == 0 else py3k], inc=(h % 4 == 3))
            yield
            yv = f1[0:np_, :].rearrange("p (h d) -> p h d", d=64)
            for g, (dstp, dk) in enumerate(((py2, py2k), (py3, py3k))):
                pg.op('dve', lambda e, g=g, dstp=dstp: e.tensor_tensor(
                    out=yv[:, g * 4:(g + 1) * 4, :], in0=dstp[0:np_, 0:256].rearrange("p (h d) -> p h d", d=64),
                    in1=b['e0'][0:np_, g * 4:(g + 1) * 4].unsqueeze(2).to_broadcast([np_, 4, 64]), op=ALU.mult),
                    r=[dk, K('ssd8')], w=['f1'])
                yield
            pg.op('dve', lambda e: e.tensor_tensor(out=f1[0:np_, :], in0=f1[0:np_, :], in1=py[0:np_, :], op=ALU.add),
                  r=['f1', pyk], w=['f1'])
            pg.op('pool', lambda e: e.tensor_tensor(out=f2[0:np_, :].rearrange("p (h d) -> p h d", d=64),
                                                    in0=b['xtk'][0:np_, :].rearrange("p (h d) -> p h d", d=64),
                                                    in1=smv('b_d', np_).unsqueeze(2).to_broadcast([np_, 8, 64]), op=ALU.mult),
                  r=[K('xtk'), 'small'], w=['f2'])
            yield
            pg.op('dve', lambda e: e.tensor_tensor(out=f1[0:np_, :], in0=f1[0:np_, :], in1=f2[0:np_, :], op=ALU.add),
                  r=['f1', 'f2'], w=['f1'])
            yield
            pg.op('dve', lambda e: e.tensor_tensor(out=f1[0:np_, :], in0=f1[0:np_, :], in1=sz[0:np_, t, :], op=ALU.mult),
                  r=['f1', 'szb'], w=['f1'])
            yield
            ssv = st8[0:np_, 4, 0:1]
            pg.op('act', lambda e: e.activation(out=sq[0:np_, 0:512], in_=f1[0:np_, :], func=AF.Square, accum_out=ssv),
                  r=['f1'], w=['sq', 'st8'])
            yield
            pg.op('dve', lambda e: e.tensor_scalar(out=ssv, in0=ssv, scalar1=1.0 / 512, scalar2=EPS,
                                                    op0=ALU.mult, op1=ALU.add), r=['st8'], w=['st8'])
            yield
            pg.op('pool', lambda e: e.tensor_tensor(out=ssv, in0=ssv, in1=nhalf[0:np_, 0:1], op=ALU.pow),
                  r=['st8', 'epst'], w=['st8'])
            yield
            pg.op('dve', lambda e: e.scalar_tensor_tensor(out=oz[0:np_, t, :], in0=f1[0:np_, :], scalar=ssv,
                                                          in1=smv('b_norm', np_), op0=ALU.mult, op1=ALU.mult),
                  r=['f1', 'st8', 'small'], w=['oz'])
            yield
            i = rot('A'); ph = psA[i]; phk = 'psA%d' % i
            for a4 in range(4):
                pg.op('pe', lambda e, a4=a4: e.matmul(
                    ph[:, a4 * 128:(a4 + 1) * 128], lhsT=b['bdec'][0:np_, a4, :, :].rearrange("p g n -> p (g n)"),
                    rhs=b['xdt'][0:np_, :, :].rearrange("p (g a) d -> p a g d", a=4)[:, a4, :, :],
                    start=True, stop=True), r=[K('bdec'), K('xdt')], w=[phk], inc=(a4 == 3))
            yield
            phv = ph[:, :].rearrange("p (a c) -> p a c", c=128)
            for g in range(2):
                sl = slice(g * 64, (g + 1) * 64)
                pg.op('dve', lambda e, g=g, sl=sl: e.tensor_tensor(
                    out=hT[sl, :, :], in0=hT[sl, :, :],
                    in1=b['e2'][sl, g * 4:(g + 1) * 4].unsqueeze(2).to_broadcast([64, 4, 64]), op=ALU.mult),
                    r=['hT', K('ssd8'), py2k, py3k], w=['hT'])
                yield
                pg.op('dve', lambda e, g=g, sl=sl: e.tensor_tensor(
                    out=hT[sl, :, :], in0=hT[sl, :, :], in1=phv[sl, :, g * 64:(g + 1) * 64], op=ALU.add),
                    r=['hT', phk], w=['hT'])
                yield
            pg.op('act', lambda e: e.activation(out=hTb[:, :, :], in_=hT[:, :, :], func=AF.Copy), r=['hT'], w=['hTb'])
            yield

        def ssd_pipelined(NT, np_):
            barrier(AEK + SET1K)
            for _ in ssdA(0, np_, 0):
                pass
            for t in range(NT):
                gens = [ssdB(t, np_, t % 2)]
                if t + 1 < NT:
                    gens.append(ssdA(t + 1, np_, (t + 1) % 2))
                while gens:
                    for g_ in list(gens):
                        try:
                            next(g_)
                        except StopIteration:
                            gens.remove(g_)
            barrier(AEK + SET1K)

        def process_group(l, kind, Q):
            samp = (kind == 's')
            np_ = 64 if samp else 128
            NT = 4
            NTOK = NT * np_
            xsrc = (D['xs'] if samp else D['xp']) if l == 0 else (x1s if samp else x1p)
            xdst = (D['ys'] if samp else D['yp']) if l == NL - 1 else (x1s if samp else x1p)
            row0 = 0 if samp else Q * 512

            def rows(t):
                return slice(row0 + t * np_, row0 + (t + 1) * np_)

            norm_tiles(lambda t: xsrc[rows(t), :], NT, np_, lambda n: smv('g_norm', n),
                       (lambda t: [('xd', kind, row0 + t * np_)]) if l > 0 else None)

            if cfg.get('marks'): print('MARK', kind, Q, 'C_proj', pg.nops)
            wt, wk = load_w(D['w_in'], l, COL['c_q'], 512)
            for t, p, pk in pipe_tiles(NT, lambda t: proj_tm(xnT, 'xnT', t * np_, np_, wt, wk, 512)):
                evac_q(p, pk, t, np_, 8, 'c_qnorm')
            wt, wk = load_w(D['w_in'], l, COL['c_k'], 512)
            if samp:
                kTs = sb_kTs
            for t, p, pk in pipe_tiles(NT, lambda t: proj_tm(xnT, 'xnT', t * np_, np_, wt, wk, 512)):
                if samp:
                    evac_k(p, pk, np_, 8, 'c_knorm', D['sc_k'][l, rows(t), :],
                           lambda t=t: kTs[:, 0:4, t * 64:(t + 1) * 64], 'kTs')
                else:
                    gt = Q * 4 + t
                    evac_k(p, pk, np_, 8, 'c_knorm', D['pc_k'][l, rows(t), :],
                           lambda gt=gt: kT_c[:, :, gt * 128:(gt + 1) * 128], 'kT_c')
            wt, wk = load_w(D['w_in'], l, COL['c_v'], 512)
            for t, p, pk in pipe_tiles(NT, lambda t: proj_tm(xnT, 'xnT', t * np_, np_, wt, wk, 512)):
                if samp:
                    evac_v(p, pk, np_, 8, D['sc_v'][l, rows(t), :], vas[0:np_, t, :, 0:64], 'vas')
                else:
                    evac_v(p, pk, np_, 8, D['pc_v'][l, rows(t), :], va_c[:, Q * 4 + t, :, 0:64], 'va_c')
            wt, wk = load_w(D['w_in'], l, COL['c_f'], 8)
            if not samp:
                psv, pk = proj8_all(NT, np_, wt, wk)
                softplus4(psv, pk, smv('c_fbias', np_), lfs[0:np_, :, :], 'lfs', np_, NT, neg=True)
                out_tokens.append(pg.dma('sp', D['pc_f'][l, row0:row0 + NT * np_, :].rearrange("(t p) h -> p t h", p=np_),
                                         lfs[0:np_, :, :], r=['lfs'], nonc=True))
                for t in range(NT):
                    gt = Q * 4 + t
                    cumsum_tile(lfs[0:np_, t, :], 'lfs', 128, cum[:, gt, :], 'cum',
                                None if gt == 0 else (S127f, cum[:, gt - 1, :]), ['cum'])
            for t, p, pk in (pipe_tiles(NT, lambda t: proj_tm(xnT, 'xnT', t * np_, np_, wt, wk, 8)) if samp else ()):
                lf = st8[0:np_, 5, :]
                softplus_from(p, pk, smv('c_fbias', np_), lf, 'st8lf', np_, neg=True)
                if samp:
                    out_tokens.append(pg.dma('sp', D['sc_f'][l, rows(t), :], lf, r=['st8lf'], nonc=True))
                    stg8 = cum[:, 0:8, :].rearrange("p k h -> p (k h)")
                    totb = cum[:, 8:16, :]
                    car = cum[:, 16:24, :]
                    pg.dma('sp', cum[:, 0:8, :], D['cc_f'][l, t].rearrange("(k p) h -> p k h", p=128), w=['cum'], nonc=True)
                    i1 = rot('S'); q1 = psS[i1]; q1k = 'psS%d' % i1
                    pg.op('pe', lambda e, q1=q1: e.matmul(q1[:, 0:64], lhsT=M1f, rhs=stg8, start=True, stop=True),
                          r=['cum', 'cst'], w=[q1k])
                    loc = cums[:, t, 0:8, :]
                    pg.op('dve', lambda e, q1=q1, loc=loc: cp(e, out=loc, in_=q1[:, 0:64].rearrange("p (k h) -> p k h", h=8)),
                          r=[q1k], w=['cums'])
                    i2 = rot('S'); q2 = psS[i2]; q2k = 'psS%d' % i2
                    pg.op('pe', lambda e, q2=q2, loc=loc: e.matmul(q2[:, 0:64], lhsT=S127f, rhs=loc.rearrange("p k h -> p (k h)"),
                                                                 start=True, stop=True), r=['cums', 'cst'], w=[q2k])
                    pg.op('act', lambda e, q2=q2: e.activation(out=totb, in_=q2[:, 0:64].rearrange("p (k h) -> p k h", h=8),
                                                              func=AF.Copy), r=[q2k], w=['cum'])
                    pg.op('pool', lambda e: e.memset(car[:, 0, :], 0.0), w=['cum'])
                    for kb in range(1, 8):
                        pg.op('dve', lambda e, kb=kb: e.tensor_tensor(out=car[:, kb, :], in0=car[:, kb - 1, :], in1=totb[:, kb - 1, :],
                                                                      op=ALU.add), r=['cum'], w=['cum'])
                    pg.op('dve', lambda e, loc=loc: e.tensor_tensor(out=loc, in0=loc, in1=car, op=ALU.add), r=['cums', 'cum'], w=['cums'])
                    cumsum_tile(lf, 'st8lf', 64, cums[0:64, t, 8, :], 'cums', (S127f[:, 0:64], cums[:, t, 7, :]), ['cums'])
                else:
                    gt = Q * 4 + t
                    out_tokens.append(pg.dma('sp', D['pc_f'][l, rows(t), :], lf, r=['st8lf'], nonc=True))
                    cumsum_tile(lf, 'st8lf', 128, cum[:, gt, :], 'cum',
                                None if gt == 0 else (S127f, cum[:, gt - 1, :]), ['cum'])
            wt, wk = load_w(D['w_in'], l, COL['c_z'], 512)
            for t, p, pk in pipe_tiles(NT, lambda t: proj_tm(xnT, 'xnT', t * np_, np_, wt, wk, 512)):
                pg.op('act', lambda e, p=p, t=t: e.activation(out=sz[0:np_, t, :], in_=p, func=AF.Silu), r=[pk], w=['sz'])
            if cfg.get('marks'): print('MARK', kind, Q, 'C_attn', pg.nops)
            if not samp:
                nkb = 4 * Q + 4
                i = rot('A'); pb = psA[i]; pbk = 'psA%d' % i
                pg.op('pe', lambda e: e.matmul(pb[:, 0:8], lhsT=S127f, rhs=cum[:, nkb - 1, :], start=True, stop=True),
                      r=['cum', 'cst'], w=[pbk])
                pg.op('act', lambda e: e.activation(out=st8[:, 6, :], in_=pb[:, 0:8], func=AF.Copy), r=[pbk], w=['st8c'])
                pg.op('dve', lambda e: e.tensor_tensor(out=biasQ[:, 0:nkb, :],
                                                       in0=st8[:, 6, :].unsqueeze(1).to_broadcast([128, nkb, 8]),
                                                       in1=cum[:, 0:nkb, :], op=ALU.subtract), r=['st8c', 'cum'], w=['biasQ'])
                for h in range(8):
                    hp, hb = h // 2, (h % 2) * 64
                    kbs = []
                    for kb in range(nkb):
                        dI = kb - 4 * Q
                        d = dict(kT=kT_c[hb:hb + 64, hp, kb * 128:(kb + 1) * 128], v=va_c[:, kb, h, 0:65], nk=128,
                                 bias=biasQ[:, kb, h:h + 1], bkeys=['biasQ'], q0=max(dI, 0) * 128, keys=['kT_c', 'va_c'])
                        kbs.append(d)
                    kb2 = []
                    for kb, d in enumerate(kbs):
                        dI = kb - 4 * Q
                        if dI < 0:
                            kb2.append(d)
                        else:
                            d['diag'] = True
                            kb2.append(d)
                    attend_fox(qT[hb:hb + 64, hp, :], 512, [(j * 128, 128) for j in range(4)], kb2, ['qT'],
                               attn_out(h * 64, 4, 128))
            else:
                for t in range(NT):
                    i = rot('A'); pb = psA[i]; pbk = 'psA%d' % i
                    pg.op('pe', lambda e, t=t: e.matmul(pb[:, 0:8], lhsT=S63f[0:64, :], rhs=cums[0:64, t, 8, :],
                                                        start=True, stop=True), r=['cums', 'cst'], w=[pbk])
                    pg.op('act', lambda e: e.activation(out=st8[:, 6, :], in_=pb[:, 0:8], func=AF.Copy), r=[pbk], w=['st8c'])
                    pg.op('dve', lambda e, t=t: e.tensor_tensor(out=biasQ[:, 0:9, :],
                                                                in0=st8[:, 6, :].unsqueeze(1).to_broadcast([128, 9, 8]),
                                                                in1=cums[:, t, :, :], op=ALU.subtract),
                          r=['st8c', 'cums'], w=['biasQ'])
                    load_cache_kv(D['cc_k'][l, t], D['cc_v'][l, t], 8, 8)
                    for h in range(8):
                        hp, hb = h // 2, (h % 2) * 64
                        kbs = []
                        for kb in range(8):
                            kbs.append(dict(kT=ckT[hb:hb + 64, hp, kb * 128:(kb + 1) * 128], v=cva_[:, kb, h, 0:65], nk=128,
                                            bias=biasQ[:, kb, h:h + 1], bkeys=['biasQ'], q0=0, keys=['ckT', 'cvaS']))
                        kbs.append(dict(kT=kTs[hb:hb + 64, hp, t * 64:(t + 1) * 64], v=vas[0:64, t, h, 0:65], nk=64,
                                        bias=biasQ[0:64, 8, h:h + 1], bkeys=['biasQ'], q0=0, keys=['kTs', 'vas'], diag=True))
                        attend_fox(qT[hb:hb + 64, hp, t * 64:(t + 1) * 64], 64, [(0, 64)], kbs, ['qT'],
                                   attn_out_s(h * 64, t))
            oz_to_T('c', NT, np_, 4)

            if cfg.get('marks'): print('MARK', kind, Q, 'A_proj', pg.nops)
            wt, wk = load_w(D['w_in'], l, COL['a_q'], 512)
            for t, p, pk in pipe_tiles(NT, lambda t: proj_tm(xnT, 'xnT', t * np_, np_, wt, wk, 512)):
                evac_q(p, pk, t, np_, 8, 'a_qnorm')
            wt, wk = load_w(D['w_in'], l, COL['a_k'], 512)
            for t, p, pk in pipe_tiles(NT, lambda t: proj_tm(xnT, 'xnT', t * np_, np_, wt, wk, 512)):
                if samp:
                    evac_k(p, pk, np_, 8, 'a_knorm', D['sa_k'][l, rows(t), :],
                           lambda t=t: kTs[:, 0:4, t * 64:(t + 1) * 64], 'kTs')
                else:
                    gt = Q * 4 + t
                    od = D['pa_k'][l, (gt - 28) * 128:(gt - 27) * 128, :] if gt >= 28 else None
                    evac_k(p, pk, np_, 8, 'a_knorm', od,
                           lambda gt=gt: kT_a[:, :, (gt % 8) * 128:(gt % 8 + 1) * 128], 'kT_a')
            wt, wk = load_w(D['w_in'], l, COL['a_v'], 512)
            for t, p, pk in pipe_tiles(NT, lambda t: proj_tm(xnT, 'xnT', t * np_, np_, wt, wk, 512)):
                if samp:
                    evac_v(p, pk, np_, 8, D['sa_v'][l, rows(t), :], vas[0:np_, t, :, 0:64], 'vas')
                else:
                    gt = Q * 4 + t
                    od = D['pa_v'][l, (gt - 28) * 128:(gt - 27) * 128, :] if gt >= 28 else None
                    evac_v(p, pk, np_, 8, od, va_a[:, gt % 8, :, 0:64], 'va_a')
            wt, wk = load_w(D['w_in'], l, COL['a_z'], 512)
            for t, p, pk in pipe_tiles(NT, lambda t: proj_tm(xnT, 'xnT', t * np_, np_, wt, wk, 512)):
                pg.op('act', lambda e, p=p, t=t: e.activation(out=sz[0:np_, t, :], in_=p, func=AF.Silu), r=[pk], w=['sz'])
            for t in range(NT):
                gt = Q * 4 + t
                if samp:
                    load_cache_kv(D['ca_k'][l, t], D['ca_v'][l, t], 4, 8)
                for hq in range(2):
                    acc = None
                    for h4 in range(4):
                        h = hq * 4 + h4
                        hp, hb = h // 2, (h % 2) * 64
                        kbs = []
                        if not samp:
                            for i5 in range(5):
                                gk = gt - 4 + i5
                                if gk < 0:
                                    continue
                                s8 = gk % 8
                                d_ = dict(kT=kT_a[hb:hb + 64, hp, s8 * 128:(s8 + 1) * 128], v=va_a[:, s8, h, 0:65], nk=128,
                                          bias=None, q0=0, keys=['kT_a', 'va_a'])
                                if i5 in (1, 2):
                                    d_['bias'] = expB[:, h, i5 * 128:i5 * 128 + 1]; d_['bkeys'] = ['expB']
                                else:
                                    d_['badd'] = expB[:, h, i5 * 128:(i5 + 1) * 128]; d_['mkeys'] = ['expB']
                                kbs.append(d_)
                        else:
                            for kb in range(4):
                                d_ = dict(kT=ckT[hb:hb + 64, hp, kb * 128:(kb + 1) * 128], v=cva_[:, kb, h, 0:65], nk=128,
                                          bias=None, q0=0, keys=['ckT', 'cvaS'])
                                if kb in (0, 1, 2):
                                    d_['bias'] = expB[:, h, kb * 128:kb * 128 + 1]; d_['bkeys'] = ['expB']
                                else:
                                    d_['badd'] = expB[:, h, kb * 128:kb * 128 + 64]; d_['mkeys'] = ['expB']
                                kbs.append(d_)
                            kbs.append(dict(kT=kTs[hb:hb + 64, hp, t * 64:(t + 1) * 64], v=vas[0:64, t, h, 0:65], nk=64,
                                            bias=None, badd=expB[:, h, 512:576], mkeys=['expB'], q0=0, keys=['kTs', 'vas']))
                        a2 = None if acc is None else (acc[0], acc[1], h4, False)
                        acc = attend(qT[hb:hb + 64, hp, t * np_:(t + 1) * np_], np_, [(0, np_)], kbs, ['qT'], None, acc=a2)
                    pov, pok = acc
                    rl = st8[0:np_, 2, 0:4]
                    pg.op('dve', lambda e, pov=pov: e.reciprocal(out=rl, in_=pov[0:np_, 0:4, 64]), r=[pok], w=['st8'])
                    tmp = f2[0:np_, 0:256].rearrange("p (j d) -> p j d", d=64)
                    pg.op('dve', lambda e, pov=pov, tmp=tmp: e.tensor_tensor(out=tmp, in0=pov[0:np_, 0:4, 0:64],
                                                                           in1=rl.unsqueeze(2).to_broadcast([np_, 4, 64]), op=ALU.mult),
                          r=[pok, 'st8'], w=['f2'])
                    pg.op('dve', lambda e, tmp=tmp, hq=hq, t=t: e.tensor_tensor(
                        out=oz[0:np_, t, hq * 256:(hq + 1) * 256].rearrange("p (j d) -> p j d", d=64), in0=tmp,
                        in1=sz[0:np_, t, hq * 256:(hq + 1) * 256].rearrange("p (j d) -> p j d", d=64), op=ALU.mult),
                        r=['f2', 'sz'], w=['oz'])
            oz_to_T('a', NT, np_, 4)

            if cfg.get('marks'): print('MARK', kind, Q, 'M', pg.nops)
            wt, wk = load_w(D['w_in'], l, COL['m_q'], 512)
            for t, p, pk in pipe_tiles(NT, lambda t: proj_tm(xnT, 'xnT', t * np_, np_, wt, wk, 512)):
                pg.op('act', lambda e, p=p, t=t: e.activation(out=sz[0:np_, t, 0:256], in_=p[:, 256:512], func=AF.Silu),
                      r=[pk], w=['sz'])
                evac_q(p[:, 0:256], pk, t, np_, 4, 'm_qnorm')
            if not samp:
                for h in range(4):
                    hp, hb = h // 2, (h % 2) * 64
                    kbs = [dict(kT=kT_m[hb:hb + 64, hp, kb * 128:(kb + 1) * 128], v=va_m[:, kb, h, 0:65], nk=128, bias=None,
                                q0=0, keys=['kT_m', 'va_m']) for kb in range(2)]
                    attend(qT[hb:hb + 64, hp, :], 512, [(j * 128, 128) for j in range(4)], kbs, ['qT'],
                           attn_out(h * 64, 4, 128))
            else:
                for t in range(NT):
                    load_cache_kv(D['cm_k'][l, t], D['cm_v'][l, t], 2, 4)
                    for h in range(4):
                        hp, hb = h // 2, (h % 2) * 64
                        kbs = [dict(kT=ckT[hb:hb + 64, hp, kb * 128:(kb + 1) * 128], v=cva_[:, kb, h, 0:65], nk=128, bias=None,
                                    q0=0, keys=['ckT', 'cvaS']) for kb in range(2)]
                        attend(qT[hb:hb + 64, hp, t * 64:(t + 1) * 64], 64, [(0, 64)], kbs, ['qT'], attn_out_s(h * 64, t))
            oz_to_T('m', NT, np_, 2)

            if cfg.get('marks'): print('MARK', kind, Q, 'B_proj', pg.nops)
            wt, wk = load_w(D['w_in'], l, COL['b_z'], 512)
            for t, p, pk in pipe_tiles(NT, lambda t: proj_tm(xnT, 'xnT', t * np_, np_, wt, wk, 512)):
                pg.op('act', lambda e, p=p, t=t: e.activation(out=sz[0:np_, t, :], in_=p, func=AF.Silu), r=[pk], w=['sz', 'szb'])
            wt, wk = load_w(D['w_in'], l, COL['b_dt'], 8)
            psv, pk = proj8_all(NT, np_, wt, wk)
            softplus4(psv, pk, smv('b_dt_bias', np_), dts[0:np_, :, :], 'dts', np_, NT)
            pg.op('dve', lambda e: e.tensor_tensor(out=asb[0:np_, :, :], in0=dts[0:np_, :, :],
                                                   in1=aneg[0:np_, :].unsqueeze(1).to_broadcast([np_, NT, 8]), op=ALU.mult),
                  r=['dts', 'aneg'], w=['asb'])
            for half in range(2):
                js = slice(half * 3, half * 3 + 3)
                if samp:
                    for t in range(NT):
                        for jj in range(3):
                            j = half * 3 + jj
                            pg.dma('sp', raw[:, jj, t, 0:3],
                                   D['sb_conv'][l, t][:, j * 128:(j + 1) * 128].rearrange("r p -> p r"), w=['raw'], nonc=True)
                else:
                    pg.op('pool', lambda e, js=js: cp(e, out=raw[:, :, 0, 0:3], in_=halo[:, js, :]), r=['halo'], w=['raw'])
                wt, wk = load_w(D['w_in'], l, COL['b_xbc'] + half * 384, 384)
                for jj in range(3):
                    i = rot('A'); p = psA[i]; pk = 'psA%d' % i
                    for kc in range(8):
                        pg.op('pe', lambda e, kc=kc, jj=jj, p=p, wt=wt: e.matmul(p[:, 0:NTOK], lhsT=wt[:, kc, jj * 128:(jj + 1) * 128],
                                                                     rhs=xnT[:, kc, 0:NTOK], start=(kc == 0), stop=(kc == 7)),
                              r=['xnT'] + list(wk), w=[pk], inc=(kc == 7))
                    pg.op('act', lambda e, jj=jj, p=p: e.activation(out=raw[:, jj, :, 3:3 + np_],
                                                             in_=p[:, 0:NTOK].rearrange("p (t n) -> p t n", n=np_), func=AF.Copy),
                          r=[pk], w=['raw'])
                if not samp:
                    for t in range(1, NT):
                        pg.op('pool', lambda e, t=t: cp(e, out=raw[:, :, t, 0:3], in_=raw[:, :, t - 1, np_:np_ + 3]),
                              r=['raw'], w=['raw'])
                    pg.op('pool', lambda e, js=js: cp(e, out=halo[:, js, :], in_=raw[:, :, NT - 1, np_:np_ + 3]),
                          r=['raw'], w=['halo'])
                    if Q == NQ - 1:
                        for jj in range(3):
                            j = half * 3 + jj
                            out_tokens.append(pg.dma('sp', D['pb_c'][l][:, j * 128:(j + 1) * 128].rearrange("r p -> p r"),
                                                     halo[:, j, :], r=['halo'], nonc=True))
                else:
                    for t in range(NT):
                        for jj in range(3):
                            j = half * 3 + jj
                            out_tokens.append(pg.dma('sp', D['sb_c'][l, t][:, j * 128:(j + 1) * 128].rearrange("r p -> p r"),
                                                     raw[:, jj, t, np_:np_ + 3], r=['raw'], nonc=True))
                for jj in range(3):
                    j = half * 3 + jj
                    cv = cva[:, 0:NTOK].rearrange("p (t n) -> p t n", n=np_)
                    pg.op('dve', lambda e, j=j, jj=jj, cv=cv: e.tensor_scalar(out=cv, in0=raw[:, jj, :, 0:np_], scalar1=cw[:, j, 0:1],
                                                                scalar2=cw[:, j, 4:5], op0=ALU.mult, op1=ALU.add),
                          r=['raw', 'cw'], w=['cva'])
                    for tap in range(1, 4):
                        pg.op('dve', lambda e, j=j, jj=jj, tap=tap, cv=cv: e.scalar_tensor_tensor(
                            out=cv, in0=raw[:, jj, :, tap:tap + np_], scalar=cw[:, j, tap:tap + 1], in1=cv,
                            op0=ALU.mult, op1=ALU.add), r=['raw', 'cw', 'cva'], w=['cva'])
                    pg.op('act', lambda e, j=j: e.activation(out=xbc[:, j, 0:NTOK], in_=cva[:, 0:NTOK], func=AF.Silu),
                          r=['cva'], w=['xbc'])
            if samp:
                barrier(AEK + SET1K)
                for _ in ssdA(0, np_, 0):
                    pass
                for t in range(NT):
                    state_load(D['sb_ssm'][l, t])
                    gens = [ssdB(t, np_, t % 2)]
                    if t + 1 < NT:
                        gens.append(ssdA(t + 1, np_, (t + 1) % 2))
                    while gens:
                        for g_ in list(gens):
                            try:
                                next(g_)
                            except StopIteration:
                                gens.remove(g_)
                    state_store(D['sb_s'][l, t])
                barrier(AEK + SET1K)
            else:
                ssd_pipelined(NT, np_)
            if (not samp) and Q == NQ - 1 and not NOSTATE:
                state_store(D['pb_s'][l])
            oz_to_T('b', NT, np_, 4)

            if cfg.get('marks'): print('MARK', kind, Q, 'merge', pg.nops)
            barrier(AEK)
            brs = (('a', 'w_pa', 4), ('b', 'w_pb', 4), ('c', 'w_pc', 4), ('m', 'w_pm', 2))
            for c in range(2):
                for bi, (br, wn, nkc) in enumerate(brs):
                    wg, wgk = load_w(D['w_in'], l, COL['gate'] + bi * 1024 + c * 512, 512)
                    wp, wpk = load_w(D[wn], l, c * 512, 512, nk=nkc)
                    def mproj(t, wg=wg, wgk=wgk, wp=wp, wpk=wpk, br=br, nkc=nkc):
                        p1, pk1 = proj_tm(xnT, 'xnT', t * np_, np_, wg, wgk, 512)
                        isx = rot('S'); p2 = psS[isx]; pk2 = 'psS%d' % isx
                        for kc in range(nkc):
                            pg.op('pe', lambda e, kc=kc: e.matmul(
                                p2[0:np_, :], lhsT=ozT[br][:, kc, t * np_:(t + 1) * np_], rhs=wp[:, kc, :],
                                start=(kc == 0), stop=(kc == nkc - 1)), r=['ozT_' + br] + list(wpk), w=[pk2], inc=(kc == nkc - 1))
                        return p1, pk1, p2, pk2
                    for t, p1, pk1, p2, pk2 in pipe_tiles(NT, mproj):
                        pg.op('act', lambda e, p1=p1: e.activation(out=sig[0:np_, :], in_=p1, func=AF.Sigmoid), r=[pk1], w=['cva'])
                        if bi == 0:
                            pg.op('dve', lambda e, t=t, p2=p2: e.tensor_tensor(out=macc[0:np_, t, :], in0=sig[0:np_, :],
                                                                               in1=p2[0:np_, :], op=ALU.mult),
                                  r=['cva', pk2], w=['macc%d' % t])
                        else:
                            pg.op('dve', lambda e, t=t, p2=p2: e.tensor_tensor(out=f1[0:np_, :], in0=sig[0:np_, :],
                                                                               in1=p2[0:np_, :], op=ALU.mult),
                                  r=['cva', pk2], w=['f1'])
                            pg.op('pool', lambda e, t=t: e.tensor_tensor(out=macc[0:np_, t, :], in0=macc[0:np_, t, :],
                                                                         in1=f1[0:np_, :], op=ALU.add),
                                  r=['f1', 'macc%d' % t], w=['macc%d' % t])
                for t in range(NT):
                    pg.op('act', lambda e, t=t: e.activation(out=b1[0:np_, :], in_=macc[0:np_, t, :], func=AF.Copy),
                          r=['macc%d' % t], w=['b1'])
                    transposes(lambda j: b1[0:np_, j * 128:(j + 1) * 128], 4, np_,
                               lambda e, v, t=t, c=c: cp(e, out=mT[:, c * 4:(c + 1) * 4, t * np_:(t + 1) * np_], in_=v),
                               ['b1'], ['xbc', 'Mh'], evac='dve')
            wos = [load_w(D['w_out'], l, c * 512, 512) for c in range(2)]
            for t in range(NT):
                i = rot('X'); xk = 'xt%d' % i
                pg.dma('sp', xt[0:np_, i, :], xsrc[rows(t), :], r=[('xd', kind, row0 + t * np_)] if l > 0 else [], w=[xk])
                for c in range(2):
                    wo, wok = wos[c]
                    p, pk = proj_tm(mT, 'xbc', t * np_, np_, wo, wok, 512, xkeys=['Mh'])
                    pg.op('dve', lambda e, i=i, p=p, c=c: e.tensor_tensor(out=xt[0:np_, i, c * 512:(c + 1) * 512],
                                                                          in0=xt[0:np_, i, c * 512:(c + 1) * 512], in1=p,
                                                                          op=ALU.add), r=[xk, pk], w=[xk])
                tok = pg.dma('sp', xdst[rows(t), :], xt[0:np_, i, :], r=[xk], w=[('xd', kind, row0 + t * np_)])
                out_tokens.append(tok)

        vflat = va_c[:, :, :, :].rearrange("p a h c -> p (a h c)")
        sb_kTs = vflat[:, 0:1024].rearrange("p (k n) -> p k n", n=256)
        vas = vflat[:, 1024:1024 + 2112].rearrange("p (t h c) -> p t h c", t=4, h=8)
        ckT = vflat[:, 3136:3136 + 4096].rearrange("p (k n) -> p k n", n=1024)
        cva_ = vflat[:, 7232:7232 + 4224].rearrange("p (a h c) -> p a h c", a=8, h=8)
        cums = kT_a[:, 0, 0:576].bitcast(F32).rearrange("p (t k h) -> p t k h", t=4, k=9)
        SKEYS = ['kTs', 'vas', 'ckT', 'cvaS']

        def load_cache_kv(kd, vd, nblk, nh):
            n = nh * 64
            for kb in range(nblk):
                stg, sk = ((b1, 'b1'), (xnb, 'xnb'))[kb % 2]
                pg.dma('pool', stg[:, 0:n], kd[kb * 128:(kb + 1) * 128, :], w=[sk])
                transposes(lambda j, stg=stg: stg[:, j * 128:(j + 1) * 128], nh // 2, 128,
                           lambda e, v, kb=kb: cp(e, out=ckT[:, 0:nh // 2, kb * 128:(kb + 1) * 128], in_=v),
                           [sk], ['ckT'], evac='act' if kb % 2 else 'dve')
                pg.dma('pool', cva_[:, kb, 0:nh, 0:64], vd[kb * 128:(kb + 1) * 128, :].rearrange("p (h d) -> p h d", d=64),
                       w=['cvaS'])

        def state_load(src):
            pg.dma('sp', cva[0:64, :].rearrange("p (h n) -> p h n", n=64), src.rearrange("h p n -> p h n"), w=['cva'])
            i = rot('A'); p = psA[i]; pk = 'psA%d' % i
            for h in range(8):
                pg.op('pe', lambda e, h=h: e.matmul(p[0:64, h * 64:(h + 1) * 64], lhsT=cva[0:64, h * 64:(h + 1) * 64],
                                                    rhs=identf[0:64, :], start=True, stop=True),
                      r=['cva', 'identf'], w=[pk], inc=(h == 7))
            for g in range(2):
                pg.op('act' if g else 'dve', lambda e, g=g: cp(
                    e, out=hT[g * 64:(g + 1) * 64, :, :], in_=p[0:64, g * 256:(g + 1) * 256].rearrange("p (a d) -> p a d", d=64)),
                    r=[pk], w=['hT'])
            pg.op('act', lambda e: e.activation(out=hTb[:, :, :], in_=hT[:, :, :], func=AF.Copy), r=['hT'], w=['hTb'])

        def state_store(dst):
            for g in range(2):
                i = rot('A'); p = psA[i]; pk = 'psA%d' % i
                for a in range(4):
                    pg.op('pe', lambda e, g=g, a=a, p=p: e.matmul(p[0:64, a * 64:(a + 1) * 64], lhsT=hT[g * 64:(g + 1) * 64, a, :],
                                                             rhs=identf[g * 64:(g + 1) * 64, :], start=True, stop=True),
                          r=['hT', 'identf'], w=[pk], inc=(a == 3))
                pg.op('act' if g else 'dve', lambda e, g=g, p=p: cp(e, out=cva[0:64, g * 256:(g + 1) * 256], in_=p[0:64, 0:256]),
                      r=[pk], w=['cva'])
            out_tokens.append(pg.dma('sp', dst.rearrange("h p n -> p h n"), cva[0:64, :].rearrange("p (h n) -> p h n", n=64),
                                     r=['cva']))

        def attn_out_t(h0col, t, np_):
            def fn(pov, pok):
                rl = st8[0:np_, 2, 0:1]
                pg.op('dve', lambda e: e.reciprocal(out=rl, in_=pov[0:np_, 0, 64:65]), r=[pok], w=['st8'])
                pg.op('dve', lambda e: e.scalar_tensor_tensor(out=oz[0:np_, t, h0col:h0col + 64], in0=pov[0:np_, 0, 0:64],
                                                              scalar=rl, in1=sz[0:np_, t, h0col:h0col + 64],
                                                              op0=ALU.mult, op1=ALU.mult), r=[pok, 'st8', 'sz'], w=['oz'])
            return fn

        def attn_out_s(h0col, t):
            return attn_out_t(h0col, t, 64)

        def attend_fox(qT_ap, NQc, qtiles, kblocks, qkeys, out_fn):
            for kb in kblocks:
                if kb.get('diag'):
                    kb['diagmask'] = True
            attend(qT_ap, NQc, qtiles, kblocks, qkeys, out_fn)

        def memory_kv(l):
            def src(t):
                return D['memp'][t * 128:(t + 1) * 128, :]
            barrier(AEK)
            pg.dma('sp', AE[:, 0:1024], D['mnorm'][l:l + 1, :].broadcast_to([128, 1024]), w=AEK)
            norm_tiles(src, 2, 128, lambda n: AE[0:n, 0:1024], lambda t: AEK)
            barrier(AEK)
            wt, wk = load_w(D['w_mkv'], l, 0, 512)
            for t, p, pk in pipe_tiles(2, lambda t: proj_tm(xnT, 'xnT', t * 128, 128, wt, wk, 512)):
                evac_v(p[:, 256:512], pk, 128, 4, D['pm_v'][l, t * 128:(t + 1) * 128, :], va_m[:, t, :, 0:64], 'va_m')
                evac_k(p[:, 0:256], pk, 128, 4, 'm_knorm', D['pm_k'][l, t * 128:(t + 1) * 128, :],
                       lambda t=t: kT_m[:, :, t * 128:(t + 1) * 128], 'kT_m')

        try:
          for l in range(NL):
            layer_consts(l)
            memory_kv(l)
            pg.op('pool', lambda e: e.memset(hT[:], 0.0), w=['hT'])
            pg.op('pool', lambda e: e.memset(hTb[:], 0.0), w=['hTb'])
            pg.op('pool', lambda e: e.memset(halo[:], 0.0), w=['halo'])
            for Q in range(NQ):
                process_group(l, 'p', Q)
            if DO_S:
                barrier(['va_c', 'kT_a', 'cums'] + SKEYS)
                pg.op('pool', lambda e: e.memset(vas, 1.0), w=['vas'])
                pg.op('pool', lambda e: e.memset(cva_, 1.0), w=['cvaS'])
                process_group(l, 's', 0)
                barrier(['va_c', 'kT_a', 'cums'] + SKEYS)
                pg.op('pool', lambda e: e.memset(va_c[:], 1.0), w=['va_c'])
        except StopBuild:
            pass
        pg.maxops = None
        pg.op('pe', lambda e: e.matmul(psA[0][0:1, 0:1], lhsT=cstb[:, 0:1], rhs=cstb[:, 0:1], start=True, stop=True), r=['cstb'], w=['psA0'])
        pg.op('act', lambda e: e.activation(out=bar[:, 1:2], in_=bar[:, 1:2], func=AF.Copy), r=['psA0'], w=['bar2'])
        pg.op('dve', lambda e: e.tensor_copy(out=bar[:, 1:2], in_=bar[:, 1:2]), w=['bar2'])
        pg.op('pool', lambda e: e.tensor_copy(out=bar[:, 1:2], in_=bar[:, 1:2]), w=['bar2'])
        out_tokens.append(pg.dma('sp', D['pb_c'][0, 0:1, 0:2], bar[0:1, 0:2], r=['bar2', 'bar'], w=['zz']) if False else None)
        out_tokens[:] = [t for t in out_tokens if t is not None]
        pg.op('pool', lambda e: e.memset(bar[:, 0:1], 0.0), r=['bar2'], w=['bar'])
        pg.finish(out_tokens)
        pg._wait('sp', ('c', 'pool', pg.cnt['pool'] - 1))
        print("ops:", pg.nops, "dmas:", pg.dma_n, "cnt:", pg.cnt)
    return nc


def _consts():
    c = np.zeros((128, NCST), np.float32)
    k = np.arange(128)[:, None]
    m = np.arange(128)[None, :]
    c[:, C_M1:C_M1 + 128] = (k <= m)
    c[:, C_M2:C_M2 + 128] = (k > m)
    c[:, C_S127:C_S127 + 128] = (k == 127)
    c[:, C_S63:C_S63 + 128] = (k == 63)
    band = np.ones((128, 5, 128), np.float32)
    s = np.arange(128)[:, None]
    t = np.arange(128)[None, :]
    band[:, 0, :] = 1.0 - ((s < 64) & (t >= 64))
    band[:, 4, :] = 1.0 - ((s >= 64) & (t < 64))
    ident = (k == m).astype(np.float32)
    return c, np.ascontiguousarray(band.reshape(128, 640)), ident


def _prep(inputs, cfg=None):
    f = lambda a: np.ascontiguousarray(np.asarray(a, dtype=np.float32))
    I = {k: f(v) for k, v in inputs.items()}
    small = np.concatenate([I[n].reshape(2, -1) for n in SM], axis=1)
    s = np.arange(128)[:, None]
    j = np.arange(640)[None, :]
    dist = 512 + (j % 128) - 128 * (j // 128) - s
    idx = np.clip(dist, -128, 128) + 128
    relb = np.ascontiguousarray(np.transpose(I['a_rel'][:, idx, :], (0, 1, 3, 2)))
    cwt = np.concatenate([I['b_conv_w'], I['b_conv_b'][:, None, :]], axis=1)
    convw = np.ascontiguousarray(np.transpose(cwt.reshape(2, 5, 6, 128), (0, 3, 2, 1)))
    cst, band, ident = _consts()
    maps = []
    for c in range(8):
        b = c % 4
        ss = slice(c * NS, (c + 1) * NS)
        m = dict(
            xp=I['x_prompt'][b], xs=I['x_sample'][ss].reshape(NS * TS, DM), memp=I['mem_prompt'][b],
            ca_k=I['cache_a_k'][:, ss].reshape(2, NS, 512, 512), ca_v=I['cache_a_v'][:, ss].reshape(2, NS, 512, 512),
            cc_k=I['cache_c_k'][:, ss].reshape(2, NS, 1024, 512), cc_v=I['cache_c_v'][:, ss].reshape(2, NS, 1024, 512),
            cc_f=I['cache_c_logf'][:, ss], sb_ssm=I['state_b_ssm'][:, ss], sb_conv=I['state_b_conv'][:, ss],
            cm_k=I['cache_mem_k'][:, ss].reshape(2, NS, 256, 256), cm_v=I['cache_mem_v'][:, ss].reshape(2, NS, 256, 256),
            w_in=I['w_in'], w_mkv=I['w_mkv'], w_pa=I['w_pa'], w_pb=I['w_pb'], w_pc=I['w_pc'], w_pm=I['w_pm'],
            w_out=I['w_out'], small=small, mnorm=I['m_norm'], band=band, ident=ident, relb=relb, convw=convw, cst=cst)
        maps.append({k: np.ascontiguousarray(v) for k, v in m.items()})
    return maps


_NC_CACHE = {}


def kernel(**inputs):
    cfg = {}
    key = 'full'
    if key not in _NC_CACHE:
        _NC_CACHE[key] = build(cfg)
    nc = _NC_CACHE[key]
    maps = _prep(inputs)
    res = run_bass_kernel_spmd(nc, maps, core_ids=list(range(8)))
    R = res.results
    P4 = range(4)
    st = lambda name, shape: np.stack([R[b][name] for b in P4], axis=1).reshape(shape)
    cat = lambda name: np.concatenate([R[c][name] for c in range(8)], axis=1)
    y_prompt = np.stack([R[b]['yp'] for b in P4], axis=0)
    y_sample = np.concatenate([R[c]['ys'].reshape(NS, TS, DM) for c in range(8)], axis=0)
    outs = [y_prompt, y_sample,
            st('pa_k', (2, 4, 512, 8, 64)), st('pa_v', (2, 4, 512, 8, 64)),
            st('pc_k', (2, 4, SEQ, 8, 64)), st('pc_v', (2, 4, SEQ, 8, 64)), st('pc_f', (2, 4, SEQ, 8)),
            st('pb_s', (2, 4, 8, 64, 64)), st('pb_c', (2, 4, 3, 768)),
            st('pm_k', (2, 4, 256, 4, 64)), st('pm_v', (2, 4, 256, 4, 64))]
    for name, tail in (('sa_k', (8, 64)), ('sa_v', (8, 64)), ('sc_k', (8, 64)), ('sc_v', (8, 64)), ('sc_f', (8,))):
        a = np.concatenate([R[c][name].reshape((2, NS, TS) + tail) for c in range(8)], axis=1)
        outs.append(a)
    outs.append(cat('sb_s'))
    outs.append(cat('sb_c'))
    return tuple(np.ascontiguousarray(o.astype(np.float32)) for o in outs)
```
